# Optimizing a Trainium2 kernel written in Bass

```python
import math
import jax, jax.numpy as jnp
from jax import lax
import numpy as np

D_MODEL = 4096
BATCH = 2
SEQ = 4096
DEPTH = 2

HEAD_DIM = 128
NORM_EPS = 1e-6
L2_EPS = 1e-6
GRID_W = 64
Q_BLOCK = 128
N_EVEN = (DEPTH + 1) // 2
N_ODD = DEPTH // 2

A_HEADS = 16
A_DK = 128
A_DV = 128
A_QK = A_HEADS * A_DK
A_V = A_HEADS * A_DV
A_QKV_CH = 2 * A_QK + A_V
A_CONV_W = 5
A_CHUNK = 64

B_PATTERNS = ((128, 1), (512, 4), (2048, 16))
B_GROUPS = len(B_PATTERNS)
B_HEADS_PER_GROUP = 8
B_HEADS = B_GROUPS * B_HEADS_PER_GROUP
B_WIDTH = B_HEADS * HEAD_DIM
B_OUT = B_HEADS_PER_GROUP * HEAD_DIM
B_ROT_DIM = HEAD_DIM // 4
B_ROPE_THETA = 500000.0

AB_IN = A_QKV_CH + A_V + 4 * A_HEADS + 3 * B_WIDTH
AB_OUT = A_V + B_OUT

C_Q_HEADS = 32
C_KV_HEADS = 8
C_REP = C_Q_HEADS // C_KV_HEADS
C_QKV = (C_Q_HEADS + 2 * C_KV_HEADS) * HEAD_DIM
C_ROPE_THETA = 10000.0

D_FF = ((-(-8 * D_MODEL // 3)) + 255) // 256 * 256

kernel_name = "hybrid_deltanet_dilated_axial_gqa_encoder"


def _rms_norm(x, w):
    xf = x.astype(jnp.float32)
    y = xf * lax.rsqrt(jnp.mean(xf * xf, axis=-1, keepdims=True) + NORM_EPS)
    return (y * w.astype(jnp.float32)).astype(x.dtype)


def _l2norm(x):
    xf = x.astype(jnp.float32)
    return xf * lax.rsqrt(jnp.sum(xf * xf, axis=-1, keepdims=True) + L2_EPS)


def _rope_cos_sin(pos, dim, theta):
    inv = 1.0 / (theta ** (jnp.arange(0, dim, 2, dtype=jnp.float32) / dim))
    ang = pos.astype(jnp.float32)[:, None] * inv[None, :]
    return jnp.cos(ang), jnp.sin(ang)


def _apply_rope(x, cos, sin):
    x1, x2 = jnp.split(x, 2, axis=-1)
    c = cos[:, None, :]
    s = sin[:, None, :]
    return jnp.concatenate([x1 * c - x2 * s, x2 * c + x1 * s], axis=-1).astype(x.dtype)


def _centred_depthwise_conv(x, w):
    pad = w.shape[0] // 2
    return lax.conv_general_dilated(
        x, w[:, None, :].astype(x.dtype), window_strides=(1,), padding=((pad, pad),),
        dimension_numbers=("NWC", "WIO", "NWC"), feature_group_count=x.shape[-1])


def _gated_delta_chunked(q, k, v, g, beta):
    bsz, seq, nh, dk = q.shape
    dv = v.shape[-1]
    c = A_CHUNK
    n = seq // c
    f32 = jnp.float32

    def chunks(t):
        t = t.astype(f32).reshape((bsz, n, c, nh) + t.shape[3:])
        return jnp.moveaxis(jnp.moveaxis(t, 1, 0), 3, 2)

    qc, kc, vc, gc, bc = (chunks(t) for t in (q, k, v, g, beta))
    gc = jnp.cumsum(gc, axis=-1)
    kb = kc * bc[..., None]
    vb = vc * bc[..., None]
    idx = jnp.arange(c)
    incl = idx[:, None] >= idx[None, :]
    strict = idx[:, None] > idx[None, :]
    decay = jnp.exp(jnp.where(incl, gc[..., :, None] - gc[..., None, :], -jnp.inf))
    a_mat = jnp.where(strict, jnp.einsum("nbhid,nbhjd->nbhij", kb, kc) * decay, 0.0)
    eye = jnp.eye(c, dtype=f32)
    t_mat = lax.linalg.triangular_solve(a_mat + eye, jnp.broadcast_to(eye, a_mat.shape),
                                        left_side=True, lower=True, unit_diagonal=True)
    u = t_mat @ vb
    w = t_mat @ (kb * jnp.exp(gc)[..., None])
    attn = jnp.einsum("nbhid,nbhjd->nbhij", qc, kc) * decay

    def step(state, xs):
        q_i, k_i, u_i, w_i, attn_i, g_i = xs
        v_new = u_i - jnp.einsum("bhck,bhkv->bhcv", w_i, state)
        o_i = (jnp.einsum("bhck,bhkv->bhcv", q_i * jnp.exp(g_i)[..., None], state)
               + jnp.einsum("bhij,bhjv->bhiv", attn_i, v_new))
        g_last = g_i[..., -1:]
        state = (state * jnp.exp(g_last)[..., None]
                 + jnp.einsum("bhck,bhcv->bhkv", k_i * jnp.exp(g_last - g_i)[..., None], v_new))
        return state, o_i

    state0 = jnp.zeros((bsz, nh, dk, dv), f32)
    _, out = lax.scan(step, state0, (qc, kc, u, w, attn, gc))
    return jnp.transpose(out, (1, 0, 3, 2, 4)).reshape(bsz, seq, nh, dv).astype(v.dtype)


def _dilated_window_attention(q, k, v):
    bsz, seq, _, hg, dh = q.shape
    scale = dh ** -0.5
    offsets = [jnp.arange(-(wd // (2 * dl)), wd // (2 * dl) + 1) * dl for wd, dl in B_PATTERNS]
    k_groups = [k[:, :, gi] for gi in range(B_GROUPS)]
    v_groups = [v[:, :, gi] for gi in range(B_GROUPS)]

    def block(i):
        start = i * Q_BLOCK
        qpos = start + jnp.arange(Q_BLOCK)
        qb = lax.dynamic_slice_in_dim(q, start, Q_BLOCK, axis=1)
        outs, lses = [], []
        for gi in range(B_GROUPS):
            kpos = qpos[:, None] + offsets[gi][None, :]
            valid = (kpos >= 0) & (kpos < seq)
            kidx = jnp.clip(kpos, 0, seq - 1)
            kg = jnp.take(k_groups[gi], kidx, axis=1)
            vg = jnp.take(v_groups[gi], kidx, axis=1)
            s = jnp.einsum("bqhd,bqkhd->bhqk", qb[:, :, gi], kg).astype(jnp.float32) * scale
            s = jnp.where(valid[None, None], s, -jnp.inf)
            lse = jax.nn.logsumexp(s, axis=-1)
            p = jnp.exp(s - lse[..., None]).astype(vg.dtype)
            outs.append(jnp.einsum("bhqk,bqkhd->bqhd", p, vg).astype(jnp.float32))
            lses.append(lse)
        wts = jax.nn.softmax(jnp.stack(lses, axis=0), axis=0)
        wts = jnp.transpose(wts, (0, 1, 3, 2))[..., None]
        o = jnp.sum(jnp.stack(outs, axis=0) * wts, axis=0)
        return o.astype(q.dtype)

    out = lax.map(block, jnp.arange(seq // Q_BLOCK))
    return jnp.moveaxis(out, 0, 1).reshape(bsz, seq, hg * dh)


def _mixer_ab(h, w_in, conv_w, a_log, dt_bias, out_norm, w_out, cos_b, sin_b):
    bsz, seq, _ = h.shape
    proj = h @ w_in
    cuts = np.cumsum([A_QKV_CH, A_V, 2 * A_HEADS, 2 * A_HEADS, B_WIDTH, B_WIDTH]).tolist()
    a_qkv, a_z, a_beta, a_alpha, b_q, b_k, b_v = jnp.split(proj, cuts, axis=-1)

    a_qkv = jax.nn.silu(_centred_depthwise_conv(a_qkv, conv_w))
    aq, ak, av = jnp.split(a_qkv, [A_QK, 2 * A_QK], axis=-1)
    aq = _l2norm(aq.reshape(bsz, seq, A_HEADS, A_DK)) * (A_DK ** -0.5)
    ak = _l2norm(ak.reshape(bsz, seq, A_HEADS, A_DK))
    av = av.reshape(bsz, seq, A_HEADS, A_DV)
    beta = jax.nn.sigmoid(a_beta.astype(jnp.float32)).reshape(bsz, seq, 2, A_HEADS)
    g = -jnp.exp(a_log.astype(jnp.float32)) * jax.nn.softplus(
        a_alpha.astype(jnp.float32).reshape(bsz, seq, 2, A_HEADS) + dt_bias.astype(jnp.float32))
    o_fwd = _gated_delta_chunked(aq, ak, av, g[:, :, 0], beta[:, :, 0])
    flip = lambda t: jnp.flip(t, axis=1)
    o_bwd = flip(_gated_delta_chunked(flip(aq), flip(ak), flip(av), flip(g[:, :, 1]), flip(beta[:, :, 1])))
    o_a = _rms_norm(o_fwd + o_bwd, out_norm) * jax.nn.silu(a_z.reshape(bsz, seq, A_HEADS, A_DV))
    o_a = o_a.reshape(bsz, seq, A_V)

    def rot(t):
        t = t.reshape(bsz, seq, B_HEADS, HEAD_DIM)
        t = jnp.concatenate([_apply_rope(t[..., :B_ROT_DIM], cos_b, sin_b), t[..., B_ROT_DIM:]], axis=-1)
        return t.reshape(bsz, seq, B_GROUPS, B_HEADS_PER_GROUP, HEAD_DIM)
    bq, bk = rot(b_q), rot(b_k)
    bv = b_v.reshape(bsz, seq, B_GROUPS, B_HEADS_PER_GROUP, HEAD_DIM)
    o_b = _dilated_window_attention(bq, bk, bv)

    return jnp.concatenate([o_a.astype(h.dtype), o_b.astype(h.dtype)], axis=-1) @ w_out


def _mixer_c(h, w_qkv, q_norm, k_norm, w_out, cos_r, sin_r, cos_c, sin_c):
    bsz, seq, _ = h.shape
    proj = h @ w_qkv
    q, k, v = jnp.split(proj, [C_Q_HEADS * HEAD_DIM, (C_Q_HEADS + C_KV_HEADS) * HEAD_DIM], axis=-1)
    q = _rms_norm(q.reshape(bsz, seq, C_Q_HEADS, HEAD_DIM), q_norm)
    k = _rms_norm(k.reshape(bsz, seq, C_KV_HEADS, HEAD_DIM), k_norm)
    v = v.reshape(bsz, seq, C_KV_HEADS, HEAD_DIM)
    half = HEAD_DIM // 2

    def axial(t):
        return jnp.concatenate([_apply_rope(t[..., :half], cos_r, sin_r),
                                _apply_rope(t[..., half:], cos_c, sin_c)], axis=-1)
    q = axial(q) * (HEAD_DIM ** -0.5)
    k = axial(k)
    qg = q.reshape(bsz, seq, C_KV_HEADS, C_REP, HEAD_DIM)

    def block(i):
        qb = lax.dynamic_slice_in_dim(qg, i * Q_BLOCK, Q_BLOCK, axis=1)
        s = jnp.einsum("bqgrd,bkgd->bgrqk", qb, k).astype(jnp.float32)
        p = jax.nn.softmax(s, axis=-1).astype(v.dtype)
        return jnp.einsum("bgrqk,bkgd->bqgrd", p, v)

    o = lax.map(block, jnp.arange(seq // Q_BLOCK))
    o = jnp.moveaxis(o, 0, 1).reshape(bsz, seq, C_Q_HEADS * HEAD_DIM)
    return o @ w_out


def _swiglu(h, w_gate, w_up, w_down):
    return (jax.nn.silu(h @ w_gate) * (h @ w_up)) @ w_down


def setup_inputs(seed: int = 0) -> dict:
    key = jax.random.key(seed)
    ks = jax.random.split(key, 20)
    f32 = jnp.float32

    def dense(k, shape, fan_in):
        return jax.random.normal(k, shape, f32) * (fan_in ** -0.5)

    def gain(k, shape):
        return 1.0 + 0.05 * jax.random.normal(k, shape, f32)

    x = jax.random.normal(ks[0], (BATCH, SEQ, D_MODEL), f32)
    norm_mix = gain(ks[1], (DEPTH, D_MODEL))
    norm_ffn = gain(ks[2], (DEPTH, D_MODEL))
    norm_final = gain(ks[3], (D_MODEL,))
    ab_w_in = dense(ks[4], (N_EVEN, D_MODEL, AB_IN), D_MODEL)
    ab_conv_w = dense(ks[5], (N_EVEN, A_CONV_W, A_QKV_CH), A_CONV_W)
    ab_a_log = jnp.log(jax.random.uniform(ks[6], (N_EVEN, 2, A_HEADS), f32, 1.0, 16.0))
    dt = jnp.exp(jax.random.uniform(ks[7], (N_EVEN, 2, A_HEADS), f32, math.log(1e-3), math.log(1e-1)))
    ab_dt_bias = dt + jnp.log(-jnp.expm1(-dt))
    ab_out_norm = gain(ks[8], (N_EVEN, A_DV))
    ab_w_out = dense(ks[9], (N_EVEN, AB_OUT, D_MODEL), AB_OUT)
    c_w_qkv = dense(ks[10], (N_ODD, D_MODEL, C_QKV), D_MODEL)
    c_q_norm = gain(ks[11], (N_ODD, HEAD_DIM))
    c_k_norm = gain(ks[12], (N_ODD, HEAD_DIM))
    c_w_out = dense(ks[13], (N_ODD, C_Q_HEADS * HEAD_DIM, D_MODEL), C_Q_HEADS * HEAD_DIM)
    ffn_w_gate = dense(ks[14], (DEPTH, D_MODEL, D_FF), D_MODEL)
    ffn_w_up = dense(ks[15], (DEPTH, D_MODEL, D_FF), D_MODEL)
    ffn_w_down = dense(ks[16], (DEPTH, D_FF, D_MODEL), D_FF)
    return {"x": x, "norm_mix": norm_mix, "norm_ffn": norm_ffn, "norm_final": norm_final,
            "ab_w_in": ab_w_in, "ab_conv_w": ab_conv_w, "ab_a_log": ab_a_log,
            "ab_dt_bias": ab_dt_bias, "ab_out_norm": ab_out_norm, "ab_w_out": ab_w_out,
            "c_w_qkv": c_w_qkv, "c_q_norm": c_q_norm, "c_k_norm": c_k_norm, "c_w_out": c_w_out,
            "ffn_w_gate": ffn_w_gate, "ffn_w_up": ffn_w_up, "ffn_w_down": ffn_w_down}


def reference(x, norm_mix, norm_ffn, norm_final, ab_w_in, ab_conv_w, ab_a_log, ab_dt_bias,
              ab_out_norm, ab_w_out, c_w_qkv, c_q_norm, c_k_norm, c_w_out,
              ffn_w_gate, ffn_w_up, ffn_w_down):
    seq = x.shape[1]
    rows = seq // GRID_W
    tok = jnp.arange(seq)
    row_pos = jnp.repeat(jnp.arange(rows), GRID_W)
    col_pos = jnp.tile(jnp.arange(GRID_W), rows)
    cos_b, sin_b = _rope_cos_sin(tok, B_ROT_DIM, B_ROPE_THETA)
    cos_r, sin_r = _rope_cos_sin(row_pos, HEAD_DIM // 2, C_ROPE_THETA)
    cos_c, sin_c = _rope_cos_sin(col_pos, HEAD_DIM // 2, C_ROPE_THETA)

    for layer in range(DEPTH):
        j = layer // 2
        h = _rms_norm(x, norm_mix[layer])
        if layer % 2 == 0:
            x = x + _mixer_ab(h, ab_w_in[j], ab_conv_w[j], ab_a_log[j], ab_dt_bias[j],
                              ab_out_norm[j], ab_w_out[j], cos_b, sin_b)
        else:
            x = x + _mixer_c(h, c_w_qkv[j], c_q_norm[j], c_k_norm[j], c_w_out[j],
                             cos_r, sin_r, cos_c, sin_c)
        h = _rms_norm(x, norm_ffn[layer])
        x = x + _swiglu(h, ffn_w_gate[layer], ffn_w_up[layer], ffn_w_down[layer])
    return _rms_norm(x, norm_final)
```

```python
from contextlib import ExitStack
import numpy as np
import concourse.bass as bass
import concourse.mybir as mybir
from concourse.bass_utils import run_bass_kernel_spmd

F32 = mybir.dt.float32
BF16 = mybir.dt.bfloat16
AF = mybir.ActivationFunctionType
ALU = mybir.AluOpType
AX = mybir.AxisListType

ENGS = ("pe", "act", "dve", "pool", "sp")
NS_DMA = 6
SAME_ENGINE_SYNC = True


class Reg:
    __slots__ = ("name", "w", "r", "rd", "psum")

    def __init__(self, name="", psum=False):
        self.name = name
        self.psum = psum
        self.w = None
        self.r = {}
        self.rd = []


class Ins:
    __slots__ = ("eng", "fn", "deps", "inc", "dma", "idx", "sem", "target", "cnt")


class Prog:
    def __init__(self, nc):
        self.nc = nc
        self.q = {e: [] for e in ENGS}
        self.dmas = {e: [] for e in ENGS}

    def op(self, eng, meth, kw, reads=(), writes=(), dma=False):
        ins = Ins()
        ins.eng = eng
        ins.fn = (meth, kw)
        ins.dma = dma
        ins.inc = dma
        ins.idx = len(self.q[eng])
        ins.cnt = 0
        deps = set()
        for r in reads:
            if r.w is not None:
                deps.add(r.w)
            if r.psum:
                for e2, i2 in r.r.items():
                    if e2 != eng:
                        deps.add(i2)
        for w in writes:
            if w.w is not None:
                deps.add(w.w)
            deps.update(w.r.values())
            deps.update(w.rd)
        for r in reads:
            if dma:
                r.rd.append(ins)
            else:
                r.r[eng] = ins
        for w in writes:
            w.w = ins
            w.r = {}
            w.rd = []
        if dma:
            lst = self.dmas[eng]
            if len(lst) >= NS_DMA:
                deps.add(lst[len(lst) - NS_DMA])
            lst.append(ins)
        deps.discard(ins)
        fin = []
        for d in deps:
            if d.eng == eng and not d.dma:
                if eng == "pe" or not SAME_ENGINE_SYNC:
                    continue
            d.inc = True
            fin.append(d)
        ins.deps = fin
        self.q[eng].append(ins)
        return ins

    def emit(self, final_waits=()):
        nc = self.nc
        with ExitStack() as es:
            csem = {e: es.enter_context(nc.semaphore("c_" + e)) for e in ENGS}
            dsem = {e: [es.enter_context(nc.semaphore("d_%s%d" % (e, i))) for i in range(NS_DMA)]
                    for e in ENGS if self.dmas[e]}
            for e in ENGS:
                c = 0
                k = 0
                for ins in self.q[e]:
                    if ins.dma:
                        ins.sem = dsem[e][k % NS_DMA]
                        ins.target = 16 * (k // NS_DMA + 1)
                        k += 1
                    elif ins.inc:
                        c += 1
                        ins.cnt = c
            block = es.enter_context(nc.Block())

            def run(e, eng):
                waited = {}
                for ins in self.q[e]:
                    need = {}
                    for d in ins.deps:
                        if d.dma:
                            key = d.sem
                            val = d.target
                        else:
                            key = csem[d.eng]
                            val = d.cnt
                        if need.get(key, 0) < val:
                            need[key] = val
                    for key, val in need.items():
                        if waited.get(key, 0) < val:
                            eng.wait_ge(key, val)
                            waited[key] = val
                    inst = getattr(eng, ins.fn[0])(**ins.fn[1])
                    if ins.dma:
                        inst.then_inc(ins.sem, 16)
                    elif ins.inc:
                        inst.then_inc(csem[e], 1)
                if self.dmas[e]:
                    lst = self.dmas[e]
                    for d in lst[-NS_DMA:]:
                        if waited.get(d.sem, 0) < d.target:
                            eng.wait_ge(d.sem, d.target)
                            waited[d.sem] = d.target

            @block.tensor
            def _(eng):
                run("pe", eng)

            @block.scalar
            def _(eng):
                run("act", eng)

            @block.vector
            def _(eng):
                run("dve", eng)

            @block.gpsimd
            def _(eng):
                run("pool", eng)

            @block.sync
            def _(eng):
                run("sp", eng)


def tp_build(cfg):
    D = cfg.get("D", 4096)
    KC = D // 128
    T = cfg.get("T", 512)
    T_tot = cfg["T_tot"]
    NTT = T // 128
    oK = cfg.get("oproj_K", 0)
    dff = cfg.get("dff", 0)
    inproj = cfg.get("inproj")
    final_norm = cfg.get("final_norm", False)
    store_x = cfg.get("store_x", False)
    FB = 8

    nc = bass.Bass("TRN2", target_bir_lowering=False)
    din = lambda name, shape, dt=F32: nc.dram_tensor(name, shape, dt, kind="ExternalInput").ap()
    dout = lambda name, shape, dt=F32: nc.dram_tensor(name, shape, dt, kind="ExternalOutput").ap()
    xT = din("xT", [D, T_tot])
    if oK:
        oT = din("oT", [oK, T_tot], BF16)
        Wo = din("Wo", [oK, D])
    if dff:
        nwa = din("nwa", [128, KC])
        Wg = din("Wg", [D, dff])
        Wu = din("Wu", [D, dff])
        Wd = din("Wd", [dff, D])
    if inproj or final_norm:
        nwb = din("nwb", [128, KC])
    if store_x:
        xoT = dout("xoT", [D, T_tot])
    if final_norm:
        outT = dout("outT", [D, T_tot])
    if inproj == "ab":
        ab = cfg["ab"]
        ncols = ab["nqkv"] + ab["nz"] + ab["nab"] + ab["nbqk"] + ab["nbv"]
        Win = din("Win", [D, ncols])
        ropeC = din("ropeC", [32, T_tot])
        ropeS = din("ropeS", [32, T_tot])
        ropeP = din("ropeP", [32, 32])
        aqkvT = dout("aqkvT", [ab["nqkv"], T_tot])
        az = dout("az", [T_tot, ab["nz"]])
        abo = dout("ab", [T_tot, ab["nab"]])
        bqkT = dout("bqkT", [ab["nbqk"], T_tot], BF16)
        bv = dout("bv", [T_tot, ab["nbv"]], BF16)
    if inproj == "c":
        cc = cfg["c"]
        ncols = cc["nq"] + cc["nk"] + cc["nv"]
        Win = din("Win", [D, ncols])
        ropeC = din("ropeC", [128, T_tot])
        ropeS = din("ropeS", [128, T_tot])
        ropeP = din("ropeP", [128, 128])
        gq = din("gq", [128, 1])
        gk = din("gk", [128, 1])
        cqkT = dout("cqkT", [cc["nq"] + cc["nk"], T_tot], BF16)
        cv = dout("cv", [T_tot, cc["nv"]], BF16)

    P = Prog(nc)
    with ExitStack() as es:
        es.enter_context(nc.allow_low_precision("bf16 matmul operands, fp32 accumulate"))
        sb = lambda name, shape, dt: es.enter_context(nc.sbuf_tensor(name, shape, dt))
        psb = lambda name: es.enter_context(nc.psum_tensor(name, [128, 512], F32))
        R = Reg
        xs = sb("xs", [128, KC, T], F32)
        hT = sb("hT", [128, KC, T], BF16)
        ones = sb("ones", [128, 128], BF16)
        epsb = sb("epsb", [128, 1], F32)
        sq = [sb("sq%d" % i, [128, T], BF16) for i in range(2)]
        rstd = sb("rstd", [128, T], F32)
        st = [sb("st%d" % i, [128, T], F32) for i in range(2)]
        stb = [sb("stb%d" % i, [128, T], BF16) for i in range(2)]
        aT = [sb("aT%d" % i, [128, FB, T], BF16) for i in range(2)]
        wA = [sb("wA%d" % i, [128, KC, 128], BF16) for i in range(4)]
        wd = [sb("wd%d" % i, [128, FB, 512], BF16) for i in range(2)]
        nwa_s = sb("nwa_s", [128, KC], F32)
        nwb_s = sb("nwb_s", [128, KC], F32)
        ps_ss = psb("ps_ss")
        psA = [psb("psA%d" % i) for i in range(4)]
        ps_y = [psb("ps_y%d" % i) for i in range(2)]
        r_xs = [R() for c in range(KC)]
        r_hT = [R() for c in range(KC)]
        r_ones, r_eps, r_rstd, r_ss, r_nwa, r_nwb = R(), R(), R(), R(psum=True), R(), R()
        r_sq = [R(), R()]
        r_st = [R(), R()]
        r_stb = [R(), R()]
        r_aT = [[R() for j in range(FB)] for i in range(2)]
        r_wA = [R() for i in range(4)]
        r_wd = [R(), R()]
        r_psA = [R(psum=True) for i in range(4)]
        r_py = [R(psum=True), R(psum=True)]
        cnt = dict(wA=0, wd=0, psA=0, py=0, st=0, stb=0)

        def nxt(k, n):
            v = cnt[k] % n
            cnt[k] += 1
            return v

        P.op("pool", "memset", dict(ap=ones[:], constant=1.0), writes=[r_ones])
        P.op("pool", "memset", dict(ap=epsb[:], constant=1e-6), writes=[r_eps])
        if dff:
            P.op("sp", "dma_start", dict(out=nwa_s[:], in_=nwa), writes=[r_nwa], dma=True)
        if inproj or final_norm:
            P.op("sp", "dma_start", dict(out=nwb_s[:], in_=nwb), writes=[r_nwb], dma=True)
        if inproj == "ab":
            rC = sb("rC", [32, T_tot], F32)
            rS = sb("rS", [32, T_tot], F32)
            rP = sb("rP", [32, 32], BF16)
            t1 = sb("t1", [32, T], F32)
            t2 = sb("t2", [32, T], F32)
            r_rope, r_t1, r_t2 = R(), R(), R()
            P.op("sp", "dma_start", dict(out=rC[:], in_=ropeC), writes=[r_rope], dma=True)
            P.op("sp", "dma_start", dict(out=rS[:], in_=ropeS), writes=[r_rope], dma=True)
            P.op("pool", "dma_start", dict(out=rP[:], in_=ropeP), writes=[r_rope], dma=True)
        if inproj == "c":
            rC = sb("rC", [128, T_tot], F32)
            rS = sb("rS", [128, T_tot], F32)
            rP = sb("rP", [128, 128], BF16)
            gq_s = sb("gq_s", [128, 1], F32)
            gk_s = sb("gk_s", [128, 1], F32)
            t1 = sb("t1", [128, T], F32)
            t2 = sb("t2", [128, T], F32)
            rn = sb("rn", [128, T], F32)
            r_rope, r_t1, r_t2, r_rn = R(), R(), R(), R()
            P.op("sp", "dma_start", dict(out=rC[:], in_=ropeC), writes=[r_rope], dma=True)
            P.op("sp", "dma_start", dict(out=rS[:], in_=ropeS), writes=[r_rope], dma=True)
            P.op("pool", "dma_start", dict(out=rP[:], in_=ropeP), writes=[r_rope], dma=True)
            P.op("sp", "dma_start", dict(out=gq_s[:], in_=gq), writes=[r_rope], dma=True)
            P.op("sp", "dma_start", dict(out=gk_s[:], in_=gk), writes=[r_rope], dma=True)

        xT_v = xT.rearrange("(c p) t -> p c t", p=128)

        def rmsnorm(nw_s, r_nw, to_x=False):
            for c in range(KC):
                s = c % 2
                P.op("act", "activation", dict(out=sq[s][:], in_=xs[:, c, :], func=AF.Square),
                     reads=[r_xs[c]], writes=[r_sq[s]])
                P.op("pe", "matmul", dict(out=ps_ss[:], lhsT=ones[:], rhs=sq[s][:], start=(c == 0), stop=(c == KC - 1)),
                     reads=[r_ones, r_sq[s]], writes=[r_ss])
            P.op("act", "activation", dict(out=rstd[:], in_=ps_ss[:], func=AF.Sqrt, scale=1.0 / D, bias=epsb[:, 0:1]),
                 reads=[r_ss, r_eps], writes=[r_rstd])
            P.op("dve", "reciprocal", dict(out=rstd[:], in_=rstd[:]), reads=[r_rstd], writes=[r_rstd])
            for c in range(KC):
                if to_x:
                    P.op("dve", "scalar_tensor_tensor",
                         dict(out=xs[:, c, :], in0=xs[:, c, :], scalar=nw_s[:, c:c + 1], in1=rstd[:], op0=ALU.mult, op1=ALU.mult),
                         reads=[r_xs[c], r_nw, r_rstd], writes=[r_xs[c]])
                else:
                    P.op("dve", "scalar_tensor_tensor",
                         dict(out=hT[:, c, :], in0=xs[:, c, :], scalar=nw_s[:, c:c + 1], in1=rstd[:], op0=ALU.mult, op1=ALU.mult),
                         reads=[r_xs[c], r_nw, r_rstd], writes=[r_hT[c]])

        def accum_block(W_v, k0, nk, sl, rhs_regs):
            for cb in range(D // 512):
                w = nxt("wd", 2)
                P.op("pool", "dma_start", dict(out=wd[w][:, 0:nk, :], in_=W_v[:, k0:k0 + nk, cb * 512:(cb + 1) * 512]),
                     writes=[r_wd[w]], dma=True)
                for ct in range(4):
                    pb = nxt("py", 2)
                    for j in range(nk):
                        P.op("pe", "matmul", dict(out=ps_y[pb][:], lhsT=wd[w][:, j, ct * 128:(ct + 1) * 128], rhs=aT[sl][:, j, :],
                                                  start=(j == 0), stop=(j == nk - 1)),
                             reads=[r_wd[w], rhs_regs[j]], writes=[r_py[pb]])
                    c = cb * 4 + ct
                    P.op("dve", "tensor_tensor", dict(out=xs[:, c, :], in0=xs[:, c, :], in1=ps_y[pb][:], op=ALU.add),
                         reads=[r_xs[c], r_py[pb]], writes=[r_xs[c]])

        def fm_tile(W_v, col0, epilogue):
            w = nxt("wA", 4)
            P.op("pool", "dma_start", dict(out=wA[w][:], in_=W_v[:, :, col0:col0 + 128]), writes=[r_wA[w]], dma=True)
            pb = nxt("psA", 4)
            for kc in range(KC):
                P.op("pe", "matmul", dict(out=psA[pb][:], lhsT=wA[w][:, kc, :], rhs=hT[:, kc, :], start=(kc == 0), stop=(kc == KC - 1)),
                     reads=[r_wA[w], r_hT[kc]], writes=[r_psA[pb]])
            epilogue(psA[pb], r_psA[pb])

        def tm_tile(W_v, col0, ncl, out_ap, t0, dt, ocol):
            w = nxt("wA", 4)
            P.op("pool", "dma_start", dict(out=wA[w][:, :, 0:ncl], in_=W_v[:, :, col0:col0 + ncl]), writes=[r_wA[w]], dma=True)
            pb = nxt("psA", 4)
            for tt in range(NTT):
                for kc in range(KC):
                    P.op("pe", "matmul", dict(out=psA[pb][:, tt * 128:tt * 128 + ncl], lhsT=hT[:, kc, tt * 128:(tt + 1) * 128],
                                              rhs=wA[w][:, kc, 0:ncl], start=(kc == 0), stop=(kc == KC - 1)),
                         reads=[r_wA[w], r_hT[kc]], writes=[r_psA[pb]])
            src = psA[pb][:].rearrange("p (t c) -> p t c", c=128)[:, :, 0:ncl]
            if dt == F32:
                s = nxt("st", 2)
                buf, rb = st[s], r_st[s]
            else:
                s = nxt("stb", 2)
                buf, rb = stb[s], r_stb[s]
            dst = buf[:].rearrange("p (t c) -> p t c", c=128)[:, :, 0:ncl]
            P.op("act", "copy", dict(out=dst, in_=src), reads=[r_psA[pb]], writes=[rb])
            P.op("sp", "dma_start", dict(out=out_ap[t0:t0 + T, ocol:ocol + ncl].rearrange("(t p) c -> p t c", p=128), in_=dst),
                 reads=[rb], dma=True)

        for t0 in range(0, T_tot, T):
            for c in range(KC):
                P.op("sp", "dma_start", dict(out=xs[:, c, :], in_=xT_v[:, c, t0:t0 + T]), writes=[r_xs[c]], dma=True)
            if oK:
                oT_v = oT.rearrange("(c p) t -> p c t", p=128)
                Wo_v = Wo.rearrange("(c p) n -> p c n", p=128)
                nob = oK // 128
                blocks = [(k0, min(k0 + FB, nob)) for k0 in range(0, nob, FB)]
                for bi, (k0, k1) in enumerate(blocks):
                    sl = bi % 2
                    for j in range(k1 - k0):
                        P.op("sp", "dma_start", dict(out=aT[sl][:, j, :], in_=oT_v[:, k0 + j, t0:t0 + T]),
                             writes=[r_aT[sl][j]], dma=True)
                    accum_block(Wo_v, k0, k1 - k0, sl, r_aT[sl])
            if dff:
                Wg_v = Wg.rearrange("(c p) n -> p c n", p=128)
                Wu_v = Wu.rearrange("(c p) n -> p c n", p=128)
                Wd_v = Wd.rearrange("(c p) n -> p c n", p=128)
                rmsnorm(nwa_s, r_nwa)
                NF = dff // 128
                blocks = [(f0, min(f0 + FB, NF)) for f0 in range(0, NF, FB)]

                def gate_up(bi):
                    f0, f1 = blocks[bi]
                    sl = bi % 2
                    for f in range(f0, f1):
                        res = []

                        def keep(ps_t, r_t):
                            res.append((ps_t, r_t))
                        fm_tile(Wg_v, f * 128, keep)
                        fm_tile(Wu_v, f * 128, keep)
                        (pg, rg), (pu, ru) = res
                        s = nxt("st", 2)
                        P.op("act", "activation", dict(out=st[s][:], in_=pg[:], func=AF.Silu), reads=[rg], writes=[r_st[s]])
                        P.op("dve", "tensor_tensor", dict(out=aT[sl][:, f - f0, :], in0=st[s][:], in1=pu[:], op=ALU.mult),
                             reads=[r_st[s], ru], writes=[r_aT[sl][f - f0]])

                gate_up(0)
                for bi in range(1, len(blocks)):
                    gate_up(bi)
                    f0, f1 = blocks[bi - 1]
                    accum_block(Wd_v, f0, f1 - f0, (bi - 1) % 2, r_aT[(bi - 1) % 2])
                f0, f1 = blocks[-1]
                accum_block(Wd_v, f0, f1 - f0, (len(blocks) - 1) % 2, r_aT[(len(blocks) - 1) % 2])
            if store_x:
                xo_v = xoT.rearrange("(c p) t -> p c t", p=128)
                for c in range(KC):
                    P.op("sp", "dma_start", dict(out=xo_v[:, c, t0:t0 + T], in_=xs[:, c, :]), reads=[r_xs[c]], dma=True)
            if inproj:
                Win_v = Win.rearrange("(c p) n -> p c n", p=128)
                rmsnorm(nwb_s, r_nwb)

                def ep_copy(out_ap, row0, dt):
                    def ep(ps_t, r_t):
                        if dt == F32:
                            s = nxt("st", 2)
                            buf, rb = st[s], r_st[s]
                        else:
                            s = nxt("stb", 2)
                            buf, rb = stb[s], r_stb[s]
                        P.op("act", "copy", dict(out=buf[:], in_=ps_t[:]), reads=[r_t], writes=[rb])
                        P.op("sp", "dma_start", dict(out=out_ap[row0:row0 + 128, t0:t0 + T], in_=buf[:]), reads=[rb], dma=True)
                    return ep

            if inproj == "ab":
                def ep_rope_b(row0):
                    def ep(ps_t, r_t):
                        s = nxt("stb", 2)
                        buf, rb = stb[s], r_stb[s]
                        P.op("act", "copy", dict(out=buf[:], in_=ps_t[:]), reads=[r_t], writes=[rb])
                        pb = nxt("py", 2)
                        P.op("pe", "matmul", dict(out=ps_y[pb][0:32, :], lhsT=rP[:, :], rhs=buf[0:32, :], start=True, stop=True),
                             reads=[r_rope, rb], writes=[r_py[pb]])
                        P.op("dve", "tensor_tensor", dict(out=t1[:], in0=buf[0:32, :], in1=rC[:, t0:t0 + T], op=ALU.mult),
                             reads=[rb, r_rope], writes=[r_t1])
                        P.op("dve", "tensor_tensor", dict(out=t2[:], in0=ps_y[pb][0:32, :], in1=rS[:, t0:t0 + T], op=ALU.mult),
                             reads=[r_py[pb], r_rope], writes=[r_t2])
                        P.op("dve", "tensor_tensor", dict(out=buf[0:32, :], in0=t1[:], in1=t2[:], op=ALU.add),
                             reads=[r_t1, r_t2], writes=[rb])
                        P.op("sp", "dma_start", dict(out=bqkT[row0:row0 + 128, t0:t0 + T], in_=buf[:]), reads=[rb], dma=True)
                    return ep
                c0 = 0
                for i in range(ab["nqkv"] // 128):
                    fm_tile(Win_v, c0 + i * 128, ep_copy(aqkvT, i * 128, F32))
                c0 += ab["nqkv"]
                for i in range(ab["nz"] // 128):
                    tm_tile(Win_v, c0 + i * 128, 128, az, t0, F32, i * 128)
                c0 += ab["nz"]
                tm_tile(Win_v, c0, ab["nab"], abo, t0, F32, 0)
                c0 += ab["nab"]
                for i in range(ab["nbqk"] // 128):
                    fm_tile(Win_v, c0 + i * 128, ep_rope_b(i * 128))
                c0 += ab["nbqk"]
                for i in range(ab["nbv"] // 128):
                    tm_tile(Win_v, c0 + i * 128, 128, bv, t0, BF16, i * 128)
            if inproj == "c":
                def ep_c(row0, g_s):
                    def ep(ps_t, r_t):
                        s = nxt("stb", 2)
                        buf, rb = stb[s], r_stb[s]
                        P.op("act", "activation", dict(out=sq[0][:], in_=ps_t[:], func=AF.Square), reads=[r_t], writes=[r_sq[0]])
                        pb = nxt("py", 2)
                        P.op("pe", "matmul", dict(out=ps_y[pb][:], lhsT=ones[:], rhs=sq[0][:], start=True, stop=True),
                             reads=[r_ones, r_sq[0]], writes=[r_py[pb]])
                        P.op("act", "activation", dict(out=rn[:], in_=ps_y[pb][:], func=AF.Sqrt, scale=1.0 / 128, bias=epsb[:, 0:1]),
                             reads=[r_py[pb], r_eps], writes=[r_rn])
                        P.op("dve", "reciprocal", dict(out=rn[:], in_=rn[:]), reads=[r_rn], writes=[r_rn])
                        P.op("dve", "scalar_tensor_tensor",
                             dict(out=buf[:], in0=ps_t[:], scalar=g_s[:, 0:1], in1=rn[:], op0=ALU.mult, op1=ALU.mult),
                             reads=[r_t, r_rope, r_rn], writes=[rb])
                        pb2 = nxt("py", 2)
                        P.op("pe", "matmul", dict(out=ps_y[pb2][:], lhsT=rP[:, :], rhs=buf[:], start=True, stop=True),
                             reads=[r_rope, rb], writes=[r_py[pb2]])
                        P.op("dve", "tensor_tensor", dict(out=t1[:], in0=buf[:], in1=rC[:, t0:t0 + T], op=ALU.mult),
                             reads=[rb, r_rope], writes=[r_t1])
                        P.op("dve", "tensor_tensor", dict(out=t2[:], in0=ps_y[pb2][:], in1=rS[:, t0:t0 + T], op=ALU.mult),
                             reads=[r_py[pb2], r_rope], writes=[r_t2])
                        P.op("dve", "tensor_tensor", dict(out=buf[:], in0=t1[:], in1=t2[:], op=ALU.add),
                             reads=[r_t1, r_t2], writes=[rb])
                        P.op("sp", "dma_start", dict(out=cqkT[row0:row0 + 128, t0:t0 + T], in_=buf[:]), reads=[rb], dma=True)
                    return ep
                nqt = cc["nq"] // 128
                nkt = cc["nk"] // 128
                for i in range(nqt):
                    fm_tile(Win_v, i * 128, ep_c(i * 128, gq_s))
                for i in range(nkt):
                    fm_tile(Win_v, cc["nq"] + i * 128, ep_c(cc["nq"] + i * 128, gk_s))
                for i in range(cc["nv"] // 128):
                    tm_tile(Win_v, cc["nq"] + cc["nk"] + i * 128, 128, cv, t0, BF16, i * 128)
            if final_norm:
                rmsnorm(nwb_s, r_nwb, to_x=True)
                o_v = outT.rearrange("(c p) t -> p c t", p=128)
                for c in range(KC):
                    P.op("sp", "dma_start", dict(out=o_v[:, c, t0:t0 + T], in_=xs[:, c, :]), reads=[r_xs[c]], dma=True)
        P.emit()
    return nc


def attn_c_build(cfg):
    S = cfg.get("S", 4096)
    NKV = cfg.get("NKV", 2)
    REP = cfg.get("REP", 4)
    NQH = NKV * REP
    NKC = S // 128
    QB = 512
    scale = 128 ** -0.5
    nc = bass.Bass("TRN2", target_bir_lowering=False)
    cqT = nc.dram_tensor("cqT", [NQH * 128, S], BF16, kind="ExternalInput").ap()
    ckT = nc.dram_tensor("ckT", [NKV * 128, S], BF16, kind="ExternalInput").ap()
    cv = nc.dram_tensor("cv", [S, NKV * 128], BF16, kind="ExternalInput").ap()
    ocT = nc.dram_tensor("ocT", [NQH * 128, S], BF16, kind="ExternalOutput").ap()
    P = Prog(nc)
    with ExitStack() as es:
        es.enter_context(nc.allow_low_precision("bf16 matmul operands, fp32 accumulate"))
        sb = lambda name, shape, dt: es.enter_context(nc.sbuf_tensor(name, shape, dt))
        psb = lambda name: es.enter_context(nc.psum_tensor(name, [128, 512], F32))
        R = Reg
        kT = [sb("kT%d" % i, [128, S], BF16) for i in range(2)]
        vv = [sb("v%d" % i, [128, NKC, 128], BF16) for i in range(2)]
        qT = [sb("qT%d" % i, [128, S], BF16) for i in range(2)]
        NE = 3
        ee = [sb("e%d" % i, [128, QB], BF16) for i in range(NE)]
        rz = sb("rz", [128, QB], F32)
        ob = [sb("ob%d" % i, [128, QB], BF16) for i in range(2)]
        ones = sb("ones", [128, 128], BF16)
        ps_s = [psb("ps_s%d" % i) for i in range(NE)]
        ps_o = [psb("ps_o%d" % i) for i in range(2)]
        ps_z = [psb("ps_z%d" % i) for i in range(2)]
        r_kT, r_v, r_qT = [R(), R()], [R(), R()], [R(), R()]
        r_e = [R() for i in range(NE)]
        r_rz, r_ones = R(), R()
        r_ob = [R(), R()]
        r_ps = [R(psum=True) for i in range(NE)]
        r_po, r_pz = [R(psum=True), R(psum=True)], [R(psum=True), R(psum=True)]
        P.op("pool", "memset", dict(ap=ones[:], constant=1.0), writes=[r_ones])
        it = 0
        blk = 0
        for kv in range(NKV):
            ks = kv % 2
            P.op("sp", "dma_start", dict(out=kT[ks][:], in_=ckT[kv * 128:(kv + 1) * 128, :]), writes=[r_kT[ks]], dma=True)
            P.op("sp", "dma_start", dict(out=vv[ks][:], in_=cv[:, kv * 128:(kv + 1) * 128].rearrange("(c p) d -> p c d", p=128)),
                 writes=[r_v[ks]], dma=True)
            for r in range(REP):
                h = kv * REP + r
                qs = h % 2
                P.op("sp", "dma_start", dict(out=qT[qs][:], in_=cqT[h * 128:(h + 1) * 128, :]), writes=[r_qT[qs]], dma=True)
                for qb in range(S // QB):
                    pb = blk % 2
                    blk += 1
                    qsl = qT[qs][:, qb * QB:(qb + 1) * QB]

                    def smm(kc, i):
                        P.op("pe", "matmul", dict(out=ps_s[i][:], lhsT=kT[ks][:, kc * 128:(kc + 1) * 128], rhs=qsl, start=True, stop=True),
                             reads=[r_kT[ks], r_qT[qs]], writes=[r_ps[i]])
                    smm(0, it % NE)
                    for kc in range(NKC):
                        i = it % NE
                        it += 1
                        if kc + 1 < NKC:
                            smm(kc + 1, it % NE)
                        P.op("act", "activation", dict(out=ee[i][:], in_=ps_s[i][:], func=AF.Exp, scale=scale),
                             reads=[r_ps[i]], writes=[r_e[i]])
                        P.op("pe", "matmul", dict(out=ps_o[pb][:], lhsT=vv[ks][:, kc, :], rhs=ee[i][:], start=(kc == 0), stop=(kc == NKC - 1)),
                             reads=[r_v[ks], r_e[i]], writes=[r_po[pb]])
                        P.op("pe", "matmul", dict(out=ps_z[pb][:], lhsT=ones[:], rhs=ee[i][:], start=(kc == 0), stop=(kc == NKC - 1)),
                             reads=[r_ones, r_e[i]], writes=[r_pz[pb]])
                    P.op("dve", "reciprocal", dict(out=rz[:], in_=ps_z[pb][:]), reads=[r_pz[pb]], writes=[r_rz])
                    P.op("dve", "tensor_tensor", dict(out=ob[pb][:], in0=ps_o[pb][:], in1=rz[:], op=ALU.mult),
                         reads=[r_po[pb], r_rz], writes=[r_ob[pb]])
                    P.op("sp", "dma_start", dict(out=ocT[h * 128:(h + 1) * 128, qb * QB:(qb + 1) * QB], in_=ob[pb][:]),
                         reads=[r_ob[pb]], dma=True)
        P.emit()
    return nc


B_PATTERNS = ((128, 1), (512, 4), (2048, 16))


def ssl(a, n, d):
    return slice(a, a + d * (n - 1) + 1, d)


def dil_b_build(cfg):
    S = cfg.get("S", 4096)
    NHS = cfg.get("NHS", 2)
    pats = cfg.get("pats", B_PATTERNS)
    NG = len(pats)
    scale = 128 ** -0.5
    nc = bass.Bass("TRN2", target_bir_lowering=False)
    bqT = nc.dram_tensor("bqT", [NHS * NG * 128, S], BF16, kind="ExternalInput").ap()
    bkT = nc.dram_tensor("bkT", [NHS * NG * 128, S], BF16, kind="ExternalInput").ap()
    bv = nc.dram_tensor("bv", [S, NHS * NG * 128], BF16, kind="ExternalInput").ap()
    bmask = nc.dram_tensor("bmask", [128, 3, 512], BF16, kind="ExternalInput").ap()
    obT = nc.dram_tensor("obT", [NHS * 128, S], BF16, kind="ExternalOutput").ap()
    P = Prog(nc)
    with ExitStack() as es:
        es.enter_context(nc.allow_low_precision("bf16 matmul operands, fp32 accumulate"))
        sb = lambda name, shape, dt: es.enter_context(nc.sbuf_tensor(name, shape, dt))
        psb = lambda name: es.enter_context(nc.psum_tensor(name, [128, 512], F32))
        R = Reg
        qT = [sb("qT%d" % i, [128, S], BF16) for i in range(2)]
        kT = [sb("kT%d" % i, [128, S], BF16) for i in range(2)]
        vp = [sb("vp%d" % i, [128, S // 128, 128], BF16) for i in range(2)]
        Uacc = sb("Uacc", [128, S], F32)
        Zacc = sb("Zacc", [128, S], F32)
        ob = sb("ob", [128, S], BF16)
        ee = [sb("e%d" % i, [128, 512], BF16) for i in range(2)]
        em = [sb("em%d" % i, [128, 512], BF16) for i in range(2)]
        mk = sb("mk", [128, 3, 512], BF16)
        ones = sb("ones", [128, 128], BF16)
        ps_s = [psb("ps_s%d" % i) for i in range(2)]
        ps_o = [psb("ps_o%d" % i) for i in range(2)]
        ps_z = [psb("ps_z%d" % i) for i in range(2)]
        r_q, r_k, r_v = [R(), R()], [R(), R()], [R(), R()]
        r_U, r_Z, r_ob, r_mk, r_ones = R(), R(), R(), R(), R()
        r_e, r_em = [R(), R()], [R(), R()]
        r_ps, r_po, r_pz = [R(psum=True), R(psum=True)], [R(psum=True), R(psum=True)], [R(psum=True), R(psum=True)]
        P.op("pool", "memset", dict(ap=ones[:], constant=1.0), writes=[r_ones])
        P.op("sp", "dma_start", dict(out=mk[:], in_=bmask), writes=[r_mk], dma=True)
        gi = 0
        sc = 0
        bc = 0
        for hs in range(NHS):
            for g, (wd_, d) in enumerate(pats):
                s = gi % 2
                gi += 1
                row = (hs * NG + g) * 128
                L = S // d
                nblk = L // 128
                P.op("sp", "dma_start", dict(out=qT[s][:], in_=bqT[row:row + 128, :]), writes=[r_q[s]], dma=True)
                P.op("sp", "dma_start", dict(out=kT[s][:], in_=bkT[row:row + 128, :]), writes=[r_k[s]], dma=True)
                P.op("sp", "dma_start",
                     dict(out=vp[s][:].rearrange("p (r i) c -> p r i c", r=d),
                          in_=bv[:, row:row + 128].rearrange("(i p r) c -> p r i c", p=128, r=d)),
                     writes=[r_v[s]], dma=True)
                for r in range(d):
                    for i0 in range(0, nblk, 4):
                        nb = min(4, nblk - i0)
                        pb = bc % 2
                        bc += 1
                        for o in (0, -1, 1):
                            blo = 0
                            bhi = nb
                            if o == -1 and i0 == 0:
                                blo = 1
                            if o == 1 and i0 + nb == nblk:
                                bhi = nb - 1
                            if bhi <= blo:
                                continue
                            ss = sc % 2
                            sc += 1
                            for b in range(blo, bhi):
                                i = i0 + b
                                ka = r + d * 128 * (i + o)
                                qa = r + d * 128 * i
                                P.op("pe", "matmul", dict(out=ps_s[ss][:, b * 128:(b + 1) * 128],
                                                          lhsT=kT[s][:, ssl(ka, 128, d)], rhs=qT[s][:, ssl(qa, 128, d)],
                                                          start=True, stop=True),
                                     reads=[r_k[s], r_q[s]], writes=[r_ps[ss]])
                            cs = slice(blo * 128, bhi * 128)
                            P.op("act", "activation", dict(out=ee[ss][:, cs], in_=ps_s[ss][:, cs], func=AF.Exp, scale=scale),
                                 reads=[r_ps[ss]], writes=[r_e[ss]])
                            P.op("dve", "tensor_tensor", dict(out=em[ss][:, cs], in0=ee[ss][:, cs], in1=mk[:, o + 1, cs], op=ALU.mult),
                                 reads=[r_e[ss], r_mk], writes=[r_em[ss]])
                            for b in range(blo, bhi):
                                i = i0 + b
                                last = (o == 1) or (o == -1 and i == nblk - 1) or (o == 0 and nblk == 1)
                                bs = slice(b * 128, (b + 1) * 128)
                                P.op("pe", "matmul", dict(out=ps_o[pb][:, bs], lhsT=vp[s][:, r * nblk + i + o, :], rhs=em[ss][:, bs],
                                                          start=(o == 0 and b == 0), stop=last, skip_group_check=True),
                                     reads=[r_v[s], r_em[ss]], writes=[r_po[pb]])
                                P.op("pe", "matmul", dict(out=ps_z[pb][:, bs], lhsT=ones[:], rhs=em[ss][:, bs],
                                                          start=(o == 0 and b == 0), stop=last, skip_group_check=True),
                                     reads=[r_ones, r_em[ss]], writes=[r_pz[pb]])
                        a0 = r + d * 128 * i0
                        usl = Uacc[:, ssl(a0, 128 * nb, d)]
                        zsl = Zacc[:, ssl(a0, 128 * nb, d)]
                        if g == 0:
                            P.op("act", "copy", dict(out=usl, in_=ps_o[pb][:, 0:nb * 128]), reads=[r_po[pb]], writes=[r_U])
                            P.op("dve", "tensor_copy", dict(out=zsl, in_=ps_z[pb][:, 0:nb * 128]), reads=[r_pz[pb]], writes=[r_Z])
                        else:
                            P.op("dve", "tensor_tensor", dict(out=usl, in0=usl, in1=ps_o[pb][:, 0:nb * 128], op=ALU.add),
                                 reads=[r_po[pb], r_U], writes=[r_U])
                            P.op("dve", "tensor_tensor", dict(out=zsl, in0=zsl, in1=ps_z[pb][:, 0:nb * 128], op=ALU.add),
                                 reads=[r_pz[pb], r_Z], writes=[r_Z])
            P.op("dve", "reciprocal", dict(out=Zacc[:], in_=Zacc[:]), reads=[r_Z], writes=[r_Z])
            P.op("dve", "tensor_tensor", dict(out=ob[:], in0=Uacc[:], in1=Zacc[:], op=ALU.mult), reads=[r_U, r_Z], writes=[r_ob])
            P.op("sp", "dma_start", dict(out=obT[hs * 128:(hs + 1) * 128, :], in_=ob[:]), reads=[r_ob], dma=True)
        P.emit()
    return nc


def dil_mask():
    import numpy as _np
    m = _np.zeros((128, 3, 512), _np.float32)
    p = _np.arange(128)[:, None]
    n = _np.arange(128)[None, :]
    for o in (-1, 0, 1):
        mm = (_np.abs(128 * o + p - n) <= 64).astype(_np.float32)
        m[:, o + 1, :] = _np.tile(mm, (1, 4))
    return m

import numpy as _np

BIG = 30000.0


def gdn_consts():
    k = _np.arange(128)[:, None]
    i = _np.arange(128)[None, :]
    c = {}
    c["ident"] = _np.eye(128, dtype=_np.float32)
    c["ucum"] = _np.stack([(k <= i), (k >= i)], 1).astype(_np.float32)
    nmd_f = BIG * (k <= i)
    nmd_b = BIG * (k >= i)
    nmt_f = -BIG * (i < k)
    nmt_b = -BIG * (i > k)
    c["nm"] = _np.stack([nmd_f, nmt_f, nmd_b, nmt_b], 1).astype(_np.float32)
    return c


def gdn_build(cfg):
    S = cfg.get("S", 4096)
    NH = cfg.get("NH", 4)
    NCH = S // 128
    NB = S // 512
    STOP = cfg.get("stop", 9)
    CHD = F32 if cfg.get("chain_fp32", True) else BF16
    SUB = cfg.get("sub", 9)
    nc = bass.Bass("TRN2", target_bir_lowering=False)
    din = lambda name, shape, dt=F32: nc.dram_tensor(name, shape, dt, kind="ExternalInput").ap()
    aqkvT = din("aqkvT", [NH * 3 * 128, S])
    az = din("az", [S, NH * 128])
    abr = din("abr", [S, 4 * NH])
    cw = din("cw", [128, NH * 3, 5])
    alog = din("alog", [128, 2 * NH])
    dtb = din("dtb", [128, 2 * NH])
    onorm = din("onorm", [128, 128])
    ident_d = din("ident_in", [128, 128])
    ucum_d = din("ucum_in", [128, 2, 128])
    nm_d = din("nm_in", [128, 4, 128])
    oaT = nc.dram_tensor("oaT", [NH * 128, S], BF16, kind="ExternalOutput").ap()
    P = Prog(nc)
    NC2 = 2 * NH
    with ExitStack() as es:
        es.enter_context(nc.allow_low_precision("bf16 matmul operands, fp32 accumulate"))
        sb = lambda name, shape, dt: es.enter_context(nc.sbuf_tensor(name, shape, dt))
        psb = lambda name, dt=F32, n=512: es.enter_context(nc.psum_tensor(name, [128, n], dt))
        R = Reg
        identf = sb("identf", [128, 128], F32)
        identb = sb("identb", [128, 128], BF16)
        ucum = sb("ucum", [128, 2, 128], F32)
        nm = sb("nm", [128, 4, 128], F32)
        onesf = sb("onesf", [128, 128], F32)
        onesb = sb("onesb", [128, 128], BF16)
        epsb = sb("epsb", [128, 1], F32)
        oneb = sb("oneb", [128, 1], F32)
        cws = sb("cws", [128, NH * 3, 5], F32)
        onorm_s = sb("onorm_s", [128, 128], F32)
        r_c = R()
        identc = identf if CHD == F32 else identb
        P.op("sp", "dma_start", dict(out=identf[:], in_=ident_d), writes=[r_c], dma=True)
        P.op("pool", "dma_start", dict(out=identb[:], in_=ident_d), writes=[r_c], dma=True)
        P.op("sp", "dma_start", dict(out=ucum[:], in_=ucum_d), writes=[r_c], dma=True)
        P.op("sp", "dma_start", dict(out=nm[:], in_=nm_d), writes=[r_c], dma=True)
        P.op("sp", "dma_start", dict(out=cws[:], in_=cw), writes=[r_c], dma=True)
        P.op("sp", "dma_start", dict(out=onorm_s[:], in_=onorm), writes=[r_c], dma=True)
        P.op("pool", "memset", dict(ap=onesf[:], constant=1.0), writes=[r_c])
        P.op("pool", "memset", dict(ap=onesb[:], constant=1.0), writes=[r_c])
        P.op("pool", "memset", dict(ap=epsb[:], constant=1e-6), writes=[r_c])
        P.op("pool", "memset", dict(ap=oneb[:], constant=1.0), writes=[r_c])

        NCOL = NCH * NC2
        raw = sb("raw", [128, NCH, 2 * NC2], F32)
        alog_s = sb("alog_s", [128, NC2], F32)
        dtb_s = sb("dtb_s", [128, NC2], F32)
        beta = sb("beta", [128, NCH, NC2], F32)
        nbeta = sb("nbeta", [128, NCH, NC2], F32)
        gg = sb("gg", [128, NCH, NC2], F32)
        gc = sb("gc", [128, NCH, NC2], F32)
        ngc = sb("ngc", [128, NCH, NC2], F32)
        gtot = sb("gtot", [128, NCH, NC2], F32)
        egc = sb("egc", [128, NCH, NC2], F32)
        begc = sb("begc", [128, NCH, NC2], F32)
        ekd = sb("ekd", [128, NCH, NC2], F32)
        egl = sb("egl", [128, NCH, NC2], F32)
        r_g = R()
        ps_m = psb("ps_m")
        r_pm = R(psum=True)
        P.op("sp", "dma_start", dict(out=raw[:], in_=abr.rearrange("(c p) n -> p c n", p=128)), writes=[r_g], dma=True)
        P.op("sp", "dma_start", dict(out=alog_s[:], in_=alog), writes=[r_g], dma=True)
        P.op("sp", "dma_start", dict(out=dtb_s[:], in_=dtb), writes=[r_g], dma=True)
        P.op("act", "activation", dict(out=beta[:], in_=raw[:, :, 0:NC2], func=AF.Sigmoid), reads=[r_g], writes=[r_g])
        P.op("dve", "tensor_scalar", dict(out=nbeta[:], in0=beta[:], scalar1=-1.0, scalar2=None, op0=ALU.mult), reads=[r_g], writes=[r_g])
        P.op("dve", "tensor_tensor", dict(out=gg[:], in0=raw[:, :, NC2:2 * NC2], in1=dtb_s[:, None, :].to_broadcast([128, NCH, NC2]), op=ALU.add),
             reads=[r_g], writes=[r_g])
        sp1 = sb("sp1", [128, NCH, NC2], F32)
        sp2 = sb("sp2", [128, NCH, NC2], F32)
        sp3 = sb("sp3", [128, NCH, NC2], F32)
        P.op("dve", "tensor_scalar", dict(out=sp1[:], in0=gg[:], scalar1=-1.0, scalar2=None, op0=ALU.mult), reads=[r_g], writes=[r_g])
        P.op("dve", "tensor_tensor", dict(out=sp1[:], in0=sp1[:], in1=gg[:], op=ALU.max), reads=[r_g], writes=[r_g])
        P.op("act", "activation", dict(out=sp1[:], in_=sp1[:], func=AF.Exp, scale=-1.0), reads=[r_g], writes=[r_g])
        P.op("dve", "tensor_scalar", dict(out=sp2[:], in0=sp1[:], scalar1=2.0, scalar2=None, op0=ALU.add), reads=[r_g], writes=[r_g])
        P.op("dve", "reciprocal", dict(out=sp2[:], in_=sp2[:]), reads=[r_g], writes=[r_g])
        P.op("dve", "tensor_tensor", dict(out=sp1[:], in0=sp1[:], in1=sp2[:], op=ALU.mult), reads=[r_g], writes=[r_g])
        P.op("dve", "tensor_tensor", dict(out=sp2[:], in0=sp1[:], in1=sp1[:], op=ALU.mult), reads=[r_g], writes=[r_g])
        P.op("dve", "tensor_scalar", dict(out=sp3[:], in0=sp2[:], scalar1=1.0 / 11, scalar2=1.0 / 9, op0=ALU.mult, op1=ALU.add), reads=[r_g], writes=[r_g])
        for cst_ in (1.0 / 7, 1.0 / 5, 1.0 / 3, 1.0):
            P.op("dve", "tensor_tensor", dict(out=sp3[:], in0=sp3[:], in1=sp2[:], op=ALU.mult), reads=[r_g], writes=[r_g])
            P.op("dve", "tensor_scalar", dict(out=sp3[:], in0=sp3[:], scalar1=cst_, scalar2=None, op0=ALU.add), reads=[r_g], writes=[r_g])
        P.op("dve", "tensor_tensor", dict(out=sp3[:], in0=sp3[:], in1=sp1[:], op=ALU.mult), reads=[r_g], writes=[r_g])
        P.op("dve", "tensor_scalar", dict(out=sp1[:], in0=gg[:], scalar1=0.0, scalar2=None, op0=ALU.max), reads=[r_g], writes=[r_g])
        P.op("dve", "scalar_tensor_tensor", dict(out=gg[:], in0=sp3[:], scalar=2.0, in1=sp1[:], op0=ALU.mult, op1=ALU.add), reads=[r_g], writes=[r_g])
        P.op("act", "activation", dict(out=alog_s[:], in_=alog_s[:], func=AF.Exp), reads=[r_g], writes=[r_g])
        P.op("dve", "scalar_tensor_tensor", dict(out=gg[:], in0=gg[:], scalar=-1.0, in1=alog_s[:, None, :].to_broadcast([128, NCH, NC2]),
                                                 op0=ALU.mult, op1=ALU.mult), reads=[r_g], writes=[r_g])
        ggv = gg[:].rearrange("p c (d h) -> p c d h", d=2)
        gcv = gc[:].rearrange("p c (d h) -> p c d h", d=2)
        gtv = gtot[:].rearrange("p c (d h) -> p c d h", d=2)
        psv = ps_m[:, 0:NCH * NC2].rearrange("p (c d h) -> p c d h", c=NCH, d=2)
        pst = ps_m[:, 256:256 + NCH * NC2].rearrange("p (c d h) -> p c d h", c=NCH, d=2)
        assert NCH * NC2 <= 256
        for d in range(2):
            P.op("pe", "matmul", dict(out=psv[:, :, d, :], lhsT=ucum[:, d, :], rhs=ggv[:, :, d, :], start=(d == 0), stop=True, skip_group_check=True),
                 reads=[r_c, r_g], writes=[r_pm])
        P.op("pe", "matmul", dict(out=ps_m[:, 256:256 + NCH * NC2], lhsT=onesf[:], rhs=gg[:].rearrange("p c n -> p (c n)"),
                                  start=False, stop=True, skip_group_check=True), reads=[r_c, r_g], writes=[r_pm])
        P.op("dve", "tensor_copy", dict(out=gc[:].rearrange("p c n -> p (c n)"), in_=ps_m[:, 0:NCH * NC2]), reads=[r_pm], writes=[r_g])
        P.op("dve", "tensor_copy", dict(out=gtot[:].rearrange("p c n -> p (c n)"), in_=ps_m[:, 256:256 + NCH * NC2]), reads=[r_pm], writes=[r_g])
        P.op("dve", "tensor_scalar", dict(out=ngc[:], in0=gc[:], scalar1=-1.0, scalar2=None, op0=ALU.mult), reads=[r_g], writes=[r_g])
        P.op("act", "activation", dict(out=egc[:], in_=gc[:], func=AF.Exp), reads=[r_g], writes=[r_g])
        P.op("dve", "tensor_tensor", dict(out=begc[:], in0=egc[:], in1=beta[:], op=ALU.mult), reads=[r_g], writes=[r_g])
        P.op("dve", "tensor_tensor", dict(out=ekd[:], in0=gtot[:], in1=gc[:], op=ALU.subtract), reads=[r_g], writes=[r_g])
        P.op("act", "activation", dict(out=ekd[:], in_=ekd[:], func=AF.Exp), reads=[r_g], writes=[r_g])
        P.op("act", "activation", dict(out=egl[:], in_=gtot[:], func=AF.Exp), reads=[r_g], writes=[r_g])

        NHX = NH if STOP >= 1 else 0
        xin = sb("xin", [128, S + 4], F32)
        acc = sb("acc", [128, S], F32)
        sqb = sb("sqb", [128, S], BF16)
        fT = [sb("fT%d" % i, [128, S], BF16) for i in range(3)]
        kbg = [sb("kbg%d" % i, [128, NCH, 128], BF16) for i in range(2)]
        kdd = [sb("kdd%d" % i, [128, NCH, 128], BF16) for i in range(2)]
        vbd = [sb("vbd%d" % i, [128, NCH, 128], BF16) for i in range(2)]
        oacc = sb("oacc", [128, NCH, 128], F32)
        zt = sb("zt", [128, NCH, 128], F32)
        rn = sb("rn", [128, 512], F32)
        ssn = sb("ssn", [128, NCH], F32)
        ogb = sb("ogb", [128, NCH, 128], BF16)
        oTs = sb("oTs", [128, S], BF16)
        r_xin, r_acc, r_sqb, r_rn = R(), R(), R(), R()
        r_fT = [R(), R(), R()]
        r_tok = R()
        r_oacc = [R() for c in range(NCH)]
        r_zt, r_ssn, r_ogb, r_oTs = R(), R(), R(), R()
        Gb = [[sb("Gb%d%d" % (d, i), [128, 128], F32) for i in range(2)] for d in range(2)]
        dec = [[sb("dec%d%d" % (d, i), [128, 2, 128], F32) for i in range(2)] for d in range(2)]
        Nb = [[sb("Nb%d%d" % (d, i), [128, 128], CHD) for i in range(2)] for d in range(2)]
        Mb = [[sb("Mb%d%d" % (d, i), [128, 128], CHD) for i in range(2)] for d in range(2)]
        Pb = [[sb("Pb%d%d" % (d, i), [128, 128], CHD) for i in range(2)] for d in range(2)]
        TT = [[sb("TT%d%d" % (d, i), [128, 128], BF16) for i in range(2)] for d in range(2)]
        wTn = [[sb("wTn%d%d" % (d, i), [128, 128], BF16) for i in range(2)] for d in range(2)]
        atT = [[sb("atT%d%d" % (d, i), [128, 128], BF16) for i in range(2)] for d in range(2)]
        vnew = [sb("vnew%d" % d, [128, 128], BF16) for d in range(2)]
        tmpo = [sb("tmpo%d" % d, [128, 128], F32) for d in range(2)]
        tmpo2 = [sb("tmpo2%d" % d, [128, 128], F32) for d in range(2)]
        Sf = [sb("Sf%d" % d, [128, 128], F32) for d in range(2)]
        Sb_ = [sb("Sb%d" % d, [128, 128], BF16) for d in range(2)]
        Sl_ = [sb("Sl%d" % d, [128, 128], BF16) for d in range(2)]
        vnl = [sb("vnl%d" % d, [128, 128], BF16) for d in range(2)]
        r_Gb = [[R(), R()], [R(), R()]]
        r_dec = [[R(), R()], [R(), R()]]
        r_N = [[R(), R()], [R(), R()]]
        r_M = [[R(), R()], [R(), R()]]
        r_P = [[R(), R()], [R(), R()]]
        r_TT = [[R(), R()], [R(), R()]]
        r_w = [[R(), R()], [R(), R()]]
        r_at = [[R(), R()], [R(), R()]]
        r_vn, r_to, r_to2, r_Sf, r_Sb = [R(), R()], [R(), R()], [R(), R()], [R(), R()], [R(), R()]
        ps_X = [psb("ps_X%d" % d) for d in range(2)]
        ps_kk = psb("ps_kk")
        ps_ch = [psb("ps_ch%d" % d) for d in range(2)]
        ps_sc = [psb("ps_sc%d" % d) for d in range(2)]
        r_pX, r_pch, r_psc = [R(psum=True), R(psum=True)], [R(psum=True), R(psum=True)], [R(psum=True), R(psum=True)]
        r_pkk = R(psum=True)
        ps_tb = ps_m[:].bitcast(BF16)

        for h in range(NHX):
            for t in range(3):
                row = (h * 3 + t) * 128
                P.op("pool", "memset", dict(ap=xin[:, 0:2], constant=0.0), writes=[r_xin])
                P.op("pool", "memset", dict(ap=xin[:, S + 2:S + 4], constant=0.0), writes=[r_xin])
                P.op("sp", "dma_start", dict(out=xin[:, 2:S + 2], in_=aqkvT[row:row + 128, :]), writes=[r_xin], dma=True)
                P.op("dve", "tensor_scalar", dict(out=acc[:], in0=xin[:, 0:S], scalar1=cws[:, h * 3 + t, 0:1], scalar2=None, op0=ALU.mult),
                     reads=[r_xin, r_c], writes=[r_acc])
                for w in range(1, 5):
                    P.op("dve", "scalar_tensor_tensor", dict(out=acc[:], in0=xin[:, w:w + S], scalar=cws[:, h * 3 + t, w:w + 1], in1=acc[:],
                                                             op0=ALU.mult, op1=ALU.add), reads=[r_xin, r_c, r_acc], writes=[r_acc])
                if t == 2:
                    P.op("act", "activation", dict(out=fT[2][:], in_=acc[:], func=AF.Silu), reads=[r_acc], writes=[r_fT[2]])
                else:
                    P.op("act", "activation", dict(out=acc[:], in_=acc[:], func=AF.Silu), reads=[r_acc], writes=[r_acc])
                    P.op("act", "activation", dict(out=sqb[:], in_=acc[:], func=AF.Square), reads=[r_acc], writes=[r_sqb])
                    for b in range(NB):
                        bs = slice(b * 512, (b + 1) * 512)
                        P.op("pe", "matmul", dict(out=ps_m[:], lhsT=onesb[:], rhs=sqb[:, bs], start=True, stop=True),
                             reads=[r_c, r_sqb], writes=[r_pm])
                        P.op("act", "activation", dict(out=rn[:], in_=ps_m[:], func=AF.Sqrt, bias=epsb[:, 0:1]), reads=[r_pm, r_c], writes=[r_rn])
                        P.op("dve", "reciprocal", dict(out=rn[:], in_=rn[:]), reads=[r_rn], writes=[r_rn])
                        P.op("dve", "scalar_tensor_tensor", dict(out=fT[t][:, bs], in0=acc[:, bs], scalar=(128 ** -0.5 if t == 0 else 1.0), in1=rn[:],
                                                                 op0=ALU.mult, op1=ALU.mult), reads=[r_acc, r_rn], writes=[r_fT[t]])
            if STOP < 2:
                continue
            for c4 in range(0, NCH, 4):
                for t in (1, 2):
                    for j in range(4):
                        c = c4 + j
                        P.op("pe", "transpose", dict(out=ps_tb[:, j * 128:(j + 1) * 128], in_=fT[t][:, c * 128:(c + 1) * 128], identity=identb[:]),
                             reads=[r_fT[t], r_c], writes=[r_pm])
                    src = ps_tb[:, 0:512].rearrange("p (c k) -> p c k", c=4)
                    for d in range(2):
                        col = d * NH + h
                        if t == 1:
                            P.op("dve", "tensor_tensor", dict(out=kbg[d][:, c4:c4 + 4, :], in0=src,
                                                              in1=begc[:, c4:c4 + 4, col:col + 1].to_broadcast([128, 4, 128]), op=ALU.mult),
                                 reads=[r_pm, r_g], writes=[r_tok])
                            P.op("dve", "tensor_tensor", dict(out=kdd[d][:, c4:c4 + 4, :], in0=src,
                                                              in1=ekd[:, c4:c4 + 4, col:col + 1].to_broadcast([128, 4, 128]), op=ALU.mult),
                                 reads=[r_pm, r_g], writes=[r_tok])
                        else:
                            P.op("dve", "tensor_tensor", dict(out=vbd[d][:, c4:c4 + 4, :], in0=src,
                                                              in1=beta[:, c4:c4 + 4, col:col + 1].to_broadcast([128, 4, 128]), op=ALU.mult),
                                 reads=[r_pm, r_g], writes=[r_tok])
            if STOP < 3:
                continue
            P.op("sp", "dma_start", dict(out=zt[:], in_=az[:, h * 128:(h + 1) * 128].rearrange("(c p) n -> p c n", p=128)), writes=[r_zt], dma=True)

            for d in range(2):
                P.op("pool", "memset", dict(ap=Sf[d][:], constant=0.0), writes=[r_Sf[d]])
                P.op("pool", "memset", dict(ap=Sb_[d][:], constant=0.0), writes=[r_Sb[d]])
                P.op("pool", "memset", dict(ap=Sl_[d][:], constant=0.0), writes=[r_Sb[d]])

            def precompute(d, c, par):
                col = d * NH + h
                cs = slice(c * 128, (c + 1) * 128)
                P.op("dve", "tensor_scalar", dict(out=Gb[d][par][:], in0=onesf[:], scalar1=gg[:, c, col:col + 1], scalar2=None, op0=ALU.mult),
                     reads=[r_c, r_g], writes=[r_Gb[d][par]])
                X = ps_X[d]
                P.op("pe", "matmul", dict(out=X[:, 0:128], lhsT=Gb[d][par][:], rhs=ucum[:, d, :], start=True, stop=False, skip_group_check=True),
                     reads=[r_Gb[d][par], r_c], writes=[r_pX[d]])
                P.op("pe", "matmul", dict(out=X[:, 0:128], lhsT=identf[:], rhs=nm[:, 2 * d, :], start=False, stop=True, skip_group_check=True),
                     reads=[r_c], writes=[r_pX[d]])
                P.op("pe", "matmul", dict(out=X[:, 128:256], lhsT=Gb[d][par][:], rhs=ucum[:, d, :], start=False, stop=False, skip_group_check=True),
                     reads=[r_Gb[d][par], r_c], writes=[r_pX[d]])
                P.op("pe", "matmul", dict(out=X[:, 128:256], lhsT=identf[:], rhs=nm[:, 2 * d + 1, :], start=False, stop=True, skip_group_check=True),
                     reads=[r_c], writes=[r_pX[d]])
                P.op("act", "activation", dict(out=dec[d][par][:, 0, :], in_=X[:, 0:128], func=AF.Exp, scale=-1.0, bias=gc[:, c, col:col + 1]),
                     reads=[r_pX[d], r_g], writes=[r_dec[d][par]])
                P.op("act", "activation", dict(out=dec[d][par][:, 1, :], in_=X[:, 128:256], func=AF.Exp, scale=1.0, bias=ngc[:, c, col:col + 1]),
                     reads=[r_pX[d], r_g], writes=[r_dec[d][par]])
                if SUB < 1:
                    return
                P.op("pe", "matmul", dict(out=ps_kk[:, 0:128], lhsT=fT[1][:, cs], rhs=fT[1][:, cs], start=True, stop=True, skip_group_check=True),
                     reads=[r_fT[1]], writes=[r_pkk])
                P.op("pe", "matmul", dict(out=ps_kk[:, 128:256], lhsT=fT[1][:, cs], rhs=fT[0][:, cs], start=False, stop=True, skip_group_check=True),
                     reads=[r_fT[1], r_fT[0]], writes=[r_pkk])
                P.op("dve", "scalar_tensor_tensor", dict(out=Nb[d][par][:], in0=ps_kk[:, 0:128], scalar=nbeta[:, c, col:col + 1], in1=dec[d][par][:, 0, :],
                                                         op0=ALU.mult, op1=ALU.mult), reads=[r_pkk, r_g, r_dec[d][par]], writes=[r_N[d][par]])
                P.op("dve", "tensor_tensor", dict(out=atT[d][par][:], in0=ps_kk[:, 128:256], in1=dec[d][par][:, 1, :], op=ALU.mult),
                     reads=[r_pkk, r_dec[d][par]], writes=[r_at[d][par]])
                if SUB < 2:
                    return
                ch = ps_ch[d]
                P.op("pe", "matmul", dict(out=ch[:, 128:256], lhsT=Nb[d][par][:], rhs=identc[:], start=True, stop=True, skip_group_check=True),
                     reads=[r_N[d][par], r_c], writes=[r_pch[d]])
                if SUB == 2 and cfg.get("sub2", 0) == 1:
                    P.op("act", "copy", dict(out=Mb[d][par][:], in_=ch[:, 128:256]), reads=[r_pch[d]], writes=[r_M[d][par]])
                    return
                P.op("pe", "matmul", dict(out=ch[:, 256:384], lhsT=identc[:], rhs=identc[:], start=False, stop=False, skip_group_check=True),
                     reads=[r_c], writes=[r_pch[d]])
                P.op("pe", "matmul", dict(out=ch[:, 256:384], lhsT=Nb[d][par][:], rhs=identc[:], start=False, stop=True, skip_group_check=True),
                     reads=[r_N[d][par], r_c], writes=[r_pch[d]])
                P.op("act", "copy", dict(out=Mb[d][par][:], in_=ch[:, 128:256]), reads=[r_pch[d]], writes=[r_M[d][par]])
                if cfg.get("sub2", 0) == 2:
                    P.op("act", "copy", dict(out=Pb[d][par][:], in_=ch[:, 256:384]), reads=[r_pch[d]], writes=[r_P[d][par]])
                else:
                    P.op("dve", "tensor_scalar", dict(scalar1=1.0, scalar2=None, op0=ALU.mult, out=Pb[d][par][:], in0=ch[:, 256:384]), reads=[r_pch[d]], writes=[r_P[d][par]])
                if SUB < 3:
                    return
                for k in range(6):
                    P.op("pe", "matmul", dict(out=ch[:, 0:128], lhsT=Mb[d][par][:], rhs=Nb[d][par][:], start=True, stop=True, skip_group_check=True),
                         reads=[r_M[d][par], r_N[d][par]], writes=[r_pch[d]])
                    if k < 5:
                        P.op("pe", "matmul", dict(out=ch[:, 128:256], lhsT=Nb[d][par][:], rhs=Mb[d][par][:], start=False, stop=True, skip_group_check=True),
                             reads=[r_M[d][par], r_N[d][par]], writes=[r_pch[d]])
                    P.op("act", "copy", dict(out=Nb[d][par][:], in_=ch[:, 0:128]), reads=[r_pch[d]], writes=[r_N[d][par]])
                    if k < 5:
                        P.op("dve", "tensor_scalar", dict(scalar1=1.0, scalar2=None, op0=ALU.mult, out=Mb[d][par][:], in0=ch[:, 128:256]), reads=[r_pch[d]], writes=[r_M[d][par]])
                    P.op("pe", "matmul", dict(out=ch[:, 256:384], lhsT=identc[:], rhs=Pb[d][par][:], start=False, stop=False, skip_group_check=True),
                         reads=[r_c, r_P[d][par]], writes=[r_pch[d]])
                    P.op("pe", "matmul", dict(out=ch[:, 256:384], lhsT=Nb[d][par][:], rhs=Pb[d][par][:], start=False, stop=True, skip_group_check=True),
                         reads=[r_N[d][par], r_P[d][par]], writes=[r_pch[d]])
                    if k < 5:
                        P.op("dve", "tensor_scalar", dict(scalar1=1.0, scalar2=None, op0=ALU.mult, out=Pb[d][par][:], in0=ch[:, 256:384]), reads=[r_pch[d]], writes=[r_P[d][par]])
                    else:
                        P.op("dve", "tensor_scalar", dict(scalar1=1.0, scalar2=None, op0=ALU.mult, out=TT[d][par][:], in0=ch[:, 256:384]), reads=[r_pch[d]], writes=[r_TT[d][par]])
                if SUB < 4:
                    return
                P.op("pe", "matmul", dict(out=ch[:, 384:512], lhsT=kbg[d][:, c, :], rhs=TT[d][par][:], start=False, stop=True, skip_group_check=True),
                     reads=[r_tok, r_TT[d][par]], writes=[r_pch[d]])
                P.op("act", "activation", dict(out=wTn[d][par][:], in_=ch[:, 384:512], func=AF.Copy, scale=-1.0), reads=[r_pch[d]], writes=[r_w[d][par]])

            first_visit = [True] * NCH

            def scan(d, c, par):
                col = d * NH + h
                cs = slice(c * 128, (c + 1) * 128)
                sc = ps_sc[d]
                P.op("pe", "matmul", dict(out=sc[:, 0:128], lhsT=TT[d][par][:], rhs=vbd[d][:, c, :], start=True, stop=False, skip_group_check=True),
                     reads=[r_TT[d][par], r_tok], writes=[r_psc[d]])
                P.op("pe", "matmul", dict(out=sc[:, 0:128], lhsT=wTn[d][par][:], rhs=Sb_[d][:], start=False, stop=False, skip_group_check=True),
                     reads=[r_w[d][par], r_Sb[d]], writes=[r_psc[d]])
                P.op("pe", "matmul", dict(out=sc[:, 0:128], lhsT=wTn[d][par][:], rhs=Sl_[d][:], start=False, stop=True, skip_group_check=True),
                     reads=[r_w[d][par], r_Sb[d]], writes=[r_psc[d]])
                P.op("act", "copy", dict(out=vnew[d][:], in_=sc[:, 0:128]), reads=[r_psc[d]], writes=[r_vn[d]])
                P.op("dve", "tensor_tensor", dict(out=vnl[d][:], in0=sc[:, 0:128], in1=vnew[d][:], op=ALU.subtract), reads=[r_psc[d], r_vn[d]], writes=[r_vn[d]])
                P.op("pe", "matmul", dict(out=sc[:, 128:256], lhsT=fT[0][:, cs], rhs=Sb_[d][:], start=False, stop=False, skip_group_check=True),
                     reads=[r_fT[0], r_Sb[d]], writes=[r_psc[d]])
                P.op("pe", "matmul", dict(out=sc[:, 128:256], lhsT=fT[0][:, cs], rhs=Sl_[d][:], start=False, stop=True, skip_group_check=True),
                     reads=[r_fT[0], r_Sb[d]], writes=[r_psc[d]])
                for vv_ in (vnew, vnl):
                    P.op("pe", "matmul", dict(out=sc[:, 256:384], lhsT=atT[d][par][:], rhs=vv_[d][:], start=False, stop=(vv_ is vnl), skip_group_check=True),
                         reads=[r_at[d][par], r_vn[d]], writes=[r_psc[d]])
                for vv_ in (vnew, vnl):
                    P.op("pe", "matmul", dict(out=sc[:, 384:512], lhsT=kdd[d][:, c, :], rhs=vv_[d][:], start=False, stop=(vv_ is vnl), skip_group_check=True),
                         reads=[r_tok, r_vn[d]], writes=[r_psc[d]])
                P.op("act", "copy", dict(out=tmpo[d][:], in_=sc[:, 256:384]), reads=[r_psc[d]], writes=[r_to[d]])
                if first_visit[c]:
                    first_visit[c] = False
                    P.op("dve", "scalar_tensor_tensor", dict(out=oacc[:, c, :], in0=sc[:, 128:256], scalar=egc[:, c, col:col + 1], in1=tmpo[d][:],
                                                             op0=ALU.mult, op1=ALU.add), reads=[r_psc[d], r_g, r_to[d]], writes=[r_oacc[c]])
                else:
                    P.op("dve", "scalar_tensor_tensor", dict(out=tmpo2[d][:], in0=sc[:, 128:256], scalar=egc[:, c, col:col + 1], in1=tmpo[d][:],
                                                             op0=ALU.mult, op1=ALU.add), reads=[r_psc[d], r_g, r_to[d]], writes=[r_to2[d]])
                    P.op("dve", "tensor_tensor", dict(out=oacc[:, c, :], in0=oacc[:, c, :], in1=tmpo2[d][:], op=ALU.add),
                         reads=[r_to2[d], r_oacc[c]], writes=[r_oacc[c]])
                P.op("dve", "scalar_tensor_tensor", dict(out=Sf[d][:], in0=Sf[d][:], scalar=egl[:, c, col:col + 1], in1=sc[:, 384:512],
                                                         op0=ALU.mult, op1=ALU.add), reads=[r_psc[d], r_g, r_Sf[d]], writes=[r_Sf[d]])
                P.op("act", "copy", dict(out=Sb_[d][:], in_=Sf[d][:]), reads=[r_Sf[d]], writes=[r_Sb[d]])
                P.op("dve", "tensor_tensor", dict(out=Sl_[d][:], in0=Sf[d][:], in1=Sb_[d][:], op=ALU.subtract), reads=[r_Sf[d], r_Sb[d]], writes=[r_Sb[d]])

            order = [[c for c in range(NCH)], [NCH - 1 - c for c in range(NCH)]]
            if STOP == 3:
                precompute(0, 0, 0)
                continue
            if STOP == 4:
                precompute(0, 0, 0)
                scan(0, 0, 0)
                continue
            for d in range(2):
                precompute(d, order[d][0], 0)
            for s in range(NCH):
                if s + 1 < NCH:
                    for d in range(2):
                        precompute(d, order[d][s + 1], (s + 1) % 2)
                for d in range(2):
                    scan(d, order[d][s], s % 2)

            allo = r_oacc
            accv = acc[:].rearrange("p (c k) -> p c k", c=NCH)
            P.op("dve", "tensor_tensor", dict(out=accv, in0=oacc[:], in1=oacc[:], op=ALU.mult), reads=allo + [r_acc], writes=[r_acc])
            P.op("dve", "tensor_reduce", dict(out=ssn[:], in_=accv, axis=AX.X, op=ALU.add), reads=[r_acc], writes=[r_ssn])
            P.op("act", "activation", dict(out=ssn[:], in_=ssn[:], func=AF.Sqrt, scale=1.0 / 128, bias=epsb[:, 0:1]), reads=[r_ssn, r_c], writes=[r_ssn])
            P.op("dve", "reciprocal", dict(out=ssn[:], in_=ssn[:]), reads=[r_ssn], writes=[r_ssn])
            P.op("dve", "tensor_tensor", dict(out=accv, in0=oacc[:], in1=ssn[:, :, None].to_broadcast([128, NCH, 128]), op=ALU.mult),
                 reads=allo + [r_ssn, r_acc], writes=[r_acc])
            P.op("dve", "tensor_tensor", dict(out=accv, in0=accv, in1=onorm_s[:, None, :].to_broadcast([128, NCH, 128]), op=ALU.mult),
                 reads=[r_c, r_acc], writes=[r_acc])
            P.op("act", "activation", dict(out=zt[:], in_=zt[:], func=AF.Silu), reads=[r_zt], writes=[r_zt])
            P.op("dve", "tensor_tensor", dict(out=ogb[:], in0=accv, in1=zt[:], op=ALU.mult), reads=[r_acc, r_zt], writes=[r_ogb])
            for c4 in range(0, NCH, 4):
                for j in range(4):
                    c = c4 + j
                    P.op("pe", "matmul", dict(out=ps_m[:, j * 128:(j + 1) * 128], lhsT=ogb[:, c, :], rhs=identb[:], start=(j == 0), stop=True, skip_group_check=True),
                         reads=[r_ogb, r_c], writes=[r_pm])
                P.op("act", "copy", dict(out=oTs[:, c4 * 128:(c4 + 4) * 128], in_=ps_m[:, 0:512]), reads=[r_pm], writes=[r_oTs])
            P.op("sp", "dma_start", dict(out=oaT[h * 128:(h + 1) * 128, :], in_=oTs[:]), reads=[r_oTs], dma=True)
        P.emit()
    return nc

import ml_dtypes as _mld

_BF = _mld.bfloat16
_PROGS = {}


def _prog(key, fn, cfg):
    if key not in _PROGS:
        _PROGS[key] = fn(cfg)
    return _PROGS[key]


def _lay(w):
    return np.ascontiguousarray(np.asarray(w, np.float32).reshape(-1, 128).T)


def _rope_tables_b(pos):
    inv = 1.0 / (500000.0 ** (np.arange(0, 32, 2, dtype=np.float32) / 32))
    ang = pos.astype(np.float32)[:, None] * inv[None, :]
    c, s = np.cos(ang), np.sin(ang)
    C = np.ascontiguousarray(np.concatenate([c, c], 1).T.astype(np.float32))
    S = np.ascontiguousarray(np.concatenate([s, s], 1).T.astype(np.float32))
    Pm = np.zeros((32, 32), np.float32)
    for i in range(16):
        Pm[i, 16 + i] = -1
        Pm[16 + i, i] = 1
    return C, S, np.ascontiguousarray(Pm.T)


def _rope_tables_c(pos):
    inv = 1.0 / (10000.0 ** (np.arange(0, 64, 2, dtype=np.float32) / 64))
    ar = (pos // 64).astype(np.float32)[:, None] * inv[None, :]
    ac = (pos % 64).astype(np.float32)[:, None] * inv[None, :]
    C = np.ascontiguousarray(np.concatenate([np.cos(ar), np.cos(ar), np.cos(ac), np.cos(ac)], 1).T.astype(np.float32))
    S = np.ascontiguousarray(np.concatenate([np.sin(ar), np.sin(ar), np.sin(ac), np.sin(ac)], 1).T.astype(np.float32))
    Pm = np.zeros((128, 128), np.float32)
    for i in range(32):
        Pm[i, 32 + i] = -1
        Pm[32 + i, i] = 1
        Pm[64 + i, 96 + i] = -1
        Pm[96 + i, 64 + i] = 1
    return C, S, np.ascontiguousarray(Pm.T)


def _run(nc, in_maps):
    res = run_bass_kernel_spmd(nc, in_maps, core_ids=list(range(8)))
    return res.results


def kernel(x, norm_mix, norm_ffn, norm_final, ab_w_in, ab_conv_w, ab_a_log, ab_dt_bias,
           ab_out_norm, ab_w_out, c_w_qkv, c_q_norm, c_k_norm, c_w_out,
           ffn_w_gate, ffn_w_up, ffn_w_down):
    f32 = np.float32
    A = lambda a: np.ascontiguousarray(np.asarray(a, f32))
    x = A(x)
    B, S, D = x.shape
    NCORE = 8
    TPC = B * S // NCORE
    QPB = S // TPC
    xf = x.reshape(B * S, D)
    xT = [np.ascontiguousarray(xf[c * TPC:(c + 1) * TPC].T) for c in range(NCORE)]
    pos = [np.arange((c % QPB) * TPC, (c % QPB + 1) * TPC) for c in range(NCORE)]

    ab = dict(nqkv=6144, nz=2048, nab=64, nbqk=6144, nbv=3072)
    nc1 = _prog("L1", tp_build, dict(T_tot=TPC, inproj="ab", ab=ab))
    Win0 = A(ab_w_in[0])
    nwb0 = _lay(norm_mix[0])
    ims = []
    for c in range(NCORE):
        C_, S_, P_ = _rope_tables_b(pos[c])
        ims.append(dict(xT=xT[c], nwb=nwb0, Win=Win0, ropeC=C_, ropeS=S_, ropeP=P_))
    r1 = _run(nc1, ims)
    del ims

    def cat_b(name, b, axis):
        return np.concatenate([r1[b * QPB + q][name] for q in range(QPB)], axis=axis)

    cst = gdn_consts()
    nc2a = _prog("L2a", gdn_build, dict(S=S, NH=4))
    nc2b = _prog("L2b", dil_b_build, dict(S=S, NHS=2))
    conv_w = A(ab_conv_w[0])
    a_log = A(ab_a_log[0])
    dt_b = A(ab_dt_bias[0])
    onorm = np.ascontiguousarray(np.tile(A(ab_out_norm[0])[None, :], (128, 1)))
    bmask = dil_mask().astype(_BF)
    ims_a, ims_b = [], []
    for b in range(B):
        aqkvT_b = cat_b("aqkvT", b, 1)
        az_b = cat_b("az", b, 0)
        ab_b = cat_b("ab", b, 0)
        bqkT_b = cat_b("bqkT", b, 1)
        bv_b = cat_b("bv", b, 0)
        for j in range(QPB):
            heads = [4 * j + i for i in range(4)]
            rows = []
            for h in heads:
                for t in range(3):
                    rows.append(aqkvT_b[t * 2048 + h * 128: t * 2048 + (h + 1) * 128])
            abcols = [d * 16 + h for d in range(2) for h in heads] + [32 + d * 16 + h for d in range(2) for h in heads]
            cwl = np.zeros((128, 12, 5), f32)
            for i, h in enumerate(heads):
                for t in range(3):
                    cwl[:, i * 3 + t, :] = conv_w[:, t * 2048 + h * 128: t * 2048 + (h + 1) * 128].T
            al = np.array([a_log[d, h] for d in range(2) for h in heads], f32)
            db = np.array([dt_b[d, h] for d in range(2) for h in heads], f32)
            ims_a.append(dict(aqkvT=np.ascontiguousarray(np.concatenate(rows, 0)),
                              az=np.ascontiguousarray(az_b[:, heads[0] * 128:(heads[-1] + 1) * 128]),
                              abr=np.ascontiguousarray(ab_b[:, abcols]), cw=cwl,
                              alog=np.ascontiguousarray(np.tile(al[None, :], (128, 1))),
                              dtb=np.ascontiguousarray(np.tile(db[None, :], (128, 1))),
                              onorm=onorm, ident_in=cst["ident"], ucum_in=cst["ucum"], nm_in=cst["nm"]))
            qrows, krows, vcols = [], [], []
            for hs in (2 * j, 2 * j + 1):
                for g in range(3):
                    hd = g * 8 + hs
                    qrows.append(bqkT_b[hd * 128:(hd + 1) * 128])
                    krows.append(bqkT_b[3072 + hd * 128: 3072 + (hd + 1) * 128])
                    vcols.append(bv_b[:, hd * 128:(hd + 1) * 128])
            ims_b.append(dict(bqT=np.ascontiguousarray(np.concatenate(qrows, 0)), bkT=np.ascontiguousarray(np.concatenate(krows, 0)),
                              bv=np.ascontiguousarray(np.concatenate(vcols, 1)), bmask=bmask))
    del r1
    r2a = _run(nc2a, ims_a)
    del ims_a
    r2b = _run(nc2b, ims_b)
    del ims_b

    cc = dict(nq=4096, nk=1024, nv=1024)
    nc3 = _prog("L3", tp_build, dict(T_tot=TPC, oproj_K=3072, dff=11008, inproj="c", c=cc, store_x=True))
    ims = []
    Wo0 = A(ab_w_out[0])
    Wg0, Wu0, Wd0 = A(ffn_w_gate[0]), A(ffn_w_up[0]), A(ffn_w_down[0])
    Wqkv = A(c_w_qkv[0])
    nwa0, nwb1 = _lay(norm_ffn[0]), _lay(norm_mix[1])
    gq = A(c_q_norm[0]).reshape(128, 1)
    gk = A(c_k_norm[0]).reshape(128, 1)
    for c in range(NCORE):
        b, q = c // QPB, c % QPB
        ts = slice(q * TPC, (q + 1) * TPC)
        oT = np.concatenate([r2a[b * QPB + j]["oaT"][:, ts] for j in range(QPB)] +
                            [r2b[b * QPB + j]["obT"][:, ts] for j in range(QPB)], 0)
        C_, S_, P_ = _rope_tables_c(pos[c])
        ims.append(dict(xT=xT[c], oT=np.ascontiguousarray(oT), Wo=Wo0, nwa=nwa0, Wg=Wg0, Wu=Wu0, Wd=Wd0, nwb=nwb1, Win=Wqkv,
                        ropeC=C_, ropeS=S_, ropeP=P_, gq=gq, gk=gk))
    del r2a, r2b
    r3 = _run(nc3, ims)
    del ims, Wg0, Wu0, Wd0

    nc4 = _prog("L4", attn_c_build, dict(S=S, NKV=2, REP=4))
    ims = []
    for b in range(B):
        cqkT_b = np.concatenate([r3[b * QPB + q]["cqkT"] for q in range(QPB)], 1)
        cv_b = np.concatenate([r3[b * QPB + q]["cv"] for q in range(QPB)], 0)
        for j in range(QPB):
            ims.append(dict(cqT=np.ascontiguousarray(cqkT_b[8 * j * 128:(8 * j + 8) * 128]),
                            ckT=np.ascontiguousarray(cqkT_b[4096 + 2 * j * 128: 4096 + (2 * j + 2) * 128]),
                            cv=np.ascontiguousarray(cv_b[:, 2 * j * 128:(2 * j + 2) * 128])))
    r4 = _run(nc4, ims)
    del ims

    nc5 = _prog("L5", tp_build, dict(T_tot=TPC, oproj_K=4096, dff=11008, final_norm=True))
    ims = []
    Wo1 = A(c_w_out[0])
    Wg1, Wu1, Wd1 = A(ffn_w_gate[1]), A(ffn_w_up[1]), A(ffn_w_down[1])
    nwa1, nwf = _lay(norm_ffn[1]), _lay(norm_final)
    for c in range(NCORE):
        b, q = c // QPB, c % QPB
        ts = slice(q * TPC, (q + 1) * TPC)
        oT = np.concatenate([r4[b * QPB + j]["ocT"][:, ts] for j in range(QPB)], 0)
        ims.append(dict(xT=r3[c]["xoT"], oT=np.ascontiguousarray(oT), Wo=Wo1, nwa=nwa1, Wg=Wg1, Wu=Wu1, Wd=Wd1, nwb=nwf))
    r5 = _run(nc5, ims)
    out = np.concatenate([np.asarray(r5[c]["outT"]).T for c in range(NCORE)], 0)
    return np.ascontiguousarray(out.reshape(B, S, D).astype(f32, copy=False))
```

```python
from contextlib import ExitStack
import numpy as np
import concourse.bass as bass
import concourse.mybir as mybir
from concourse.bass_utils import run_bass_kernel_spmd

F32 = mybir.dt.float32
BF16 = mybir.dt.bfloat16
AF = mybir.ActivationFunctionType
ALU = mybir.AluOpType
AX = mybir.AxisListType

ENGS = ("pe", "act", "dve", "pool", "sp")
_UNIQ = [0]


def uniq():
    _UNIQ[0] += 1
    return "u%d_" % _UNIQ[0]
NS_DMA = 6
SAME_ENGINE_SYNC = True


class Reg:
    __slots__ = ("name", "w", "r", "rd", "psum")

    def __init__(self, name="", psum=False):
        self.name = name
        self.psum = psum
        self.w = None
        self.r = {}
        self.rd = []


class Ins:
    __slots__ = ("eng", "fn", "deps", "inc", "dma", "idx", "sem", "target", "cnt", "cc")


class Prog:
    def __init__(self, nc):
        self.nc = nc
        self.q = {e: [] for e in ENGS}
        self.dmas = {e: [] for e in ENGS}
        self.ccs = []

    def op(self, eng, meth, kw, reads=(), writes=(), dma=False, cc=False):
        ins = Ins()
        ins.cc = cc
        if cc:
            dma = True
            self.ccs.append(ins)
        ins.eng = eng
        ins.fn = (meth, kw)
        ins.dma = dma
        ins.inc = dma
        ins.idx = len(self.q[eng])
        ins.cnt = 0
        deps = set()
        for r in reads:
            if r.w is not None:
                deps.add(r.w)
            if r.psum:
                for e2, i2 in r.r.items():
                    if e2 != eng:
                        deps.add(i2)
        for w in writes:
            if w.w is not None:
                deps.add(w.w)
            deps.update(w.r.values())
            deps.update(w.rd)
        for r in reads:
            if dma:
                r.rd.append(ins)
            else:
                r.r[eng] = ins
        for w in writes:
            w.w = ins
            w.r = {}
            w.rd = []
        if dma and not cc:
            lst = self.dmas[eng]
            if len(lst) >= NS_DMA:
                deps.add(lst[len(lst) - NS_DMA])
            lst.append(ins)
        deps.discard(ins)
        fin = []
        for d in deps:
            if d.eng == eng and not d.dma:
                if eng == "pe" or not SAME_ENGINE_SYNC:
                    continue
            d.inc = True
            fin.append(d)
        ins.deps = fin
        self.q[eng].append(ins)
        return ins

    def emit(self, final_waits=()):
        nc = self.nc
        allsems = []

        def newsem(name):
            h = nc.alloc_semaphore(name=name)
            allsems.append(h)
            return h
        with ExitStack() as es:
            pf = uniq()
            csem = {e: newsem(pf + "c_" + e) for e in ENGS}
            dsem = {e: [newsem(pf + "d_%s%d" % (e, i)) for i in range(NS_DMA)]
                    for e in ENGS if self.dmas[e]}
            es_cc = []
            for e in ENGS:
                c = 0
                k = 0
                for ins in self.q[e]:
                    if ins.cc:
                        ins.sem = newsem(pf + "cc%d" % len(es_cc))
                        es_cc.append(ins)
                        ins.target = 1
                    elif ins.dma:
                        ins.sem = dsem[e][k % NS_DMA]
                        ins.target = 16 * (k // NS_DMA + 1)
                        k += 1
                    elif ins.inc:
                        c += 1
                        ins.cnt = c
            block = es.enter_context(nc.Block())

            def run(e, eng):
                waited = {}
                for ins in self.q[e]:
                    need = {}
                    for d in ins.deps:
                        if d.dma:
                            key = d.sem
                            val = d.target
                        else:
                            key = csem[d.eng]
                            val = d.cnt
                        if need.get(key, 0) < val:
                            need[key] = val
                    for key, val in need.items():
                        if waited.get(key, 0) < val:
                            eng.wait_ge(key, val)
                            waited[key] = val
                    inst = getattr(eng, ins.fn[0])(**ins.fn[1])
                    if ins.cc:
                        inst.then_inc(ins.sem)
                    elif ins.dma:
                        inst.then_inc(ins.sem, 16)
                    elif ins.inc:
                        inst.then_inc(csem[e], 1)
                for d in self.ccs:
                    if d.eng == e and waited.get(d.sem, 0) < d.target:
                        eng.wait_ge(d.sem, d.target)
                        waited[d.sem] = d.target
                if self.dmas[e]:
                    lst = self.dmas[e]
                    for d in lst[-NS_DMA:]:
                        if waited.get(d.sem, 0) < d.target:
                            eng.wait_ge(d.sem, d.target)
                            waited[d.sem] = d.target

            @block.tensor
            def _(eng):
                run("pe", eng)

            @block.scalar
            def _(eng):
                run("act", eng)

            @block.vector
            def _(eng):
                run("dve", eng)

            @block.gpsimd
            def _(eng):
                run("pool", eng)

            @block.sync
            def _(eng):
                run("sp", eng)
        nc.all_engine_barrier()
        nc.clear_and_free_semaphores(allsems)
        nc.all_engine_barrier()


def gat(G, row_off, col_off, W):
    full = G.shape[1]
    A = full // W
    v = G if A == 1 else G.rearrange("r (a w) -> (r a) w", w=W)
    return dict(in_=v, element_offset=row_off * full + col_off)


def tp_build(cfg):
    D = cfg.get("D", 4096)
    KC = D // 128
    T = cfg.get("T", 512)
    T_tot = cfg["T_tot"]
    NTT = T // 128
    oK = cfg.get("oproj_K", 0)
    dff = cfg.get("dff", 0)
    inproj = cfg.get("inproj")
    final_norm = cfg.get("final_norm", False)
    store_x = cfg.get("store_x", False)
    FB = 8

    TT_ = cfg.get("TD")
    FUSED = TT_ is not None
    nc = cfg["nc"] if FUSED else bass.Bass("TRN2", target_bir_lowering=False)
    if FUSED:
        din = lambda name, shape, dt=F32: TT_[name]
        dout = lambda name, shape, dt=F32: TT_[name]
    else:
        din = lambda name, shape, dt=F32: nc.dram_tensor(name, shape, dt, kind="ExternalInput").ap()
        dout = lambda name, shape, dt=F32: nc.dram_tensor(name, shape, dt, kind="ExternalOutput").ap()
    xT = din("xT", [D, T_tot])
    if oK:
        if not FUSED:
            oT = din("oT", [oK, T_tot], BF16)
        Wo = din("Wo", [oK, D])
    if dff:
        nwa = din("nwa", [128, KC])
        Wg = din("Wg", [D, dff])
        Wu = din("Wu", [D, dff])
        Wd = din("Wd", [dff, D])
    if inproj or final_norm:
        nwb = din("nwb", [128, KC])
    if store_x:
        xoT = dout("xoT", [D, T_tot])
    if final_norm:
        outT = dout("outT", [D, T_tot])
    if inproj == "ab":
        ab = cfg["ab"]
        ncols = ab["nqkv"] + ab["nz"] + ab["nab"] + ab["nbqk"] + ab["nbv"]
        Win = din("Win", [D, ncols])
        ropeC = din("ropeC", [32, T_tot])
        ropeS = din("ropeS", [32, T_tot])
        ropeP = din("ropeP", [32, 32])
        if FUSED:
            aqkvT = az = abo = bqkT = bv = None
        else:
            aqkvT = dout("aqkvT", [ab["nqkv"], T_tot])
            az = dout("az", [T_tot, ab["nz"]])
            abo = dout("ab", [T_tot, ab["nab"]])
            bqkT = dout("bqkT", [ab["nbqk"], T_tot], BF16)
            bv = dout("bv", [T_tot, ab["nbv"]], BF16)
    if inproj == "c":
        cc = cfg["c"]
        ncols = cc["nq"] + cc["nk"] + cc["nv"]
        Win = din("Win", [D, ncols])
        ropeC = din("ropeC", [128, T_tot])
        ropeS = din("ropeS", [128, T_tot])
        ropeP = din("ropeP", [128, 128])
        gq = din("gq", [128, 1])
        gk = din("gk", [128, 1])
        if FUSED:
            cqkT = cv = None
        else:
            cqkT = dout("cqkT", [cc["nq"] + cc["nk"], T_tot], BF16)
            cv = dout("cv", [T_tot, cc["nv"]], BF16)

    P = Prog(nc)
    with ExitStack() as es:
        es.enter_context(nc.allow_low_precision("bf16 matmul operands, fp32 accumulate"))
        pfx = uniq()
        sb = lambda name, shape, dt: es.enter_context(nc.sbuf_tensor(pfx + name, shape, dt))
        psb = lambda name: es.enter_context(nc.psum_tensor(pfx + name, [128, 512], F32))
        R = Reg
        xs = sb("xs", [128, KC, T], F32)
        hT = sb("hT", [128, KC, T], BF16)
        ones = sb("ones", [128, 128], BF16)
        epsb = sb("epsb", [128, 1], F32)
        sq = [sb("sq%d" % i, [128, T], BF16) for i in range(2)]
        rstd = sb("rstd", [128, T], F32)
        st = [sb("st%d" % i, [128, T], F32) for i in range(2)]
        stb = [sb("stb%d" % i, [128, T], BF16) for i in range(2)]
        aT = [sb("aT%d" % i, [128, FB, T], BF16) for i in range(2)]
        wA = [sb("wA%d" % i, [128, KC, 128], BF16) for i in range(4)]
        wd = [sb("wd%d" % i, [128, FB, 512], BF16) for i in range(2)]
        nwa_s = sb("nwa_s", [128, KC], F32)
        nwb_s = sb("nwb_s", [128, KC], F32)
        ps_ss = psb("ps_ss")
        psA = [psb("psA%d" % i) for i in range(4)]
        ps_y = [psb("ps_y%d" % i) for i in range(2)]
        r_xs = [R() for c in range(KC)]
        r_hT = [R() for c in range(KC)]
        r_ones, r_eps, r_rstd, r_ss, r_nwa, r_nwb = R(), R(), R(), R(psum=True), R(), R()
        r_sq = [R(), R()]
        r_st = [R(), R()]
        r_stb = [R(), R()]
        r_aT = [[R() for j in range(FB)] for i in range(2)]
        r_wA = [R() for i in range(4)]
        r_wd = [R(), R()]
        r_psA = [R(psum=True) for i in range(4)]
        r_py = [R(psum=True), R(psum=True)]
        cnt = dict(wA=0, wd=0, psA=0, py=0, st=0, stb=0)

        def nxt(k, n):
            v = cnt[k] % n
            cnt[k] += 1
            return v

        P.op("pool", "memset", dict(ap=ones[:], constant=1.0), writes=[r_ones])
        P.op("pool", "memset", dict(ap=epsb[:], constant=1e-6), writes=[r_eps])
        if dff:
            P.op("sp", "dma_start", dict(out=nwa_s[:], in_=nwa), writes=[r_nwa], dma=True)
        if inproj or final_norm:
            P.op("sp", "dma_start", dict(out=nwb_s[:], in_=nwb), writes=[r_nwb], dma=True)
        if inproj == "ab":
            rC = sb("rC", [32, T_tot], F32)
            rS = sb("rS", [32, T_tot], F32)
            rP = sb("rP", [32, 32], BF16)
            t1 = sb("t1", [32, T], F32)
            t2 = sb("t2", [32, T], F32)
            r_rope, r_t1, r_t2 = R(), R(), R()
            P.op("sp", "dma_start", dict(out=rC[:], in_=ropeC), writes=[r_rope], dma=True)
            P.op("sp", "dma_start", dict(out=rS[:], in_=ropeS), writes=[r_rope], dma=True)
            P.op("pool", "dma_start", dict(out=rP[:], in_=ropeP), writes=[r_rope], dma=True)
        if inproj == "c":
            rC = sb("rC", [128, T_tot], F32)
            rS = sb("rS", [128, T_tot], F32)
            rP = sb("rP", [128, 128], BF16)
            gq_s = sb("gq_s", [128, 1], F32)
            gk_s = sb("gk_s", [128, 1], F32)
            t1 = sb("t1", [128, T], F32)
            t2 = sb("t2", [128, T], F32)
            rn = sb("rn", [128, T], F32)
            r_rope, r_t1, r_t2, r_rn = R(), R(), R(), R()
            P.op("sp", "dma_start", dict(out=rC[:], in_=ropeC), writes=[r_rope], dma=True)
            P.op("sp", "dma_start", dict(out=rS[:], in_=ropeS), writes=[r_rope], dma=True)
            P.op("pool", "dma_start", dict(out=rP[:], in_=ropeP), writes=[r_rope], dma=True)
            P.op("sp", "dma_start", dict(out=gq_s[:], in_=gq), writes=[r_rope], dma=True)
            P.op("sp", "dma_start", dict(out=gk_s[:], in_=gk), writes=[r_rope], dma=True)

        xT_v = xT.rearrange("(c p) t -> p c t", p=128)
        if FUSED and oK:
            idx_s = sb("idx_s", [128, 8], mybir.dt.int32)
            r_idx = R()
            P.op("sp", "dma_start", dict(out=idx_s[:], in_=TT_["idx"]), writes=[r_idx], dma=True)

        def rmsnorm(nw_s, r_nw, to_x=False):
            for c in range(KC):
                s = c % 2
                P.op("act", "activation", dict(out=sq[s][:], in_=xs[:, c, :], func=AF.Square),
                     reads=[r_xs[c]], writes=[r_sq[s]])
                P.op("pe", "matmul", dict(out=ps_ss[:], lhsT=ones[:], rhs=sq[s][:], start=(c == 0), stop=(c == KC - 1)),
                     reads=[r_ones, r_sq[s]], writes=[r_ss])
            P.op("act", "activation", dict(out=rstd[:], in_=ps_ss[:], func=AF.Sqrt, scale=1.0 / D, bias=epsb[:, 0:1]),
                 reads=[r_ss, r_eps], writes=[r_rstd])
            P.op("dve", "reciprocal", dict(out=rstd[:], in_=rstd[:]), reads=[r_rstd], writes=[r_rstd])
            for c in range(KC):
                if to_x:
                    P.op("dve", "scalar_tensor_tensor",
                         dict(out=xs[:, c, :], in0=xs[:, c, :], scalar=nw_s[:, c:c + 1], in1=rstd[:], op0=ALU.mult, op1=ALU.mult),
                         reads=[r_xs[c], r_nw, r_rstd], writes=[r_xs[c]])
                else:
                    P.op("dve", "scalar_tensor_tensor",
                         dict(out=hT[:, c, :], in0=xs[:, c, :], scalar=nw_s[:, c:c + 1], in1=rstd[:], op0=ALU.mult, op1=ALU.mult),
                         reads=[r_xs[c], r_nw, r_rstd], writes=[r_hT[c]])

        def accum_block(W_v, k0, nk, sl, rhs_regs):
            for cb in range(D // 512):
                w = nxt("wd", 2)
                P.op("pool", "dma_start", dict(out=wd[w][:, 0:nk, :], in_=W_v[:, k0:k0 + nk, cb * 512:(cb + 1) * 512]),
                     writes=[r_wd[w]], dma=True)
                for ct in range(4):
                    pb = nxt("py", 2)
                    for j in range(nk):
                        P.op("pe", "matmul", dict(out=ps_y[pb][:], lhsT=wd[w][:, j, ct * 128:(ct + 1) * 128], rhs=aT[sl][:, j, :],
                                                  start=(j == 0), stop=(j == nk - 1)),
                             reads=[r_wd[w], rhs_regs[j]], writes=[r_py[pb]])
                    c = cb * 4 + ct
                    P.op("dve", "tensor_tensor", dict(out=xs[:, c, :], in0=xs[:, c, :], in1=ps_y[pb][:], op=ALU.add),
                         reads=[r_xs[c], r_py[pb]], writes=[r_xs[c]])

        def fm_tile(W_v, col0, epilogue):
            w = nxt("wA", 4)
            P.op("pool", "dma_start", dict(out=wA[w][:], in_=W_v[:, :, col0:col0 + 128]), writes=[r_wA[w]], dma=True)
            pb = nxt("psA", 4)
            for kc in range(KC):
                P.op("pe", "matmul", dict(out=psA[pb][:], lhsT=wA[w][:, kc, :], rhs=hT[:, kc, :], start=(kc == 0), stop=(kc == KC - 1)),
                     reads=[r_wA[w], r_hT[kc]], writes=[r_psA[pb]])
            epilogue(psA[pb], r_psA[pb])

        def tm_tile(W_v, col0, ncl, out_ap, t0, dt, ocol, ab_special=False, fname=None, ti=0):
            w = nxt("wA", 4)
            P.op("pool", "dma_start", dict(out=wA[w][:, :, 0:ncl], in_=W_v[:, :, col0:col0 + ncl]), writes=[r_wA[w]], dma=True)
            pb = nxt("psA", 4)
            for tt in range(NTT):
                for kc in range(KC):
                    P.op("pe", "matmul", dict(out=psA[pb][:, tt * 128:tt * 128 + ncl], lhsT=hT[:, kc, tt * 128:(tt + 1) * 128],
                                              rhs=wA[w][:, kc, 0:ncl], start=(kc == 0), stop=(kc == KC - 1)),
                         reads=[r_wA[w], r_hT[kc]], writes=[r_psA[pb]])
            src = psA[pb][:].rearrange("p (t c) -> p t c", c=128)[:, :, 0:ncl]
            if dt == F32:
                s = nxt("st", 2)
                buf, rb = st[s], r_st[s]
            else:
                s = nxt("stb", 2)
                buf, rb = stb[s], r_stb[s]
            dst = buf[:].rearrange("p (t c) -> p t c", c=128)[:, :, 0:ncl]
            P.op("act", "copy", dict(out=dst, in_=src), reads=[r_psA[pb]], writes=[rb])
            if ab_special:
                for g_ in range(4):
                    for tt_ in range(NTT):
                        srcv = dst.rearrange("p t (x g h) -> p t x g h", x=4, g=4)[:, tt_, :, g_, :]
                        dstv = cfg["dst_ab"](t0 // T, g_, tt_).rearrange("p (x h) -> p x h", x=4)
                        P.op("sp", "dma_start", dict(out=dstv, in_=srcv), reads=[rb], dma=True)
                return
            if FUSED:
                for tt_ in range(NTT):
                    P.op("sp", "dma_start", dict(out=cfg["dst_tm"](fname, ti, t0 // T, tt_), in_=dst[:, tt_, :]), reads=[rb], dma=True)
                return
            P.op("sp", "dma_start", dict(out=out_ap[t0:t0 + T, ocol:ocol + ncl].rearrange("(t p) c -> p t c", p=128), in_=dst),
                 reads=[rb], dma=True)

        for t0 in range(0, T_tot, T):
            for c in range(KC):
                P.op("sp", "dma_start", dict(out=xs[:, c, :], in_=xT_v[:, c, t0:t0 + T]), writes=[r_xs[c]], dma=True)
            if oK:
                if not FUSED:
                    oT_v = oT.rearrange("(c p) t -> p c t", p=128)
                Wo_v = Wo.rearrange("(c p) n -> p c n", p=128)
                nob = oK // 128
                blocks = [(k0, min(k0 + FB, nob)) for k0 in range(0, nob, FB)]
                for bi, (k0, k1) in enumerate(blocks):
                    sl = bi % 2
                    for j in range(k1 - k0):
                        if FUSED:
                            G_, off_, ic_ = cfg["o_src"][t0 // T][k0 + j]
                            P.op("pool", "indirect_dma_start",
                                 dict(out=aT[sl][:, j, :], out_offset=None,
                                      in_offset=bass.IndirectOffsetOnAxis(ap=idx_s[:, ic_:ic_ + 1], axis=0), **gat(G_, off_, 0, T)),
                                 reads=[r_idx], writes=[r_aT[sl][j]], dma=True)
                        else:
                            P.op("sp", "dma_start", dict(out=aT[sl][:, j, :], in_=oT_v[:, k0 + j, t0:t0 + T]),
                                 writes=[r_aT[sl][j]], dma=True)
                    accum_block(Wo_v, k0, k1 - k0, sl, r_aT[sl])
            if dff:
                Wg_v = Wg.rearrange("(c p) n -> p c n", p=128)
                Wu_v = Wu.rearrange("(c p) n -> p c n", p=128)
                Wd_v = Wd.rearrange("(c p) n -> p c n", p=128)
                rmsnorm(nwa_s, r_nwa)
                NF = dff // 128
                blocks = [(f0, min(f0 + FB, NF)) for f0 in range(0, NF, FB)]

                def gate_up(bi):
                    f0, f1 = blocks[bi]
                    sl = bi % 2
                    for f in range(f0, f1):
                        res = []

                        def keep(ps_t, r_t):
                            res.append((ps_t, r_t))
                        fm_tile(Wg_v, f * 128, keep)
                        fm_tile(Wu_v, f * 128, keep)
                        (pg, rg), (pu, ru) = res
                        s = nxt("st", 2)
                        P.op("act", "activation", dict(out=st[s][:], in_=pg[:], func=AF.Silu), reads=[rg], writes=[r_st[s]])
                        P.op("dve", "tensor_tensor", dict(out=aT[sl][:, f - f0, :], in0=st[s][:], in1=pu[:], op=ALU.mult),
                             reads=[r_st[s], ru], writes=[r_aT[sl][f - f0]])

                gate_up(0)
                for bi in range(1, len(blocks)):
                    gate_up(bi)
                    f0, f1 = blocks[bi - 1]
                    accum_block(Wd_v, f0, f1 - f0, (bi - 1) % 2, r_aT[(bi - 1) % 2])
                f0, f1 = blocks[-1]
                accum_block(Wd_v, f0, f1 - f0, (len(blocks) - 1) % 2, r_aT[(len(blocks) - 1) % 2])
            if store_x:
                xo_v = xoT.rearrange("(c p) t -> p c t", p=128)
                for c in range(KC):
                    P.op("sp", "dma_start", dict(out=xo_v[:, c, t0:t0 + T], in_=xs[:, c, :]), reads=[r_xs[c]], dma=True)
            if inproj:
                Win_v = Win.rearrange("(c p) n -> p c n", p=128)
                rmsnorm(nwb_s, r_nwb)

                def ep_copy(out_ap, row0, dt, fname=None, ti=0):
                    def ep(ps_t, r_t):
                        if dt == F32:
                            s = nxt("st", 2)
                            buf, rb = st[s], r_st[s]
                        else:
                            s = nxt("stb", 2)
                            buf, rb = stb[s], r_stb[s]
                        P.op("act", "copy", dict(out=buf[:], in_=ps_t[:]), reads=[r_t], writes=[rb])
                        dst_ = cfg["dst_fm"](fname, ti, t0 // T) if FUSED else out_ap[row0:row0 + 128, t0:t0 + T]
                        P.op("sp", "dma_start", dict(out=dst_, in_=buf[:]), reads=[rb], dma=True)
                    return ep

            if inproj == "ab":
                def ep_rope_b(row0, ti=0):
                    def ep(ps_t, r_t):
                        s = nxt("stb", 2)
                        buf, rb = stb[s], r_stb[s]
                        P.op("act", "copy", dict(out=buf[:], in_=ps_t[:]), reads=[r_t], writes=[rb])
                        pb = nxt("py", 2)
                        P.op("pe", "matmul", dict(out=ps_y[pb][0:32, :], lhsT=rP[:, :], rhs=buf[0:32, :], start=True, stop=True),
                             reads=[r_rope, rb], writes=[r_py[pb]])
                        P.op("dve", "tensor_tensor", dict(out=t1[:], in0=buf[0:32, :], in1=rC[:, t0:t0 + T], op=ALU.mult),
                             reads=[rb, r_rope], writes=[r_t1])
                        P.op("dve", "tensor_tensor", dict(out=t2[:], in0=ps_y[pb][0:32, :], in1=rS[:, t0:t0 + T], op=ALU.mult),
                             reads=[r_py[pb], r_rope], writes=[r_t2])
                        P.op("dve", "tensor_tensor", dict(out=buf[0:32, :], in0=t1[:], in1=t2[:], op=ALU.add),
                             reads=[r_t1, r_t2], writes=[rb])
                        dst_ = cfg["dst_fm"]("bqk", ti, t0 // T) if FUSED else bqkT[row0:row0 + 128, t0:t0 + T]
                        P.op("sp", "dma_start", dict(out=dst_, in_=buf[:]), reads=[rb], dma=True)
                    return ep
                c0 = 0
                for i in range(ab["nqkv"] // 128):
                    fm_tile(Win_v, c0 + i * 128, ep_copy(aqkvT, i * 128, F32, "aqkv", i))
                c0 += ab["nqkv"]
                for i in range(ab["nz"] // 128):
                    if FUSED:
                        tm_tile(Win_v, c0 + i * 128, 128, None, t0, F32, 0, fname="az", ti=i)
                    else:
                        tm_tile(Win_v, c0 + i * 128, 128, az, t0, F32, i * 128)
                c0 += ab["nz"]
                tm_tile(Win_v, c0, ab["nab"], abo, t0, F32, 0, ab_special=FUSED)
                c0 += ab["nab"]
                for i in range(ab["nbqk"] // 128):
                    fm_tile(Win_v, c0 + i * 128, ep_rope_b(i * 128, i))
                c0 += ab["nbqk"]
                for i in range(ab["nbv"] // 128):
                    if FUSED:
                        tm_tile(Win_v, c0 + i * 128, 128, None, t0, BF16, 0, fname="bv", ti=i)
                    else:
                        tm_tile(Win_v, c0 + i * 128, 128, bv, t0, BF16, i * 128)
            if inproj == "c":
                def ep_c(row0, g_s, ti=0):
                    def ep(ps_t, r_t):
                        s = nxt("stb", 2)
                        buf, rb = stb[s], r_stb[s]
                        P.op("act", "activation", dict(out=sq[0][:], in_=ps_t[:], func=AF.Square), reads=[r_t], writes=[r_sq[0]])
                        pb = nxt("py", 2)
                        P.op("pe", "matmul", dict(out=ps_y[pb][:], lhsT=ones[:], rhs=sq[0][:], start=True, stop=True),
                             reads=[r_ones, r_sq[0]], writes=[r_py[pb]])
                        P.op("act", "activation", dict(out=rn[:], in_=ps_y[pb][:], func=AF.Sqrt, scale=1.0 / 128, bias=epsb[:, 0:1]),
                             reads=[r_py[pb], r_eps], writes=[r_rn])
                        P.op("dve", "reciprocal", dict(out=rn[:], in_=rn[:]), reads=[r_rn], writes=[r_rn])
                        P.op("dve", "scalar_tensor_tensor",
                             dict(out=buf[:], in0=ps_t[:], scalar=g_s[:, 0:1], in1=rn[:], op0=ALU.mult, op1=ALU.mult),
                             reads=[r_t, r_rope, r_rn], writes=[rb])
                        pb2 = nxt("py", 2)
                        P.op("pe", "matmul", dict(out=ps_y[pb2][:], lhsT=rP[:, :], rhs=buf[:], start=True, stop=True),
                             reads=[r_rope, rb], writes=[r_py[pb2]])
                        P.op("dve", "tensor_tensor", dict(out=t1[:], in0=buf[:], in1=rC[:, t0:t0 + T], op=ALU.mult),
                             reads=[rb, r_rope], writes=[r_t1])
                        P.op("dve", "tensor_tensor", dict(out=t2[:], in0=ps_y[pb2][:], in1=rS[:, t0:t0 + T], op=ALU.mult),
                             reads=[r_py[pb2], r_rope], writes=[r_t2])
                        P.op("dve", "tensor_tensor", dict(out=buf[:], in0=t1[:], in1=t2[:], op=ALU.add),
                             reads=[r_t1, r_t2], writes=[rb])
                        dst_ = cfg["dst_fm"]("cqk", ti, t0 // T) if FUSED else cqkT[row0:row0 + 128, t0:t0 + T]
                        P.op("sp", "dma_start", dict(out=dst_, in_=buf[:]), reads=[rb], dma=True)
                    return ep
                nqt = cc["nq"] // 128
                nkt = cc["nk"] // 128
                for i in range(nqt):
                    fm_tile(Win_v, i * 128, ep_c(i * 128, gq_s, i))
                for i in range(nkt):
                    fm_tile(Win_v, cc["nq"] + i * 128, ep_c(cc["nq"] + i * 128, gk_s, nqt + i))
                for i in range(cc["nv"] // 128):
                    if FUSED:
                        tm_tile(Win_v, cc["nq"] + cc["nk"] + i * 128, 128, None, t0, BF16, 0, fname="cv", ti=i)
                    else:
                        tm_tile(Win_v, cc["nq"] + cc["nk"] + i * 128, 128, cv, t0, BF16, i * 128)
            if final_norm:
                rmsnorm(nwb_s, r_nwb, to_x=True)
                o_v = outT.rearrange("(c p) t -> p c t", p=128)
                for c in range(KC):
                    P.op("sp", "dma_start", dict(out=o_v[:, c, t0:t0 + T], in_=xs[:, c, :]), reads=[r_xs[c]], dma=True)
        P.emit()
    if FUSED:
        nc.all_engine_barrier()
    return nc


def attn_c_build(cfg):
    S = cfg.get("S", 4096)
    NKV = cfg.get("NKV", 2)
    REP = cfg.get("REP", 4)
    NQH = NKV * REP
    NKC = S // 128
    QB = 512
    scale = 128 ** -0.5
    TT_ = cfg.get("TD")
    FUSED = TT_ is not None
    if FUSED:
        nc = cfg["nc"]
        GQ, GK, GV, OCB = TT_["GQ"], TT_["GK"], TT_["GV"], TT_["OCB"]
    else:
        nc = bass.Bass("TRN2", target_bir_lowering=False)
        cqT = nc.dram_tensor("cqT", [NQH * 128, S], BF16, kind="ExternalInput").ap()
        ckT = nc.dram_tensor("ckT", [NKV * 128, S], BF16, kind="ExternalInput").ap()
        cv = nc.dram_tensor("cv", [S, NKV * 128], BF16, kind="ExternalInput").ap()
        ocT = nc.dram_tensor("ocT", [NQH * 128, S], BF16, kind="ExternalOutput").ap()
    P = Prog(nc)
    with ExitStack() as es:
        es.enter_context(nc.allow_low_precision("bf16 matmul operands, fp32 accumulate"))
        pfx = uniq()
        sb = lambda name, shape, dt: es.enter_context(nc.sbuf_tensor(pfx + name, shape, dt))
        psb = lambda name: es.enter_context(nc.psum_tensor(pfx + name, [128, 512], F32))
        R = Reg
        kT = [sb("kT%d" % i, [128, S], BF16) for i in range(2)]
        vv = [sb("v%d" % i, [128, NKC, 128], BF16) for i in range(2)]
        qT = [sb("qT%d" % i, [128, S], BF16) for i in range(2)]
        NE = 3
        ee = [sb("e%d" % i, [128, QB], BF16) for i in range(NE)]
        rz = sb("rz", [128, QB], F32)
        ob = [sb("ob%d" % i, [128, QB], BF16) for i in range(2)]
        ones = sb("ones", [128, 128], BF16)
        ps_s = [psb("ps_s%d" % i) for i in range(NE)]
        ps_o = [psb("ps_o%d" % i) for i in range(2)]
        ps_z = [psb("ps_z%d" % i) for i in range(2)]
        r_kT, r_v, r_qT = [R(), R()], [R(), R()], [R(), R()]
        r_e = [R() for i in range(NE)]
        r_rz, r_ones = R(), R()
        r_ob = [R(), R()]
        r_ps = [R(psum=True) for i in range(NE)]
        r_po, r_pz = [R(psum=True), R(psum=True)], [R(psum=True), R(psum=True)]
        P.op("pool", "memset", dict(ap=ones[:], constant=1.0), writes=[r_ones])
        if FUSED:
            idx_s = sb("idx_s", [128, 8], mybir.dt.int32)
            r_idx = R()
            P.op("sp", "dma_start", dict(out=idx_s[:], in_=TT_["idx"]), writes=[r_idx], dma=True)
            QS = S // 4

            def gather(out_ap, G_, off_, span_, ic_, cols, wr):
                P.op("pool", "indirect_dma_start",
                     dict(out=out_ap, out_offset=None,
                          in_offset=bass.IndirectOffsetOnAxis(ap=idx_s[:, ic_:ic_ + 1], axis=0), **gat(G_, off_, cols.start, cols.stop - cols.start)),
                     reads=[r_idx], writes=[wr], dma=True)
        it = 0
        blk = 0
        for kv in range(NKV):
            ks = kv % 2
            if FUSED:
                for r_ in range(4):
                    for ps_ in range(2):
                        gather(kT[ks][:, r_ * QS + ps_ * 512:r_ * QS + (ps_ + 1) * 512], GK[ps_][kv], r_ * 512, None, 0, slice(0, 512), r_kT[ks])
                for c_ in range(NKC):
                    r_, tl_ = c_ // 8, c_ % 8
                    gather(vv[ks][:, c_, :], GV[tl_ // 4][tl_ % 4], r_ * 512, None, 1, slice(kv * 128, (kv + 1) * 128), r_v[ks])
            else:
                P.op("sp", "dma_start", dict(out=kT[ks][:], in_=ckT[kv * 128:(kv + 1) * 128, :]), writes=[r_kT[ks]], dma=True)
                P.op("sp", "dma_start", dict(out=vv[ks][:], in_=cv[:, kv * 128:(kv + 1) * 128].rearrange("(c p) d -> p c d", p=128)),
                     writes=[r_v[ks]], dma=True)
            for r in range(REP):
                h = kv * REP + r
                qs = h % 2
                if FUSED:
                    for r_ in range(4):
                        for ps_ in range(2):
                            gather(qT[qs][:, r_ * QS + ps_ * 512:r_ * QS + (ps_ + 1) * 512], GQ[ps_][h], r_ * 512, None, 0, slice(0, 512), r_qT[qs])
                else:
                    P.op("sp", "dma_start", dict(out=qT[qs][:], in_=cqT[h * 128:(h + 1) * 128, :]), writes=[r_qT[qs]], dma=True)
                for qb in range(S // QB):
                    pb = blk % 2
                    blk += 1
                    qsl = qT[qs][:, qb * QB:(qb + 1) * QB]

                    def smm(kc, i):
                        P.op("pe", "matmul", dict(out=ps_s[i][:], lhsT=kT[ks][:, kc * 128:(kc + 1) * 128], rhs=qsl, start=True, stop=True),
                             reads=[r_kT[ks], r_qT[qs]], writes=[r_ps[i]])
                    smm(0, it % NE)
                    for kc in range(NKC):
                        i = it % NE
                        it += 1
                        if kc + 1 < NKC:
                            smm(kc + 1, it % NE)
                        P.op("act", "activation", dict(out=ee[i][:], in_=ps_s[i][:], func=AF.Exp, scale=scale),
                             reads=[r_ps[i]], writes=[r_e[i]])
                        P.op("pe", "matmul", dict(out=ps_o[pb][:], lhsT=vv[ks][:, kc, :], rhs=ee[i][:], start=(kc == 0), stop=(kc == NKC - 1)),
                             reads=[r_v[ks], r_e[i]], writes=[r_po[pb]])
                        P.op("pe", "matmul", dict(out=ps_z[pb][:], lhsT=ones[:], rhs=ee[i][:], start=(kc == 0), stop=(kc == NKC - 1)),
                             reads=[r_ones, r_e[i]], writes=[r_pz[pb]])
                    P.op("dve", "reciprocal", dict(out=rz[:], in_=ps_z[pb][:]), reads=[r_pz[pb]], writes=[r_rz])
                    P.op("dve", "tensor_tensor", dict(out=ob[pb][:], in0=ps_o[pb][:], in1=rz[:], op=ALU.mult),
                         reads=[r_po[pb], r_rz], writes=[r_ob[pb]])
                    if FUSED:
                        q_, th_ = qb // 2, qb % 2
                        P.op("sp", "dma_start", dict(out=OCB[h * 2 + th_][q_ * 128:(q_ + 1) * 128, :], in_=ob[pb][:]),
                             reads=[r_ob[pb]], dma=True)
                    else:
                        P.op("sp", "dma_start", dict(out=ocT[h * 128:(h + 1) * 128, qb * QB:(qb + 1) * QB], in_=ob[pb][:]),
                             reads=[r_ob[pb]], dma=True)
        P.emit()
    if FUSED:
        nc.all_engine_barrier()
    return nc


B_PATTERNS = ((128, 1), (512, 4), (2048, 16))


def ssl(a, n, d):
    return slice(a, a + d * (n - 1) + 1, d)


def dil_b_build(cfg):
    S = cfg.get("S", 4096)
    NHS = cfg.get("NHS", 2)
    pats = cfg.get("pats", B_PATTERNS)
    NG = len(pats)
    scale = 128 ** -0.5
    TT_ = cfg.get("TD")
    FUSED = TT_ is not None
    if FUSED:
        nc = cfg["nc"]
        bqT, bkT, bv, bmask, ob_q = TT_["bqT"], TT_["bkT"], TT_["bv"], TT_["bmask"], TT_["ob_q"]
    else:
        nc = bass.Bass("TRN2", target_bir_lowering=False)
        bqT = nc.dram_tensor("bqT", [NHS * NG * 128, S], BF16, kind="ExternalInput").ap()
        bkT = nc.dram_tensor("bkT", [NHS * NG * 128, S], BF16, kind="ExternalInput").ap()
        bv = nc.dram_tensor("bv", [S, NHS * NG * 128], BF16, kind="ExternalInput").ap()
        bmask = nc.dram_tensor("bmask", [128, 3, 512], BF16, kind="ExternalInput").ap()
        obT = nc.dram_tensor("obT", [NHS * 128, S], BF16, kind="ExternalOutput").ap()
    P = Prog(nc)
    with ExitStack() as es:
        es.enter_context(nc.allow_low_precision("bf16 matmul operands, fp32 accumulate"))
        pfx = uniq()
        sb = lambda name, shape, dt: es.enter_context(nc.sbuf_tensor(pfx + name, shape, dt))
        psb = lambda name: es.enter_context(nc.psum_tensor(pfx + name, [128, 512], F32))
        R = Reg
        qT = [sb("qT%d" % i, [128, S], BF16) for i in range(2)]
        kT = [sb("kT%d" % i, [128, S], BF16) for i in range(2)]
        vp = [sb("vp%d" % i, [128, S // 128, 128], BF16) for i in range(2)]
        Uacc = sb("Uacc", [128, S], F32)
        Zacc = sb("Zacc", [128, S], F32)
        ob = sb("ob", [128, S], BF16)
        ee = [sb("e%d" % i, [128, 512], BF16) for i in range(2)]
        em = [sb("em%d" % i, [128, 512], BF16) for i in range(2)]
        mk = sb("mk", [128, 3, 512], BF16)
        ones = sb("ones", [128, 128], BF16)
        ps_s = [psb("ps_s%d" % i) for i in range(2)]
        ps_o = [psb("ps_o%d" % i) for i in range(2)]
        ps_z = [psb("ps_z%d" % i) for i in range(2)]
        r_q, r_k, r_v = [R(), R()], [R(), R()], [R(), R()]
        r_U, r_Z, r_ob, r_mk, r_ones = R(), R(), R(), R(), R()
        r_e, r_em = [R(), R()], [R(), R()]
        r_ps, r_po, r_pz = [R(psum=True), R(psum=True)], [R(psum=True), R(psum=True)], [R(psum=True), R(psum=True)]
        P.op("pool", "memset", dict(ap=ones[:], constant=1.0), writes=[r_ones])
        P.op("sp", "dma_start", dict(out=mk[:], in_=bmask), writes=[r_mk], dma=True)
        gi = 0
        sc = 0
        bc = 0
        for hs in range(NHS):
            for g, (wd_, d) in enumerate(pats):
                s = gi % 2
                gi += 1
                row = (hs * NG + g) * 128
                L = S // d
                nblk = L // 128
                P.op("sp", "dma_start", dict(out=qT[s][:], in_=bqT[row:row + 128, :]), writes=[r_q[s]], dma=True)
                P.op("sp", "dma_start", dict(out=kT[s][:], in_=bkT[row:row + 128, :]), writes=[r_k[s]], dma=True)
                P.op("sp", "dma_start",
                     dict(out=vp[s][:].rearrange("p (r i) c -> p r i c", r=d),
                          in_=bv[:, row:row + 128].rearrange("(i p r) c -> p r i c", p=128, r=d)),
                     writes=[r_v[s]], dma=True)
                for r in range(d):
                    for i0 in range(0, nblk, 4):
                        nb = min(4, nblk - i0)
                        pb = bc % 2
                        bc += 1
                        for o in (0, -1, 1):
                            blo = 0
                            bhi = nb
                            if o == -1 and i0 == 0:
                                blo = 1
                            if o == 1 and i0 + nb == nblk:
                                bhi = nb - 1
                            if bhi <= blo:
                                continue
                            ss = sc % 2
                            sc += 1
                            for b in range(blo, bhi):
                                i = i0 + b
                                ka = r + d * 128 * (i + o)
                                qa = r + d * 128 * i
                                P.op("pe", "matmul", dict(out=ps_s[ss][:, b * 128:(b + 1) * 128],
                                                          lhsT=kT[s][:, ssl(ka, 128, d)], rhs=qT[s][:, ssl(qa, 128, d)],
                                                          start=True, stop=True),
                                     reads=[r_k[s], r_q[s]], writes=[r_ps[ss]])
                            cs = slice(blo * 128, bhi * 128)
                            P.op("act", "activation", dict(out=ee[ss][:, cs], in_=ps_s[ss][:, cs], func=AF.Exp, scale=scale),
                                 reads=[r_ps[ss]], writes=[r_e[ss]])
                            P.op("dve", "tensor_tensor", dict(out=em[ss][:, cs], in0=ee[ss][:, cs], in1=mk[:, o + 1, cs], op=ALU.mult),
                                 reads=[r_e[ss], r_mk], writes=[r_em[ss]])
                            for b in range(blo, bhi):
                                i = i0 + b
                                last = (o == 1) or (o == -1 and i == nblk - 1) or (o == 0 and nblk == 1)
                                bs = slice(b * 128, (b + 1) * 128)
                                P.op("pe", "matmul", dict(out=ps_o[pb][:, bs], lhsT=vp[s][:, r * nblk + i + o, :], rhs=em[ss][:, bs],
                                                          start=(o == 0 and b == 0), stop=last, skip_group_check=True),
                                     reads=[r_v[s], r_em[ss]], writes=[r_po[pb]])
                                P.op("pe", "matmul", dict(out=ps_z[pb][:, bs], lhsT=ones[:], rhs=em[ss][:, bs],
                                                          start=(o == 0 and b == 0), stop=last, skip_group_check=True),
                                     reads=[r_ones, r_em[ss]], writes=[r_pz[pb]])
                        a0 = r + d * 128 * i0
                        usl = Uacc[:, ssl(a0, 128 * nb, d)]
                        zsl = Zacc[:, ssl(a0, 128 * nb, d)]
                        if g == 0:
                            P.op("act", "copy", dict(out=usl, in_=ps_o[pb][:, 0:nb * 128]), reads=[r_po[pb]], writes=[r_U])
                            P.op("dve", "tensor_copy", dict(out=zsl, in_=ps_z[pb][:, 0:nb * 128]), reads=[r_pz[pb]], writes=[r_Z])
                        else:
                            P.op("dve", "tensor_tensor", dict(out=usl, in0=usl, in1=ps_o[pb][:, 0:nb * 128], op=ALU.add),
                                 reads=[r_po[pb], r_U], writes=[r_U])
                            P.op("dve", "tensor_tensor", dict(out=zsl, in0=zsl, in1=ps_z[pb][:, 0:nb * 128], op=ALU.add),
                                 reads=[r_pz[pb], r_Z], writes=[r_Z])
            P.op("dve", "reciprocal", dict(out=Zacc[:], in_=Zacc[:]), reads=[r_Z], writes=[r_Z])
            P.op("dve", "tensor_tensor", dict(out=ob[:], in0=Uacc[:], in1=Zacc[:], op=ALU.mult), reads=[r_U, r_Z], writes=[r_ob])
            if FUSED:
                for th_ in range(2):
                    P.op("sp", "dma_start", dict(out=ob_q[hs * 2 + th_].rearrange("(q p) t -> p q t", q=4),
                                                 in_=ob[:].rearrange("p (q h t) -> p q h t", q=4, h=2)[:, :, th_, :]), reads=[r_ob], dma=True)
            else:
                P.op("sp", "dma_start", dict(out=obT[hs * 128:(hs + 1) * 128, :], in_=ob[:]), reads=[r_ob], dma=True)
        P.emit()
    if FUSED:
        nc.all_engine_barrier()
    return nc


def dil_mask():
    import numpy as _np
    m = _np.zeros((128, 3, 512), _np.float32)
    p = _np.arange(128)[:, None]
    n = _np.arange(128)[None, :]
    for o in (-1, 0, 1):
        mm = (_np.abs(128 * o + p - n) <= 64).astype(_np.float32)
        m[:, o + 1, :] = _np.tile(mm, (1, 4))
    return m

import numpy as _np

BIG = 30000.0


def gdn_consts():
    k = _np.arange(128)[:, None]
    i = _np.arange(128)[None, :]
    c = {}
    c["ident"] = _np.eye(128, dtype=_np.float32)
    c["ucum"] = _np.stack([(k <= i), (k >= i)], 1).astype(_np.float32)
    nmd_f = BIG * (k <= i)
    nmd_b = BIG * (k >= i)
    nmt_f = -BIG * (i < k)
    nmt_b = -BIG * (i > k)
    c["nm"] = _np.stack([nmd_f, nmt_f, nmd_b, nmt_b], 1).astype(_np.float32)
    return c


def gdn_build(cfg):
    S = cfg.get("S", 4096)
    NH = cfg.get("NH", 4)
    NCH = S // 128
    NB = S // 512
    STOP = cfg.get("stop", 9)
    CHD = F32 if cfg.get("chain_fp32", True) else BF16
    SUB = cfg.get("sub", 9)
    TT_ = cfg.get("TD")
    FUSED = TT_ is not None
    nc = cfg["nc"] if FUSED else bass.Bass("TRN2", target_bir_lowering=False)
    if FUSED:
        din = lambda name, shape, dt=F32: TT_[name]
    else:
        din = lambda name, shape, dt=F32: nc.dram_tensor(name, shape, dt, kind="ExternalInput").ap()
    aqkvT = din("aqkvT", [NH * 3 * 128, S])
    az = din("az", [S, NH * 128])
    abr = din("abr", [S, 4 * NH])
    cw = din("cw", [128, NH * 3, 5])
    alog = din("alog", [128, 2 * NH])
    dtb = din("dtb", [128, 2 * NH])
    onorm = din("onorm", [128, 128])
    ident_d = din("ident_in", [128, 128])
    ucum_d = din("ucum_in", [128, 2, 128])
    nm_d = din("nm_in", [128, 4, 128])
    oaT = TT_["oa_q"] if FUSED else nc.dram_tensor("oaT", [NH * 128, S], BF16, kind="ExternalOutput").ap()
    P = Prog(nc)
    NC2 = 2 * NH
    with ExitStack() as es:
        es.enter_context(nc.allow_low_precision("bf16 matmul operands, fp32 accumulate"))
        pfx = uniq()
        sb = lambda name, shape, dt: es.enter_context(nc.sbuf_tensor(pfx + name, shape, dt))
        psb = lambda name, dt=F32, n=512: es.enter_context(nc.psum_tensor(pfx + name, [128, n], dt))
        R = Reg
        identf = sb("identf", [128, 128], F32)
        identb = sb("identb", [128, 128], BF16)
        ucum = sb("ucum", [128, 2, 128], F32)
        nm = sb("nm", [128, 4, 128], F32)
        onesf = sb("onesf", [128, 128], F32)
        onesb = sb("onesb", [128, 128], BF16)
        epsb = sb("epsb", [128, 1], F32)
        oneb = sb("oneb", [128, 1], F32)
        cws = sb("cws", [128, NH * 3, 5], F32)
        onorm_s = sb("onorm_s", [128, 128], F32)
        r_c = R()
        identc = identf if CHD == F32 else identb
        P.op("sp", "dma_start", dict(out=identf[:], in_=ident_d), writes=[r_c], dma=True)
        P.op("pool", "dma_start", dict(out=identb[:], in_=ident_d), writes=[r_c], dma=True)
        P.op("sp", "dma_start", dict(out=ucum[:], in_=ucum_d), writes=[r_c], dma=True)
        P.op("sp", "dma_start", dict(out=nm[:], in_=nm_d), writes=[r_c], dma=True)
        P.op("sp", "dma_start", dict(out=cws[:], in_=cw), writes=[r_c], dma=True)
        P.op("sp", "dma_start", dict(out=onorm_s[:], in_=onorm), writes=[r_c], dma=True)
        P.op("pool", "memset", dict(ap=onesf[:], constant=1.0), writes=[r_c])
        P.op("pool", "memset", dict(ap=onesb[:], constant=1.0), writes=[r_c])
        P.op("pool", "memset", dict(ap=epsb[:], constant=1e-6), writes=[r_c])
        P.op("pool", "memset", dict(ap=oneb[:], constant=1.0), writes=[r_c])

        NCOL = NCH * NC2
        raw = sb("raw", [128, NCH, 2 * NC2], F32)
        alog_s = sb("alog_s", [128, NC2], F32)
        dtb_s = sb("dtb_s", [128, NC2], F32)
        beta = sb("beta", [128, NCH, NC2], F32)
        nbeta = sb("nbeta", [128, NCH, NC2], F32)
        gg = sb("gg", [128, NCH, NC2], F32)
        gc = sb("gc", [128, NCH, NC2], F32)
        ngc = sb("ngc", [128, NCH, NC2], F32)
        gtot = sb("gtot", [128, NCH, NC2], F32)
        egc = sb("egc", [128, NCH, NC2], F32)
        begc = sb("begc", [128, NCH, NC2], F32)
        ekd = sb("ekd", [128, NCH, NC2], F32)
        egl = sb("egl", [128, NCH, NC2], F32)
        r_g = R()
        ps_m = psb("ps_m")
        r_pm = R(psum=True)
        P.op("sp", "dma_start", dict(out=raw[:], in_=abr.rearrange("(c p) n -> p c n", p=128)), writes=[r_g], dma=True)
        P.op("sp", "dma_start", dict(out=alog_s[:], in_=alog), writes=[r_g], dma=True)
        P.op("sp", "dma_start", dict(out=dtb_s[:], in_=dtb), writes=[r_g], dma=True)
        P.op("act", "activation", dict(out=beta[:], in_=raw[:, :, 0:NC2], func=AF.Sigmoid), reads=[r_g], writes=[r_g])
        P.op("dve", "tensor_scalar", dict(out=nbeta[:], in0=beta[:], scalar1=-1.0, scalar2=None, op0=ALU.mult), reads=[r_g], writes=[r_g])
        P.op("dve", "tensor_tensor", dict(out=gg[:], in0=raw[:, :, NC2:2 * NC2], in1=dtb_s[:, None, :].to_broadcast([128, NCH, NC2]), op=ALU.add),
             reads=[r_g], writes=[r_g])
        sp1 = sb("sp1", [128, NCH, NC2], F32)
        sp2 = sb("sp2", [128, NCH, NC2], F32)
        sp3 = sb("sp3", [128, NCH, NC2], F32)
        P.op("dve", "tensor_scalar", dict(out=sp1[:], in0=gg[:], scalar1=-1.0, scalar2=None, op0=ALU.mult), reads=[r_g], writes=[r_g])
        P.op("dve", "tensor_tensor", dict(out=sp1[:], in0=sp1[:], in1=gg[:], op=ALU.max), reads=[r_g], writes=[r_g])
        P.op("act", "activation", dict(out=sp1[:], in_=sp1[:], func=AF.Exp, scale=-1.0), reads=[r_g], writes=[r_g])
        P.op("dve", "tensor_scalar", dict(out=sp2[:], in0=sp1[:], scalar1=2.0, scalar2=None, op0=ALU.add), reads=[r_g], writes=[r_g])
        P.op("dve", "reciprocal", dict(out=sp2[:], in_=sp2[:]), reads=[r_g], writes=[r_g])
        P.op("dve", "tensor_tensor", dict(out=sp1[:], in0=sp1[:], in1=sp2[:], op=ALU.mult), reads=[r_g], writes=[r_g])
        P.op("dve", "tensor_tensor", dict(out=sp2[:], in0=sp1[:], in1=sp1[:], op=ALU.mult), reads=[r_g], writes=[r_g])
        P.op("dve", "tensor_scalar", dict(out=sp3[:], in0=sp2[:], scalar1=1.0 / 11, scalar2=1.0 / 9, op0=ALU.mult, op1=ALU.add), reads=[r_g], writes=[r_g])
        for cst_ in (1.0 / 7, 1.0 / 5, 1.0 / 3, 1.0):
            P.op("dve", "tensor_tensor", dict(out=sp3[:], in0=sp3[:], in1=sp2[:], op=ALU.mult), reads=[r_g], writes=[r_g])
            P.op("dve", "tensor_scalar", dict(out=sp3[:], in0=sp3[:], scalar1=cst_, scalar2=None, op0=ALU.add), reads=[r_g], writes=[r_g])
        P.op("dve", "tensor_tensor", dict(out=sp3[:], in0=sp3[:], in1=sp1[:], op=ALU.mult), reads=[r_g], writes=[r_g])
        P.op("dve", "tensor_scalar", dict(out=sp1[:], in0=gg[:], scalar1=0.0, scalar2=None, op0=ALU.max), reads=[r_g], writes=[r_g])
        P.op("dve", "scalar_tensor_tensor", dict(out=gg[:], in0=sp3[:], scalar=2.0, in1=sp1[:], op0=ALU.mult, op1=ALU.add), reads=[r_g], writes=[r_g])
        P.op("act", "activation", dict(out=alog_s[:], in_=alog_s[:], func=AF.Exp), reads=[r_g], writes=[r_g])
        P.op("dve", "scalar_tensor_tensor", dict(out=gg[:], in0=gg[:], scalar=-1.0, in1=alog_s[:, None, :].to_broadcast([128, NCH, NC2]),
                                                 op0=ALU.mult, op1=ALU.mult), reads=[r_g], writes=[r_g])
        ggv = gg[:].rearrange("p c (d h) -> p c d h", d=2)
        gcv = gc[:].rearrange("p c (d h) -> p c d h", d=2)
        gtv = gtot[:].rearrange("p c (d h) -> p c d h", d=2)
        psv = ps_m[:, 0:NCH * NC2].rearrange("p (c d h) -> p c d h", c=NCH, d=2)
        pst = ps_m[:, 256:256 + NCH * NC2].rearrange("p (c d h) -> p c d h", c=NCH, d=2)
        assert NCH * NC2 <= 256
        for d in range(2):
            P.op("pe", "matmul", dict(out=psv[:, :, d, :], lhsT=ucum[:, d, :], rhs=ggv[:, :, d, :], start=(d == 0), stop=True, skip_group_check=True),
                 reads=[r_c, r_g], writes=[r_pm])
        P.op("pe", "matmul", dict(out=ps_m[:, 256:256 + NCH * NC2], lhsT=onesf[:], rhs=gg[:].rearrange("p c n -> p (c n)"),
                                  start=False, stop=True, skip_group_check=True), reads=[r_c, r_g], writes=[r_pm])
        P.op("dve", "tensor_copy", dict(out=gc[:].rearrange("p c n -> p (c n)"), in_=ps_m[:, 0:NCH * NC2]), reads=[r_pm], writes=[r_g])
        P.op("dve", "tensor_copy", dict(out=gtot[:].rearrange("p c n -> p (c n)"), in_=ps_m[:, 256:256 + NCH * NC2]), reads=[r_pm], writes=[r_g])
        P.op("dve", "tensor_scalar", dict(out=ngc[:], in0=gc[:], scalar1=-1.0, scalar2=None, op0=ALU.mult), reads=[r_g], writes=[r_g])
        P.op("act", "activation", dict(out=egc[:], in_=gc[:], func=AF.Exp), reads=[r_g], writes=[r_g])
        P.op("dve", "tensor_tensor", dict(out=begc[:], in0=egc[:], in1=beta[:], op=ALU.mult), reads=[r_g], writes=[r_g])
        P.op("dve", "tensor_tensor", dict(out=ekd[:], in0=gtot[:], in1=gc[:], op=ALU.subtract), reads=[r_g], writes=[r_g])
        P.op("act", "activation", dict(out=ekd[:], in_=ekd[:], func=AF.Exp), reads=[r_g], writes=[r_g])
        P.op("act", "activation", dict(out=egl[:], in_=gtot[:], func=AF.Exp), reads=[r_g], writes=[r_g])

        NHX = NH if STOP >= 1 else 0
        xin = sb("xin", [128, S + 4], F32)
        acc = sb("acc", [128, S], F32)
        sqb = sb("sqb", [128, S], BF16)
        fT = [sb("fT%d" % i, [128, S], BF16) for i in range(3)]
        kbg = [sb("kbg%d" % i, [128, NCH, 128], BF16) for i in range(2)]
        kdd = [sb("kdd%d" % i, [128, NCH, 128], BF16) for i in range(2)]
        vbd = [sb("vbd%d" % i, [128, NCH, 128], BF16) for i in range(2)]
        oacc = sb("oacc", [128, NCH, 128], F32)
        zt = sb("zt", [128, NCH, 128], F32)
        rn = sb("rn", [128, 512], F32)
        ssn = sb("ssn", [128, NCH], F32)
        ogb = sb("ogb", [128, NCH, 128], BF16)
        oTs = sb("oTs", [128, S], BF16)
        r_xin, r_acc, r_sqb, r_rn = R(), R(), R(), R()
        r_fT = [R(), R(), R()]
        r_tok = R()
        r_oacc = [R() for c in range(NCH)]
        r_zt, r_ssn, r_ogb, r_oTs = R(), R(), R(), R()
        Gb = [[sb("Gb%d%d" % (d, i), [128, 128], F32) for i in range(2)] for d in range(2)]
        dec = [[sb("dec%d%d" % (d, i), [128, 2, 128], F32) for i in range(2)] for d in range(2)]
        Nb = [[sb("Nb%d%d" % (d, i), [128, 128], CHD) for i in range(2)] for d in range(2)]
        Mb = [[sb("Mb%d%d" % (d, i), [128, 128], CHD) for i in range(2)] for d in range(2)]
        Pb = [[sb("Pb%d%d" % (d, i), [128, 128], CHD) for i in range(2)] for d in range(2)]
        TT = [[sb("TT%d%d" % (d, i), [128, 128], BF16) for i in range(2)] for d in range(2)]
        wTn = [[sb("wTn%d%d" % (d, i), [128, 128], BF16) for i in range(2)] for d in range(2)]
        atT = [[sb("atT%d%d" % (d, i), [128, 128], BF16) for i in range(2)] for d in range(2)]
        vnew = [sb("vnew%d" % d, [128, 128], BF16) for d in range(2)]
        tmpo = [sb("tmpo%d" % d, [128, 128], F32) for d in range(2)]
        tmpo2 = [sb("tmpo2%d" % d, [128, 128], F32) for d in range(2)]
        Sf = [sb("Sf%d" % d, [128, 128], F32) for d in range(2)]
        Sb_ = [sb("Sb%d" % d, [128, 128], BF16) for d in range(2)]
        Sl_ = [sb("Sl%d" % d, [128, 128], BF16) for d in range(2)]
        vnl = [sb("vnl%d" % d, [128, 128], BF16) for d in range(2)]
        r_Gb = [[R(), R()], [R(), R()]]
        r_dec = [[R(), R()], [R(), R()]]
        r_N = [[R(), R()], [R(), R()]]
        r_M = [[R(), R()], [R(), R()]]
        r_P = [[R(), R()], [R(), R()]]
        r_TT = [[R(), R()], [R(), R()]]
        r_w = [[R(), R()], [R(), R()]]
        r_at = [[R(), R()], [R(), R()]]
        r_vn, r_to, r_to2, r_Sf, r_Sb = [R(), R()], [R(), R()], [R(), R()], [R(), R()], [R(), R()]
        ps_X = [psb("ps_X%d" % d) for d in range(2)]
        ps_kk = psb("ps_kk")
        ps_ch = [psb("ps_ch%d" % d) for d in range(2)]
        ps_sc = [psb("ps_sc%d" % d) for d in range(2)]
        r_pX, r_pch, r_psc = [R(psum=True), R(psum=True)], [R(psum=True), R(psum=True)], [R(psum=True), R(psum=True)]
        r_pkk = R(psum=True)
        ps_tb = ps_m[:].bitcast(BF16)

        for h in range(NHX):
            for t in range(3):
                row = (h * 3 + t) * 128
                P.op("pool", "memset", dict(ap=xin[:, 0:2], constant=0.0), writes=[r_xin])
                P.op("pool", "memset", dict(ap=xin[:, S + 2:S + 4], constant=0.0), writes=[r_xin])
                P.op("sp", "dma_start", dict(out=xin[:, 2:S + 2], in_=aqkvT[row:row + 128, :]), writes=[r_xin], dma=True)
                P.op("dve", "tensor_scalar", dict(out=acc[:], in0=xin[:, 0:S], scalar1=cws[:, h * 3 + t, 0:1], scalar2=None, op0=ALU.mult),
                     reads=[r_xin, r_c], writes=[r_acc])
                for w in range(1, 5):
                    P.op("dve", "scalar_tensor_tensor", dict(out=acc[:], in0=xin[:, w:w + S], scalar=cws[:, h * 3 + t, w:w + 1], in1=acc[:],
                                                             op0=ALU.mult, op1=ALU.add), reads=[r_xin, r_c, r_acc], writes=[r_acc])
                if t == 2:
                    P.op("act", "activation", dict(out=fT[2][:], in_=acc[:], func=AF.Silu), reads=[r_acc], writes=[r_fT[2]])
                else:
                    P.op("act", "activation", dict(out=acc[:], in_=acc[:], func=AF.Silu), reads=[r_acc], writes=[r_acc])
                    P.op("act", "activation", dict(out=sqb[:], in_=acc[:], func=AF.Square), reads=[r_acc], writes=[r_sqb])
                    for b in range(NB):
                        bs = slice(b * 512, (b + 1) * 512)
                        P.op("pe", "matmul", dict(out=ps_m[:], lhsT=onesb[:], rhs=sqb[:, bs], start=True, stop=True),
                             reads=[r_c, r_sqb], writes=[r_pm])
                        P.op("act", "activation", dict(out=rn[:], in_=ps_m[:], func=AF.Sqrt, bias=epsb[:, 0:1]), reads=[r_pm, r_c], writes=[r_rn])
                        P.op("dve", "reciprocal", dict(out=rn[:], in_=rn[:]), reads=[r_rn], writes=[r_rn])
                        P.op("dve", "scalar_tensor_tensor", dict(out=fT[t][:, bs], in0=acc[:, bs], scalar=(128 ** -0.5 if t == 0 else 1.0), in1=rn[:],
                                                                 op0=ALU.mult, op1=ALU.mult), reads=[r_acc, r_rn], writes=[r_fT[t]])
            if STOP < 2:
                continue
            for c4 in range(0, NCH, 4):
                for t in (1, 2):
                    for j in range(4):
                        c = c4 + j
                        P.op("pe", "transpose", dict(out=ps_tb[:, j * 128:(j + 1) * 128], in_=fT[t][:, c * 128:(c + 1) * 128], identity=identb[:]),
                             reads=[r_fT[t], r_c], writes=[r_pm])
                    src = ps_tb[:, 0:512].rearrange("p (c k) -> p c k", c=4)
                    for d in range(2):
                        col = d * NH + h
                        if t == 1:
                            P.op("dve", "tensor_tensor", dict(out=kbg[d][:, c4:c4 + 4, :], in0=src,
                                                              in1=begc[:, c4:c4 + 4, col:col + 1].to_broadcast([128, 4, 128]), op=ALU.mult),
                                 reads=[r_pm, r_g], writes=[r_tok])
                            P.op("dve", "tensor_tensor", dict(out=kdd[d][:, c4:c4 + 4, :], in0=src,
                                                              in1=ekd[:, c4:c4 + 4, col:col + 1].to_broadcast([128, 4, 128]), op=ALU.mult),
                                 reads=[r_pm, r_g], writes=[r_tok])
                        else:
                            P.op("dve", "tensor_tensor", dict(out=vbd[d][:, c4:c4 + 4, :], in0=src,
                                                              in1=beta[:, c4:c4 + 4, col:col + 1].to_broadcast([128, 4, 128]), op=ALU.mult),
                                 reads=[r_pm, r_g], writes=[r_tok])
            if STOP < 3:
                continue
            P.op("sp", "dma_start", dict(out=zt[:], in_=az[:, h * 128:(h + 1) * 128].rearrange("(c p) n -> p c n", p=128)), writes=[r_zt], dma=True)

            for d in range(2):
                P.op("pool", "memset", dict(ap=Sf[d][:], constant=0.0), writes=[r_Sf[d]])
                P.op("pool", "memset", dict(ap=Sb_[d][:], constant=0.0), writes=[r_Sb[d]])
                P.op("pool", "memset", dict(ap=Sl_[d][:], constant=0.0), writes=[r_Sb[d]])

            def precompute(d, c, par):
                col = d * NH + h
                cs = slice(c * 128, (c + 1) * 128)
                P.op("dve", "tensor_scalar", dict(out=Gb[d][par][:], in0=onesf[:], scalar1=gg[:, c, col:col + 1], scalar2=None, op0=ALU.mult),
                     reads=[r_c, r_g], writes=[r_Gb[d][par]])
                X = ps_X[d]
                P.op("pe", "matmul", dict(out=X[:, 0:128], lhsT=Gb[d][par][:], rhs=ucum[:, d, :], start=True, stop=False, skip_group_check=True),
                     reads=[r_Gb[d][par], r_c], writes=[r_pX[d]])
                P.op("pe", "matmul", dict(out=X[:, 0:128], lhsT=identf[:], rhs=nm[:, 2 * d, :], start=False, stop=True, skip_group_check=True),
                     reads=[r_c], writes=[r_pX[d]])
                P.op("pe", "matmul", dict(out=X[:, 128:256], lhsT=Gb[d][par][:], rhs=ucum[:, d, :], start=False, stop=False, skip_group_check=True),
                     reads=[r_Gb[d][par], r_c], writes=[r_pX[d]])
                P.op("pe", "matmul", dict(out=X[:, 128:256], lhsT=identf[:], rhs=nm[:, 2 * d + 1, :], start=False, stop=True, skip_group_check=True),
                     reads=[r_c], writes=[r_pX[d]])
                P.op("act", "activation", dict(out=dec[d][par][:, 0, :], in_=X[:, 0:128], func=AF.Exp, scale=-1.0, bias=gc[:, c, col:col + 1]),
                     reads=[r_pX[d], r_g], writes=[r_dec[d][par]])
                P.op("act", "activation", dict(out=dec[d][par][:, 1, :], in_=X[:, 128:256], func=AF.Exp, scale=1.0, bias=ngc[:, c, col:col + 1]),
                     reads=[r_pX[d], r_g], writes=[r_dec[d][par]])
                if SUB < 1:
                    return
                P.op("pe", "matmul", dict(out=ps_kk[:, 0:128], lhsT=fT[1][:, cs], rhs=fT[1][:, cs], start=True, stop=True, skip_group_check=True),
                     reads=[r_fT[1]], writes=[r_pkk])
                P.op("pe", "matmul", dict(out=ps_kk[:, 128:256], lhsT=fT[1][:, cs], rhs=fT[0][:, cs], start=False, stop=True, skip_group_check=True),
                     reads=[r_fT[1], r_fT[0]], writes=[r_pkk])
                P.op("dve", "scalar_tensor_tensor", dict(out=Nb[d][par][:], in0=ps_kk[:, 0:128], scalar=nbeta[:, c, col:col + 1], in1=dec[d][par][:, 0, :],
                                                         op0=ALU.mult, op1=ALU.mult), reads=[r_pkk, r_g, r_dec[d][par]], writes=[r_N[d][par]])
                P.op("dve", "tensor_tensor", dict(out=atT[d][par][:], in0=ps_kk[:, 128:256], in1=dec[d][par][:, 1, :], op=ALU.mult),
                     reads=[r_pkk, r_dec[d][par]], writes=[r_at[d][par]])
                if SUB < 2:
                    return
                ch = ps_ch[d]
                P.op("pe", "matmul", dict(out=ch[:, 128:256], lhsT=Nb[d][par][:], rhs=identc[:], start=True, stop=True, skip_group_check=True),
                     reads=[r_N[d][par], r_c], writes=[r_pch[d]])
                if SUB == 2 and cfg.get("sub2", 0) == 1:
                    P.op("act", "copy", dict(out=Mb[d][par][:], in_=ch[:, 128:256]), reads=[r_pch[d]], writes=[r_M[d][par]])
                    return
                P.op("pe", "matmul", dict(out=ch[:, 256:384], lhsT=identc[:], rhs=identc[:], start=False, stop=False, skip_group_check=True),
                     reads=[r_c], writes=[r_pch[d]])
                P.op("pe", "matmul", dict(out=ch[:, 256:384], lhsT=Nb[d][par][:], rhs=identc[:], start=False, stop=True, skip_group_check=True),
                     reads=[r_N[d][par], r_c], writes=[r_pch[d]])
                P.op("act", "copy", dict(out=Mb[d][par][:], in_=ch[:, 128:256]), reads=[r_pch[d]], writes=[r_M[d][par]])
                if cfg.get("sub2", 0) == 2:
                    P.op("act", "copy", dict(out=Pb[d][par][:], in_=ch[:, 256:384]), reads=[r_pch[d]], writes=[r_P[d][par]])
                else:
                    P.op("dve", "tensor_scalar", dict(scalar1=1.0, scalar2=None, op0=ALU.mult, out=Pb[d][par][:], in0=ch[:, 256:384]), reads=[r_pch[d]], writes=[r_P[d][par]])
                if SUB < 3:
                    return
                for k in range(6):
                    P.op("pe", "matmul", dict(out=ch[:, 0:128], lhsT=Mb[d][par][:], rhs=Nb[d][par][:], start=True, stop=True, skip_group_check=True),
                         reads=[r_M[d][par], r_N[d][par]], writes=[r_pch[d]])
                    if k < 5:
                        P.op("pe", "matmul", dict(out=ch[:, 128:256], lhsT=Nb[d][par][:], rhs=Mb[d][par][:], start=False, stop=True, skip_group_check=True),
                             reads=[r_M[d][par], r_N[d][par]], writes=[r_pch[d]])
                    P.op("act", "copy", dict(out=Nb[d][par][:], in_=ch[:, 0:128]), reads=[r_pch[d]], writes=[r_N[d][par]])
                    if k < 5:
                        P.op("dve", "tensor_scalar", dict(scalar1=1.0, scalar2=None, op0=ALU.mult, out=Mb[d][par][:], in0=ch[:, 128:256]), reads=[r_pch[d]], writes=[r_M[d][par]])
                    P.op("pe", "matmul", dict(out=ch[:, 256:384], lhsT=identc[:], rhs=Pb[d][par][:], start=False, stop=False, skip_group_check=True),
                         reads=[r_c, r_P[d][par]], writes=[r_pch[d]])
                    P.op("pe", "matmul", dict(out=ch[:, 256:384], lhsT=Nb[d][par][:], rhs=Pb[d][par][:], start=False, stop=True, skip_group_check=True),
                         reads=[r_N[d][par], r_P[d][par]], writes=[r_pch[d]])
                    if k < 5:
                        P.op("dve", "tensor_scalar", dict(scalar1=1.0, scalar2=None, op0=ALU.mult, out=Pb[d][par][:], in0=ch[:, 256:384]), reads=[r_pch[d]], writes=[r_P[d][par]])
                    else:
                        P.op("dve", "tensor_scalar", dict(scalar1=1.0, scalar2=None, op0=ALU.mult, out=TT[d][par][:], in0=ch[:, 256:384]), reads=[r_pch[d]], writes=[r_TT[d][par]])
                if SUB < 4:
                    return
                P.op("pe", "matmul", dict(out=ch[:, 384:512], lhsT=kbg[d][:, c, :], rhs=TT[d][par][:], start=False, stop=True, skip_group_check=True),
                     reads=[r_tok, r_TT[d][par]], writes=[r_pch[d]])
                P.op("act", "activation", dict(out=wTn[d][par][:], in_=ch[:, 384:512], func=AF.Copy, scale=-1.0), reads=[r_pch[d]], writes=[r_w[d][par]])

            first_visit = [True] * NCH

            def scan(d, c, par):
                col = d * NH + h
                cs = slice(c * 128, (c + 1) * 128)
                sc = ps_sc[d]
                P.op("pe", "matmul", dict(out=sc[:, 0:128], lhsT=TT[d][par][:], rhs=vbd[d][:, c, :], start=True, stop=False, skip_group_check=True),
                     reads=[r_TT[d][par], r_tok], writes=[r_psc[d]])
                P.op("pe", "matmul", dict(out=sc[:, 0:128], lhsT=wTn[d][par][:], rhs=Sb_[d][:], start=False, stop=False, skip_group_check=True),
                     reads=[r_w[d][par], r_Sb[d]], writes=[r_psc[d]])
                P.op("pe", "matmul", dict(out=sc[:, 0:128], lhsT=wTn[d][par][:], rhs=Sl_[d][:], start=False, stop=True, skip_group_check=True),
                     reads=[r_w[d][par], r_Sb[d]], writes=[r_psc[d]])
                P.op("act", "copy", dict(out=vnew[d][:], in_=sc[:, 0:128]), reads=[r_psc[d]], writes=[r_vn[d]])
                P.op("dve", "tensor_tensor", dict(out=vnl[d][:], in0=sc[:, 0:128], in1=vnew[d][:], op=ALU.subtract), reads=[r_psc[d], r_vn[d]], writes=[r_vn[d]])
                P.op("pe", "matmul", dict(out=sc[:, 128:256], lhsT=fT[0][:, cs], rhs=Sb_[d][:], start=False, stop=False, skip_group_check=True),
                     reads=[r_fT[0], r_Sb[d]], writes=[r_psc[d]])
                P.op("pe", "matmul", dict(out=sc[:, 128:256], lhsT=fT[0][:, cs], rhs=Sl_[d][:], start=False, stop=True, skip_group_check=True),
                     reads=[r_fT[0], r_Sb[d]], writes=[r_psc[d]])
                for vv_ in (vnew, vnl):
                    P.op("pe", "matmul", dict(out=sc[:, 256:384], lhsT=atT[d][par][:], rhs=vv_[d][:], start=False, stop=(vv_ is vnl), skip_group_check=True),
                         reads=[r_at[d][par], r_vn[d]], writes=[r_psc[d]])
                for vv_ in (vnew, vnl):
                    P.op("pe", "matmul", dict(out=sc[:, 384:512], lhsT=kdd[d][:, c, :], rhs=vv_[d][:], start=False, stop=(vv_ is vnl), skip_group_check=True),
                         reads=[r_tok, r_vn[d]], writes=[r_psc[d]])
                P.op("act", "copy", dict(out=tmpo[d][:], in_=sc[:, 256:384]), reads=[r_psc[d]], writes=[r_to[d]])
                if first_visit[c]:
                    first_visit[c] = False
                    P.op("dve", "scalar_tensor_tensor", dict(out=oacc[:, c, :], in0=sc[:, 128:256], scalar=egc[:, c, col:col + 1], in1=tmpo[d][:],
                                                             op0=ALU.mult, op1=ALU.add), reads=[r_psc[d], r_g, r_to[d]], writes=[r_oacc[c]])
                else:
                    P.op("dve", "scalar_tensor_tensor", dict(out=tmpo2[d][:], in0=sc[:, 128:256], scalar=egc[:, c, col:col + 1], in1=tmpo[d][:],
                                                             op0=ALU.mult, op1=ALU.add), reads=[r_psc[d], r_g, r_to[d]], writes=[r_to2[d]])
                    P.op("dve", "tensor_tensor", dict(out=oacc[:, c, :], in0=oacc[:, c, :], in1=tmpo2[d][:], op=ALU.add),
                         reads=[r_to2[d], r_oacc[c]], writes=[r_oacc[c]])
                P.op("dve", "scalar_tensor_tensor", dict(out=Sf[d][:], in0=Sf[d][:], scalar=egl[:, c, col:col + 1], in1=sc[:, 384:512],
                                                         op0=ALU.mult, op1=ALU.add), reads=[r_psc[d], r_g, r_Sf[d]], writes=[r_Sf[d]])
                P.op("act", "copy", dict(out=Sb_[d][:], in_=Sf[d][:]), reads=[r_Sf[d]], writes=[r_Sb[d]])
                P.op("dve", "tensor_tensor", dict(out=Sl_[d][:], in0=Sf[d][:], in1=Sb_[d][:], op=ALU.subtract), reads=[r_Sf[d], r_Sb[d]], writes=[r_Sb[d]])

            order = [[c for c in range(NCH)], [NCH - 1 - c for c in range(NCH)]]
            if STOP == 3:
                precompute(0, 0, 0)
                continue
            if STOP == 4:
                precompute(0, 0, 0)
                scan(0, 0, 0)
                continue
            for d in range(2):
                precompute(d, order[d][0], 0)
            for s in range(NCH):
                if s + 1 < NCH:
                    for d in range(2):
                        precompute(d, order[d][s + 1], (s + 1) % 2)
                for d in range(2):
                    scan(d, order[d][s], s % 2)

            allo = r_oacc
            accv = acc[:].rearrange("p (c k) -> p c k", c=NCH)
            P.op("dve", "tensor_tensor", dict(out=accv, in0=oacc[:], in1=oacc[:], op=ALU.mult), reads=allo + [r_acc], writes=[r_acc])
            P.op("dve", "tensor_reduce", dict(out=ssn[:], in_=accv, axis=AX.X, op=ALU.add), reads=[r_acc], writes=[r_ssn])
            P.op("act", "activation", dict(out=ssn[:], in_=ssn[:], func=AF.Sqrt, scale=1.0 / 128, bias=epsb[:, 0:1]), reads=[r_ssn, r_c], writes=[r_ssn])
            P.op("dve", "reciprocal", dict(out=ssn[:], in_=ssn[:]), reads=[r_ssn], writes=[r_ssn])
            P.op("dve", "tensor_tensor", dict(out=accv, in0=oacc[:], in1=ssn[:, :, None].to_broadcast([128, NCH, 128]), op=ALU.mult),
                 reads=allo + [r_ssn, r_acc], writes=[r_acc])
            P.op("dve", "tensor_tensor", dict(out=accv, in0=accv, in1=onorm_s[:, None, :].to_broadcast([128, NCH, 128]), op=ALU.mult),
                 reads=[r_c, r_acc], writes=[r_acc])
            P.op("act", "activation", dict(out=zt[:], in_=zt[:], func=AF.Silu), reads=[r_zt], writes=[r_zt])
            P.op("dve", "tensor_tensor", dict(out=ogb[:], in0=accv, in1=zt[:], op=ALU.mult), reads=[r_acc, r_zt], writes=[r_ogb])
            for c4 in range(0, NCH, 4):
                for j in range(4):
                    c = c4 + j
                    P.op("pe", "matmul", dict(out=ps_m[:, j * 128:(j + 1) * 128], lhsT=ogb[:, c, :], rhs=identb[:], start=(j == 0), stop=True, skip_group_check=True),
                         reads=[r_ogb, r_c], writes=[r_pm])
                P.op("act", "copy", dict(out=oTs[:, c4 * 128:(c4 + 4) * 128], in_=ps_m[:, 0:512]), reads=[r_pm], writes=[r_oTs])
            if FUSED:
                for th_ in range(2):
                    P.op("sp", "dma_start", dict(out=oaT[h * 2 + th_].rearrange("(q p) t -> p q t", q=4),
                                                 in_=oTs[:].rearrange("p (q h t) -> p q h t", q=4, h=2)[:, :, th_, :]), reads=[r_oTs], dma=True)
            else:
                P.op("sp", "dma_start", dict(out=oaT[h * 128:(h + 1) * 128, :], in_=oTs[:]), reads=[r_oTs], dma=True)
        P.emit()
    if FUSED:
        nc.all_engine_barrier()
    return nc

I32 = mybir.dt.int32


def relayout_emit(nc, jobs, idx_ap):
    P = Prog(nc)
    with ExitStack() as es:
        pfx = uniq()
        sb = lambda name, shape, dt: es.enter_context(nc.sbuf_tensor(pfx + name, shape, dt))
        NBUF = 6
        bf = [sb("rl_f%d" % i, [128, 1024], F32) for i in range(NBUF)]
        bb = [sb("rl_b%d" % i, [128, 1024], BF16) for i in range(NBUF)]
        rf = [Reg() for i in range(NBUF)]
        rb = [Reg() for i in range(NBUF)]
        idx_s = sb("rl_idx", [128, 8], I32)
        r_idx = Reg()
        P.op("sp", "dma_start", dict(out=idx_s[:], in_=idx_ap), writes=[r_idx], dma=True)
        kf = kb = 0
        for in_ap, ic, out_ap, dt in jobs:
            W = out_ap.shape[-1]
            if dt == F32:
                buf, rr = bf[kf % NBUF], rf[kf % NBUF]
                kf += 1
            else:
                buf, rr = bb[kb % NBUF], rb[kb % NBUF]
                kb += 1
            G_, off_ = in_ap
            assert G_.shape[1] == W
            P.op("pool", "indirect_dma_start",
                 dict(out=buf[:, 0:W], out_offset=None, in_offset=bass.IndirectOffsetOnAxis(ap=idx_s[:, ic:ic + 1], axis=0), **gat(G_, off_, 0, W)),
                 reads=[r_idx], writes=[rr], dma=True)
            P.op("sp", "dma_start", dict(out=out_ap, in_=buf[:, 0:W]), reads=[rr], dma=True)
        P.emit()
    nc.all_engine_barrier()


def allgather_emit(nc, pairs):
    rg = [[0, 1, 2, 3], [4, 5, 6, 7]]
    sem = nc.alloc_semaphore(name=uniq() + "ag")
    with nc.Block() as block:
        @block.gpsimd
        def _(g):
            for src, dst in pairs:
                g.collective_compute(kind="AllGather", op=ALU.bypass, replica_groups=rg, ins=[src.opt()], outs=[dst.opt()]).then_inc(sem)
            g.wait_ge(sem, len(pairs))
    nc.all_engine_barrier()
    nc.clear_and_free_semaphores([sem])
    nc.all_engine_barrier()


def build_fused(stop=99):
    S, D, TPC, DFF = 4096, 4096, 1024, 11008
    nc = bass.Bass("TRN2", target_bir_lowering=False)
    ein = lambda name, shape, dt=F32: nc.dram_tensor(name, shape, dt, kind="ExternalInput").ap()
    itn = lambda name, shape, dt=F32: nc.dram_tensor(name, shape, dt, kind="Internal").ap()
    KC = D // 128
    SHAPES = {}
    SHAPES["xT"] = ([D, TPC], F32)
    SHAPES["idx"] = ([128, 8], I32)
    SHAPES["nw_fin"] = ([128, KC], F32)
    SHAPES["Win0"] = ([D, 17472], F32)
    SHAPES["Wo0"] = ([3072, D], F32)
    SHAPES["Wqkv"] = ([D, 6144], F32)
    SHAPES["Wo1"] = ([4096, D], F32)
    SHAPES["ropeCb"] = ([32, TPC], F32)
    SHAPES["ropeSb"] = ([32, TPC], F32)
    SHAPES["ropePb"] = ([32, 32], F32)
    SHAPES["ropeCc"] = ([128, TPC], F32)
    SHAPES["ropeSc"] = ([128, TPC], F32)
    SHAPES["ropePc"] = ([128, 128], F32)
    SHAPES["gq"] = ([128, 1], F32)
    SHAPES["gk"] = ([128, 1], F32)
    SHAPES["cw"] = ([128, 12, 5], F32)
    SHAPES["alog"] = ([128, 8], F32)
    SHAPES["dtb"] = ([128, 8], F32)
    SHAPES["onorm"] = ([128, 128], F32)
    SHAPES["ident_in"] = ([128, 128], F32)
    SHAPES["ucum_in"] = ([128, 2, 128], F32)
    SHAPES["nm_in"] = ([128, 4, 128], F32)
    SHAPES["bmask"] = ([128, 3, 512], BF16)
    for l in range(2):
        for nm_, sh_ in (("nw_mix", [128, KC]), ("nw_ffn", [128, KC]), ("Wg", [D, DFF]), ("Wu", [D, DFF]), ("Wd", [DFF, D])):
            SHAPES["%s%d" % (nm_, l)] = (sh_, F32)

    class _Lazy(dict):
        def __missing__(self, k):
            sh_, dt_ = SHAPES[k]
            v = ein(k, sh_, dt_)
            self[k] = v
            return v
    E = _Lazy()
    outT = nc.dram_tensor("outT", [D, TPC], F32, kind="ExternalOutput").ap()

    pairs1, pairs2, pairs3, pairs4 = [], [], [], []

    def blk(name, rows, W, dt, pairs):
        src = itn("s_" + name, [rows, W], dt)
        dst = itn("g_" + name, [4 * rows, W], dt)
        pairs.append((src, dst))
        return src, dst
    A_b = [[blk("aq%d_%d" % (ps, i), 512, 512, F32, pairs1) for i in range(12)] for ps in range(2)]
    Q_b = [[blk("bq%d_%d" % (ps, i), 1024, 512, BF16, pairs1) for i in range(6)] for ps in range(2)]
    Z_b = [[blk("az%d_%d" % (ps, i), 512, 512, F32, pairs1) for i in range(4)] for ps in range(2)]
    V_b = [[blk("bv%d_%d" % (ps, i), 512, 768, BF16, pairs1) for i in range(4)] for ps in range(2)]
    AB_b = [blk("ab%d" % ps, 2048, 16, F32, pairs1) for ps in range(2)]
    OA_b = [blk("oa%d" % i, 512, 512, BF16, pairs2) for i in range(8)]
    OB_b = [blk("ob%d" % i, 512, 512, BF16, pairs2) for i in range(4)]
    CQ_b = [[blk("cq%d_%d" % (ps, i), 512, 512, BF16, pairs3) for i in range(8)] for ps in range(2)]
    CK_b = [[blk("ck%d_%d" % (ps, i), 512, 512, BF16, pairs3) for i in range(2)] for ps in range(2)]
    CV_b = [[blk("cv%d_%d" % (ps, i), 512, 256, BF16, pairs3) for i in range(4)] for ps in range(2)]
    OC_b = [blk("oc%d" % i, 512, 512, BF16, pairs4) for i in range(16)]
    aqkvT_l = itn("l_aqkvT", [1536, S])
    az_l = itn("l_az", [S, 512])
    abr_l = itn("l_abr", [S, 16])
    bqT_l = itn("l_bqT", [768, S], BF16)
    bkT_l = itn("l_bkT", [768, S], BF16)
    bv_l = itn("l_bv", [S, 768], BF16)
    x2T = itn("i_x2T", [D, TPC])

    def dst_fm1(name, i, ps):
        if name == "aqkv":
            t, head = i // 16, i % 16
            return A_b[ps][t * 4 + head % 4][0][(head // 4) * 128:(head // 4 + 1) * 128, :]
        qk, hd = i // 24, i % 24
        g, hs = hd // 8, hd % 8
        return Q_b[ps][qk * 3 + g][0][hs * 128:(hs + 1) * 128, :]

    def dst_tm1(name, i, ps, tt):
        if name == "az":
            j, hl = i // 4, i % 4
            return Z_b[ps][tt][0][j * 128:(j + 1) * 128, hl * 128:(hl + 1) * 128]
        g, hs = i // 8, i % 8
        j, hl = hs // 2, hs % 2
        return V_b[ps][tt][0][j * 128:(j + 1) * 128, (hl * 3 + g) * 128:(hl * 3 + g + 1) * 128]

    def dst_ab1(ps, g, tt):
        return AB_b[ps][0][g * 512 + tt * 128:g * 512 + (tt + 1) * 128, :]
    ab = dict(nqkv=6144, nz=2048, nab=64, nbqk=6144, nbv=3072)
    tp_build(dict(nc=nc, T_tot=TPC, inproj="ab", ab=ab, dst_fm=dst_fm1, dst_tm=dst_tm1, dst_ab=dst_ab1,
                  TD=dict(xT=E["xT"], nwb=E["nw_mix0"], Win=E["Win0"], ropeC=E["ropeCb"], ropeS=E["ropeSb"], ropeP=E["ropePb"])))
    if stop == 1:
        return nc
    allgather_emit(nc, pairs1)
    if stop == 2:
        return nc
    jobs = []
    for hl in range(4):
        for t in range(3):
            for r in range(4):
                for ps in range(2):
                    c0 = r * TPC + ps * 512
                    jobs.append(((A_b[ps][t * 4 + hl][1], r * 512), 0, aqkvT_l[(hl * 3 + t) * 128:(hl * 3 + t + 1) * 128, c0:c0 + 512], F32))
    for r in range(4):
        for ps in range(2):
            for tt in range(4):
                r0 = r * TPC + ps * 512 + tt * 128
                jobs.append(((Z_b[ps][tt][1], r * 512), 0, az_l[r0:r0 + 128, :], F32))
                jobs.append(((V_b[ps][tt][1], r * 512), 0, bv_l[r0:r0 + 128, :], BF16))
                jobs.append(((AB_b[ps][1], r * 2048 + tt * 128), 3, abr_l[r0:r0 + 128, :], F32))
    for hl in range(2):
        for g in range(3):
            for r in range(4):
                for ps in range(2):
                    c0 = r * TPC + ps * 512
                    jobs.append(((Q_b[ps][g][1], r * 1024 + hl * 128), 2, bqT_l[(hl * 3 + g) * 128:(hl * 3 + g + 1) * 128, c0:c0 + 512], BF16))
                    jobs.append(((Q_b[ps][3 + g][1], r * 1024 + hl * 128), 2, bkT_l[(hl * 3 + g) * 128:(hl * 3 + g + 1) * 128, c0:c0 + 512], BF16))
    relayout_emit(nc, jobs, E["idx"])
    if stop == 3:
        return nc
    gdn_build(dict(nc=nc, S=S, NH=4, TD=dict(aqkvT=aqkvT_l, az=az_l, abr=abr_l, cw=E["cw"], alog=E["alog"], dtb=E["dtb"], onorm=E["onorm"],
                                             ident_in=E["ident_in"], ucum_in=E["ucum_in"], nm_in=E["nm_in"], oa_q=[b_[0] for b_ in OA_b])))
    if stop == 4:
        return nc
    dil_b_build(dict(nc=nc, S=S, NHS=2, TD=dict(bqT=bqT_l, bkT=bkT_l, bv=bv_l, bmask=E["bmask"], ob_q=[b_[0] for b_ in OB_b])))
    if stop == 5:
        return nc
    allgather_emit(nc, pairs2)
    if stop == 6:
        return nc
    o_src = []
    for ps in range(2):
        lst = []
        for k in range(16):
            j, hl = k // 4, k % 4
            lst.append((OA_b[hl * 2 + ps][1], j * 512, 0))
        for kk in range(8):
            j, hl = kk // 2, kk % 2
            lst.append((OB_b[hl * 2 + ps][1], j * 512, 0))
        o_src.append(lst)

    def dst_fm3(name, i, ps):
        if i < 32:
            return CQ_b[ps][i % 8][0][(i // 8) * 128:(i // 8 + 1) * 128, :]
        kvh = i - 32
        return CK_b[ps][kvh % 2][0][(kvh // 2) * 128:(kvh // 2 + 1) * 128, :]

    def dst_tm3(name, i, ps, tt):
        return CV_b[ps][tt][0][(i // 2) * 128:(i // 2 + 1) * 128, (i % 2) * 128:(i % 2 + 1) * 128]
    cc = dict(nq=4096, nk=1024, nv=1024)
    tp_build(dict(nc=nc, T_tot=TPC, oproj_K=3072, dff=DFF, inproj="c", c=cc, store_x=True, o_src=o_src, dst_fm=dst_fm3, dst_tm=dst_tm3,
                  TD=dict(xT=E["xT"], idx=E["idx"], Wo=E["Wo0"], nwa=E["nw_ffn0"], Wg=E["Wg0"], Wu=E["Wu0"], Wd=E["Wd0"], nwb=E["nw_mix1"],
                          Win=E["Wqkv"], ropeC=E["ropeCc"], ropeS=E["ropeSc"], ropeP=E["ropePc"], gq=E["gq"], gk=E["gk"], xoT=x2T)))
    if stop == 7:
        return nc
    allgather_emit(nc, pairs3)
    if stop == 8:
        return nc
    attn_c_build(dict(nc=nc, S=S, NKV=2, REP=4, TD=dict(GQ=[[b_[1] for b_ in CQ_b[ps]] for ps in range(2)], GK=[[b_[1] for b_ in CK_b[ps]] for ps in range(2)],
                                                          GV=[[b_[1] for b_ in CV_b[ps]] for ps in range(2)], OCB=[b_[0] for b_ in OC_b], idx=E["idx"])))
    if stop == 9:
        return nc
    allgather_emit(nc, pairs4)
    if stop == 10:
        return nc
    o_src = []
    for ps in range(2):
        o_src.append([(OC_b[(k % 8) * 2 + ps][1], (k // 8) * 512, 0) for k in range(32)])
    tp_build(dict(nc=nc, T_tot=TPC, oproj_K=4096, dff=DFF, final_norm=True, o_src=o_src,
                  TD=dict(xT=x2T, idx=E["idx"], Wo=E["Wo1"], nwa=E["nw_ffn1"], Wg=E["Wg1"], Wu=E["Wu1"], Wd=E["Wd1"], nwb=E["nw_fin"], outT=outT)))
    return nc

import ml_dtypes as _mld

_BF = _mld.bfloat16
_PROGS = {}


def _lay(w):
    return np.ascontiguousarray(np.asarray(w, np.float32).reshape(-1, 128).T)


def _rope_tables_b(pos):
    inv = 1.0 / (500000.0 ** (np.arange(0, 32, 2, dtype=np.float32) / 32))
    ang = pos.astype(np.float32)[:, None] * inv[None, :]
    c, s = np.cos(ang), np.sin(ang)
    C = np.ascontiguousarray(np.concatenate([c, c], 1).T.astype(np.float32))
    S = np.ascontiguousarray(np.concatenate([s, s], 1).T.astype(np.float32))
    Pm = np.zeros((32, 32), np.float32)
    for i in range(16):
        Pm[i, 16 + i] = -1
        Pm[16 + i, i] = 1
    return C, S, np.ascontiguousarray(Pm.T)


def _rope_tables_c(pos):
    inv = 1.0 / (10000.0 ** (np.arange(0, 64, 2, dtype=np.float32) / 64))
    ar = (pos // 64).astype(np.float32)[:, None] * inv[None, :]
    ac = (pos % 64).astype(np.float32)[:, None] * inv[None, :]
    C = np.ascontiguousarray(np.concatenate([np.cos(ar), np.cos(ar), np.cos(ac), np.cos(ac)], 1).T.astype(np.float32))
    S = np.ascontiguousarray(np.concatenate([np.sin(ar), np.sin(ar), np.sin(ac), np.sin(ac)], 1).T.astype(np.float32))
    Pm = np.zeros((128, 128), np.float32)
    for i in range(32):
        Pm[i, 32 + i] = -1
        Pm[32 + i, i] = 1
        Pm[64 + i, 96 + i] = -1
        Pm[96 + i, 64 + i] = 1
    return C, S, np.ascontiguousarray(Pm.T)


def kernel(x, norm_mix, norm_ffn, norm_final, ab_w_in, ab_conv_w, ab_a_log, ab_dt_bias,
           ab_out_norm, ab_w_out, c_w_qkv, c_q_norm, c_k_norm, c_w_out,
           ffn_w_gate, ffn_w_up, ffn_w_down):
    f32 = np.float32
    A = lambda a: np.ascontiguousarray(np.asarray(a, f32))
    x = A(x)
    B, S, D = x.shape
    NCORE = 8
    TPC = B * S // NCORE
    QPB = S // TPC
    xf = x.reshape(B * S, D)
    if "F" not in _PROGS:
        _PROGS["F"] = build_fused()
    nc = _PROGS["F"]
    cst = gdn_consts()
    conv_w = A(ab_conv_w[0])
    a_log = A(ab_a_log[0])
    dt_b = A(ab_dt_bias[0])
    shared = dict(
        nw_mix0=_lay(norm_mix[0]), nw_mix1=_lay(norm_mix[1]), nw_ffn0=_lay(norm_ffn[0]), nw_ffn1=_lay(norm_ffn[1]), nw_fin=_lay(norm_final),
        Wg0=A(ffn_w_gate[0]), Wu0=A(ffn_w_up[0]), Wd0=A(ffn_w_down[0]), Wg1=A(ffn_w_gate[1]), Wu1=A(ffn_w_up[1]), Wd1=A(ffn_w_down[1]),
        Win0=A(ab_w_in[0]), Wo0=A(ab_w_out[0]), Wqkv=A(c_w_qkv[0]), Wo1=A(c_w_out[0]),
        gq=A(c_q_norm[0]).reshape(128, 1), gk=A(c_k_norm[0]).reshape(128, 1),
        onorm=np.ascontiguousarray(np.tile(A(ab_out_norm[0])[None, :], (128, 1))),
        ident_in=cst["ident"], ucum_in=cst["ucum"], nm_in=cst["nm"], bmask=dil_mask().astype(_BF))
    ims = []
    p = np.arange(128, dtype=np.int32)
    for c in range(NCORE):
        rk = c % QPB
        pos = np.arange(rk * TPC, (rk + 1) * TPC)
        Cb, Sb, Pb = _rope_tables_b(pos)
        Cc, Sc, Pc = _rope_tables_c(pos)
        heads = [4 * rk + i for i in range(4)]
        cwl = np.zeros((128, 12, 5), f32)
        for i, h in enumerate(heads):
            for t in range(3):
                cwl[:, i * 3 + t, :] = conv_w[:, t * 2048 + h * 128: t * 2048 + (h + 1) * 128].T
        al = np.array([a_log[d, h] for d in range(2) for h in heads], f32)
        db = np.array([dt_b[d, h] for d in range(2) for h in heads], f32)
        idx = np.stack([rk * 128 + p, 2 * (rk * 128 + p), rk * 256 + p, rk * 512 + p, p, p, p, p], 1).astype(np.int32)
        im = dict(shared)
        im.update(xT=np.ascontiguousarray(xf[c * TPC:(c + 1) * TPC].T), idx=np.ascontiguousarray(idx),
                  ropeCb=Cb, ropeSb=Sb, ropePb=Pb, ropeCc=Cc, ropeSc=Sc, ropePc=Pc, cw=cwl,
                  alog=np.ascontiguousarray(np.tile(al[None, :], (128, 1))), dtb=np.ascontiguousarray(np.tile(db[None, :], (128, 1))))
        ims.append(im)
    res = run_bass_kernel_spmd(nc, ims, core_ids=list(range(NCORE))).results
    out = np.concatenate([np.asarray(res[c]["outT"]).T for c in range(NCORE)], 0)
    return np.ascontiguousarray(out.reshape(B, S, D).astype(f32, copy=False))
```

```python
from contextlib import ExitStack
import numpy as np
import concourse.bass as bass
import concourse.mybir as mybir
from concourse.bass_utils import run_bass_kernel_spmd

F32 = mybir.dt.float32
BF16 = mybir.dt.bfloat16
AF = mybir.ActivationFunctionType
ALU = mybir.AluOpType
AX = mybir.AxisListType

ENGS = ("pe", "act", "dve", "pool", "sp")
_UNIQ = [0]


def uniq():
    _UNIQ[0] += 1
    return "u%d_" % _UNIQ[0]
NS_DMA = 6
SAME_ENGINE_SYNC = True


class Reg:
    __slots__ = ("name", "w", "r", "rd", "psum")

    def __init__(self, name="", psum=False):
        self.name = name
        self.psum = psum
        self.w = None
        self.r = {}
        self.rd = []


class Ins:
    __slots__ = ("eng", "fn", "deps", "inc", "dma", "idx", "sem", "target", "cnt", "cc")


class Prog:
    def __init__(self, nc):
        self.nc = nc
        self.q = {e: [] for e in ENGS}
        self.dmas = {e: [] for e in ENGS}
        self.ccs = []

    def op(self, eng, meth, kw, reads=(), writes=(), dma=False, cc=False):
        ins = Ins()
        ins.cc = cc
        if cc:
            dma = True
            self.ccs.append(ins)
        ins.eng = eng
        ins.fn = (meth, kw)
        ins.dma = dma
        ins.inc = dma
        ins.idx = len(self.q[eng])
        ins.cnt = 0
        deps = set()
        for r in reads:
            if r.w is not None:
                deps.add(r.w)
            if r.psum:
                for e2, i2 in r.r.items():
                    if e2 != eng:
                        deps.add(i2)
        for w in writes:
            if w.w is not None:
                deps.add(w.w)
            deps.update(w.r.values())
            deps.update(w.rd)
        for r in reads:
            if dma:
                r.rd.append(ins)
            else:
                r.r[eng] = ins
        for w in writes:
            w.w = ins
            w.r = {}
            w.rd = []
        if dma and not cc:
            lst = self.dmas[eng]
            if len(lst) >= NS_DMA:
                deps.add(lst[len(lst) - NS_DMA])
            lst.append(ins)
        deps.discard(ins)
        fin = []
        for d in deps:
            if d.eng == eng and not d.dma:
                if eng == "pe" or not SAME_ENGINE_SYNC:
                    continue
            d.inc = True
            fin.append(d)
        ins.deps = fin
        self.q[eng].append(ins)
        return ins

    def emit(self, final_waits=()):
        nc = self.nc
        allsems = []

        def newsem(name):
            h = nc.alloc_semaphore(name=name)
            allsems.append(h)
            return h
        with ExitStack() as es:
            pf = uniq()
            csem = {e: newsem(pf + "c_" + e) for e in ENGS}
            dsem = {e: [newsem(pf + "d_%s%d" % (e, i)) for i in range(NS_DMA)]
                    for e in ENGS if self.dmas[e]}
            es_cc = []
            for e in ENGS:
                c = 0
                k = 0
                for ins in self.q[e]:
                    if ins.cc:
                        ins.sem = newsem(pf + "cc%d" % len(es_cc))
                        es_cc.append(ins)
                        ins.target = 1
                    elif ins.dma:
                        ins.sem = dsem[e][k % NS_DMA]
                        ins.target = 16 * (k // NS_DMA + 1)
                        k += 1
                    elif ins.inc:
                        c += 1
                        ins.cnt = c
            block = es.enter_context(nc.Block())

            def run(e, eng):
                waited = {}
                for ins in self.q[e]:
                    need = {}
                    for d in ins.deps:
                        if d.dma:
                            key = d.sem
                            val = d.target
                        else:
                            key = csem[d.eng]
                            val = d.cnt
                        if need.get(key, 0) < val:
                            need[key] = val
                    for key, val in need.items():
                        if waited.get(key, 0) < val:
                            eng.wait_ge(key, val)
                            waited[key] = val
                    inst = getattr(eng, ins.fn[0])(**ins.fn[1])
                    if ins.cc:
                        inst.then_inc(ins.sem)
                    elif ins.dma:
                        inst.then_inc(ins.sem, 16)
                    elif ins.inc:
                        inst.then_inc(csem[e], 1)
                for d in self.ccs:
                    if d.eng == e and waited.get(d.sem, 0) < d.target:
                        eng.wait_ge(d.sem, d.target)
                        waited[d.sem] = d.target
                if self.dmas[e]:
                    lst = self.dmas[e]
                    for d in lst[-NS_DMA:]:
                        if waited.get(d.sem, 0) < d.target:
                            eng.wait_ge(d.sem, d.target)
                            waited[d.sem] = d.target

            @block.tensor
            def _(eng):
                run("pe", eng)

            @block.scalar
            def _(eng):
                run("act", eng)

            @block.vector
            def _(eng):
                run("dve", eng)

            @block.gpsimd
            def _(eng):
                run("pool", eng)

            @block.sync
            def _(eng):
                run("sp", eng)
        nc.all_engine_barrier()
        nc.clear_and_free_semaphores(allsems)
        nc.all_engine_barrier()


def gat(G, row_off, col_off, W):
    full = G.shape[1]
    A = full // W
    v = G if A == 1 else G.rearrange("r (a w) -> (r a) w", w=W)
    return dict(in_=v, element_offset=row_off * full + col_off)


def tp_build(cfg):
    D = cfg.get("D", 4096)
    KC = D // 128
    T = cfg.get("T", 512)
    T_tot = cfg["T_tot"]
    NTT = T // 128
    oK = cfg.get("oproj_K", 0)
    dff = cfg.get("dff", 0)
    inproj = cfg.get("inproj")
    final_norm = cfg.get("final_norm", False)
    store_x = cfg.get("store_x", False)
    FB = 8

    TT_ = cfg.get("TD")
    FUSED = TT_ is not None
    nc = cfg["nc"] if FUSED else bass.Bass("TRN2", target_bir_lowering=False)
    if FUSED:
        din = lambda name, shape, dt=F32: TT_[name]
        dout = lambda name, shape, dt=F32: TT_[name]
    else:
        din = lambda name, shape, dt=F32: nc.dram_tensor(name, shape, dt, kind="ExternalInput").ap()
        dout = lambda name, shape, dt=F32: nc.dram_tensor(name, shape, dt, kind="ExternalOutput").ap()
    xT = din("xT", [D, T_tot])
    if oK:
        if not FUSED:
            oT = din("oT", [oK, T_tot], BF16)
        Wo = din("Wo", [oK, D])
    if dff:
        nwa = din("nwa", [128, KC])
        Wg = din("Wg", [D, dff])
        Wu = din("Wu", [D, dff])
        Wd = din("Wd", [dff, D])
    if inproj or final_norm:
        nwb = din("nwb", [128, KC])
    if store_x:
        xoT = dout("xoT", [D, T_tot])
    if final_norm:
        outT = dout("outT", [D, T_tot])
    if inproj == "ab":
        ab = cfg["ab"]
        ncols = ab["nqkv"] + ab["nz"] + ab["nab"] + ab["nbqk"] + ab["nbv"]
        Win = din("Win", [D, ncols])
        ropeC = din("ropeC", [32, T_tot])
        ropeS = din("ropeS", [32, T_tot])
        ropeP = din("ropeP", [32, 32])
        if FUSED:
            aqkvT = az = abo = bqkT = bv = None
        else:
            aqkvT = dout("aqkvT", [ab["nqkv"], T_tot])
            az = dout("az", [T_tot, ab["nz"]])
            abo = dout("ab", [T_tot, ab["nab"]])
            bqkT = dout("bqkT", [ab["nbqk"], T_tot], BF16)
            bv = dout("bv", [T_tot, ab["nbv"]], BF16)
    if inproj == "c":
        cc = cfg["c"]
        ncols = cc["nq"] + cc["nk"] + cc["nv"]
        Win = din("Win", [D, ncols])
        ropeC = din("ropeC", [128, T_tot])
        ropeS = din("ropeS", [128, T_tot])
        ropeP = din("ropeP", [128, 128])
        gq = din("gq", [128, 1])
        gk = din("gk", [128, 1])
        if FUSED:
            cqkT = cv = None
        else:
            cqkT = dout("cqkT", [cc["nq"] + cc["nk"], T_tot], BF16)
            cv = dout("cv", [T_tot, cc["nv"]], BF16)

    P = Prog(nc)
    with ExitStack() as es:
        es.enter_context(nc.allow_low_precision("bf16 matmul operands, fp32 accumulate"))
        pfx = uniq()
        sb = lambda name, shape, dt: es.enter_context(nc.sbuf_tensor(pfx + name, shape, dt))
        psb = lambda name: es.enter_context(nc.psum_tensor(pfx + name, [128, 512], F32))
        R = Reg
        xs = sb("xs", [128, KC, T], F32)
        hT = sb("hT", [128, KC, T], BF16)
        ones = sb("ones", [128, 128], BF16)
        epsb = sb("epsb", [128, 1], F32)
        sq = [sb("sq%d" % i, [128, T], BF16) for i in range(2)]
        rstd = sb("rstd", [128, T], F32)
        st = [sb("st%d" % i, [128, T], F32) for i in range(2)]
        stb = [sb("stb%d" % i, [128, T], BF16) for i in range(2)]
        aT = [sb("aT%d" % i, [128, FB, T], BF16) for i in range(2)]
        wA = [sb("wA%d" % i, [128, KC, 128], BF16) for i in range(4)]
        wd = [sb("wd%d" % i, [128, FB, 512], BF16) for i in range(2)]
        nwa_s = sb("nwa_s", [128, KC], F32)
        nwb_s = sb("nwb_s", [128, KC], F32)
        ps_ss = psb("ps_ss")
        psA = [psb("psA%d" % i) for i in range(4)]
        ps_y = [psb("ps_y%d" % i) for i in range(2)]
        r_xs = [R() for c in range(KC)]
        r_hT = [R() for c in range(KC)]
        r_ones, r_eps, r_rstd, r_ss, r_nwa, r_nwb = R(), R(), R(), R(psum=True), R(), R()
        r_sq = [R(), R()]
        r_st = [R(), R()]
        r_stb = [R(), R()]
        r_aT = [[R() for j in range(FB)] for i in range(2)]
        r_wA = [R() for i in range(4)]
        r_wd = [R(), R()]
        r_psA = [R(psum=True) for i in range(4)]
        r_py = [R(psum=True), R(psum=True)]
        cnt = dict(wA=0, wd=0, psA=0, py=0, st=0, stb=0)

        def nxt(k, n):
            v = cnt[k] % n
            cnt[k] += 1
            return v

        P.op("pool", "memset", dict(ap=ones[:], constant=1.0), writes=[r_ones])
        P.op("pool", "memset", dict(ap=epsb[:], constant=1e-6), writes=[r_eps])
        if dff:
            P.op("sp", "dma_start", dict(out=nwa_s[:], in_=nwa), writes=[r_nwa], dma=True)
        if inproj or final_norm:
            P.op("sp", "dma_start", dict(out=nwb_s[:], in_=nwb), writes=[r_nwb], dma=True)
        if inproj == "ab":
            rC = sb("rC", [32, T_tot], F32)
            rS = sb("rS", [32, T_tot], F32)
            rP = sb("rP", [32, 32], BF16)
            t1 = sb("t1", [32, T], F32)
            t2 = sb("t2", [32, T], F32)
            r_rope, r_t1, r_t2 = R(), R(), R()
            P.op("sp", "dma_start", dict(out=rC[:], in_=ropeC), writes=[r_rope], dma=True)
            P.op("sp", "dma_start", dict(out=rS[:], in_=ropeS), writes=[r_rope], dma=True)
            P.op("pool", "dma_start", dict(out=rP[:], in_=ropeP), writes=[r_rope], dma=True)
        if inproj == "c":
            rC = sb("rC", [128, T_tot], F32)
            rS = sb("rS", [128, T_tot], F32)
            rP = sb("rP", [128, 128], BF16)
            gq_s = sb("gq_s", [128, 1], F32)
            gk_s = sb("gk_s", [128, 1], F32)
            t1 = sb("t1", [128, T], F32)
            t2 = sb("t2", [128, T], F32)
            rn = sb("rn", [128, T], F32)
            r_rope, r_t1, r_t2, r_rn = R(), R(), R(), R()
            P.op("sp", "dma_start", dict(out=rC[:], in_=ropeC), writes=[r_rope], dma=True)
            P.op("sp", "dma_start", dict(out=rS[:], in_=ropeS), writes=[r_rope], dma=True)
            P.op("pool", "dma_start", dict(out=rP[:], in_=ropeP), writes=[r_rope], dma=True)
            P.op("sp", "dma_start", dict(out=gq_s[:], in_=gq), writes=[r_rope], dma=True)
            P.op("sp", "dma_start", dict(out=gk_s[:], in_=gk), writes=[r_rope], dma=True)

        xT_v = xT.rearrange("(c p) t -> p c t", p=128)
        if FUSED and oK:
            idx_s = sb("idx_s", [128, 8], mybir.dt.int32)
            r_idx = R()
            P.op("sp", "dma_start", dict(out=idx_s[:], in_=TT_["idx"]), writes=[r_idx], dma=True)

        def rmsnorm(nw_s, r_nw, to_x=False):
            for c in range(KC):
                s = c % 2
                P.op("act", "activation", dict(out=sq[s][:], in_=xs[:, c, :], func=AF.Square),
                     reads=[r_xs[c]], writes=[r_sq[s]])
                P.op("pe", "matmul", dict(out=ps_ss[:], lhsT=ones[:], rhs=sq[s][:], start=(c == 0), stop=(c == KC - 1)),
                     reads=[r_ones, r_sq[s]], writes=[r_ss])
            P.op("act", "activation", dict(out=rstd[:], in_=ps_ss[:], func=AF.Sqrt, scale=1.0 / D, bias=epsb[:, 0:1]),
                 reads=[r_ss, r_eps], writes=[r_rstd])
            P.op("dve", "reciprocal", dict(out=rstd[:], in_=rstd[:]), reads=[r_rstd], writes=[r_rstd])
            for c in range(KC):
                if to_x:
                    P.op("dve", "scalar_tensor_tensor",
                         dict(out=xs[:, c, :], in0=xs[:, c, :], scalar=nw_s[:, c:c + 1], in1=rstd[:], op0=ALU.mult, op1=ALU.mult),
                         reads=[r_xs[c], r_nw, r_rstd], writes=[r_xs[c]])
                else:
                    P.op("dve", "scalar_tensor_tensor",
                         dict(out=hT[:, c, :], in0=xs[:, c, :], scalar=nw_s[:, c:c + 1], in1=rstd[:], op0=ALU.mult, op1=ALU.mult),
                         reads=[r_xs[c], r_nw, r_rstd], writes=[r_hT[c]])

        def accum_block(W_v, k0, nk, sl, rhs_regs):
            for cb in range(D // 512):
                w = nxt("wd", 2)
                P.op("pool", "dma_start", dict(out=wd[w][:, 0:nk, :], in_=W_v[:, k0:k0 + nk, cb * 512:(cb + 1) * 512]),
                     writes=[r_wd[w]], dma=True)
                for ct in range(4):
                    pb = nxt("py", 2)
                    for j in range(nk):
                        P.op("pe", "matmul", dict(out=ps_y[pb][:], lhsT=wd[w][:, j, ct * 128:(ct + 1) * 128], rhs=aT[sl][:, j, :],
                                                  start=(j == 0), stop=(j == nk - 1)),
                             reads=[r_wd[w], rhs_regs[j]], writes=[r_py[pb]])
                    c = cb * 4 + ct
                    P.op("dve", "tensor_tensor", dict(out=xs[:, c, :], in0=xs[:, c, :], in1=ps_y[pb][:], op=ALU.add),
                         reads=[r_xs[c], r_py[pb]], writes=[r_xs[c]])

        def fm_tile(W_v, col0, epilogue):
            w = nxt("wA", 4)
            P.op("pool", "dma_start", dict(out=wA[w][:], in_=W_v[:, :, col0:col0 + 128]), writes=[r_wA[w]], dma=True)
            pb = nxt("psA", 4)
            for kc in range(KC):
                P.op("pe", "matmul", dict(out=psA[pb][:], lhsT=wA[w][:, kc, :], rhs=hT[:, kc, :], start=(kc == 0), stop=(kc == KC - 1)),
                     reads=[r_wA[w], r_hT[kc]], writes=[r_psA[pb]])
            epilogue(psA[pb], r_psA[pb])

        def tm_tile(W_v, col0, ncl, out_ap, t0, dt, ocol, ab_special=False, fname=None, ti=0):
            w = nxt("wA", 4)
            P.op("pool", "dma_start", dict(out=wA[w][:, :, 0:ncl], in_=W_v[:, :, col0:col0 + ncl]), writes=[r_wA[w]], dma=True)
            pb = nxt("psA", 4)
            for tt in range(NTT):
                for kc in range(KC):
                    P.op("pe", "matmul", dict(out=psA[pb][:, tt * 128:tt * 128 + ncl], lhsT=hT[:, kc, tt * 128:(tt + 1) * 128],
                                              rhs=wA[w][:, kc, 0:ncl], start=(kc == 0), stop=(kc == KC - 1)),
                         reads=[r_wA[w], r_hT[kc]], writes=[r_psA[pb]])
            src = psA[pb][:].rearrange("p (t c) -> p t c", c=128)[:, :, 0:ncl]
            if dt == F32:
                s = nxt("st", 2)
                buf, rb = st[s], r_st[s]
            else:
                s = nxt("stb", 2)
                buf, rb = stb[s], r_stb[s]
            dst = buf[:].rearrange("p (t c) -> p t c", c=128)[:, :, 0:ncl]
            P.op("act", "copy", dict(out=dst, in_=src), reads=[r_psA[pb]], writes=[rb])
            if ab_special:
                for g_ in range(4):
                    for tt_ in range(NTT):
                        srcv = dst.rearrange("p t (x g h) -> p t x g h", x=4, g=4)[:, tt_, :, g_, :]
                        dstv = cfg["dst_ab"](t0 // T, g_, tt_).rearrange("p (x h) -> p x h", x=4)
                        P.op("sp", "dma_start", dict(out=dstv, in_=srcv), reads=[rb], dma=True)
                return
            if FUSED:
                for tt_ in range(NTT):
                    P.op("sp", "dma_start", dict(out=cfg["dst_tm"](fname, ti, t0 // T, tt_), in_=dst[:, tt_, :]), reads=[rb], dma=True)
                return
            P.op("sp", "dma_start", dict(out=out_ap[t0:t0 + T, ocol:ocol + ncl].rearrange("(t p) c -> p t c", p=128), in_=dst),
                 reads=[rb], dma=True)

        for t0 in range(0, T_tot, T):
            for c in range(KC):
                P.op("sp", "dma_start", dict(out=xs[:, c, :], in_=xT_v[:, c, t0:t0 + T]), writes=[r_xs[c]], dma=True)
            if oK:
                if not FUSED:
                    oT_v = oT.rearrange("(c p) t -> p c t", p=128)
                Wo_v = Wo.rearrange("(c p) n -> p c n", p=128)
                nob = oK // 128
                blocks = [(k0, min(k0 + FB, nob)) for k0 in range(0, nob, FB)]
                for bi, (k0, k1) in enumerate(blocks):
                    sl = bi % 2
                    for j in range(k1 - k0):
                        if FUSED:
                            G_, off_, ic_ = cfg["o_src"][t0 // T][k0 + j]
                            P.op("pool", "indirect_dma_start",
                                 dict(out=aT[sl][:, j, :], out_offset=None,
                                      in_offset=bass.IndirectOffsetOnAxis(ap=idx_s[:, ic_:ic_ + 1], axis=0), **gat(G_, off_, 0, T)),
                                 reads=[r_idx], writes=[r_aT[sl][j]], dma=True)
                        else:
                            P.op("sp", "dma_start", dict(out=aT[sl][:, j, :], in_=oT_v[:, k0 + j, t0:t0 + T]),
                                 writes=[r_aT[sl][j]], dma=True)
                    accum_block(Wo_v, k0, k1 - k0, sl, r_aT[sl])
            if dff:
                Wg_v = Wg.rearrange("(c p) n -> p c n", p=128)
                Wu_v = Wu.rearrange("(c p) n -> p c n", p=128)
                Wd_v = Wd.rearrange("(c p) n -> p c n", p=128)
                rmsnorm(nwa_s, r_nwa)
                NF = dff // 128
                blocks = [(f0, min(f0 + FB, NF)) for f0 in range(0, NF, FB)]

                def gate_up(bi):
                    f0, f1 = blocks[bi]
                    sl = bi % 2
                    for f in range(f0, f1):
                        res = []

                        def keep(ps_t, r_t):
                            res.append((ps_t, r_t))
                        fm_tile(Wg_v, f * 128, keep)
                        fm_tile(Wu_v, f * 128, keep)
                        (pg, rg), (pu, ru) = res
                        s = nxt("st", 2)
                        P.op("act", "activation", dict(out=st[s][:], in_=pg[:], func=AF.Silu), reads=[rg], writes=[r_st[s]])
                        P.op("dve", "tensor_tensor", dict(out=aT[sl][:, f - f0, :], in0=st[s][:], in1=pu[:], op=ALU.mult),
                             reads=[r_st[s], ru], writes=[r_aT[sl][f - f0]])

                gate_up(0)
                for bi in range(1, len(blocks)):
                    gate_up(bi)
                    f0, f1 = blocks[bi - 1]
                    accum_block(Wd_v, f0, f1 - f0, (bi - 1) % 2, r_aT[(bi - 1) % 2])
                f0, f1 = blocks[-1]
                accum_block(Wd_v, f0, f1 - f0, (len(blocks) - 1) % 2, r_aT[(len(blocks) - 1) % 2])
            if store_x:
                xo_v = xoT.rearrange("(c p) t -> p c t", p=128)
                for c in range(KC):
                    P.op("sp", "dma_start", dict(out=xo_v[:, c, t0:t0 + T], in_=xs[:, c, :]), reads=[r_xs[c]], dma=True)
            if inproj:
                Win_v = Win.rearrange("(c p) n -> p c n", p=128)
                rmsnorm(nwb_s, r_nwb)

                def ep_copy(out_ap, row0, dt, fname=None, ti=0):
                    def ep(ps_t, r_t):
                        if dt == F32:
                            s = nxt("st", 2)
                            buf, rb = st[s], r_st[s]
                        else:
                            s = nxt("stb", 2)
                            buf, rb = stb[s], r_stb[s]
                        P.op("act", "copy", dict(out=buf[:], in_=ps_t[:]), reads=[r_t], writes=[rb])
                        dst_ = cfg["dst_fm"](fname, ti, t0 // T) if FUSED else out_ap[row0:row0 + 128, t0:t0 + T]
                        P.op("sp", "dma_start", dict(out=dst_, in_=buf[:]), reads=[rb], dma=True)
                    return ep

            if inproj == "ab":
                def ep_rope_b(row0, ti=0):
                    def ep(ps_t, r_t):
                        s = nxt("stb", 2)
                        buf, rb = stb[s], r_stb[s]
                        P.op("act", "copy", dict(out=buf[:], in_=ps_t[:]), reads=[r_t], writes=[rb])
                        pb = nxt("py", 2)
                        P.op("pe", "matmul", dict(out=ps_y[pb][0:32, :], lhsT=rP[:, :], rhs=buf[0:32, :], start=True, stop=True),
                             reads=[r_rope, rb], writes=[r_py[pb]])
                        P.op("dve", "tensor_tensor", dict(out=t1[:], in0=buf[0:32, :], in1=rC[:, t0:t0 + T], op=ALU.mult),
                             reads=[rb, r_rope], writes=[r_t1])
                        P.op("dve", "tensor_tensor", dict(out=t2[:], in0=ps_y[pb][0:32, :], in1=rS[:, t0:t0 + T], op=ALU.mult),
                             reads=[r_py[pb], r_rope], writes=[r_t2])
                        P.op("dve", "tensor_tensor", dict(out=buf[0:32, :], in0=t1[:], in1=t2[:], op=ALU.add),
                             reads=[r_t1, r_t2], writes=[rb])
                        dst_ = cfg["dst_fm"]("bqk", ti, t0 // T) if FUSED else bqkT[row0:row0 + 128, t0:t0 + T]
                        P.op("sp", "dma_start", dict(out=dst_, in_=buf[:]), reads=[rb], dma=True)
                    return ep
                c0 = 0
                for i in range(ab["nqkv"] // 128):
                    fm_tile(Win_v, c0 + i * 128, ep_copy(aqkvT, i * 128, F32, "aqkv", i))
                c0 += ab["nqkv"]
                for i in range(ab["nz"] // 128):
                    if FUSED:
                        tm_tile(Win_v, c0 + i * 128, 128, None, t0, F32, 0, fname="az", ti=i)
                    else:
                        tm_tile(Win_v, c0 + i * 128, 128, az, t0, F32, i * 128)
                c0 += ab["nz"]
                tm_tile(Win_v, c0, ab["nab"], abo, t0, F32, 0, ab_special=FUSED)
                c0 += ab["nab"]
                for i in range(ab["nbqk"] // 128):
                    fm_tile(Win_v, c0 + i * 128, ep_rope_b(i * 128, i))
                c0 += ab["nbqk"]
                for i in range(ab["nbv"] // 128):
                    if FUSED:
                        tm_tile(Win_v, c0 + i * 128, 128, None, t0, BF16, 0, fname="bv", ti=i)
                    else:
                        tm_tile(Win_v, c0 + i * 128, 128, bv, t0, BF16, i * 128)
            if inproj == "c":
                def ep_c(row0, g_s, ti=0):
                    def ep(ps_t, r_t):
                        s = nxt("stb", 2)
                        buf, rb = stb[s], r_stb[s]
                        P.op("act", "activation", dict(out=sq[0][:], in_=ps_t[:], func=AF.Square), reads=[r_t], writes=[r_sq[0]])
                        pb = nxt("py", 2)
                        P.op("pe", "matmul", dict(out=ps_y[pb][:], lhsT=ones[:], rhs=sq[0][:], start=True, stop=True),
                             reads=[r_ones, r_sq[0]], writes=[r_py[pb]])
                        P.op("act", "activation", dict(out=rn[:], in_=ps_y[pb][:], func=AF.Sqrt, scale=1.0 / 128, bias=epsb[:, 0:1]),
                             reads=[r_py[pb], r_eps], writes=[r_rn])
                        P.op("dve", "reciprocal", dict(out=rn[:], in_=rn[:]), reads=[r_rn], writes=[r_rn])
                        P.op("dve", "scalar_tensor_tensor",
                             dict(out=buf[:], in0=ps_t[:], scalar=g_s[:, 0:1], in1=rn[:], op0=ALU.mult, op1=ALU.mult),
                             reads=[r_t, r_rope, r_rn], writes=[rb])
                        pb2 = nxt("py", 2)
                        P.op("pe", "matmul", dict(out=ps_y[pb2][:], lhsT=rP[:, :], rhs=buf[:], start=True, stop=True),
                             reads=[r_rope, rb], writes=[r_py[pb2]])
                        P.op("dve", "tensor_tensor", dict(out=t1[:], in0=buf[:], in1=rC[:, t0:t0 + T], op=ALU.mult),
                             reads=[rb, r_rope], writes=[r_t1])
                        P.op("dve", "tensor_tensor", dict(out=t2[:], in0=ps_y[pb2][:], in1=rS[:, t0:t0 + T], op=ALU.mult),
                             reads=[r_py[pb2], r_rope], writes=[r_t2])
                        P.op("dve", "tensor_tensor", dict(out=buf[:], in0=t1[:], in1=t2[:], op=ALU.add),
                             reads=[r_t1, r_t2], writes=[rb])
                        dst_ = cfg["dst_fm"]("cqk", ti, t0 // T) if FUSED else cqkT[row0:row0 + 128, t0:t0 + T]
                        P.op("sp", "dma_start", dict(out=dst_, in_=buf[:]), reads=[rb], dma=True)
                    return ep
                nqt = cc["nq"] // 128
                nkt = cc["nk"] // 128
                for i in range(nqt):
                    fm_tile(Win_v, i * 128, ep_c(i * 128, gq_s, i))
                for i in range(nkt):
                    fm_tile(Win_v, cc["nq"] + i * 128, ep_c(cc["nq"] + i * 128, gk_s, nqt + i))
                for i in range(cc["nv"] // 128):
                    if FUSED:
                        tm_tile(Win_v, cc["nq"] + cc["nk"] + i * 128, 128, None, t0, BF16, 0, fname="cv", ti=i)
                    else:
                        tm_tile(Win_v, cc["nq"] + cc["nk"] + i * 128, 128, cv, t0, BF16, i * 128)
            if final_norm:
                rmsnorm(nwb_s, r_nwb, to_x=True)
                o_v = outT.rearrange("(c p) t -> p c t", p=128)
                for c in range(KC):
                    P.op("sp", "dma_start", dict(out=o_v[:, c, t0:t0 + T], in_=xs[:, c, :]), reads=[r_xs[c]], dma=True)
        P.emit()
    if FUSED:
        nc.all_engine_barrier()
    return nc


def attn_c_build(cfg):
    S = cfg.get("S", 4096)
    NKV = cfg.get("NKV", 2)
    REP = cfg.get("REP", 4)
    NQH = NKV * REP
    NKC = S // 128
    QB = 512
    scale = 128 ** -0.5
    TT_ = cfg.get("TD")
    FUSED = TT_ is not None
    if FUSED:
        nc = cfg["nc"]
        GQ, GK, GV, OCB = TT_["GQ"], TT_["GK"], TT_["GV"], TT_["OCB"]
    else:
        nc = bass.Bass("TRN2", target_bir_lowering=False)
        cqT = nc.dram_tensor("cqT", [NQH * 128, S], BF16, kind="ExternalInput").ap()
        ckT = nc.dram_tensor("ckT", [NKV * 128, S], BF16, kind="ExternalInput").ap()
        cv = nc.dram_tensor("cv", [S, NKV * 128], BF16, kind="ExternalInput").ap()
        ocT = nc.dram_tensor("ocT", [NQH * 128, S], BF16, kind="ExternalOutput").ap()
    P = Prog(nc)
    with ExitStack() as es:
        es.enter_context(nc.allow_low_precision("bf16 matmul operands, fp32 accumulate"))
        pfx = uniq()
        sb = lambda name, shape, dt: es.enter_context(nc.sbuf_tensor(pfx + name, shape, dt))
        psb = lambda name: es.enter_context(nc.psum_tensor(pfx + name, [128, 512], F32))
        R = Reg
        kT = [sb("kT%d" % i, [128, S], BF16) for i in range(2)]
        vv = [sb("v%d" % i, [128, NKC, 128], BF16) for i in range(2)]
        qT = [sb("qT%d" % i, [128, S], BF16) for i in range(2)]
        NE = 3
        ee = [sb("e%d" % i, [128, QB], BF16) for i in range(NE)]
        rz = sb("rz", [128, QB], F32)
        ob = [sb("ob%d" % i, [128, QB], BF16) for i in range(2)]
        ones = sb("ones", [128, 128], BF16)
        ps_s = [psb("ps_s%d" % i) for i in range(NE)]
        ps_o = [psb("ps_o%d" % i) for i in range(2)]
        ps_z = [psb("ps_z%d" % i) for i in range(2)]
        r_kT, r_v, r_qT = [R(), R()], [R(), R()], [R(), R()]
        r_e = [R() for i in range(NE)]
        r_rz, r_ones = R(), R()
        r_ob = [R(), R()]
        r_ps = [R(psum=True) for i in range(NE)]
        r_po, r_pz = [R(psum=True), R(psum=True)], [R(psum=True), R(psum=True)]
        P.op("pool", "memset", dict(ap=ones[:], constant=1.0), writes=[r_ones])
        if FUSED:
            idx_s = sb("idx_s", [128, 8], mybir.dt.int32)
            r_idx = R()
            P.op("sp", "dma_start", dict(out=idx_s[:], in_=TT_["idx"]), writes=[r_idx], dma=True)
            QS = S // 4

            def gather(out_ap, G_, off_, span_, ic_, cols, wr):
                P.op("pool", "indirect_dma_start",
                     dict(out=out_ap, out_offset=None,
                          in_offset=bass.IndirectOffsetOnAxis(ap=idx_s[:, ic_:ic_ + 1], axis=0), **gat(G_, off_, cols.start, cols.stop - cols.start)),
                     reads=[r_idx], writes=[wr], dma=True)
        it = 0
        blk = 0
        for kv in range(NKV):
            ks = kv % 2
            if FUSED:
                for r_ in range(4):
                    for ps_ in range(2):
                        gather(kT[ks][:, r_ * QS + ps_ * 512:r_ * QS + (ps_ + 1) * 512], GK[ps_][kv], r_ * 512, None, 0, slice(0, 512), r_kT[ks])
                for c_ in range(NKC):
                    r_, tl_ = c_ // 8, c_ % 8
                    gather(vv[ks][:, c_, :], GV[tl_ // 4][tl_ % 4], r_ * 512, None, 1, slice(kv * 128, (kv + 1) * 128), r_v[ks])
            else:
                P.op("sp", "dma_start", dict(out=kT[ks][:], in_=ckT[kv * 128:(kv + 1) * 128, :]), writes=[r_kT[ks]], dma=True)
                P.op("sp", "dma_start", dict(out=vv[ks][:], in_=cv[:, kv * 128:(kv + 1) * 128].rearrange("(c p) d -> p c d", p=128)),
                     writes=[r_v[ks]], dma=True)
            for r in range(REP):
                h = kv * REP + r
                qs = h % 2
                if FUSED:
                    for r_ in range(4):
                        for ps_ in range(2):
                            gather(qT[qs][:, r_ * QS + ps_ * 512:r_ * QS + (ps_ + 1) * 512], GQ[ps_][h], r_ * 512, None, 0, slice(0, 512), r_qT[qs])
                else:
                    P.op("sp", "dma_start", dict(out=qT[qs][:], in_=cqT[h * 128:(h + 1) * 128, :]), writes=[r_qT[qs]], dma=True)
                for qb in range(S // QB):
                    pb = blk % 2
                    blk += 1
                    qsl = qT[qs][:, qb * QB:(qb + 1) * QB]

                    def smm(kc, i):
                        P.op("pe", "matmul", dict(out=ps_s[i][:], lhsT=kT[ks][:, kc * 128:(kc + 1) * 128], rhs=qsl, start=True, stop=True),
                             reads=[r_kT[ks], r_qT[qs]], writes=[r_ps[i]])
                    smm(0, it % NE)
                    for kc in range(NKC):
                        i = it % NE
                        it += 1
                        if kc + 1 < NKC:
                            smm(kc + 1, it % NE)
                        P.op("act", "activation", dict(out=ee[i][:], in_=ps_s[i][:], func=AF.Exp, scale=scale),
                             reads=[r_ps[i]], writes=[r_e[i]])
                        P.op("pe", "matmul", dict(out=ps_o[pb][:], lhsT=vv[ks][:, kc, :], rhs=ee[i][:], start=(kc == 0), stop=(kc == NKC - 1)),
                             reads=[r_v[ks], r_e[i]], writes=[r_po[pb]])
                        P.op("pe", "matmul", dict(out=ps_z[pb][:], lhsT=ones[:], rhs=ee[i][:], start=(kc == 0), stop=(kc == NKC - 1)),
                             reads=[r_ones, r_e[i]], writes=[r_pz[pb]])
                    P.op("dve", "reciprocal", dict(out=rz[:], in_=ps_z[pb][:]), reads=[r_pz[pb]], writes=[r_rz])
                    P.op("dve", "tensor_tensor", dict(out=ob[pb][:], in0=ps_o[pb][:], in1=rz[:], op=ALU.mult),
                         reads=[r_po[pb], r_rz], writes=[r_ob[pb]])
                    if FUSED:
                        q_, th_ = qb // 2, qb % 2
                        P.op("sp", "dma_start", dict(out=OCB[h * 2 + th_][q_ * 128:(q_ + 1) * 128, :], in_=ob[pb][:]),
                             reads=[r_ob[pb]], dma=True)
                    else:
                        P.op("sp", "dma_start", dict(out=ocT[h * 128:(h + 1) * 128, qb * QB:(qb + 1) * QB], in_=ob[pb][:]),
                             reads=[r_ob[pb]], dma=True)
        P.emit()
    if FUSED:
        nc.all_engine_barrier()
    return nc


B_PATTERNS = ((128, 1), (512, 4), (2048, 16))


def ssl(a, n, d):
    return slice(a, a + d * (n - 1) + 1, d)


def dil_b_build(cfg):
    S = cfg.get("S", 4096)
    NHS = cfg.get("NHS", 2)
    pats = cfg.get("pats", B_PATTERNS)
    NG = len(pats)
    scale = 128 ** -0.5
    TT_ = cfg.get("TD")
    FUSED = TT_ is not None
    if FUSED:
        nc = cfg["nc"]
        bqT, bkT, bv, bmask, ob_q = TT_["bqT"], TT_["bkT"], TT_["bv"], TT_["bmask"], TT_["ob_q"]
    else:
        nc = bass.Bass("TRN2", target_bir_lowering=False)
        bqT = nc.dram_tensor("bqT", [NHS * NG * 128, S], BF16, kind="ExternalInput").ap()
        bkT = nc.dram_tensor("bkT", [NHS * NG * 128, S], BF16, kind="ExternalInput").ap()
        bv = nc.dram_tensor("bv", [S, NHS * NG * 128], BF16, kind="ExternalInput").ap()
        bmask = nc.dram_tensor("bmask", [128, 3, 512], BF16, kind="ExternalInput").ap()
        obT = nc.dram_tensor("obT", [NHS * 128, S], BF16, kind="ExternalOutput").ap()
    P = Prog(nc)
    with ExitStack() as es:
        es.enter_context(nc.allow_low_precision("bf16 matmul operands, fp32 accumulate"))
        pfx = uniq()
        sb = lambda name, shape, dt: es.enter_context(nc.sbuf_tensor(pfx + name, shape, dt))
        psb = lambda name: es.enter_context(nc.psum_tensor(pfx + name, [128, 512], F32))
        R = Reg
        qT = [sb("qT%d" % i, [128, S], BF16) for i in range(2)]
        kT = [sb("kT%d" % i, [128, S], BF16) for i in range(2)]
        vp = [sb("vp%d" % i, [128, S // 128, 128], BF16) for i in range(2)]
        Uacc = sb("Uacc", [128, S], F32)
        Zacc = sb("Zacc", [128, S], F32)
        ob = sb("ob", [128, S], BF16)
        ee = [sb("e%d" % i, [128, 512], BF16) for i in range(2)]
        em = [sb("em%d" % i, [128, 512], BF16) for i in range(2)]
        mk = sb("mk", [128, 3, 512], BF16)
        ones = sb("ones", [128, 128], BF16)
        ps_s = [psb("ps_s%d" % i) for i in range(2)]
        ps_o = [psb("ps_o%d" % i) for i in range(2)]
        ps_z = [psb("ps_z%d" % i) for i in range(2)]
        r_q, r_k, r_v = [R(), R()], [R(), R()], [R(), R()]
        r_U, r_Z, r_ob, r_mk, r_ones = R(), R(), R(), R(), R()
        r_e, r_em = [R(), R()], [R(), R()]
        r_ps, r_po, r_pz = [R(psum=True), R(psum=True)], [R(psum=True), R(psum=True)], [R(psum=True), R(psum=True)]
        P.op("pool", "memset", dict(ap=ones[:], constant=1.0), writes=[r_ones])
        P.op("sp", "dma_start", dict(out=mk[:], in_=bmask), writes=[r_mk], dma=True)
        gi = 0
        sc = 0
        bc = 0
        for hs in range(NHS):
            for g, (wd_, d) in enumerate(pats):
                s = gi % 2
                gi += 1
                row = (hs * NG + g) * 128
                L = S // d
                nblk = L // 128
                P.op("sp", "dma_start", dict(out=qT[s][:], in_=bqT[row:row + 128, :]), writes=[r_q[s]], dma=True)
                P.op("sp", "dma_start", dict(out=kT[s][:], in_=bkT[row:row + 128, :]), writes=[r_k[s]], dma=True)
                P.op("sp", "dma_start",
                     dict(out=vp[s][:].rearrange("p (r i) c -> p r i c", r=d),
                          in_=bv[:, row:row + 128].rearrange("(i p r) c -> p r i c", p=128, r=d)),
                     writes=[r_v[s]], dma=True)
                for r in range(d):
                    for i0 in range(0, nblk, 4):
                        nb = min(4, nblk - i0)
                        pb = bc % 2
                        bc += 1
                        for o in (0, -1, 1):
                            blo = 0
                            bhi = nb
                            if o == -1 and i0 == 0:
                                blo = 1
                            if o == 1 and i0 + nb == nblk:
                                bhi = nb - 1
                            if bhi <= blo:
                                continue
                            ss = sc % 2
                            sc += 1
                            for b in range(blo, bhi):
                                i = i0 + b
                                ka = r + d * 128 * (i + o)
                                qa = r + d * 128 * i
                                P.op("pe", "matmul", dict(out=ps_s[ss][:, b * 128:(b + 1) * 128],
                                                          lhsT=kT[s][:, ssl(ka, 128, d)], rhs=qT[s][:, ssl(qa, 128, d)],
                                                          start=True, stop=True),
                                     reads=[r_k[s], r_q[s]], writes=[r_ps[ss]])
                            cs = slice(blo * 128, bhi * 128)
                            P.op("act", "activation", dict(out=ee[ss][:, cs], in_=ps_s[ss][:, cs], func=AF.Exp, scale=scale),
                                 reads=[r_ps[ss]], writes=[r_e[ss]])
                            P.op("dve", "tensor_tensor", dict(out=em[ss][:, cs], in0=ee[ss][:, cs], in1=mk[:, o + 1, cs], op=ALU.mult),
                                 reads=[r_e[ss], r_mk], writes=[r_em[ss]])
                            for b in range(blo, bhi):
                                i = i0 + b
                                last = (o == 1) or (o == -1 and i == nblk - 1) or (o == 0 and nblk == 1)
                                bs = slice(b * 128, (b + 1) * 128)
                                P.op("pe", "matmul", dict(out=ps_o[pb][:, bs], lhsT=vp[s][:, r * nblk + i + o, :], rhs=em[ss][:, bs],
                                                          start=(o == 0 and b == 0), stop=last, skip_group_check=True),
                                     reads=[r_v[s], r_em[ss]], writes=[r_po[pb]])
                                P.op("pe", "matmul", dict(out=ps_z[pb][:, bs], lhsT=ones[:], rhs=em[ss][:, bs],
                                                          start=(o == 0 and b == 0), stop=last, skip_group_check=True),
                                     reads=[r_ones, r_em[ss]], writes=[r_pz[pb]])
                        a0 = r + d * 128 * i0
                        usl = Uacc[:, ssl(a0, 128 * nb, d)]
                        zsl = Zacc[:, ssl(a0, 128 * nb, d)]
                        if g == 0:
                            P.op("act", "copy", dict(out=usl, in_=ps_o[pb][:, 0:nb * 128]), reads=[r_po[pb]], writes=[r_U])
                            P.op("dve", "tensor_copy", dict(out=zsl, in_=ps_z[pb][:, 0:nb * 128]), reads=[r_pz[pb]], writes=[r_Z])
                        else:
                            P.op("dve", "tensor_tensor", dict(out=usl, in0=usl, in1=ps_o[pb][:, 0:nb * 128], op=ALU.add),
                                 reads=[r_po[pb], r_U], writes=[r_U])
                            P.op("dve", "tensor_tensor", dict(out=zsl, in0=zsl, in1=ps_z[pb][:, 0:nb * 128], op=ALU.add),
                                 reads=[r_pz[pb], r_Z], writes=[r_Z])
            P.op("dve", "reciprocal", dict(out=Zacc[:], in_=Zacc[:]), reads=[r_Z], writes=[r_Z])
            P.op("dve", "tensor_tensor", dict(out=ob[:], in0=Uacc[:], in1=Zacc[:], op=ALU.mult), reads=[r_U, r_Z], writes=[r_ob])
            if FUSED:
                for th_ in range(2):
                    P.op("sp", "dma_start", dict(out=ob_q[hs * 2 + th_].rearrange("(q p) t -> p q t", q=4),
                                                 in_=ob[:].rearrange("p (q h t) -> p q h t", q=4, h=2)[:, :, th_, :]), reads=[r_ob], dma=True)
            else:
                P.op("sp", "dma_start", dict(out=obT[hs * 128:(hs + 1) * 128, :], in_=ob[:]), reads=[r_ob], dma=True)
        P.emit()
    if FUSED:
        nc.all_engine_barrier()
    return nc


def dil_mask():
    import numpy as _np
    m = _np.zeros((128, 3, 512), _np.float32)
    p = _np.arange(128)[:, None]
    n = _np.arange(128)[None, :]
    for o in (-1, 0, 1):
        mm = (_np.abs(128 * o + p - n) <= 64).astype(_np.float32)
        m[:, o + 1, :] = _np.tile(mm, (1, 4))
    return m

import numpy as _np

BIG = 30000.0


def gdn_consts():
    k = _np.arange(128)[:, None]
    i = _np.arange(128)[None, :]
    c = {}
    c["ident"] = _np.eye(128, dtype=_np.float32)
    c["ucum"] = _np.stack([(k <= i), (k >= i)], 1).astype(_np.float32)
    nmd_f = BIG * (k <= i)
    nmd_b = BIG * (k >= i)
    nmt_f = -BIG * (i < k)
    nmt_b = -BIG * (i > k)
    c["nm"] = _np.stack([nmd_f, nmt_f, nmd_b, nmt_b], 1).astype(_np.float32)
    return c


def gdn_build(cfg):
    S = cfg.get("S", 4096)
    NH = cfg.get("NH", 4)
    NCH = S // 128
    NB = S // 512
    STOP = cfg.get("stop", 9)
    CHD = F32 if cfg.get("chain_fp32", True) else BF16
    SUB = cfg.get("sub", 9)
    TT_ = cfg.get("TD")
    FUSED = TT_ is not None
    nc = cfg["nc"] if FUSED else bass.Bass("TRN2", target_bir_lowering=False)
    if FUSED:
        din = lambda name, shape, dt=F32: TT_[name]
    else:
        din = lambda name, shape, dt=F32: nc.dram_tensor(name, shape, dt, kind="ExternalInput").ap()
    aqkvT = din("aqkvT", [NH * 3 * 128, S])
    az = din("az", [S, NH * 128])
    abr = din("abr", [S, 4 * NH])
    cw = din("cw", [128, NH * 3, 5])
    alog = din("alog", [128, 2 * NH])
    dtb = din("dtb", [128, 2 * NH])
    onorm = din("onorm", [128, 128])
    ident_d = din("ident_in", [128, 128])
    ucum_d = din("ucum_in", [128, 2, 128])
    nm_d = din("nm_in", [128, 4, 128])
    oaT = TT_["oa_q"] if FUSED else nc.dram_tensor("oaT", [NH * 128, S], BF16, kind="ExternalOutput").ap()
    P = Prog(nc)
    NC2 = 2 * NH
    with ExitStack() as es:
        es.enter_context(nc.allow_low_precision("bf16 matmul operands, fp32 accumulate"))
        pfx = uniq()
        sb = lambda name, shape, dt: es.enter_context(nc.sbuf_tensor(pfx + name, shape, dt))
        psb = lambda name, dt=F32, n=512: es.enter_context(nc.psum_tensor(pfx + name, [128, n], dt))
        R = Reg
        identf = sb("identf", [128, 128], F32)
        identb = sb("identb", [128, 128], BF16)
        ucum = sb("ucum", [128, 2, 128], F32)
        nm = sb("nm", [128, 4, 128], F32)
        onesf = sb("onesf", [128, 128], F32)
        onesb = sb("onesb", [128, 128], BF16)
        epsb = sb("epsb", [128, 1], F32)
        oneb = sb("oneb", [128, 1], F32)
        cws = sb("cws", [128, NH * 3, 5], F32)
        onorm_s = sb("onorm_s", [128, 128], F32)
        r_c = R()
        identc = identf if CHD == F32 else identb
        P.op("sp", "dma_start", dict(out=identf[:], in_=ident_d), writes=[r_c], dma=True)
        P.op("pool", "dma_start", dict(out=identb[:], in_=ident_d), writes=[r_c], dma=True)
        P.op("sp", "dma_start", dict(out=ucum[:], in_=ucum_d), writes=[r_c], dma=True)
        P.op("sp", "dma_start", dict(out=nm[:], in_=nm_d), writes=[r_c], dma=True)
        P.op("sp", "dma_start", dict(out=cws[:], in_=cw), writes=[r_c], dma=True)
        P.op("sp", "dma_start", dict(out=onorm_s[:], in_=onorm), writes=[r_c], dma=True)
        P.op("pool", "memset", dict(ap=onesf[:], constant=1.0), writes=[r_c])
        P.op("pool", "memset", dict(ap=onesb[:], constant=1.0), writes=[r_c])
        P.op("pool", "memset", dict(ap=epsb[:], constant=1e-6), writes=[r_c])
        P.op("pool", "memset", dict(ap=oneb[:], constant=1.0), writes=[r_c])

        NCOL = NCH * NC2
        raw = sb("raw", [128, NCH, 2 * NC2], F32)
        alog_s = sb("alog_s", [128, NC2], F32)
        dtb_s = sb("dtb_s", [128, NC2], F32)
        beta = sb("beta", [128, NCH, NC2], F32)
        nbeta = sb("nbeta", [128, NCH, NC2], F32)
        gg = sb("gg", [128, NCH, NC2], F32)
        gc = sb("gc", [128, NCH, NC2], F32)
        ngc = sb("ngc", [128, NCH, NC2], F32)
        gtot = sb("gtot", [128, NCH, NC2], F32)
        egc = sb("egc", [128, NCH, NC2], F32)
        begc = sb("begc", [128, NCH, NC2], F32)
        ekd = sb("ekd", [128, NCH, NC2], F32)
        egl = sb("egl", [128, NCH, NC2], F32)
        r_g = R()
        ps_m = psb("ps_m")
        r_pm = R(psum=True)
        P.op("sp", "dma_start", dict(out=raw[:], in_=abr.rearrange("(c p) n -> p c n", p=128)), writes=[r_g], dma=True)
        P.op("sp", "dma_start", dict(out=alog_s[:], in_=alog), writes=[r_g], dma=True)
        P.op("sp", "dma_start", dict(out=dtb_s[:], in_=dtb), writes=[r_g], dma=True)
        P.op("act", "activation", dict(out=beta[:], in_=raw[:, :, 0:NC2], func=AF.Sigmoid), reads=[r_g], writes=[r_g])
        P.op("dve", "tensor_scalar", dict(out=nbeta[:], in0=beta[:], scalar1=-1.0, scalar2=None, op0=ALU.mult), reads=[r_g], writes=[r_g])
        P.op("dve", "tensor_tensor", dict(out=gg[:], in0=raw[:, :, NC2:2 * NC2], in1=dtb_s[:, None, :].to_broadcast([128, NCH, NC2]), op=ALU.add),
             reads=[r_g], writes=[r_g])
        sp1 = sb("sp1", [128, NCH, NC2], F32)
        sp2 = sb("sp2", [128, NCH, NC2], F32)
        sp3 = sb("sp3", [128, NCH, NC2], F32)
        P.op("dve", "tensor_scalar", dict(out=sp1[:], in0=gg[:], scalar1=-1.0, scalar2=None, op0=ALU.mult), reads=[r_g], writes=[r_g])
        P.op("dve", "tensor_tensor", dict(out=sp1[:], in0=sp1[:], in1=gg[:], op=ALU.max), reads=[r_g], writes=[r_g])
        P.op("act", "activation", dict(out=sp1[:], in_=sp1[:], func=AF.Exp, scale=-1.0), reads=[r_g], writes=[r_g])
        P.op("dve", "tensor_scalar", dict(out=sp2[:], in0=sp1[:], scalar1=2.0, scalar2=None, op0=ALU.add), reads=[r_g], writes=[r_g])
        P.op("dve", "reciprocal", dict(out=sp2[:], in_=sp2[:]), reads=[r_g], writes=[r_g])
        P.op("dve", "tensor_tensor", dict(out=sp1[:], in0=sp1[:], in1=sp2[:], op=ALU.mult), reads=[r_g], writes=[r_g])
        P.op("dve", "tensor_tensor", dict(out=sp2[:], in0=sp1[:], in1=sp1[:], op=ALU.mult), reads=[r_g], writes=[r_g])
        P.op("dve", "tensor_scalar", dict(out=sp3[:], in0=sp2[:], scalar1=1.0 / 11, scalar2=1.0 / 9, op0=ALU.mult, op1=ALU.add), reads=[r_g], writes=[r_g])
        for cst_ in (1.0 / 7, 1.0 / 5, 1.0 / 3, 1.0):
            P.op("dve", "tensor_tensor", dict(out=sp3[:], in0=sp3[:], in1=sp2[:], op=ALU.mult), reads=[r_g], writes=[r_g])
            P.op("dve", "tensor_scalar", dict(out=sp3[:], in0=sp3[:], scalar1=cst_, scalar2=None, op0=ALU.add), reads=[r_g], writes=[r_g])
        P.op("dve", "tensor_tensor", dict(out=sp3[:], in0=sp3[:], in1=sp1[:], op=ALU.mult), reads=[r_g], writes=[r_g])
        P.op("dve", "tensor_scalar", dict(out=sp1[:], in0=gg[:], scalar1=0.0, scalar2=None, op0=ALU.max), reads=[r_g], writes=[r_g])
        P.op("dve", "scalar_tensor_tensor", dict(out=gg[:], in0=sp3[:], scalar=2.0, in1=sp1[:], op0=ALU.mult, op1=ALU.add), reads=[r_g], writes=[r_g])
        P.op("act", "activation", dict(out=alog_s[:], in_=alog_s[:], func=AF.Exp), reads=[r_g], writes=[r_g])
        P.op("dve", "scalar_tensor_tensor", dict(out=gg[:], in0=gg[:], scalar=-1.0, in1=alog_s[:, None, :].to_broadcast([128, NCH, NC2]),
                                                 op0=ALU.mult, op1=ALU.mult), reads=[r_g], writes=[r_g])
        ggv = gg[:].rearrange("p c (d h) -> p c d h", d=2)
        gcv = gc[:].rearrange("p c (d h) -> p c d h", d=2)
        gtv = gtot[:].rearrange("p c (d h) -> p c d h", d=2)
        psv = ps_m[:, 0:NCH * NC2].rearrange("p (c d h) -> p c d h", c=NCH, d=2)
        pst = ps_m[:, 256:256 + NCH * NC2].rearrange("p (c d h) -> p c d h", c=NCH, d=2)
        assert NCH * NC2 <= 256
        for d in range(2):
            P.op("pe", "matmul", dict(out=psv[:, :, d, :], lhsT=ucum[:, d, :], rhs=ggv[:, :, d, :], start=(d == 0), stop=True, skip_group_check=True),
                 reads=[r_c, r_g], writes=[r_pm])
        P.op("pe", "matmul", dict(out=ps_m[:, 256:256 + NCH * NC2], lhsT=onesf[:], rhs=gg[:].rearrange("p c n -> p (c n)"),
                                  start=False, stop=True, skip_group_check=True), reads=[r_c, r_g], writes=[r_pm])
        P.op("dve", "tensor_copy", dict(out=gc[:].rearrange("p c n -> p (c n)"), in_=ps_m[:, 0:NCH * NC2]), reads=[r_pm], writes=[r_g])
        P.op("dve", "tensor_copy", dict(out=gtot[:].rearrange("p c n -> p (c n)"), in_=ps_m[:, 256:256 + NCH * NC2]), reads=[r_pm], writes=[r_g])
        P.op("dve", "tensor_scalar", dict(out=ngc[:], in0=gc[:], scalar1=-1.0, scalar2=None, op0=ALU.mult), reads=[r_g], writes=[r_g])
        P.op("act", "activation", dict(out=egc[:], in_=gc[:], func=AF.Exp), reads=[r_g], writes=[r_g])
        P.op("dve", "tensor_tensor", dict(out=begc[:], in0=egc[:], in1=beta[:], op=ALU.mult), reads=[r_g], writes=[r_g])
        P.op("dve", "tensor_tensor", dict(out=ekd[:], in0=gtot[:], in1=gc[:], op=ALU.subtract), reads=[r_g], writes=[r_g])
        P.op("act", "activation", dict(out=ekd[:], in_=ekd[:], func=AF.Exp), reads=[r_g], writes=[r_g])
        P.op("act", "activation", dict(out=egl[:], in_=gtot[:], func=AF.Exp), reads=[r_g], writes=[r_g])

        NHX = NH if STOP >= 1 else 0
        xin = sb("xin", [128, S + 4], F32)
        acc = sb("acc", [128, S], F32)
        sqb = sb("sqb", [128, S], BF16)
        fT = [sb("fT%d" % i, [128, S], BF16) for i in range(3)]
        kbg = [sb("kbg%d" % i, [128, NCH, 128], BF16) for i in range(2)]
        kdd = [sb("kdd%d" % i, [128, NCH, 128], BF16) for i in range(2)]
        vbd = [sb("vbd%d" % i, [128, NCH, 128], BF16) for i in range(2)]
        oacc = sb("oacc", [128, NCH, 128], F32)
        zt = sb("zt", [128, NCH, 128], F32)
        rn = sb("rn", [128, 512], F32)
        ssn = sb("ssn", [128, NCH], F32)
        ogb = sb("ogb", [128, NCH, 128], BF16)
        oTs = sb("oTs", [128, S], BF16)
        r_xin, r_acc, r_sqb, r_rn = R(), R(), R(), R()
        r_fT = [R(), R(), R()]
        r_tok = R()
        r_oacc = [R() for c in range(NCH)]
        r_zt, r_ssn, r_ogb, r_oTs = R(), R(), R(), R()
        Gb = [[sb("Gb%d%d" % (d, i), [128, 128], F32) for i in range(2)] for d in range(2)]
        dec = [[sb("dec%d%d" % (d, i), [128, 2, 128], F32) for i in range(2)] for d in range(2)]
        Nb = [[sb("Nb%d%d" % (d, i), [128, 128], CHD) for i in range(2)] for d in range(2)]
        Mb = [[sb("Mb%d%d" % (d, i), [128, 128], CHD) for i in range(2)] for d in range(2)]
        Pb = [[sb("Pb%d%d" % (d, i), [128, 128], CHD) for i in range(2)] for d in range(2)]
        TT = [[sb("TT%d%d" % (d, i), [128, 128], BF16) for i in range(2)] for d in range(2)]
        wTn = [[sb("wTn%d%d" % (d, i), [128, 128], BF16) for i in range(2)] for d in range(2)]
        atT = [[sb("atT%d%d" % (d, i), [128, 128], BF16) for i in range(2)] for d in range(2)]
        vnew = [sb("vnew%d" % d, [128, 128], BF16) for d in range(2)]
        tmpo = [sb("tmpo%d" % d, [128, 128], F32) for d in range(2)]
        tmpo2 = [sb("tmpo2%d" % d, [128, 128], F32) for d in range(2)]
        Sf = [sb("Sf%d" % d, [128, 128], F32) for d in range(2)]
        Sb_ = [sb("Sb%d" % d, [128, 128], BF16) for d in range(2)]
        Sl_ = [sb("Sl%d" % d, [128, 128], BF16) for d in range(2)]
        vnl = [sb("vnl%d" % d, [128, 128], BF16) for d in range(2)]
        r_Gb = [[R(), R()], [R(), R()]]
        r_dec = [[R(), R()], [R(), R()]]
        r_N = [[R(), R()], [R(), R()]]
        r_M = [[R(), R()], [R(), R()]]
        r_P = [[R(), R()], [R(), R()]]
        r_TT = [[R(), R()], [R(), R()]]
        r_w = [[R(), R()], [R(), R()]]
        r_at = [[R(), R()], [R(), R()]]
        r_vn, r_to, r_to2, r_Sf, r_Sb = [R(), R()], [R(), R()], [R(), R()], [R(), R()], [R(), R()]
        ps_X = [psb("ps_X%d" % d) for d in range(2)]
        ps_kk = psb("ps_kk")
        ps_ch = [psb("ps_ch%d" % d) for d in range(2)]
        ps_sc = [psb("ps_sc%d" % d) for d in range(2)]
        r_pX, r_pch, r_psc = [R(psum=True), R(psum=True)], [R(psum=True), R(psum=True)], [R(psum=True), R(psum=True)]
        r_pkk = R(psum=True)
        ps_tb = ps_m[:].bitcast(BF16)

        for h in range(NHX):
            for t in range(3):
                row = (h * 3 + t) * 128
                P.op("pool", "memset", dict(ap=xin[:, 0:2], constant=0.0), writes=[r_xin])
                P.op("pool", "memset", dict(ap=xin[:, S + 2:S + 4], constant=0.0), writes=[r_xin])
                P.op("sp", "dma_start", dict(out=xin[:, 2:S + 2], in_=aqkvT[row:row + 128, :]), writes=[r_xin], dma=True)
                P.op("dve", "tensor_scalar", dict(out=acc[:], in0=xin[:, 0:S], scalar1=cws[:, h * 3 + t, 0:1], scalar2=None, op0=ALU.mult),
                     reads=[r_xin, r_c], writes=[r_acc])
                for w in range(1, 5):
                    P.op("dve", "scalar_tensor_tensor", dict(out=acc[:], in0=xin[:, w:w + S], scalar=cws[:, h * 3 + t, w:w + 1], in1=acc[:],
                                                             op0=ALU.mult, op1=ALU.add), reads=[r_xin, r_c, r_acc], writes=[r_acc])
                if t == 2:
                    P.op("act", "activation", dict(out=fT[2][:], in_=acc[:], func=AF.Silu), reads=[r_acc], writes=[r_fT[2]])
                else:
                    P.op("act", "activation", dict(out=acc[:], in_=acc[:], func=AF.Silu), reads=[r_acc], writes=[r_acc])
                    P.op("act", "activation", dict(out=sqb[:], in_=acc[:], func=AF.Square), reads=[r_acc], writes=[r_sqb])
                    for b in range(NB):
                        bs = slice(b * 512, (b + 1) * 512)
                        P.op("pe", "matmul", dict(out=ps_m[:], lhsT=onesb[:], rhs=sqb[:, bs], start=True, stop=True),
                             reads=[r_c, r_sqb], writes=[r_pm])
                        P.op("act", "activation", dict(out=rn[:], in_=ps_m[:], func=AF.Sqrt, bias=epsb[:, 0:1]), reads=[r_pm, r_c], writes=[r_rn])
                        P.op("dve", "reciprocal", dict(out=rn[:], in_=rn[:]), reads=[r_rn], writes=[r_rn])
                        P.op("dve", "scalar_tensor_tensor", dict(out=fT[t][:, bs], in0=acc[:, bs], scalar=(128 ** -0.5 if t == 0 else 1.0), in1=rn[:],
                                                                 op0=ALU.mult, op1=ALU.mult), reads=[r_acc, r_rn], writes=[r_fT[t]])
            if STOP < 2:
                continue
            for c4 in range(0, NCH, 4):
                for t in (1, 2):
                    for j in range(4):
                        c = c4 + j
                        P.op("pe", "transpose", dict(out=ps_tb[:, j * 128:(j + 1) * 128], in_=fT[t][:, c * 128:(c + 1) * 128], identity=identb[:]),
                             reads=[r_fT[t], r_c], writes=[r_pm])
                    src = ps_tb[:, 0:512].rearrange("p (c k) -> p c k", c=4)
                    for d in range(2):
                        col = d * NH + h
                        if t == 1:
                            P.op("dve", "tensor_tensor", dict(out=kbg[d][:, c4:c4 + 4, :], in0=src,
                                                              in1=begc[:, c4:c4 + 4, col:col + 1].to_broadcast([128, 4, 128]), op=ALU.mult),
                                 reads=[r_pm, r_g], writes=[r_tok])
                            P.op("dve", "tensor_tensor", dict(out=kdd[d][:, c4:c4 + 4, :], in0=src,
                                                              in1=ekd[:, c4:c4 + 4, col:col + 1].to_broadcast([128, 4, 128]), op=ALU.mult),
                                 reads=[r_pm, r_g], writes=[r_tok])
                        else:
                            P.op("dve", "tensor_tensor", dict(out=vbd[d][:, c4:c4 + 4, :], in0=src,
                                                              in1=beta[:, c4:c4 + 4, col:col + 1].to_broadcast([128, 4, 128]), op=ALU.mult),
                                 reads=[r_pm, r_g], writes=[r_tok])
            if STOP < 3:
                continue
            P.op("sp", "dma_start", dict(out=zt[:], in_=az[:, h * 128:(h + 1) * 128].rearrange("(c p) n -> p c n", p=128)), writes=[r_zt], dma=True)

            for d in range(2):
                P.op("pool", "memset", dict(ap=Sf[d][:], constant=0.0), writes=[r_Sf[d]])
                P.op("pool", "memset", dict(ap=Sb_[d][:], constant=0.0), writes=[r_Sb[d]])
                P.op("pool", "memset", dict(ap=Sl_[d][:], constant=0.0), writes=[r_Sb[d]])

            def precompute(d, c, par):
                col = d * NH + h
                cs = slice(c * 128, (c + 1) * 128)
                P.op("dve", "tensor_scalar", dict(out=Gb[d][par][:], in0=onesf[:], scalar1=gg[:, c, col:col + 1], scalar2=None, op0=ALU.mult),
                     reads=[r_c, r_g], writes=[r_Gb[d][par]])
                X = ps_X[d]
                P.op("pe", "matmul", dict(out=X[:, 0:128], lhsT=Gb[d][par][:], rhs=ucum[:, d, :], start=True, stop=False, skip_group_check=True),
                     reads=[r_Gb[d][par], r_c], writes=[r_pX[d]])
                P.op("pe", "matmul", dict(out=X[:, 0:128], lhsT=identf[:], rhs=nm[:, 2 * d, :], start=False, stop=True, skip_group_check=True),
                     reads=[r_c], writes=[r_pX[d]])
                P.op("pe", "matmul", dict(out=X[:, 128:256], lhsT=Gb[d][par][:], rhs=ucum[:, d, :], start=False, stop=False, skip_group_check=True),
                     reads=[r_Gb[d][par], r_c], writes=[r_pX[d]])
                P.op("pe", "matmul", dict(out=X[:, 128:256], lhsT=identf[:], rhs=nm[:, 2 * d + 1, :], start=False, stop=True, skip_group_check=True),
                     reads=[r_c], writes=[r_pX[d]])
                yield
                P.op("act", "activation", dict(out=dec[d][par][:, 0, :], in_=X[:, 0:128], func=AF.Exp, scale=-1.0, bias=gc[:, c, col:col + 1]),
                     reads=[r_pX[d], r_g], writes=[r_dec[d][par]])
                P.op("act", "activation", dict(out=dec[d][par][:, 1, :], in_=X[:, 128:256], func=AF.Exp, scale=1.0, bias=ngc[:, c, col:col + 1]),
                     reads=[r_pX[d], r_g], writes=[r_dec[d][par]])
                if SUB < 1:
                    return
                P.op("pe", "matmul", dict(out=X[:, 256:384], lhsT=fT[1][:, cs], rhs=fT[1][:, cs], start=False, stop=True, skip_group_check=True),
                     reads=[r_fT[1]], writes=[r_pX[d]])
                P.op("pe", "matmul", dict(out=X[:, 384:512], lhsT=fT[1][:, cs], rhs=fT[0][:, cs], start=False, stop=True, skip_group_check=True),
                     reads=[r_fT[1], r_fT[0]], writes=[r_pX[d]])
                yield
                P.op("dve", "scalar_tensor_tensor", dict(out=Nb[d][par][:], in0=X[:, 256:384], scalar=nbeta[:, c, col:col + 1], in1=dec[d][par][:, 0, :],
                                                         op0=ALU.mult, op1=ALU.mult), reads=[r_pX[d], r_g, r_dec[d][par]], writes=[r_N[d][par]])
                P.op("dve", "tensor_tensor", dict(out=atT[d][par][:], in0=X[:, 384:512], in1=dec[d][par][:, 1, :], op=ALU.mult),
                     reads=[r_pX[d], r_dec[d][par]], writes=[r_at[d][par]])
                yield
                if SUB < 2:
                    return
                ch = ps_ch[d]
                P.op("pe", "matmul", dict(out=ch[:, 128:256], lhsT=Nb[d][par][:], rhs=identc[:], start=True, stop=True, skip_group_check=True),
                     reads=[r_N[d][par], r_c], writes=[r_pch[d]])
                if SUB == 2 and cfg.get("sub2", 0) == 1:
                    P.op("act", "copy", dict(out=Mb[d][par][:], in_=ch[:, 128:256]), reads=[r_pch[d]], writes=[r_M[d][par]])
                    return
                P.op("pe", "matmul", dict(out=ch[:, 256:384], lhsT=identc[:], rhs=identc[:], start=False, stop=False, skip_group_check=True),
                     reads=[r_c], writes=[r_pch[d]])
                P.op("pe", "matmul", dict(out=ch[:, 256:384], lhsT=Nb[d][par][:], rhs=identc[:], start=False, stop=True, skip_group_check=True),
                     reads=[r_N[d][par], r_c], writes=[r_pch[d]])
                yield
                P.op("act", "copy", dict(out=Mb[d][par][:], in_=ch[:, 128:256]), reads=[r_pch[d]], writes=[r_M[d][par]])
                if cfg.get("sub2", 0) == 2:
                    P.op("act", "copy", dict(out=Pb[d][par][:], in_=ch[:, 256:384]), reads=[r_pch[d]], writes=[r_P[d][par]])
                else:
                    P.op("dve", "tensor_scalar", dict(scalar1=1.0, scalar2=None, op0=ALU.mult, out=Pb[d][par][:], in0=ch[:, 256:384]), reads=[r_pch[d]], writes=[r_P[d][par]])
                if SUB < 3:
                    return
                for k in range(6):
                    P.op("pe", "matmul", dict(out=ch[:, 0:128], lhsT=Mb[d][par][:], rhs=Nb[d][par][:], start=True, stop=True, skip_group_check=True),
                         reads=[r_M[d][par], r_N[d][par]], writes=[r_pch[d]])
                    if k < 5:
                        P.op("pe", "matmul", dict(out=ch[:, 128:256], lhsT=Nb[d][par][:], rhs=Mb[d][par][:], start=False, stop=True, skip_group_check=True),
                             reads=[r_M[d][par], r_N[d][par]], writes=[r_pch[d]])
                    yield
                    P.op("act", "copy", dict(out=Nb[d][par][:], in_=ch[:, 0:128]), reads=[r_pch[d]], writes=[r_N[d][par]])
                    if k < 5:
                        P.op("dve", "tensor_scalar", dict(scalar1=1.0, scalar2=None, op0=ALU.mult, out=Mb[d][par][:], in0=ch[:, 128:256]), reads=[r_pch[d]], writes=[r_M[d][par]])
                    yield
                    P.op("pe", "matmul", dict(out=ch[:, 256:384], lhsT=identc[:], rhs=Pb[d][par][:], start=False, stop=False, skip_group_check=True),
                         reads=[r_c, r_P[d][par]], writes=[r_pch[d]])
                    P.op("pe", "matmul", dict(out=ch[:, 256:384], lhsT=Nb[d][par][:], rhs=Pb[d][par][:], start=False, stop=True, skip_group_check=True),
                         reads=[r_N[d][par], r_P[d][par]], writes=[r_pch[d]])
                    yield
                    if k < 5:
                        P.op("dve", "tensor_scalar", dict(scalar1=1.0, scalar2=None, op0=ALU.mult, out=Pb[d][par][:], in0=ch[:, 256:384]), reads=[r_pch[d]], writes=[r_P[d][par]])
                    else:
                        P.op("dve", "tensor_scalar", dict(scalar1=1.0, scalar2=None, op0=ALU.mult, out=TT[d][par][:], in0=ch[:, 256:384]), reads=[r_pch[d]], writes=[r_TT[d][par]])
                if SUB < 4:
                    return
                yield
                P.op("pe", "matmul", dict(out=ch[:, 384:512], lhsT=kbg[d][:, c, :], rhs=TT[d][par][:], start=False, stop=True, skip_group_check=True),
                     reads=[r_tok, r_TT[d][par]], writes=[r_pch[d]])
                yield
                P.op("act", "activation", dict(out=wTn[d][par][:], in_=ch[:, 384:512], func=AF.Copy, scale=-1.0), reads=[r_pch[d]], writes=[r_w[d][par]])

            first_visit = [True] * NCH

            def scan(d, c, par):
                col = d * NH + h
                cs = slice(c * 128, (c + 1) * 128)
                sc = ps_sc[d]
                P.op("pe", "matmul", dict(out=sc[:, 0:128], lhsT=TT[d][par][:], rhs=vbd[d][:, c, :], start=True, stop=False, skip_group_check=True),
                     reads=[r_TT[d][par], r_tok], writes=[r_psc[d]])
                P.op("pe", "matmul", dict(out=sc[:, 0:128], lhsT=wTn[d][par][:], rhs=Sb_[d][:], start=False, stop=False, skip_group_check=True),
                     reads=[r_w[d][par], r_Sb[d]], writes=[r_psc[d]])
                P.op("pe", "matmul", dict(out=sc[:, 0:128], lhsT=wTn[d][par][:], rhs=Sl_[d][:], start=False, stop=True, skip_group_check=True),
                     reads=[r_w[d][par], r_Sb[d]], writes=[r_psc[d]])
                yield
                P.op("act", "copy", dict(out=vnew[d][:], in_=sc[:, 0:128]), reads=[r_psc[d]], writes=[r_vn[d]])
                P.op("dve", "tensor_tensor", dict(out=vnl[d][:], in0=sc[:, 0:128], in1=vnew[d][:], op=ALU.subtract), reads=[r_psc[d], r_vn[d]], writes=[r_vn[d]])
                yield
                P.op("pe", "matmul", dict(out=sc[:, 128:256], lhsT=fT[0][:, cs], rhs=Sb_[d][:], start=False, stop=False, skip_group_check=True),
                     reads=[r_fT[0], r_Sb[d]], writes=[r_psc[d]])
                P.op("pe", "matmul", dict(out=sc[:, 128:256], lhsT=fT[0][:, cs], rhs=Sl_[d][:], start=False, stop=True, skip_group_check=True),
                     reads=[r_fT[0], r_Sb[d]], writes=[r_psc[d]])
                for vv_ in (vnew, vnl):
                    P.op("pe", "matmul", dict(out=sc[:, 256:384], lhsT=atT[d][par][:], rhs=vv_[d][:], start=False, stop=(vv_ is vnl), skip_group_check=True),
                         reads=[r_at[d][par], r_vn[d]], writes=[r_psc[d]])
                for vv_ in (vnew, vnl):
                    P.op("pe", "matmul", dict(out=sc[:, 384:512], lhsT=kdd[d][:, c, :], rhs=vv_[d][:], start=False, stop=(vv_ is vnl), skip_group_check=True),
                         reads=[r_tok, r_vn[d]], writes=[r_psc[d]])
                yield
                P.op("act", "copy", dict(out=tmpo[d][:], in_=sc[:, 256:384]), reads=[r_psc[d]], writes=[r_to[d]])
                if first_visit[c]:
                    first_visit[c] = False
                    P.op("dve", "scalar_tensor_tensor", dict(out=oacc[:, c, :], in0=sc[:, 128:256], scalar=egc[:, c, col:col + 1], in1=tmpo[d][:],
                                                             op0=ALU.mult, op1=ALU.add), reads=[r_psc[d], r_g, r_to[d]], writes=[r_oacc[c]])
                else:
                    P.op("dve", "scalar_tensor_tensor", dict(out=tmpo2[d][:], in0=sc[:, 128:256], scalar=egc[:, c, col:col + 1], in1=tmpo[d][:],
                                                             op0=ALU.mult, op1=ALU.add), reads=[r_psc[d], r_g, r_to[d]], writes=[r_to2[d]])
                    P.op("dve", "tensor_tensor", dict(out=oacc[:, c, :], in0=oacc[:, c, :], in1=tmpo2[d][:], op=ALU.add),
                         reads=[r_to2[d], r_oacc[c]], writes=[r_oacc[c]])
                P.op("dve", "scalar_tensor_tensor", dict(out=Sf[d][:], in0=Sf[d][:], scalar=egl[:, c, col:col + 1], in1=sc[:, 384:512],
                                                         op0=ALU.mult, op1=ALU.add), reads=[r_psc[d], r_g, r_Sf[d]], writes=[r_Sf[d]])
                yield
                P.op("act", "copy", dict(out=Sb_[d][:], in_=Sf[d][:]), reads=[r_Sf[d]], writes=[r_Sb[d]])
                P.op("dve", "tensor_tensor", dict(out=Sl_[d][:], in0=Sf[d][:], in1=Sb_[d][:], op=ALU.subtract), reads=[r_Sf[d], r_Sb[d]], writes=[r_Sb[d]])

            order = [[c for c in range(NCH)], [NCH - 1 - c for c in range(NCH)]]
            def run_gens(gens):
                gens = list(gens)
                while gens:
                    alive = []
                    for g_ in gens:
                        try:
                            next(g_)
                            alive.append(g_)
                        except StopIteration:
                            pass
                    gens = alive
            run_gens([precompute(d, order[d][0], 0) for d in range(2)])
            for s in range(NCH):
                gens = []
                if s + 1 < NCH:
                    gens += [precompute(d, order[d][s + 1], (s + 1) % 2) for d in range(2)]
                gens += [scan(d, order[d][s], s % 2) for d in range(2)]
                run_gens(gens)

            allo = r_oacc
            accv = acc[:].rearrange("p (c k) -> p c k", c=NCH)
            P.op("dve", "tensor_tensor", dict(out=accv, in0=oacc[:], in1=oacc[:], op=ALU.mult), reads=allo + [r_acc], writes=[r_acc])
            P.op("dve", "tensor_reduce", dict(out=ssn[:], in_=accv, axis=AX.X, op=ALU.add), reads=[r_acc], writes=[r_ssn])
            P.op("act", "activation", dict(out=ssn[:], in_=ssn[:], func=AF.Sqrt, scale=1.0 / 128, bias=epsb[:, 0:1]), reads=[r_ssn, r_c], writes=[r_ssn])
            P.op("dve", "reciprocal", dict(out=ssn[:], in_=ssn[:]), reads=[r_ssn], writes=[r_ssn])
            P.op("dve", "tensor_tensor", dict(out=accv, in0=oacc[:], in1=ssn[:, :, None].to_broadcast([128, NCH, 128]), op=ALU.mult),
                 reads=allo + [r_ssn, r_acc], writes=[r_acc])
            P.op("dve", "tensor_tensor", dict(out=accv, in0=accv, in1=onorm_s[:, None, :].to_broadcast([128, NCH, 128]), op=ALU.mult),
                 reads=[r_c, r_acc], writes=[r_acc])
            P.op("act", "activation", dict(out=zt[:], in_=zt[:], func=AF.Silu), reads=[r_zt], writes=[r_zt])
            P.op("dve", "tensor_tensor", dict(out=ogb[:], in0=accv, in1=zt[:], op=ALU.mult), reads=[r_acc, r_zt], writes=[r_ogb])
            for c4 in range(0, NCH, 4):
                for j in range(4):
                    c = c4 + j
                    P.op("pe", "matmul", dict(out=ps_m[:, j * 128:(j + 1) * 128], lhsT=ogb[:, c, :], rhs=identb[:], start=(j == 0), stop=True, skip_group_check=True),
                         reads=[r_ogb, r_c], writes=[r_pm])
                P.op("act", "copy", dict(out=oTs[:, c4 * 128:(c4 + 4) * 128], in_=ps_m[:, 0:512]), reads=[r_pm], writes=[r_oTs])
            if FUSED:
                for th_ in range(2):
                    P.op("sp", "dma_start", dict(out=oaT[h * 2 + th_].rearrange("(q p) t -> p q t", q=4),
                                                 in_=oTs[:].rearrange("p (q h t) -> p q h t", q=4, h=2)[:, :, th_, :]), reads=[r_oTs], dma=True)
            else:
                P.op("sp", "dma_start", dict(out=oaT[h * 128:(h + 1) * 128, :], in_=oTs[:]), reads=[r_oTs], dma=True)
        P.emit()
    if FUSED:
        nc.all_engine_barrier()
    return nc

I32 = mybir.dt.int32


def relayout_emit(nc, jobs, idx_ap):
    P = Prog(nc)
    with ExitStack() as es:
        pfx = uniq()
        sb = lambda name, shape, dt: es.enter_context(nc.sbuf_tensor(pfx + name, shape, dt))
        NBUF = 6
        bf = [sb("rl_f%d" % i, [128, 1024], F32) for i in range(NBUF)]
        bb = [sb("rl_b%d" % i, [128, 1024], BF16) for i in range(NBUF)]
        rf = [Reg() for i in range(NBUF)]
        rb = [Reg() for i in range(NBUF)]
        idx_s = sb("rl_idx", [128, 8], I32)
        r_idx = Reg()
        P.op("sp", "dma_start", dict(out=idx_s[:], in_=idx_ap), writes=[r_idx], dma=True)
        kf = kb = 0
        for in_ap, ic, out_ap, dt in jobs:
            W = out_ap.shape[-1]
            if dt == F32:
                buf, rr = bf[kf % NBUF], rf[kf % NBUF]
                kf += 1
            else:
                buf, rr = bb[kb % NBUF], rb[kb % NBUF]
                kb += 1
            G_, off_ = in_ap
            assert G_.shape[1] == W
            P.op("pool", "indirect_dma_start",
                 dict(out=buf[:, 0:W], out_offset=None, in_offset=bass.IndirectOffsetOnAxis(ap=idx_s[:, ic:ic + 1], axis=0), **gat(G_, off_, 0, W)),
                 reads=[r_idx], writes=[rr], dma=True)
            P.op("sp", "dma_start", dict(out=out_ap, in_=buf[:, 0:W]), reads=[rr], dma=True)
        P.emit()
    nc.all_engine_barrier()


def allgather_emit(nc, pairs):
    rg = [[0, 1, 2, 3], [4, 5, 6, 7]]
    sem = nc.alloc_semaphore(name=uniq() + "ag")
    with nc.Block() as block:
        @block.gpsimd
        def _(g):
            for src, dst in pairs:
                g.collective_compute(kind="AllGather", op=ALU.bypass, replica_groups=rg, ins=[src.opt()], outs=[dst.opt()]).then_inc(sem)
            g.wait_ge(sem, len(pairs))
    nc.all_engine_barrier()
    nc.clear_and_free_semaphores([sem])
    nc.all_engine_barrier()


def build_fused(stop=99):
    S, D, TPC, DFF = 4096, 4096, 1024, 11008
    nc = bass.Bass("TRN2", target_bir_lowering=False)
    ein = lambda name, shape, dt=F32: nc.dram_tensor(name, shape, dt, kind="ExternalInput").ap()
    itn = lambda name, shape, dt=F32: nc.dram_tensor(name, shape, dt, kind="Internal").ap()
    KC = D // 128
    SHAPES = {}
    SHAPES["xT"] = ([D, TPC], F32)
    SHAPES["idx"] = ([128, 8], I32)
    SHAPES["nw_fin"] = ([128, KC], F32)
    SHAPES["Win0"] = ([D, 17472], F32)
    SHAPES["Wo0"] = ([3072, D], F32)
    SHAPES["Wqkv"] = ([D, 6144], F32)
    SHAPES["Wo1"] = ([4096, D], F32)
    SHAPES["ropeCb"] = ([32, TPC], F32)
    SHAPES["ropeSb"] = ([32, TPC], F32)
    SHAPES["ropePb"] = ([32, 32], F32)
    SHAPES["ropeCc"] = ([128, TPC], F32)
    SHAPES["ropeSc"] = ([128, TPC], F32)
    SHAPES["ropePc"] = ([128, 128], F32)
    SHAPES["gq"] = ([128, 1], F32)
    SHAPES["gk"] = ([128, 1], F32)
    SHAPES["cw"] = ([128, 12, 5], F32)
    SHAPES["alog"] = ([128, 8], F32)
    SHAPES["dtb"] = ([128, 8], F32)
    SHAPES["onorm"] = ([128, 128], F32)
    SHAPES["ident_in"] = ([128, 128], F32)
    SHAPES["ucum_in"] = ([128, 2, 128], F32)
    SHAPES["nm_in"] = ([128, 4, 128], F32)
    SHAPES["bmask"] = ([128, 3, 512], BF16)
    for l in range(2):
        for nm_, sh_ in (("nw_mix", [128, KC]), ("nw_ffn", [128, KC]), ("Wg", [D, DFF]), ("Wu", [D, DFF]), ("Wd", [DFF, D])):
            SHAPES["%s%d" % (nm_, l)] = (sh_, F32)

    class _Lazy(dict):
        def __missing__(self, k):
            sh_, dt_ = SHAPES[k]
            v = ein(k, sh_, dt_)
            self[k] = v
            return v
    E = _Lazy()
    outT = nc.dram_tensor("outT", [D, TPC], F32, kind="ExternalOutput").ap()

    pairs1, pairs2, pairs3, pairs4 = [], [], [], []

    def blk(name, rows, W, dt, pairs):
        src = itn("s_" + name, [rows, W], dt)
        dst = itn("g_" + name, [4 * rows, W], dt)
        pairs.append((src, dst))
        return src, dst
    A_b = [[blk("aq%d_%d" % (ps, i), 512, 512, F32, pairs1) for i in range(12)] for ps in range(2)]
    Q_b = [[blk("bq%d_%d" % (ps, i), 1024, 512, BF16, pairs1) for i in range(6)] for ps in range(2)]
    Z_b = [[blk("az%d_%d" % (ps, i), 512, 512, F32, pairs1) for i in range(4)] for ps in range(2)]
    V_b = [[blk("bv%d_%d" % (ps, i), 512, 768, BF16, pairs1) for i in range(4)] for ps in range(2)]
    AB_b = [blk("ab%d" % ps, 2048, 16, F32, pairs1) for ps in range(2)]
    OA_b = [blk("oa%d" % i, 512, 512, BF16, pairs2) for i in range(8)]
    OB_b = [blk("ob%d" % i, 512, 512, BF16, pairs2) for i in range(4)]
    CQ_b = [[blk("cq%d_%d" % (ps, i), 512, 512, BF16, pairs3) for i in range(8)] for ps in range(2)]
    CK_b = [[blk("ck%d_%d" % (ps, i), 512, 512, BF16, pairs3) for i in range(2)] for ps in range(2)]
    CV_b = [[blk("cv%d_%d" % (ps, i), 512, 256, BF16, pairs3) for i in range(4)] for ps in range(2)]
    OC_b = [blk("oc%d" % i, 512, 512, BF16, pairs4) for i in range(16)]
    aqkvT_l = itn("l_aqkvT", [1536, S])
    az_l = itn("l_az", [S, 512])
    abr_l = itn("l_abr", [S, 16])
    bqT_l = itn("l_bqT", [768, S], BF16)
    bkT_l = itn("l_bkT", [768, S], BF16)
    bv_l = itn("l_bv", [S, 768], BF16)
    x2T = itn("i_x2T", [D, TPC])

    def dst_fm1(name, i, ps):
        if name == "aqkv":
            t, head = i // 16, i % 16
            return A_b[ps][t * 4 + head % 4][0][(head // 4) * 128:(head // 4 + 1) * 128, :]
        qk, hd = i // 24, i % 24
        g, hs = hd // 8, hd % 8
        return Q_b[ps][qk * 3 + g][0][hs * 128:(hs + 1) * 128, :]

    def dst_tm1(name, i, ps, tt):
        if name == "az":
            j, hl = i // 4, i % 4
            return Z_b[ps][tt][0][j * 128:(j + 1) * 128, hl * 128:(hl + 1) * 128]
        g, hs = i // 8, i % 8
        j, hl = hs // 2, hs % 2
        return V_b[ps][tt][0][j * 128:(j + 1) * 128, (hl * 3 + g) * 128:(hl * 3 + g + 1) * 128]

    def dst_ab1(ps, g, tt):
        return AB_b[ps][0][g * 512 + tt * 128:g * 512 + (tt + 1) * 128, :]
    ab = dict(nqkv=6144, nz=2048, nab=64, nbqk=6144, nbv=3072)
    tp_build(dict(nc=nc, T_tot=TPC, inproj="ab", ab=ab, dst_fm=dst_fm1, dst_tm=dst_tm1, dst_ab=dst_ab1,
                  TD=dict(xT=E["xT"], nwb=E["nw_mix0"], Win=E["Win0"], ropeC=E["ropeCb"], ropeS=E["ropeSb"], ropeP=E["ropePb"])))
    if stop == 1:
        return nc
    allgather_emit(nc, pairs1)
    if stop == 2:
        return nc
    jobs = []
    for hl in range(4):
        for t in range(3):
            for r in range(4):
                for ps in range(2):
                    c0 = r * TPC + ps * 512
                    jobs.append(((A_b[ps][t * 4 + hl][1], r * 512), 0, aqkvT_l[(hl * 3 + t) * 128:(hl * 3 + t + 1) * 128, c0:c0 + 512], F32))
    for r in range(4):
        for ps in range(2):
            for tt in range(4):
                r0 = r * TPC + ps * 512 + tt * 128
                jobs.append(((Z_b[ps][tt][1], r * 512), 0, az_l[r0:r0 + 128, :], F32))
                jobs.append(((V_b[ps][tt][1], r * 512), 0, bv_l[r0:r0 + 128, :], BF16))
                jobs.append(((AB_b[ps][1], r * 2048 + tt * 128), 3, abr_l[r0:r0 + 128, :], F32))
    for hl in range(2):
        for g in range(3):
            for r in range(4):
                for ps in range(2):
                    c0 = r * TPC + ps * 512
                    jobs.append(((Q_b[ps][g][1], r * 1024 + hl * 128), 2, bqT_l[(hl * 3 + g) * 128:(hl * 3 + g + 1) * 128, c0:c0 + 512], BF16))
                    jobs.append(((Q_b[ps][3 + g][1], r * 1024 + hl * 128), 2, bkT_l[(hl * 3 + g) * 128:(hl * 3 + g + 1) * 128, c0:c0 + 512], BF16))
    relayout_emit(nc, jobs, E["idx"])
    if stop == 3:
        return nc
    gdn_build(dict(nc=nc, S=S, NH=4, TD=dict(aqkvT=aqkvT_l, az=az_l, abr=abr_l, cw=E["cw"], alog=E["alog"], dtb=E["dtb"], onorm=E["onorm"],
                                             ident_in=E["ident_in"], ucum_in=E["ucum_in"], nm_in=E["nm_in"], oa_q=[b_[0] for b_ in OA_b])))
    if stop == 4:
        return nc
    dil_b_build(dict(nc=nc, S=S, NHS=2, TD=dict(bqT=bqT_l, bkT=bkT_l, bv=bv_l, bmask=E["bmask"], ob_q=[b_[0] for b_ in OB_b])))
    if stop == 5:
        return nc
    allgather_emit(nc, pairs2)
    if stop == 6:
        return nc
    o_src = []
    for ps in range(2):
        lst = []
        for k in range(16):
            j, hl = k // 4, k % 4
            lst.append((OA_b[hl * 2 + ps][1], j * 512, 0))
        for kk in range(8):
            j, hl = kk // 2, kk % 2
            lst.append((OB_b[hl * 2 + ps][1], j * 512, 0))
        o_src.append(lst)

    def dst_fm3(name, i, ps):
        if i < 32:
            return CQ_b[ps][i % 8][0][(i // 8) * 128:(i // 8 + 1) * 128, :]
        kvh = i - 32
        return CK_b[ps][kvh % 2][0][(kvh // 2) * 128:(kvh // 2 + 1) * 128, :]

    def dst_tm3(name, i, ps, tt):
        return CV_b[ps][tt][0][(i // 2) * 128:(i // 2 + 1) * 128, (i % 2) * 128:(i % 2 + 1) * 128]
    cc = dict(nq=4096, nk=1024, nv=1024)
    tp_build(dict(nc=nc, T_tot=TPC, oproj_K=3072, dff=DFF, inproj="c", c=cc, store_x=True, o_src=o_src, dst_fm=dst_fm3, dst_tm=dst_tm3,
                  TD=dict(xT=E["xT"], idx=E["idx"], Wo=E["Wo0"], nwa=E["nw_ffn0"], Wg=E["Wg0"], Wu=E["Wu0"], Wd=E["Wd0"], nwb=E["nw_mix1"],
                          Win=E["Wqkv"], ropeC=E["ropeCc"], ropeS=E["ropeSc"], ropeP=E["ropePc"], gq=E["gq"], gk=E["gk"], xoT=x2T)))
    if stop == 7:
        return nc
    allgather_emit(nc, pairs3)
    if stop == 8:
        return nc
    attn_c_build(dict(nc=nc, S=S, NKV=2, REP=4, TD=dict(GQ=[[b_[1] for b_ in CQ_b[ps]] for ps in range(2)], GK=[[b_[1] for b_ in CK_b[ps]] for ps in range(2)],
                                                          GV=[[b_[1] for b_ in CV_b[ps]] for ps in range(2)], OCB=[b_[0] for b_ in OC_b], idx=E["idx"])))
    if stop == 9:
        return nc
    allgather_emit(nc, pairs4)
    if stop == 10:
        return nc
    o_src = []
    for ps in range(2):
        o_src.append([(OC_b[(k % 8) * 2 + ps][1], (k // 8) * 512, 0) for k in range(32)])
    tp_build(dict(nc=nc, T_tot=TPC, oproj_K=4096, dff=DFF, final_norm=True, o_src=o_src,
                  TD=dict(xT=x2T, idx=E["idx"], Wo=E["Wo1"], nwa=E["nw_ffn1"], Wg=E["Wg1"], Wu=E["Wu1"], Wd=E["Wd1"], nwb=E["nw_fin"], outT=outT)))
    return nc

import ml_dtypes as _mld

_BF = _mld.bfloat16
_PROGS = {}


def _lay(w):
    return np.ascontiguousarray(np.asarray(w, np.float32).reshape(-1, 128).T)


def _rope_tables_b(pos):
    inv = 1.0 / (500000.0 ** (np.arange(0, 32, 2, dtype=np.float32) / 32))
    ang = pos.astype(np.float32)[:, None] * inv[None, :]
    c, s = np.cos(ang), np.sin(ang)
    C = np.ascontiguousarray(np.concatenate([c, c], 1).T.astype(np.float32))
    S = np.ascontiguousarray(np.concatenate([s, s], 1).T.astype(np.float32))
    Pm = np.zeros((32, 32), np.float32)
    for i in range(16):
        Pm[i, 16 + i] = -1
        Pm[16 + i, i] = 1
    return C, S, np.ascontiguousarray(Pm.T)


def _rope_tables_c(pos):
    inv = 1.0 / (10000.0 ** (np.arange(0, 64, 2, dtype=np.float32) / 64))
    ar = (pos // 64).astype(np.float32)[:, None] * inv[None, :]
    ac = (pos % 64).astype(np.float32)[:, None] * inv[None, :]
    C = np.ascontiguousarray(np.concatenate([np.cos(ar), np.cos(ar), np.cos(ac), np.cos(ac)], 1).T.astype(np.float32))
    S = np.ascontiguousarray(np.concatenate([np.sin(ar), np.sin(ar), np.sin(ac), np.sin(ac)], 1).T.astype(np.float32))
    Pm = np.zeros((128, 128), np.float32)
    for i in range(32):
        Pm[i, 32 + i] = -1
        Pm[32 + i, i] = 1
        Pm[64 + i, 96 + i] = -1
        Pm[96 + i, 64 + i] = 1
    return C, S, np.ascontiguousarray(Pm.T)


def kernel(x, norm_mix, norm_ffn, norm_final, ab_w_in, ab_conv_w, ab_a_log, ab_dt_bias,
           ab_out_norm, ab_w_out, c_w_qkv, c_q_norm, c_k_norm, c_w_out,
           ffn_w_gate, ffn_w_up, ffn_w_down):
    f32 = np.float32
    A = lambda a: np.ascontiguousarray(np.asarray(a, f32))
    x = A(x)
    B, S, D = x.shape
    NCORE = 8
    TPC = B * S // NCORE
    QPB = S // TPC
    xf = x.reshape(B * S, D)
    if "F" not in _PROGS:
        _PROGS["F"] = build_fused()
    nc = _PROGS["F"]
    cst = gdn_consts()
    conv_w = A(ab_conv_w[0])
    a_log = A(ab_a_log[0])
    dt_b = A(ab_dt_bias[0])
    shared = dict(
        nw_mix0=_lay(norm_mix[0]), nw_mix1=_lay(norm_mix[1]), nw_ffn0=_lay(norm_ffn[0]), nw_ffn1=_lay(norm_ffn[1]), nw_fin=_lay(norm_final),
        Wg0=A(ffn_w_gate[0]), Wu0=A(ffn_w_up[0]), Wd0=A(ffn_w_down[0]), Wg1=A(ffn_w_gate[1]), Wu1=A(ffn_w_up[1]), Wd1=A(ffn_w_down[1]),
        Win0=A(ab_w_in[0]), Wo0=A(ab_w_out[0]), Wqkv=A(c_w_qkv[0]), Wo1=A(c_w_out[0]),
        gq=A(c_q_norm[0]).reshape(128, 1), gk=A(c_k_norm[0]).reshape(128, 1),
        onorm=np.ascontiguousarray(np.tile(A(ab_out_norm[0])[None, :], (128, 1))),
        ident_in=cst["ident"], ucum_in=cst["ucum"], nm_in=cst["nm"], bmask=dil_mask().astype(_BF))
    ims = []
    p = np.arange(128, dtype=np.int32)
    for c in range(NCORE):
        rk = c % QPB
        pos = np.arange(rk * TPC, (rk + 1) * TPC)
        Cb, Sb, Pb = _rope_tables_b(pos)
        Cc, Sc, Pc = _rope_tables_c(pos)
        heads = [4 * rk + i for i in range(4)]
        cwl = np.zeros((128, 12, 5), f32)
        for i, h in enumerate(heads):
            for t in range(3):
                cwl[:, i * 3 + t, :] = conv_w[:, t * 2048 + h * 128: t * 2048 + (h + 1) * 128].T
        al = np.array([a_log[d, h] for d in range(2) for h in heads], f32)
        db = np.array([dt_b[d, h] for d in range(2) for h in heads], f32)
        idx = np.stack([rk * 128 + p, 2 * (rk * 128 + p), rk * 256 + p, rk * 512 + p, p, p, p, p], 1).astype(np.int32)
        im = dict(shared)
        im.update(xT=np.ascontiguousarray(xf[c * TPC:(c + 1) * TPC].T), idx=np.ascontiguousarray(idx),
                  ropeCb=Cb, ropeSb=Sb, ropePb=Pb, ropeCc=Cc, ropeSc=Sc, ropePc=Pc, cw=cwl,
                  alog=np.ascontiguousarray(np.tile(al[None, :], (128, 1))), dtb=np.ascontiguousarray(np.tile(db[None, :], (128, 1))))
        ims.append(im)
    res = run_bass_kernel_spmd(nc, ims, core_ids=list(range(NCORE))).results
    out = np.concatenate([np.asarray(res[c]["outT"]).T for c in range(NCORE)], 0)
    return np.ascontiguousarray(out.reshape(B, S, D).astype(f32, copy=False))
```

```python
from contextlib import ExitStack
import numpy as np
import concourse.bass as bass
import concourse.mybir as mybir
from concourse.bass_utils import run_bass_kernel_spmd

F32 = mybir.dt.float32
BF16 = mybir.dt.bfloat16
AF = mybir.ActivationFunctionType
ALU = mybir.AluOpType
AX = mybir.AxisListType

ENGS = ("pe", "act", "dve", "pool", "sp")
_UNIQ = [0]


def uniq():
    _UNIQ[0] += 1
    return "u%d_" % _UNIQ[0]
NS_DMA = 6
SAME_ENGINE_SYNC = True


class Reg:
    __slots__ = ("name", "w", "r", "rd", "psum")

    def __init__(self, name="", psum=False):
        self.name = name
        self.psum = psum
        self.w = None
        self.r = {}
        self.rd = []


class Ins:
    __slots__ = ("eng", "fn", "deps", "inc", "dma", "idx", "sem", "target", "cnt", "cc")


class Prog:
    def __init__(self, nc):
        self.nc = nc
        self.q = {e: [] for e in ENGS}
        self.dmas = {e: [] for e in ENGS}
        self.ccs = []
        self.nshared = 0

    def op(self, eng, meth, kw, reads=(), writes=(), dma=False, cc=False):
        ins = Ins()
        ins.cc = cc
        if cc == "shared":
            dma = True
            self.nshared += 1
        elif cc:
            dma = True
            self.ccs.append(ins)
        ins.eng = eng
        ins.fn = (meth, kw)
        ins.dma = dma
        ins.inc = dma
        ins.idx = len(self.q[eng])
        ins.cnt = 0
        deps = set()
        for r in reads:
            if r.w is not None:
                deps.add(r.w)
            if r.psum:
                for e2, i2 in r.r.items():
                    if e2 != eng:
                        deps.add(i2)
        for w in writes:
            if w.w is not None:
                deps.add(w.w)
            deps.update(w.r.values())
            deps.update(w.rd)
        for r in reads:
            if dma:
                r.rd.append(ins)
            else:
                r.r[eng] = ins
        for w in writes:
            w.w = ins
            w.r = {}
            w.rd = []
        if dma and not cc:
            lst = self.dmas[eng]
            if len(lst) >= NS_DMA:
                deps.add(lst[len(lst) - NS_DMA])
            lst.append(ins)
        deps.discard(ins)
        fin = []
        for d in deps:
            if d.eng == eng and not d.dma:
                if eng == "pe" or not SAME_ENGINE_SYNC:
                    continue
            d.inc = True
            fin.append(d)
        ins.deps = fin
        self.q[eng].append(ins)
        return ins

    def emit(self, final_waits=()):
        nc = self.nc
        allsems = []

        def newsem(name):
            h = nc.alloc_semaphore(name=name)
            allsems.append(h)
            return h
        with ExitStack() as es:
            pf = uniq()
            csem = {e: newsem(pf + "c_" + e) for e in ENGS}
            dsem = {e: [newsem(pf + "d_%s%d" % (e, i)) for i in range(NS_DMA)]
                    for e in ENGS if self.dmas[e]}
            es_cc = []
            shared_sem = [newsem(pf + "ccs")] if self.nshared else [None]
            for e in ENGS:
                c = 0
                k = 0
                for ins in self.q[e]:
                    if ins.cc == "shared":
                        ins.sem = shared_sem[0]
                        ins.target = 0
                    elif ins.cc:
                        ins.sem = newsem(pf + "cc%d" % len(es_cc))
                        es_cc.append(ins)
                        ins.target = 1
                    elif ins.dma:
                        ins.sem = dsem[e][k % NS_DMA]
                        ins.target = 16 * (k // NS_DMA + 1)
                        k += 1
                    elif ins.inc:
                        c += 1
                        ins.cnt = c
            block = es.enter_context(nc.Block())

            def run(e, eng):
                waited = {}
                nsh = [0]
                for ins in self.q[e]:
                    need = {}
                    for d in ins.deps:
                        if d.dma:
                            key = d.sem
                            val = d.target
                        else:
                            key = csem[d.eng]
                            val = d.cnt
                        if need.get(key, 0) < val:
                            need[key] = val
                    for key, val in need.items():
                        if waited.get(key, 0) < val:
                            eng.wait_ge(key, val)
                            waited[key] = val
                    inst = getattr(eng, ins.fn[0])(**ins.fn[1])
                    if ins.cc:
                        inst.then_inc(ins.sem)
                        if ins.cc == "shared":
                            nsh[0] += 1
                    elif ins.dma:
                        inst.then_inc(ins.sem, 16)
                    elif ins.inc:
                        inst.then_inc(csem[e], 1)
                if nsh[0]:
                    eng.wait_ge(shared_sem[0], nsh[0])
                for d in self.ccs:
                    if d.eng == e and waited.get(d.sem, 0) < d.target:
                        eng.wait_ge(d.sem, d.target)
                        waited[d.sem] = d.target
                if self.dmas[e]:
                    lst = self.dmas[e]
                    for d in lst[-NS_DMA:]:
                        if waited.get(d.sem, 0) < d.target:
                            eng.wait_ge(d.sem, d.target)
                            waited[d.sem] = d.target

            @block.tensor
            def _(eng):
                run("pe", eng)

            @block.scalar
            def _(eng):
                run("act", eng)

            @block.vector
            def _(eng):
                run("dve", eng)

            @block.gpsimd
            def _(eng):
                run("pool", eng)

            @block.sync
            def _(eng):
                run("sp", eng)
        nc.all_engine_barrier()
        nc.clear_and_free_semaphores(allsems)
        nc.all_engine_barrier()


def gat(G, row_off, col_off, W):
    full = G.shape[1]
    A = full // W
    v = G if A == 1 else G.rearrange("r (a w) -> (r a) w", w=W)
    return dict(in_=v, element_offset=row_off * full + col_off)


def tp_build(cfg):
    D = cfg.get("D", 4096)
    KC = D // 128
    T = cfg.get("T", 512)
    T_tot = cfg["T_tot"]
    NTT = T // 128
    oK = cfg.get("oproj_K", 0)
    dff = cfg.get("dff", 0)
    inproj = cfg.get("inproj")
    final_norm = cfg.get("final_norm", False)
    store_x = cfg.get("store_x", False)
    FB = 8

    TT_ = cfg.get("TD")
    FUSED = TT_ is not None
    nc = cfg["nc"] if FUSED else bass.Bass("TRN2", target_bir_lowering=False)
    if FUSED:
        din = lambda name, shape, dt=F32: TT_[name]
        dout = lambda name, shape, dt=F32: TT_[name]
    else:
        din = lambda name, shape, dt=F32: nc.dram_tensor(name, shape, dt, kind="ExternalInput").ap()
        dout = lambda name, shape, dt=F32: nc.dram_tensor(name, shape, dt, kind="ExternalOutput").ap()
    xT = din("xT", [D, T_tot])
    if oK:
        if not FUSED:
            oT = din("oT", [oK, T_tot], BF16)
        Wo = din("Wo", [oK, D])
    if dff:
        nwa = din("nwa", [128, KC])
        Wg = din("Wg", [D, dff])
        Wu = din("Wu", [D, dff])
        Wd = din("Wd", [dff, D])
    if inproj or final_norm:
        nwb = din("nwb", [128, KC])
    if store_x:
        xoT = dout("xoT", [D, T_tot])
    if final_norm:
        outT = dout("outT", [D, T_tot])
    if inproj == "ab":
        ab = cfg["ab"]
        ncols = ab["nqkv"] + ab["nz"] + ab["nab"] + ab["nbqk"] + ab["nbv"]
        Win = din("Win", [D, ncols])
        ropeC = din("ropeC", [32, T_tot])
        ropeS = din("ropeS", [32, T_tot])
        ropeP = din("ropeP", [32, 32])
        if FUSED:
            aqkvT = az = abo = bqkT = bv = None
        else:
            aqkvT = dout("aqkvT", [ab["nqkv"], T_tot])
            az = dout("az", [T_tot, ab["nz"]])
            abo = dout("ab", [T_tot, ab["nab"]])
            bqkT = dout("bqkT", [ab["nbqk"], T_tot], BF16)
            bv = dout("bv", [T_tot, ab["nbv"]], BF16)
    if inproj == "c":
        cc = cfg["c"]
        ncols = cc["nq"] + cc["nk"] + cc["nv"]
        Win = din("Win", [D, ncols])
        ropeC = din("ropeC", [128, T_tot])
        ropeS = din("ropeS", [128, T_tot])
        ropeP = din("ropeP", [128, 128])
        gq = din("gq", [128, 1])
        gk = din("gk", [128, 1])
        if FUSED:
            cqkT = cv = None
        else:
            cqkT = dout("cqkT", [cc["nq"] + cc["nk"], T_tot], BF16)
            cv = dout("cv", [T_tot, cc["nv"]], BF16)

    P = Prog(nc)
    with ExitStack() as es:
        es.enter_context(nc.allow_low_precision("bf16 matmul operands, fp32 accumulate"))
        pfx = uniq()
        sb = lambda name, shape, dt: es.enter_context(nc.sbuf_tensor(pfx + name, shape, dt))
        psb = lambda name: es.enter_context(nc.psum_tensor(pfx + name, [128, 512], F32))
        R = Reg
        xs = sb("xs", [128, KC, T], F32)
        hT = sb("hT", [128, KC, T], BF16)
        ones = sb("ones", [128, 128], BF16)
        epsb = sb("epsb", [128, 1], F32)
        sq = [sb("sq%d" % i, [128, T], BF16) for i in range(2)]
        rstd = sb("rstd", [128, T], F32)
        st = [sb("st%d" % i, [128, T], F32) for i in range(2)]
        stb = [sb("stb%d" % i, [128, T], BF16) for i in range(2)]
        aT = [sb("aT%d" % i, [128, FB, T], BF16) for i in range(2)]
        wA = [sb("wA%d" % i, [128, KC, 128], BF16) for i in range(4)]
        wd = [sb("wd%d" % i, [128, FB, 512], BF16) for i in range(2)]
        nwa_s = sb("nwa_s", [128, KC], F32)
        nwb_s = sb("nwb_s", [128, KC], F32)
        ps_ss = psb("ps_ss")
        psA = [psb("psA%d" % i) for i in range(4)]
        ps_y = [psb("ps_y%d" % i) for i in range(2)]
        r_xs = [R() for c in range(KC)]
        r_hT = [R() for c in range(KC)]
        r_ones, r_eps, r_rstd, r_ss, r_nwa, r_nwb = R(), R(), R(), R(psum=True), R(), R()
        r_sq = [R(), R()]
        r_st = [R(), R()]
        r_stb = [R(), R()]
        r_aT = [[R() for j in range(FB)] for i in range(2)]
        r_wA = [R() for i in range(4)]
        r_wd = [R(), R()]
        r_psA = [R(psum=True) for i in range(4)]
        r_py = [R(psum=True), R(psum=True)]
        cnt = dict(wA=0, wd=0, psA=0, py=0, st=0, stb=0)

        def nxt(k, n):
            v = cnt[k] % n
            cnt[k] += 1
            return v

        P.op("pool", "memset", dict(ap=ones[:], constant=1.0), writes=[r_ones])
        P.op("pool", "memset", dict(ap=epsb[:], constant=1e-6), writes=[r_eps])
        if dff:
            P.op("sp", "dma_start", dict(out=nwa_s[:], in_=nwa), writes=[r_nwa], dma=True)
        if inproj or final_norm:
            P.op("sp", "dma_start", dict(out=nwb_s[:], in_=nwb), writes=[r_nwb], dma=True)
        if inproj == "ab":
            rC = sb("rC", [32, T_tot], F32)
            rS = sb("rS", [32, T_tot], F32)
            rP = sb("rP", [32, 32], BF16)
            t1 = sb("t1", [32, T], F32)
            t2 = sb("t2", [32, T], F32)
            r_rope, r_t1, r_t2 = R(), R(), R()
            P.op("sp", "dma_start", dict(out=rC[:], in_=ropeC), writes=[r_rope], dma=True)
            P.op("sp", "dma_start", dict(out=rS[:], in_=ropeS), writes=[r_rope], dma=True)
            P.op("pool", "dma_start", dict(out=rP[:], in_=ropeP), writes=[r_rope], dma=True)
        if inproj == "c":
            rC = sb("rC", [128, T_tot], F32)
            rS = sb("rS", [128, T_tot], F32)
            rP = sb("rP", [128, 128], BF16)
            gq_s = sb("gq_s", [128, 1], F32)
            gk_s = sb("gk_s", [128, 1], F32)
            t1 = sb("t1", [128, T], F32)
            t2 = sb("t2", [128, T], F32)
            rn = sb("rn", [128, T], F32)
            r_rope, r_t1, r_t2, r_rn = R(), R(), R(), R()
            P.op("sp", "dma_start", dict(out=rC[:], in_=ropeC), writes=[r_rope], dma=True)
            P.op("sp", "dma_start", dict(out=rS[:], in_=ropeS), writes=[r_rope], dma=True)
            P.op("pool", "dma_start", dict(out=rP[:], in_=ropeP), writes=[r_rope], dma=True)
            P.op("sp", "dma_start", dict(out=gq_s[:], in_=gq), writes=[r_rope], dma=True)
            P.op("sp", "dma_start", dict(out=gk_s[:], in_=gk), writes=[r_rope], dma=True)

        xT_v = xT.rearrange("(c p) t -> p c t", p=128)
        if FUSED and oK:
            idx_s = sb("idx_s", [128, 8], mybir.dt.int32)
            r_idx = R()
            P.op("sp", "dma_start", dict(out=idx_s[:], in_=TT_["idx"]), writes=[r_idx], dma=True)

        def rmsnorm(nw_s, r_nw, to_x=False):
            for c in range(KC):
                s = c % 2
                P.op("act", "activation", dict(out=sq[s][:], in_=xs[:, c, :], func=AF.Square),
                     reads=[r_xs[c]], writes=[r_sq[s]])
                P.op("pe", "matmul", dict(out=ps_ss[:], lhsT=ones[:], rhs=sq[s][:], start=(c == 0), stop=(c == KC - 1)),
                     reads=[r_ones, r_sq[s]], writes=[r_ss])
            P.op("act", "activation", dict(out=rstd[:], in_=ps_ss[:], func=AF.Sqrt, scale=1.0 / D, bias=epsb[:, 0:1]),
                 reads=[r_ss, r_eps], writes=[r_rstd])
            P.op("dve", "reciprocal", dict(out=rstd[:], in_=rstd[:]), reads=[r_rstd], writes=[r_rstd])
            for c in range(KC):
                if to_x:
                    P.op("dve", "scalar_tensor_tensor",
                         dict(out=xs[:, c, :], in0=xs[:, c, :], scalar=nw_s[:, c:c + 1], in1=rstd[:], op0=ALU.mult, op1=ALU.mult),
                         reads=[r_xs[c], r_nw, r_rstd], writes=[r_xs[c]])
                else:
                    P.op("dve", "scalar_tensor_tensor",
                         dict(out=hT[:, c, :], in0=xs[:, c, :], scalar=nw_s[:, c:c + 1], in1=rstd[:], op0=ALU.mult, op1=ALU.mult),
                         reads=[r_xs[c], r_nw, r_rstd], writes=[r_hT[c]])

        def accum_block(W_v, k0, nk, sl, rhs_regs):
            for cb in range(D // 512):
                w = nxt("wd", 2)
                P.op("pool", "dma_start", dict(out=wd[w][:, 0:nk, :], in_=W_v[:, k0:k0 + nk, cb * 512:(cb + 1) * 512]),
                     writes=[r_wd[w]], dma=True)
                for ct in range(4):
                    pb = nxt("py", 2)
                    for j in range(nk):
                        P.op("pe", "matmul", dict(out=ps_y[pb][:], lhsT=wd[w][:, j, ct * 128:(ct + 1) * 128], rhs=aT[sl][:, j, :],
                                                  start=(j == 0), stop=(j == nk - 1)),
                             reads=[r_wd[w], rhs_regs[j]], writes=[r_py[pb]])
                    c = cb * 4 + ct
                    P.op("dve", "tensor_tensor", dict(out=xs[:, c, :], in0=xs[:, c, :], in1=ps_y[pb][:], op=ALU.add),
                         reads=[r_xs[c], r_py[pb]], writes=[r_xs[c]])

        def fm_tile(W_v, col0, epilogue):
            w = nxt("wA", 4)
            P.op("pool", "dma_start", dict(out=wA[w][:], in_=W_v[:, :, col0:col0 + 128]), writes=[r_wA[w]], dma=True)
            pb = nxt("psA", 4)
            for kc in range(KC):
                P.op("pe", "matmul", dict(out=psA[pb][:], lhsT=wA[w][:, kc, :], rhs=hT[:, kc, :], start=(kc == 0), stop=(kc == KC - 1)),
                     reads=[r_wA[w], r_hT[kc]], writes=[r_psA[pb]])
            epilogue(psA[pb], r_psA[pb])

        def tm_tile(W_v, col0, ncl, out_ap, t0, dt, ocol, ab_special=False, fname=None, ti=0):
            w = nxt("wA", 4)
            P.op("pool", "dma_start", dict(out=wA[w][:, :, 0:ncl], in_=W_v[:, :, col0:col0 + ncl]), writes=[r_wA[w]], dma=True)
            pb = nxt("psA", 4)
            for tt in range(NTT):
                for kc in range(KC):
                    P.op("pe", "matmul", dict(out=psA[pb][:, tt * 128:tt * 128 + ncl], lhsT=hT[:, kc, tt * 128:(tt + 1) * 128],
                                              rhs=wA[w][:, kc, 0:ncl], start=(kc == 0), stop=(kc == KC - 1)),
                         reads=[r_wA[w], r_hT[kc]], writes=[r_psA[pb]])
            src = psA[pb][:].rearrange("p (t c) -> p t c", c=128)[:, :, 0:ncl]
            if dt == F32:
                s = nxt("st", 2)
                buf, rb = st[s], r_st[s]
            else:
                s = nxt("stb", 2)
                buf, rb = stb[s], r_stb[s]
            dst = buf[:].rearrange("p (t c) -> p t c", c=128)[:, :, 0:ncl]
            P.op("act", "copy", dict(out=dst, in_=src), reads=[r_psA[pb]], writes=[rb])
            if ab_special:
                for g_ in range(4):
                    for tt_ in range(NTT):
                        srcv = dst.rearrange("p t (x g h) -> p t x g h", x=4, g=4)[:, tt_, :, g_, :]
                        dstv = cfg["dst_ab"](t0 // T, g_, tt_).rearrange("p (x h) -> p x h", x=4)
                        P.op("sp", "dma_start", dict(out=dstv, in_=srcv), reads=[rb] + XB, dma=True)
                return
            if FUSED:
                for tt_ in range(NTT):
                    P.op("sp", "dma_start", dict(out=cfg["dst_tm"](fname, ti, t0 // T, tt_), in_=dst[:, tt_, :]), reads=[rb] + XB, dma=True)
                return
            P.op("sp", "dma_start", dict(out=out_ap[t0:t0 + T, ocol:ocol + ncl].rearrange("(t p) c -> p t c", p=128), in_=dst),
                 reads=[rb], dma=True)

        for t0 in range(0, T_tot, T):
            r_xb = R()
            XB = [r_xb] if FUSED else []
            for c in range(KC):
                P.op("sp", "dma_start", dict(out=xs[:, c, :], in_=xT_v[:, c, t0:t0 + T]), writes=[r_xs[c]], dma=True)
            if oK:
                if not FUSED:
                    oT_v = oT.rearrange("(c p) t -> p c t", p=128)
                Wo_v = Wo.rearrange("(c p) n -> p c n", p=128)
                nob = oK // 128
                blocks = [(k0, min(k0 + FB, nob)) for k0 in range(0, nob, FB)]
                for bi, (k0, k1) in enumerate(blocks):
                    sl = bi % 2
                    for j in range(k1 - k0):
                        if FUSED:
                            G_, off_, ic_ = cfg["o_src"][t0 // T][k0 + j]
                            P.op("pool", "indirect_dma_start",
                                 dict(out=aT[sl][:, j, :], out_offset=None,
                                      in_offset=bass.IndirectOffsetOnAxis(ap=idx_s[:, ic_:ic_ + 1], axis=0), **gat(G_, off_, 0, T)),
                                 reads=[r_idx], writes=[r_aT[sl][j]], dma=True)
                        else:
                            P.op("sp", "dma_start", dict(out=aT[sl][:, j, :], in_=oT_v[:, k0 + j, t0:t0 + T]),
                                 writes=[r_aT[sl][j]], dma=True)
                    accum_block(Wo_v, k0, k1 - k0, sl, r_aT[sl])
            if dff:
                Wg_v = Wg.rearrange("(c p) n -> p c n", p=128)
                Wu_v = Wu.rearrange("(c p) n -> p c n", p=128)
                Wd_v = Wd.rearrange("(c p) n -> p c n", p=128)
                rmsnorm(nwa_s, r_nwa)
                NF = dff // 128
                blocks = [(f0, min(f0 + FB, NF)) for f0 in range(0, NF, FB)]

                def gate_up(bi):
                    f0, f1 = blocks[bi]
                    sl = bi % 2
                    for f in range(f0, f1):
                        res = []

                        def keep(ps_t, r_t):
                            res.append((ps_t, r_t))
                        fm_tile(Wg_v, f * 128, keep)
                        fm_tile(Wu_v, f * 128, keep)
                        (pg, rg), (pu, ru) = res
                        s = nxt("st", 2)
                        P.op("act", "activation", dict(out=st[s][:], in_=pg[:], func=AF.Silu), reads=[rg], writes=[r_st[s]])
                        P.op("dve", "tensor_tensor", dict(out=aT[sl][:, f - f0, :], in0=st[s][:], in1=pu[:], op=ALU.mult),
                             reads=[r_st[s], ru], writes=[r_aT[sl][f - f0]])

                gate_up(0)
                for bi in range(1, len(blocks)):
                    gate_up(bi)
                    f0, f1 = blocks[bi - 1]
                    accum_block(Wd_v, f0, f1 - f0, (bi - 1) % 2, r_aT[(bi - 1) % 2])
                f0, f1 = blocks[-1]
                accum_block(Wd_v, f0, f1 - f0, (len(blocks) - 1) % 2, r_aT[(len(blocks) - 1) % 2])
            if store_x:
                xo_v = xoT.rearrange("(c p) t -> p c t", p=128)
                for c in range(KC):
                    P.op("sp", "dma_start", dict(out=xo_v[:, c, t0:t0 + T], in_=xs[:, c, :]), reads=[r_xs[c]], dma=True)
            if inproj:
                Win_v = Win.rearrange("(c p) n -> p c n", p=128)
                rmsnorm(nwb_s, r_nwb)

                def ep_copy(out_ap, row0, dt, fname=None, ti=0):
                    def ep(ps_t, r_t):
                        if dt == F32:
                            s = nxt("st", 2)
                            buf, rb = st[s], r_st[s]
                        else:
                            s = nxt("stb", 2)
                            buf, rb = stb[s], r_stb[s]
                        P.op("act", "copy", dict(out=buf[:], in_=ps_t[:]), reads=[r_t], writes=[rb])
                        dst_ = cfg["dst_fm"](fname, ti, t0 // T) if FUSED else out_ap[row0:row0 + 128, t0:t0 + T]
                        P.op("sp", "dma_start", dict(out=dst_, in_=buf[:]), reads=[rb] + XB, dma=True)
                    return ep

            if inproj == "ab":
                def ep_rope_b(row0, ti=0):
                    def ep(ps_t, r_t):
                        s = nxt("stb", 2)
                        buf, rb = stb[s], r_stb[s]
                        P.op("act", "copy", dict(out=buf[:], in_=ps_t[:]), reads=[r_t], writes=[rb])
                        pb = nxt("py", 2)
                        P.op("pe", "matmul", dict(out=ps_y[pb][0:32, :], lhsT=rP[:, :], rhs=buf[0:32, :], start=True, stop=True),
                             reads=[r_rope, rb], writes=[r_py[pb]])
                        P.op("dve", "tensor_tensor", dict(out=t1[:], in0=buf[0:32, :], in1=rC[:, t0:t0 + T], op=ALU.mult),
                             reads=[rb, r_rope], writes=[r_t1])
                        P.op("dve", "tensor_tensor", dict(out=t2[:], in0=ps_y[pb][0:32, :], in1=rS[:, t0:t0 + T], op=ALU.mult),
                             reads=[r_py[pb], r_rope], writes=[r_t2])
                        P.op("dve", "tensor_tensor", dict(out=buf[0:32, :], in0=t1[:], in1=t2[:], op=ALU.add),
                             reads=[r_t1, r_t2], writes=[rb])
                        dst_ = cfg["dst_fm"]("bqk", ti, t0 // T) if FUSED else bqkT[row0:row0 + 128, t0:t0 + T]
                        P.op("sp", "dma_start", dict(out=dst_, in_=buf[:]), reads=[rb] + XB, dma=True)
                    return ep
                c0 = 0
                for i in range(ab["nqkv"] // 128):
                    fm_tile(Win_v, c0 + i * 128, ep_copy(aqkvT, i * 128, F32, "aqkv", i))
                c0 += ab["nqkv"]
                for i in range(ab["nz"] // 128):
                    if FUSED:
                        tm_tile(Win_v, c0 + i * 128, 128, None, t0, F32, 0, fname="az", ti=i)
                    else:
                        tm_tile(Win_v, c0 + i * 128, 128, az, t0, F32, i * 128)
                c0 += ab["nz"]
                tm_tile(Win_v, c0, ab["nab"], abo, t0, F32, 0, ab_special=FUSED)
                c0 += ab["nab"]
                for i in range(ab["nbqk"] // 128):
                    fm_tile(Win_v, c0 + i * 128, ep_rope_b(i * 128, i))
                c0 += ab["nbqk"]
                for i in range(ab["nbv"] // 128):
                    if FUSED:
                        tm_tile(Win_v, c0 + i * 128, 128, None, t0, BF16, 0, fname="bv", ti=i)
                    else:
                        tm_tile(Win_v, c0 + i * 128, 128, bv, t0, BF16, i * 128)
            if inproj == "c":
                def ep_c(row0, g_s, ti=0):
                    def ep(ps_t, r_t):
                        s = nxt("stb", 2)
                        buf, rb = stb[s], r_stb[s]
                        P.op("act", "activation", dict(out=sq[0][:], in_=ps_t[:], func=AF.Square), reads=[r_t], writes=[r_sq[0]])
                        pb = nxt("py", 2)
                        P.op("pe", "matmul", dict(out=ps_y[pb][:], lhsT=ones[:], rhs=sq[0][:], start=True, stop=True),
                             reads=[r_ones, r_sq[0]], writes=[r_py[pb]])
                        P.op("act", "activation", dict(out=rn[:], in_=ps_y[pb][:], func=AF.Sqrt, scale=1.0 / 128, bias=epsb[:, 0:1]),
                             reads=[r_py[pb], r_eps], writes=[r_rn])
                        P.op("dve", "reciprocal", dict(out=rn[:], in_=rn[:]), reads=[r_rn], writes=[r_rn])
                        P.op("dve", "scalar_tensor_tensor",
                             dict(out=buf[:], in0=ps_t[:], scalar=g_s[:, 0:1], in1=rn[:], op0=ALU.mult, op1=ALU.mult),
                             reads=[r_t, r_rope, r_rn], writes=[rb])
                        pb2 = nxt("py", 2)
                        P.op("pe", "matmul", dict(out=ps_y[pb2][:], lhsT=rP[:, :], rhs=buf[:], start=True, stop=True),
                             reads=[r_rope, rb], writes=[r_py[pb2]])
                        P.op("dve", "tensor_tensor", dict(out=t1[:], in0=buf[:], in1=rC[:, t0:t0 + T], op=ALU.mult),
                             reads=[rb, r_rope], writes=[r_t1])
                        P.op("dve", "tensor_tensor", dict(out=t2[:], in0=ps_y[pb2][:], in1=rS[:, t0:t0 + T], op=ALU.mult),
                             reads=[r_py[pb2], r_rope], writes=[r_t2])
                        P.op("dve", "tensor_tensor", dict(out=buf[:], in0=t1[:], in1=t2[:], op=ALU.add),
                             reads=[r_t1, r_t2], writes=[rb])
                        dst_ = cfg["dst_fm"]("cqk", ti, t0 // T) if FUSED else cqkT[row0:row0 + 128, t0:t0 + T]
                        P.op("sp", "dma_start", dict(out=dst_, in_=buf[:]), reads=[rb] + XB, dma=True)
                    return ep
                nqt = cc["nq"] // 128
                nkt = cc["nk"] // 128
                for i in range(nqt):
                    fm_tile(Win_v, i * 128, ep_c(i * 128, gq_s, i))
                for i in range(nkt):
                    fm_tile(Win_v, cc["nq"] + i * 128, ep_c(cc["nq"] + i * 128, gk_s, nqt + i))
                for i in range(cc["nv"] // 128):
                    if FUSED:
                        tm_tile(Win_v, cc["nq"] + cc["nk"] + i * 128, 128, None, t0, BF16, 0, fname="cv", ti=i)
                    else:
                        tm_tile(Win_v, cc["nq"] + cc["nk"] + i * 128, 128, cv, t0, BF16, i * 128)
            if FUSED and cfg.get("cc_pairs") and cfg["cc_pairs"][t0 // T]:
                rg_ = [[0, 1, 2, 3], [4, 5, 6, 7]]
                for i_, (src_, dst_g) in enumerate(cfg["cc_pairs"][t0 // T]):
                    P.op("pool", "collective_compute", dict(kind="AllGather", op=ALU.bypass, replica_groups=rg_, ins=[src_.opt()], outs=[dst_g.opt()]),
                         writes=([r_xb] if i_ == 0 else []), cc="shared")
            if final_norm:
                rmsnorm(nwb_s, r_nwb, to_x=True)
                o_v = outT.rearrange("(c p) t -> p c t", p=128)
                for c in range(KC):
                    P.op("sp", "dma_start", dict(out=o_v[:, c, t0:t0 + T], in_=xs[:, c, :]), reads=[r_xs[c]], dma=True)
        P.emit()
    if FUSED:
        nc.all_engine_barrier()
    return nc


def attn_c_build(cfg):
    S = cfg.get("S", 4096)
    NKV = cfg.get("NKV", 2)
    REP = cfg.get("REP", 4)
    NQH = NKV * REP
    NKC = S // 128
    QB = 512
    scale = 128 ** -0.5
    TT_ = cfg.get("TD")
    FUSED = TT_ is not None
    if FUSED:
        nc = cfg["nc"]
        GQ, GK, GV, OCB = TT_["GQ"], TT_["GK"], TT_["GV"], TT_["OCB"]
    else:
        nc = bass.Bass("TRN2", target_bir_lowering=False)
        cqT = nc.dram_tensor("cqT", [NQH * 128, S], BF16, kind="ExternalInput").ap()
        ckT = nc.dram_tensor("ckT", [NKV * 128, S], BF16, kind="ExternalInput").ap()
        cv = nc.dram_tensor("cv", [S, NKV * 128], BF16, kind="ExternalInput").ap()
        ocT = nc.dram_tensor("ocT", [NQH * 128, S], BF16, kind="ExternalOutput").ap()
    P = Prog(nc)
    with ExitStack() as es:
        es.enter_context(nc.allow_low_precision("bf16 matmul operands, fp32 accumulate"))
        pfx = uniq()
        sb = lambda name, shape, dt: es.enter_context(nc.sbuf_tensor(pfx + name, shape, dt))
        psb = lambda name: es.enter_context(nc.psum_tensor(pfx + name, [128, 512], F32))
        R = Reg
        kT = [sb("kT%d" % i, [128, S], BF16) for i in range(2)]
        vv = [sb("v%d" % i, [128, NKC, 128], BF16) for i in range(2)]
        qT = [sb("qT%d" % i, [128, S], BF16) for i in range(2)]
        NE = 3
        ee = [sb("e%d" % i, [128, QB], BF16) for i in range(NE)]
        rz = sb("rz", [128, QB], F32)
        ob = [sb("ob%d" % i, [128, QB], BF16) for i in range(2)]
        ones = sb("ones", [128, 128], BF16)
        ps_s = [psb("ps_s%d" % i) for i in range(NE)]
        ps_o = [psb("ps_o%d" % i) for i in range(2)]
        ps_z = [psb("ps_z%d" % i) for i in range(2)]
        r_kT, r_v, r_qT = [R(), R()], [R(), R()], [R(), R()]
        r_e = [R() for i in range(NE)]
        r_rz, r_ones = R(), R()
        r_ob = [R(), R()]
        r_ps = [R(psum=True) for i in range(NE)]
        r_po, r_pz = [R(psum=True), R(psum=True)], [R(psum=True), R(psum=True)]
        P.op("pool", "memset", dict(ap=ones[:], constant=1.0), writes=[r_ones])
        if FUSED:
            idx_s = sb("idx_s", [128, 8], mybir.dt.int32)
            r_idx = R()
            P.op("sp", "dma_start", dict(out=idx_s[:], in_=TT_["idx"]), writes=[r_idx], dma=True)
            QS = S // 4

            def gather(out_ap, G_, off_, span_, ic_, cols, wr):
                P.op("pool", "indirect_dma_start",
                     dict(out=out_ap, out_offset=None,
                          in_offset=bass.IndirectOffsetOnAxis(ap=idx_s[:, ic_:ic_ + 1], axis=0), **gat(G_, off_, cols.start, cols.stop - cols.start)),
                     reads=[r_idx], writes=[wr], dma=True)
        it = 0
        blk = 0
        for kv in range(NKV):
            ks = kv % 2
            if FUSED:
                for r_ in range(4):
                    for ps_ in range(2):
                        gather(kT[ks][:, r_ * QS + ps_ * 512:r_ * QS + (ps_ + 1) * 512], GK[ps_][kv], r_ * 512, None, 0, slice(0, 512), r_kT[ks])
                for c_ in range(NKC):
                    r_, tl_ = c_ // 8, c_ % 8
                    gather(vv[ks][:, c_, :], GV[tl_ // 4][tl_ % 4], r_ * 512, None, 1, slice(kv * 128, (kv + 1) * 128), r_v[ks])
            else:
                P.op("sp", "dma_start", dict(out=kT[ks][:], in_=ckT[kv * 128:(kv + 1) * 128, :]), writes=[r_kT[ks]], dma=True)
                P.op("sp", "dma_start", dict(out=vv[ks][:], in_=cv[:, kv * 128:(kv + 1) * 128].rearrange("(c p) d -> p c d", p=128)),
                     writes=[r_v[ks]], dma=True)
            for r in range(REP):
                h = kv * REP + r
                qs = h % 2
                if FUSED:
                    for r_ in range(4):
                        for ps_ in range(2):
                            gather(qT[qs][:, r_ * QS + ps_ * 512:r_ * QS + (ps_ + 1) * 512], GQ[ps_][h], r_ * 512, None, 0, slice(0, 512), r_qT[qs])
                else:
                    P.op("sp", "dma_start", dict(out=qT[qs][:], in_=cqT[h * 128:(h + 1) * 128, :]), writes=[r_qT[qs]], dma=True)
                for qb in range(S // QB):
                    pb = blk % 2
                    blk += 1
                    qsl = qT[qs][:, qb * QB:(qb + 1) * QB]

                    def smm(kc, i):
                        P.op("pe", "matmul", dict(out=ps_s[i][:], lhsT=kT[ks][:, kc * 128:(kc + 1) * 128], rhs=qsl, start=True, stop=True),
                             reads=[r_kT[ks], r_qT[qs]], writes=[r_ps[i]])
                    smm(0, it % NE)
                    for kc in range(NKC):
                        i = it % NE
                        it += 1
                        if kc + 1 < NKC:
                            smm(kc + 1, it % NE)
                        P.op("act", "activation", dict(out=ee[i][:], in_=ps_s[i][:], func=AF.Exp, scale=scale),
                             reads=[r_ps[i]], writes=[r_e[i]])
                        P.op("pe", "matmul", dict(out=ps_o[pb][:], lhsT=vv[ks][:, kc, :], rhs=ee[i][:], start=(kc == 0), stop=(kc == NKC - 1)),
                             reads=[r_v[ks], r_e[i]], writes=[r_po[pb]])
                        P.op("pe", "matmul", dict(out=ps_z[pb][:], lhsT=ones[:], rhs=ee[i][:], start=(kc == 0), stop=(kc == NKC - 1)),
                             reads=[r_ones, r_e[i]], writes=[r_pz[pb]])
                    P.op("dve", "reciprocal", dict(out=rz[:], in_=ps_z[pb][:]), reads=[r_pz[pb]], writes=[r_rz])
                    P.op("dve", "tensor_tensor", dict(out=ob[pb][:], in0=ps_o[pb][:], in1=rz[:], op=ALU.mult),
                         reads=[r_po[pb], r_rz], writes=[r_ob[pb]])
                    if FUSED:
                        q_, th_ = qb // 2, qb % 2
                        P.op("sp", "dma_start", dict(out=OCB[h * 2 + th_][q_ * 128:(q_ + 1) * 128, :], in_=ob[pb][:]),
                             reads=[r_ob[pb]], dma=True)
                    else:
                        P.op("sp", "dma_start", dict(out=ocT[h * 128:(h + 1) * 128, qb * QB:(qb + 1) * QB], in_=ob[pb][:]),
                             reads=[r_ob[pb]], dma=True)
        P.emit()
    if FUSED:
        nc.all_engine_barrier()
    return nc


B_PATTERNS = ((128, 1), (512, 4), (2048, 16))


def ssl(a, n, d):
    return slice(a, a + d * (n - 1) + 1, d)


def dil_b_build(cfg):
    S = cfg.get("S", 4096)
    NHS = cfg.get("NHS", 2)
    pats = cfg.get("pats", B_PATTERNS)
    NG = len(pats)
    scale = 128 ** -0.5
    TT_ = cfg.get("TD")
    FUSED = TT_ is not None
    if FUSED:
        nc = cfg["nc"]
        bqT, bkT, bv, bmask, ob_q = TT_["bqT"], TT_["bkT"], TT_["bv"], TT_["bmask"], TT_["ob_q"]
    else:
        nc = bass.Bass("TRN2", target_bir_lowering=False)
        bqT = nc.dram_tensor("bqT", [NHS * NG * 128, S], BF16, kind="ExternalInput").ap()
        bkT = nc.dram_tensor("bkT", [NHS * NG * 128, S], BF16, kind="ExternalInput").ap()
        bv = nc.dram_tensor("bv", [S, NHS * NG * 128], BF16, kind="ExternalInput").ap()
        bmask = nc.dram_tensor("bmask", [128, 3, 512], BF16, kind="ExternalInput").ap()
        obT = nc.dram_tensor("obT", [NHS * 128, S], BF16, kind="ExternalOutput").ap()
    P = Prog(nc)
    with ExitStack() as es:
        es.enter_context(nc.allow_low_precision("bf16 matmul operands, fp32 accumulate"))
        pfx = uniq()
        sb = lambda name, shape, dt: es.enter_context(nc.sbuf_tensor(pfx + name, shape, dt))
        psb = lambda name: es.enter_context(nc.psum_tensor(pfx + name, [128, 512], F32))
        R = Reg
        qT = [sb("qT%d" % i, [128, S], BF16) for i in range(2)]
        kT = [sb("kT%d" % i, [128, S], BF16) for i in range(2)]
        vp = [sb("vp%d" % i, [128, S // 128, 128], BF16) for i in range(2)]
        Uacc = sb("Uacc", [128, S], F32)
        Zacc = sb("Zacc", [128, S], F32)
        ob = sb("ob", [128, S], BF16)
        ee = [sb("e%d" % i, [128, 512], BF16) for i in range(2)]
        em = [sb("em%d" % i, [128, 512], BF16) for i in range(2)]
        mk = sb("mk", [128, 3, 512], BF16)
        ones = sb("ones", [128, 128], BF16)
        ps_s = [psb("ps_s%d" % i) for i in range(2)]
        ps_o = [psb("ps_o%d" % i) for i in range(2)]
        ps_z = [psb("ps_z%d" % i) for i in range(2)]
        r_q, r_k, r_v = [R(), R()], [R(), R()], [R(), R()]
        r_U, r_Z, r_ob, r_mk, r_ones = R(), R(), R(), R(), R()
        r_e, r_em = [R(), R()], [R(), R()]
        r_ps, r_po, r_pz = [R(psum=True), R(psum=True)], [R(psum=True), R(psum=True)], [R(psum=True), R(psum=True)]
        P.op("pool", "memset", dict(ap=ones[:], constant=1.0), writes=[r_ones])
        P.op("sp", "dma_start", dict(out=mk[:], in_=bmask), writes=[r_mk], dma=True)
        gi = 0
        sc = 0
        bc = 0
        for hs in range(NHS):
            for g, (wd_, d) in enumerate(pats):
                s = gi % 2
                gi += 1
                row = (hs * NG + g) * 128
                L = S // d
                nblk = L // 128
                P.op("sp", "dma_start", dict(out=qT[s][:], in_=bqT[row:row + 128, :]), writes=[r_q[s]], dma=True)
                P.op("sp", "dma_start", dict(out=kT[s][:], in_=bkT[row:row + 128, :]), writes=[r_k[s]], dma=True)
                P.op("sp", "dma_start",
                     dict(out=vp[s][:].rearrange("p (r i) c -> p r i c", r=d),
                          in_=bv[:, row:row + 128].rearrange("(i p r) c -> p r i c", p=128, r=d)),
                     writes=[r_v[s]], dma=True)
                for r in range(d):
                    for i0 in range(0, nblk, 4):
                        nb = min(4, nblk - i0)
                        pb = bc % 2
                        bc += 1
                        for o in (0, -1, 1):
                            blo = 0
                            bhi = nb
                            if o == -1 and i0 == 0:
                                blo = 1
                            if o == 1 and i0 + nb == nblk:
                                bhi = nb - 1
                            if bhi <= blo:
                                continue
                            ss = sc % 2
                            sc += 1
                            for b in range(blo, bhi):
                                i = i0 + b
                                ka = r + d * 128 * (i + o)
                                qa = r + d * 128 * i
                                P.op("pe", "matmul", dict(out=ps_s[ss][:, b * 128:(b + 1) * 128],
                                                          lhsT=kT[s][:, ssl(ka, 128, d)], rhs=qT[s][:, ssl(qa, 128, d)],
                                                          start=True, stop=True),
                                     reads=[r_k[s], r_q[s]], writes=[r_ps[ss]])
                            cs = slice(blo * 128, bhi * 128)
                            P.op("act", "activation", dict(out=ee[ss][:, cs], in_=ps_s[ss][:, cs], func=AF.Exp, scale=scale),
                                 reads=[r_ps[ss]], writes=[r_e[ss]])
                            P.op("dve", "tensor_tensor", dict(out=em[ss][:, cs], in0=ee[ss][:, cs], in1=mk[:, o + 1, cs], op=ALU.mult),
                                 reads=[r_e[ss], r_mk], writes=[r_em[ss]])
                            for b in range(blo, bhi):
                                i = i0 + b
                                last = (o == 1) or (o == -1 and i == nblk - 1) or (o == 0 and nblk == 1)
                                bs = slice(b * 128, (b + 1) * 128)
                                P.op("pe", "matmul", dict(out=ps_o[pb][:, bs], lhsT=vp[s][:, r * nblk + i + o, :], rhs=em[ss][:, bs],
                                                          start=(o == 0 and b == 0), stop=last, skip_group_check=True),
                                     reads=[r_v[s], r_em[ss]], writes=[r_po[pb]])
                                P.op("pe", "matmul", dict(out=ps_z[pb][:, bs], lhsT=ones[:], rhs=em[ss][:, bs],
                                                          start=(o == 0 and b == 0), stop=last, skip_group_check=True),
                                     reads=[r_ones, r_em[ss]], writes=[r_pz[pb]])
                        a0 = r + d * 128 * i0
                        usl = Uacc[:, ssl(a0, 128 * nb, d)]
                        zsl = Zacc[:, ssl(a0, 128 * nb, d)]
                        if g == 0:
                            P.op("act", "copy", dict(out=usl, in_=ps_o[pb][:, 0:nb * 128]), reads=[r_po[pb]], writes=[r_U])
                            P.op("dve", "tensor_copy", dict(out=zsl, in_=ps_z[pb][:, 0:nb * 128]), reads=[r_pz[pb]], writes=[r_Z])
                        else:
                            P.op("dve", "tensor_tensor", dict(out=usl, in0=usl, in1=ps_o[pb][:, 0:nb * 128], op=ALU.add),
                                 reads=[r_po[pb], r_U], writes=[r_U])
                            P.op("dve", "tensor_tensor", dict(out=zsl, in0=zsl, in1=ps_z[pb][:, 0:nb * 128], op=ALU.add),
                                 reads=[r_pz[pb], r_Z], writes=[r_Z])
            P.op("dve", "reciprocal", dict(out=Zacc[:], in_=Zacc[:]), reads=[r_Z], writes=[r_Z])
            P.op("dve", "tensor_tensor", dict(out=ob[:], in0=Uacc[:], in1=Zacc[:], op=ALU.mult), reads=[r_U, r_Z], writes=[r_ob])
            if FUSED:
                for th_ in range(2):
                    P.op("sp", "dma_start", dict(out=ob_q[hs * 2 + th_].rearrange("(q p) t -> p q t", q=4),
                                                 in_=ob[:].rearrange("p (q h t) -> p q h t", q=4, h=2)[:, :, th_, :]), reads=[r_ob], dma=True)
            else:
                P.op("sp", "dma_start", dict(out=obT[hs * 128:(hs + 1) * 128, :], in_=ob[:]), reads=[r_ob], dma=True)
        P.emit()
    if FUSED:
        nc.all_engine_barrier()
    return nc


def dil_mask():
    import numpy as _np
    m = _np.zeros((128, 3, 512), _np.float32)
    p = _np.arange(128)[:, None]
    n = _np.arange(128)[None, :]
    for o in (-1, 0, 1):
        mm = (_np.abs(128 * o + p - n) <= 64).astype(_np.float32)
        m[:, o + 1, :] = _np.tile(mm, (1, 4))
    return m

import numpy as _np

BIG = 30000.0


def gdn_consts():
    k = _np.arange(128)[:, None]
    i = _np.arange(128)[None, :]
    c = {}
    c["ident"] = _np.eye(128, dtype=_np.float32)
    c["ucum"] = _np.stack([(k <= i), (k >= i)], 1).astype(_np.float32)
    nmd_f = BIG * (k <= i)
    nmd_b = BIG * (k >= i)
    nmt_f = -BIG * (i < k)
    nmt_b = -BIG * (i > k)
    c["nm"] = _np.stack([nmd_f, nmt_f, nmd_b, nmt_b], 1).astype(_np.float32)
    return c


def gdn_build(cfg):
    S = cfg.get("S", 4096)
    NH = cfg.get("NH", 4)
    NCH = S // 128
    NB = S // 512
    STOP = cfg.get("stop", 9)
    CHD = F32 if cfg.get("chain_fp32", True) else BF16
    SUB = cfg.get("sub", 9)
    TT_ = cfg.get("TD")
    FUSED = TT_ is not None
    nc = cfg["nc"] if FUSED else bass.Bass("TRN2", target_bir_lowering=False)
    if FUSED:
        din = lambda name, shape, dt=F32: TT_[name]
    else:
        din = lambda name, shape, dt=F32: nc.dram_tensor(name, shape, dt, kind="ExternalInput").ap()
    aqkvT = din("aqkvT", [NH * 3 * 128, S])
    az = din("az", [S, NH * 128])
    abr = din("abr", [S, 4 * NH])
    cw = din("cw", [128, NH * 3, 5])
    alog = din("alog", [128, 2 * NH])
    dtb = din("dtb", [128, 2 * NH])
    onorm = din("onorm", [128, 128])
    ident_d = din("ident_in", [128, 128])
    ucum_d = din("ucum_in", [128, 2, 128])
    nm_d = din("nm_in", [128, 4, 128])
    oaT = TT_["oa_q"] if FUSED else nc.dram_tensor("oaT", [NH * 128, S], BF16, kind="ExternalOutput").ap()
    P = Prog(nc)
    NC2 = 2 * NH
    with ExitStack() as es:
        es.enter_context(nc.allow_low_precision("bf16 matmul operands, fp32 accumulate"))
        pfx = uniq()
        sb = lambda name, shape, dt: es.enter_context(nc.sbuf_tensor(pfx + name, shape, dt))
        psb = lambda name, dt=F32, n=512: es.enter_context(nc.psum_tensor(pfx + name, [128, n], dt))
        R = Reg
        identf = sb("identf", [128, 128], F32)
        identb = sb("identb", [128, 128], BF16)
        ucum = sb("ucum", [128, 2, 128], F32)
        nm = sb("nm", [128, 4, 128], F32)
        onesf = sb("onesf", [128, 128], F32)
        onesb = sb("onesb", [128, 128], BF16)
        epsb = sb("epsb", [128, 1], F32)
        oneb = sb("oneb", [128, 1], F32)
        cws = sb("cws", [128, NH * 3, 5], F32)
        onorm_s = sb("onorm_s", [128, 128], F32)
        r_c = R()
        identc = identf if CHD == F32 else identb
        P.op("sp", "dma_start", dict(out=identf[:], in_=ident_d), writes=[r_c], dma=True)
        P.op("pool", "dma_start", dict(out=identb[:], in_=ident_d), writes=[r_c], dma=True)
        P.op("sp", "dma_start", dict(out=ucum[:], in_=ucum_d), writes=[r_c], dma=True)
        P.op("sp", "dma_start", dict(out=nm[:], in_=nm_d), writes=[r_c], dma=True)
        P.op("sp", "dma_start", dict(out=cws[:], in_=cw), writes=[r_c], dma=True)
        P.op("sp", "dma_start", dict(out=onorm_s[:], in_=onorm), writes=[r_c], dma=True)
        P.op("pool", "memset", dict(ap=onesf[:], constant=1.0), writes=[r_c])
        P.op("pool", "memset", dict(ap=onesb[:], constant=1.0), writes=[r_c])
        P.op("pool", "memset", dict(ap=epsb[:], constant=1e-6), writes=[r_c])
        P.op("pool", "memset", dict(ap=oneb[:], constant=1.0), writes=[r_c])

        NCOL = NCH * NC2
        raw = sb("raw", [128, NCH, 2 * NC2], F32)
        alog_s = sb("alog_s", [128, NC2], F32)
        dtb_s = sb("dtb_s", [128, NC2], F32)
        beta = sb("beta", [128, NCH, NC2], F32)
        nbeta = sb("nbeta", [128, NCH, NC2], F32)
        gg = sb("gg", [128, NCH, NC2], F32)
        gc = sb("gc", [128, NCH, NC2], F32)
        ngc = sb("ngc", [128, NCH, NC2], F32)
        gtot = sb("gtot", [128, NCH, NC2], F32)
        egc = sb("egc", [128, NCH, NC2], F32)
        begc = sb("begc", [128, NCH, NC2], F32)
        ekd = sb("ekd", [128, NCH, NC2], F32)
        egl = sb("egl", [128, NCH, NC2], F32)
        r_g = R()
        ps_m = psb("ps_m")
        r_pm = R(psum=True)
        P.op("sp", "dma_start", dict(out=raw[:], in_=abr.rearrange("(c p) n -> p c n", p=128)), writes=[r_g], dma=True)
        P.op("sp", "dma_start", dict(out=alog_s[:], in_=alog), writes=[r_g], dma=True)
        P.op("sp", "dma_start", dict(out=dtb_s[:], in_=dtb), writes=[r_g], dma=True)
        P.op("act", "activation", dict(out=beta[:], in_=raw[:, :, 0:NC2], func=AF.Sigmoid), reads=[r_g], writes=[r_g])
        P.op("dve", "tensor_scalar", dict(out=nbeta[:], in0=beta[:], scalar1=-1.0, scalar2=None, op0=ALU.mult), reads=[r_g], writes=[r_g])
        P.op("dve", "tensor_tensor", dict(out=gg[:], in0=raw[:, :, NC2:2 * NC2], in1=dtb_s[:, None, :].to_broadcast([128, NCH, NC2]), op=ALU.add),
             reads=[r_g], writes=[r_g])
        sp1 = sb("sp1", [128, NCH, NC2], F32)
        sp2 = sb("sp2", [128, NCH, NC2], F32)
        sp3 = sb("sp3", [128, NCH, NC2], F32)
        P.op("dve", "tensor_scalar", dict(out=sp1[:], in0=gg[:], scalar1=-1.0, scalar2=None, op0=ALU.mult), reads=[r_g], writes=[r_g])
        P.op("dve", "tensor_tensor", dict(out=sp1[:], in0=sp1[:], in1=gg[:], op=ALU.max), reads=[r_g], writes=[r_g])
        P.op("act", "activation", dict(out=sp1[:], in_=sp1[:], func=AF.Exp, scale=-1.0), reads=[r_g], writes=[r_g])
        P.op("dve", "tensor_scalar", dict(out=sp2[:], in0=sp1[:], scalar1=2.0, scalar2=None, op0=ALU.add), reads=[r_g], writes=[r_g])
        P.op("dve", "reciprocal", dict(out=sp2[:], in_=sp2[:]), reads=[r_g], writes=[r_g])
        P.op("dve", "tensor_tensor", dict(out=sp1[:], in0=sp1[:], in1=sp2[:], op=ALU.mult), reads=[r_g], writes=[r_g])
        P.op("dve", "tensor_tensor", dict(out=sp2[:], in0=sp1[:], in1=sp1[:], op=ALU.mult), reads=[r_g], writes=[r_g])
        P.op("dve", "tensor_scalar", dict(out=sp3[:], in0=sp2[:], scalar1=1.0 / 11, scalar2=1.0 / 9, op0=ALU.mult, op1=ALU.add), reads=[r_g], writes=[r_g])
        for cst_ in (1.0 / 7, 1.0 / 5, 1.0 / 3, 1.0):
            P.op("dve", "tensor_tensor", dict(out=sp3[:], in0=sp3[:], in1=sp2[:], op=ALU.mult), reads=[r_g], writes=[r_g])
            P.op("dve", "tensor_scalar", dict(out=sp3[:], in0=sp3[:], scalar1=cst_, scalar2=None, op0=ALU.add), reads=[r_g], writes=[r_g])
        P.op("dve", "tensor_tensor", dict(out=sp3[:], in0=sp3[:], in1=sp1[:], op=ALU.mult), reads=[r_g], writes=[r_g])
        P.op("dve", "tensor_scalar", dict(out=sp1[:], in0=gg[:], scalar1=0.0, scalar2=None, op0=ALU.max), reads=[r_g], writes=[r_g])
        P.op("dve", "scalar_tensor_tensor", dict(out=gg[:], in0=sp3[:], scalar=2.0, in1=sp1[:], op0=ALU.mult, op1=ALU.add), reads=[r_g], writes=[r_g])
        P.op("act", "activation", dict(out=alog_s[:], in_=alog_s[:], func=AF.Exp), reads=[r_g], writes=[r_g])
        P.op("dve", "scalar_tensor_tensor", dict(out=gg[:], in0=gg[:], scalar=-1.0, in1=alog_s[:, None, :].to_broadcast([128, NCH, NC2]),
                                                 op0=ALU.mult, op1=ALU.mult), reads=[r_g], writes=[r_g])
        ggv = gg[:].rearrange("p c (d h) -> p c d h", d=2)
        gcv = gc[:].rearrange("p c (d h) -> p c d h", d=2)
        gtv = gtot[:].rearrange("p c (d h) -> p c d h", d=2)
        psv = ps_m[:, 0:NCH * NC2].rearrange("p (c d h) -> p c d h", c=NCH, d=2)
        pst = ps_m[:, 256:256 + NCH * NC2].rearrange("p (c d h) -> p c d h", c=NCH, d=2)
        assert NCH * NC2 <= 256
        for d in range(2):
            P.op("pe", "matmul", dict(out=psv[:, :, d, :], lhsT=ucum[:, d, :], rhs=ggv[:, :, d, :], start=(d == 0), stop=True, skip_group_check=True),
                 reads=[r_c, r_g], writes=[r_pm])
        P.op("pe", "matmul", dict(out=ps_m[:, 256:256 + NCH * NC2], lhsT=onesf[:], rhs=gg[:].rearrange("p c n -> p (c n)"),
                                  start=False, stop=True, skip_group_check=True), reads=[r_c, r_g], writes=[r_pm])
        P.op("dve", "tensor_copy", dict(out=gc[:].rearrange("p c n -> p (c n)"), in_=ps_m[:, 0:NCH * NC2]), reads=[r_pm], writes=[r_g])
        P.op("dve", "tensor_copy", dict(out=gtot[:].rearrange("p c n -> p (c n)"), in_=ps_m[:, 256:256 + NCH * NC2]), reads=[r_pm], writes=[r_g])
        P.op("dve", "tensor_scalar", dict(out=ngc[:], in0=gc[:], scalar1=-1.0, scalar2=None, op0=ALU.mult), reads=[r_g], writes=[r_g])
        P.op("act", "activation", dict(out=egc[:], in_=gc[:], func=AF.Exp), reads=[r_g], writes=[r_g])
        P.op("dve", "tensor_tensor", dict(out=begc[:], in0=egc[:], in1=beta[:], op=ALU.mult), reads=[r_g], writes=[r_g])
        P.op("dve", "tensor_tensor", dict(out=ekd[:], in0=gtot[:], in1=gc[:], op=ALU.subtract), reads=[r_g], writes=[r_g])
        P.op("act", "activation", dict(out=ekd[:], in_=ekd[:], func=AF.Exp), reads=[r_g], writes=[r_g])
        P.op("act", "activation", dict(out=egl[:], in_=gtot[:], func=AF.Exp), reads=[r_g], writes=[r_g])

        NHX = NH if STOP >= 1 else 0
        xin = sb("xin", [128, S + 4], F32)
        acc = sb("acc", [128, S], F32)
        sqb = sb("sqb", [128, S], BF16)
        fT = [sb("fT%d" % i, [128, S], BF16) for i in range(3)]
        kbg = [sb("kbg%d" % i, [128, NCH, 128], BF16) for i in range(2)]
        kdd = [sb("kdd%d" % i, [128, NCH, 128], BF16) for i in range(2)]
        vbd = [sb("vbd%d" % i, [128, NCH, 128], BF16) for i in range(2)]
        oacc = sb("oacc", [128, NCH, 128], F32)
        zt = sb("zt", [128, NCH, 128], F32)
        rn = sb("rn", [128, 512], F32)
        ssn = sb("ssn", [128, NCH], F32)
        ogb = sb("ogb", [128, NCH, 128], BF16)
        oTs = sb("oTs", [128, S], BF16)
        r_xin, r_acc, r_sqb, r_rn = R(), R(), R(), R()
        r_fT = [R(), R(), R()]
        r_tok = R()
        r_oacc = [R() for c in range(NCH)]
        r_zt, r_ssn, r_ogb, r_oTs = R(), R(), R(), R()
        Gb = [[sb("Gb%d%d" % (d, i), [128, 128], F32) for i in range(2)] for d in range(2)]
        dec = [[sb("dec%d%d" % (d, i), [128, 2, 128], F32) for i in range(2)] for d in range(2)]
        Nb = [[sb("Nb%d%d" % (d, i), [128, 128], CHD) for i in range(2)] for d in range(2)]
        Mb = [[sb("Mb%d%d" % (d, i), [128, 128], CHD) for i in range(2)] for d in range(2)]
        Pb = [[sb("Pb%d%d" % (d, i), [128, 128], CHD) for i in range(2)] for d in range(2)]
        TT = [[sb("TT%d%d" % (d, i), [128, 128], BF16) for i in range(2)] for d in range(2)]
        wTn = [[sb("wTn%d%d" % (d, i), [128, 128], BF16) for i in range(2)] for d in range(2)]
        atT = [[sb("atT%d%d" % (d, i), [128, 128], BF16) for i in range(2)] for d in range(2)]
        vnew = [sb("vnew%d" % d, [128, 128], BF16) for d in range(2)]
        tmpo = [sb("tmpo%d" % d, [128, 128], F32) for d in range(2)]
        tmpo2 = [sb("tmpo2%d" % d, [128, 128], F32) for d in range(2)]
        Sf = [sb("Sf%d" % d, [128, 128], F32) for d in range(2)]
        Sb_ = [sb("Sb%d" % d, [128, 128], BF16) for d in range(2)]
        Sl_ = [sb("Sl%d" % d, [128, 128], BF16) for d in range(2)]
        vnl = [sb("vnl%d" % d, [128, 128], BF16) for d in range(2)]
        r_Gb = [[R(), R()], [R(), R()]]
        r_dec = [[R(), R()], [R(), R()]]
        r_N = [[R(), R()], [R(), R()]]
        r_M = [[R(), R()], [R(), R()]]
        r_P = [[R(), R()], [R(), R()]]
        r_TT = [[R(), R()], [R(), R()]]
        r_w = [[R(), R()], [R(), R()]]
        r_at = [[R(), R()], [R(), R()]]
        r_vn, r_to, r_to2, r_Sf, r_Sb = [R(), R()], [R(), R()], [R(), R()], [R(), R()], [R(), R()]
        ps_X = [psb("ps_X%d" % d) for d in range(2)]
        ps_kk = psb("ps_kk")
        ps_ch = [psb("ps_ch%d" % d) for d in range(2)]
        ps_sc = [psb("ps_sc%d" % d) for d in range(2)]
        r_pX, r_pch, r_psc = [R(psum=True), R(psum=True)], [R(psum=True), R(psum=True)], [R(psum=True), R(psum=True)]
        r_pkk = R(psum=True)
        ps_tb = ps_m[:].bitcast(BF16)

        for h in range(NHX):
            for t in range(3):
                row = (h * 3 + t) * 128
                P.op("pool", "memset", dict(ap=xin[:, 0:2], constant=0.0), writes=[r_xin])
                P.op("pool", "memset", dict(ap=xin[:, S + 2:S + 4], constant=0.0), writes=[r_xin])
                P.op("sp", "dma_start", dict(out=xin[:, 2:S + 2], in_=aqkvT[row:row + 128, :]), writes=[r_xin], dma=True)
                P.op("dve", "tensor_scalar", dict(out=acc[:], in0=xin[:, 0:S], scalar1=cws[:, h * 3 + t, 0:1], scalar2=None, op0=ALU.mult),
                     reads=[r_xin, r_c], writes=[r_acc])
                for w in range(1, 5):
                    P.op("dve", "scalar_tensor_tensor", dict(out=acc[:], in0=xin[:, w:w + S], scalar=cws[:, h * 3 + t, w:w + 1], in1=acc[:],
                                                             op0=ALU.mult, op1=ALU.add), reads=[r_xin, r_c, r_acc], writes=[r_acc])
                if t == 2:
                    P.op("act", "activation", dict(out=fT[2][:], in_=acc[:], func=AF.Silu), reads=[r_acc], writes=[r_fT[2]])
                else:
                    P.op("act", "activation", dict(out=acc[:], in_=acc[:], func=AF.Silu), reads=[r_acc], writes=[r_acc])
                    P.op("act", "activation", dict(out=sqb[:], in_=acc[:], func=AF.Square), reads=[r_acc], writes=[r_sqb])
                    for b in range(NB):
                        bs = slice(b * 512, (b + 1) * 512)
                        P.op("pe", "matmul", dict(out=ps_m[:], lhsT=onesb[:], rhs=sqb[:, bs], start=True, stop=True),
                             reads=[r_c, r_sqb], writes=[r_pm])
                        P.op("act", "activation", dict(out=rn[:], in_=ps_m[:], func=AF.Sqrt, bias=epsb[:, 0:1]), reads=[r_pm, r_c], writes=[r_rn])
                        P.op("dve", "reciprocal", dict(out=rn[:], in_=rn[:]), reads=[r_rn], writes=[r_rn])
                        P.op("dve", "scalar_tensor_tensor", dict(out=fT[t][:, bs], in0=acc[:, bs], scalar=(128 ** -0.5 if t == 0 else 1.0), in1=rn[:],
                                                                 op0=ALU.mult, op1=ALU.mult), reads=[r_acc, r_rn], writes=[r_fT[t]])
            if STOP < 2:
                continue
            for c4 in range(0, NCH, 4):
                for t in (1, 2):
                    for j in range(4):
                        c = c4 + j
                        P.op("pe", "transpose", dict(out=ps_tb[:, j * 128:(j + 1) * 128], in_=fT[t][:, c * 128:(c + 1) * 128], identity=identb[:]),
                             reads=[r_fT[t], r_c], writes=[r_pm])
                    src = ps_tb[:, 0:512].rearrange("p (c k) -> p c k", c=4)
                    for d in range(2):
                        col = d * NH + h
                        if t == 1:
                            P.op("dve", "tensor_tensor", dict(out=kbg[d][:, c4:c4 + 4, :], in0=src,
                                                              in1=begc[:, c4:c4 + 4, col:col + 1].to_broadcast([128, 4, 128]), op=ALU.mult),
                                 reads=[r_pm, r_g], writes=[r_tok])
                            P.op("dve", "tensor_tensor", dict(out=kdd[d][:, c4:c4 + 4, :], in0=src,
                                                              in1=ekd[:, c4:c4 + 4, col:col + 1].to_broadcast([128, 4, 128]), op=ALU.mult),
                                 reads=[r_pm, r_g], writes=[r_tok])
                        else:
                            P.op("dve", "tensor_tensor", dict(out=vbd[d][:, c4:c4 + 4, :], in0=src,
                                                              in1=beta[:, c4:c4 + 4, col:col + 1].to_broadcast([128, 4, 128]), op=ALU.mult),
                                 reads=[r_pm, r_g], writes=[r_tok])
            if STOP < 3:
                continue
            P.op("sp", "dma_start", dict(out=zt[:], in_=az[:, h * 128:(h + 1) * 128].rearrange("(c p) n -> p c n", p=128)), writes=[r_zt], dma=True)

            for d in range(2):
                P.op("pool", "memset", dict(ap=Sf[d][:], constant=0.0), writes=[r_Sf[d]])
                P.op("pool", "memset", dict(ap=Sb_[d][:], constant=0.0), writes=[r_Sb[d]])
                P.op("pool", "memset", dict(ap=Sl_[d][:], constant=0.0), writes=[r_Sb[d]])

            def precompute(d, c, par):
                col = d * NH + h
                cs = slice(c * 128, (c + 1) * 128)
                P.op("dve", "tensor_scalar", dict(out=Gb[d][par][:], in0=onesf[:], scalar1=gg[:, c, col:col + 1], scalar2=None, op0=ALU.mult),
                     reads=[r_c, r_g], writes=[r_Gb[d][par]])
                X = ps_X[d]
                P.op("pe", "matmul", dict(out=X[:, 0:128], lhsT=Gb[d][par][:], rhs=ucum[:, d, :], start=True, stop=False, skip_group_check=True),
                     reads=[r_Gb[d][par], r_c], writes=[r_pX[d]])
                P.op("pe", "matmul", dict(out=X[:, 0:128], lhsT=identf[:], rhs=nm[:, 2 * d, :], start=False, stop=True, skip_group_check=True),
                     reads=[r_c], writes=[r_pX[d]])
                P.op("pe", "matmul", dict(out=X[:, 128:256], lhsT=Gb[d][par][:], rhs=ucum[:, d, :], start=False, stop=False, skip_group_check=True),
                     reads=[r_Gb[d][par], r_c], writes=[r_pX[d]])
                P.op("pe", "matmul", dict(out=X[:, 128:256], lhsT=identf[:], rhs=nm[:, 2 * d + 1, :], start=False, stop=True, skip_group_check=True),
                     reads=[r_c], writes=[r_pX[d]])
                yield
                P.op("act", "activation", dict(out=dec[d][par][:, 0, :], in_=X[:, 0:128], func=AF.Exp, scale=-1.0, bias=gc[:, c, col:col + 1]),
                     reads=[r_pX[d], r_g], writes=[r_dec[d][par]])
                P.op("act", "activation", dict(out=dec[d][par][:, 1, :], in_=X[:, 128:256], func=AF.Exp, scale=1.0, bias=ngc[:, c, col:col + 1]),
                     reads=[r_pX[d], r_g], writes=[r_dec[d][par]])
                if SUB < 1:
                    return
                P.op("pe", "matmul", dict(out=X[:, 256:384], lhsT=fT[1][:, cs], rhs=fT[1][:, cs], start=False, stop=True, skip_group_check=True),
                     reads=[r_fT[1]], writes=[r_pX[d]])
                P.op("pe", "matmul", dict(out=X[:, 384:512], lhsT=fT[1][:, cs], rhs=fT[0][:, cs], start=False, stop=True, skip_group_check=True),
                     reads=[r_fT[1], r_fT[0]], writes=[r_pX[d]])
                yield
                P.op("dve", "scalar_tensor_tensor", dict(out=Nb[d][par][:], in0=X[:, 256:384], scalar=nbeta[:, c, col:col + 1], in1=dec[d][par][:, 0, :],
                                                         op0=ALU.mult, op1=ALU.mult), reads=[r_pX[d], r_g, r_dec[d][par]], writes=[r_N[d][par]])
                P.op("dve", "tensor_tensor", dict(out=atT[d][par][:], in0=X[:, 384:512], in1=dec[d][par][:, 1, :], op=ALU.mult),
                     reads=[r_pX[d], r_dec[d][par]], writes=[r_at[d][par]])
                yield
                if SUB < 2:
                    return
                ch = ps_ch[d]
                P.op("pe", "matmul", dict(out=ch[:, 128:256], lhsT=Nb[d][par][:], rhs=identc[:], start=True, stop=True, skip_group_check=True),
                     reads=[r_N[d][par], r_c], writes=[r_pch[d]])
                if SUB == 2 and cfg.get("sub2", 0) == 1:
                    P.op("act", "copy", dict(out=Mb[d][par][:], in_=ch[:, 128:256]), reads=[r_pch[d]], writes=[r_M[d][par]])
                    return
                P.op("pe", "matmul", dict(out=ch[:, 256:384], lhsT=identc[:], rhs=identc[:], start=False, stop=False, skip_group_check=True),
                     reads=[r_c], writes=[r_pch[d]])
                P.op("pe", "matmul", dict(out=ch[:, 256:384], lhsT=Nb[d][par][:], rhs=identc[:], start=False, stop=True, skip_group_check=True),
                     reads=[r_N[d][par], r_c], writes=[r_pch[d]])
                yield
                P.op("act", "copy", dict(out=Mb[d][par][:], in_=ch[:, 128:256]), reads=[r_pch[d]], writes=[r_M[d][par]])
                if cfg.get("sub2", 0) == 2:
                    P.op("act", "copy", dict(out=Pb[d][par][:], in_=ch[:, 256:384]), reads=[r_pch[d]], writes=[r_P[d][par]])
                else:
                    P.op("dve", "tensor_scalar", dict(scalar1=1.0, scalar2=None, op0=ALU.mult, out=Pb[d][par][:], in0=ch[:, 256:384]), reads=[r_pch[d]], writes=[r_P[d][par]])
                if SUB < 3:
                    return
                for k in range(6):
                    P.op("pe", "matmul", dict(out=ch[:, 0:128], lhsT=Mb[d][par][:], rhs=Nb[d][par][:], start=True, stop=True, skip_group_check=True),
                         reads=[r_M[d][par], r_N[d][par]], writes=[r_pch[d]])
                    if k < 5:
                        P.op("pe", "matmul", dict(out=ch[:, 128:256], lhsT=Nb[d][par][:], rhs=Mb[d][par][:], start=False, stop=True, skip_group_check=True),
                             reads=[r_M[d][par], r_N[d][par]], writes=[r_pch[d]])
                    yield
                    P.op("act", "copy", dict(out=Nb[d][par][:], in_=ch[:, 0:128]), reads=[r_pch[d]], writes=[r_N[d][par]])
                    if k < 5:
                        P.op("dve", "tensor_scalar", dict(scalar1=1.0, scalar2=None, op0=ALU.mult, out=Mb[d][par][:], in0=ch[:, 128:256]), reads=[r_pch[d]], writes=[r_M[d][par]])
                    yield
                    P.op("pe", "matmul", dict(out=ch[:, 256:384], lhsT=identc[:], rhs=Pb[d][par][:], start=False, stop=False, skip_group_check=True),
                         reads=[r_c, r_P[d][par]], writes=[r_pch[d]])
                    P.op("pe", "matmul", dict(out=ch[:, 256:384], lhsT=Nb[d][par][:], rhs=Pb[d][par][:], start=False, stop=True, skip_group_check=True),
                         reads=[r_N[d][par], r_P[d][par]], writes=[r_pch[d]])
                    yield
                    if k < 5:
                        P.op("dve", "tensor_scalar", dict(scalar1=1.0, scalar2=None, op0=ALU.mult, out=Pb[d][par][:], in0=ch[:, 256:384]), reads=[r_pch[d]], writes=[r_P[d][par]])
                    else:
                        P.op("dve", "tensor_scalar", dict(scalar1=1.0, scalar2=None, op0=ALU.mult, out=TT[d][par][:], in0=ch[:, 256:384]), reads=[r_pch[d]], writes=[r_TT[d][par]])
                if SUB < 4:
                    return
                yield
                P.op("pe", "matmul", dict(out=ch[:, 384:512], lhsT=kbg[d][:, c, :], rhs=TT[d][par][:], start=False, stop=True, skip_group_check=True),
                     reads=[r_tok, r_TT[d][par]], writes=[r_pch[d]])
                yield
                P.op("act", "activation", dict(out=wTn[d][par][:], in_=ch[:, 384:512], func=AF.Copy, scale=-1.0), reads=[r_pch[d]], writes=[r_w[d][par]])

            first_visit = [True] * NCH

            def scan(d, c, par):
                col = d * NH + h
                cs = slice(c * 128, (c + 1) * 128)
                sc = ps_sc[d]
                P.op("pe", "matmul", dict(out=sc[:, 0:128], lhsT=TT[d][par][:], rhs=vbd[d][:, c, :], start=True, stop=False, skip_group_check=True),
                     reads=[r_TT[d][par], r_tok], writes=[r_psc[d]])
                P.op("pe", "matmul", dict(out=sc[:, 0:128], lhsT=wTn[d][par][:], rhs=Sb_[d][:], start=False, stop=False, skip_group_check=True),
                     reads=[r_w[d][par], r_Sb[d]], writes=[r_psc[d]])
                P.op("pe", "matmul", dict(out=sc[:, 0:128], lhsT=wTn[d][par][:], rhs=Sl_[d][:], start=False, stop=True, skip_group_check=True),
                     reads=[r_w[d][par], r_Sb[d]], writes=[r_psc[d]])
                yield
                P.op("act", "copy", dict(out=vnew[d][:], in_=sc[:, 0:128]), reads=[r_psc[d]], writes=[r_vn[d]])
                P.op("dve", "tensor_tensor", dict(out=vnl[d][:], in0=sc[:, 0:128], in1=vnew[d][:], op=ALU.subtract), reads=[r_psc[d], r_vn[d]], writes=[r_vn[d]])
                yield
                P.op("pe", "matmul", dict(out=sc[:, 128:256], lhsT=fT[0][:, cs], rhs=Sb_[d][:], start=False, stop=False, skip_group_check=True),
                     reads=[r_fT[0], r_Sb[d]], writes=[r_psc[d]])
                P.op("pe", "matmul", dict(out=sc[:, 128:256], lhsT=fT[0][:, cs], rhs=Sl_[d][:], start=False, stop=True, skip_group_check=True),
                     reads=[r_fT[0], r_Sb[d]], writes=[r_psc[d]])
                for vv_ in (vnew, vnl):
                    P.op("pe", "matmul", dict(out=sc[:, 256:384], lhsT=atT[d][par][:], rhs=vv_[d][:], start=False, stop=(vv_ is vnl), skip_group_check=True),
                         reads=[r_at[d][par], r_vn[d]], writes=[r_psc[d]])
                for vv_ in (vnew, vnl):
                    P.op("pe", "matmul", dict(out=sc[:, 384:512], lhsT=kdd[d][:, c, :], rhs=vv_[d][:], start=False, stop=(vv_ is vnl), skip_group_check=True),
                         reads=[r_tok, r_vn[d]], writes=[r_psc[d]])
                yield
                P.op("act", "copy", dict(out=tmpo[d][:], in_=sc[:, 256:384]), reads=[r_psc[d]], writes=[r_to[d]])
                if first_visit[c]:
                    first_visit[c] = False
                    P.op("dve", "scalar_tensor_tensor", dict(out=oacc[:, c, :], in0=sc[:, 128:256], scalar=egc[:, c, col:col + 1], in1=tmpo[d][:],
                                                             op0=ALU.mult, op1=ALU.add), reads=[r_psc[d], r_g, r_to[d]], writes=[r_oacc[c]])
                else:
                    P.op("dve", "scalar_tensor_tensor", dict(out=tmpo2[d][:], in0=sc[:, 128:256], scalar=egc[:, c, col:col + 1], in1=tmpo[d][:],
                                                             op0=ALU.mult, op1=ALU.add), reads=[r_psc[d], r_g, r_to[d]], writes=[r_to2[d]])
                    P.op("dve", "tensor_tensor", dict(out=oacc[:, c, :], in0=oacc[:, c, :], in1=tmpo2[d][:], op=ALU.add),
                         reads=[r_to2[d], r_oacc[c]], writes=[r_oacc[c]])
                P.op("dve", "scalar_tensor_tensor", dict(out=Sf[d][:], in0=Sf[d][:], scalar=egl[:, c, col:col + 1], in1=sc[:, 384:512],
                                                         op0=ALU.mult, op1=ALU.add), reads=[r_psc[d], r_g, r_Sf[d]], writes=[r_Sf[d]])
                yield
                P.op("act", "copy", dict(out=Sb_[d][:], in_=Sf[d][:]), reads=[r_Sf[d]], writes=[r_Sb[d]])
                P.op("dve", "tensor_tensor", dict(out=Sl_[d][:], in0=Sf[d][:], in1=Sb_[d][:], op=ALU.subtract), reads=[r_Sf[d], r_Sb[d]], writes=[r_Sb[d]])

            order = [[c for c in range(NCH)], [NCH - 1 - c for c in range(NCH)]]
            def run_gens(gens):
                gens = list(gens)
                while gens:
                    alive = []
                    for g_ in gens:
                        try:
                            next(g_)
                            alive.append(g_)
                        except StopIteration:
                            pass
                    gens = alive
            run_gens([precompute(d, order[d][0], 0) for d in range(2)])
            for s in range(NCH):
                gens = []
                if s + 1 < NCH:
                    gens += [precompute(d, order[d][s + 1], (s + 1) % 2) for d in range(2)]
                gens += [scan(d, order[d][s], s % 2) for d in range(2)]
                run_gens(gens)

            allo = r_oacc
            accv = acc[:].rearrange("p (c k) -> p c k", c=NCH)
            P.op("dve", "tensor_tensor", dict(out=accv, in0=oacc[:], in1=oacc[:], op=ALU.mult), reads=allo + [r_acc], writes=[r_acc])
            P.op("dve", "tensor_reduce", dict(out=ssn[:], in_=accv, axis=AX.X, op=ALU.add), reads=[r_acc], writes=[r_ssn])
            P.op("act", "activation", dict(out=ssn[:], in_=ssn[:], func=AF.Sqrt, scale=1.0 / 128, bias=epsb[:, 0:1]), reads=[r_ssn, r_c], writes=[r_ssn])
            P.op("dve", "reciprocal", dict(out=ssn[:], in_=ssn[:]), reads=[r_ssn], writes=[r_ssn])
            P.op("dve", "tensor_tensor", dict(out=accv, in0=oacc[:], in1=ssn[:, :, None].to_broadcast([128, NCH, 128]), op=ALU.mult),
                 reads=allo + [r_ssn, r_acc], writes=[r_acc])
            P.op("dve", "tensor_tensor", dict(out=accv, in0=accv, in1=onorm_s[:, None, :].to_broadcast([128, NCH, 128]), op=ALU.mult),
                 reads=[r_c, r_acc], writes=[r_acc])
            P.op("act", "activation", dict(out=zt[:], in_=zt[:], func=AF.Silu), reads=[r_zt], writes=[r_zt])
            P.op("dve", "tensor_tensor", dict(out=ogb[:], in0=accv, in1=zt[:], op=ALU.mult), reads=[r_acc, r_zt], writes=[r_ogb])
            for c4 in range(0, NCH, 4):
                for j in range(4):
                    c = c4 + j
                    P.op("pe", "matmul", dict(out=ps_m[:, j * 128:(j + 1) * 128], lhsT=ogb[:, c, :], rhs=identb[:], start=(j == 0), stop=True, skip_group_check=True),
                         reads=[r_ogb, r_c], writes=[r_pm])
                P.op("act", "copy", dict(out=oTs[:, c4 * 128:(c4 + 4) * 128], in_=ps_m[:, 0:512]), reads=[r_pm], writes=[r_oTs])
            if FUSED:
                for th_ in range(2):
                    P.op("sp", "dma_start", dict(out=oaT[h * 2 + th_].rearrange("(q p) t -> p q t", q=4),
                                                 in_=oTs[:].rearrange("p (q h t) -> p q h t", q=4, h=2)[:, :, th_, :]), reads=[r_oTs], dma=True)
            else:
                P.op("sp", "dma_start", dict(out=oaT[h * 128:(h + 1) * 128, :], in_=oTs[:]), reads=[r_oTs], dma=True)
        P.emit()
    if FUSED:
        nc.all_engine_barrier()
    return nc

I32 = mybir.dt.int32


def relayout_emit(nc, jobs, idx_ap):
    P = Prog(nc)
    with ExitStack() as es:
        pfx = uniq()
        sb = lambda name, shape, dt: es.enter_context(nc.sbuf_tensor(pfx + name, shape, dt))
        NBUF = 6
        bf = [sb("rl_f%d" % i, [128, 1024], F32) for i in range(NBUF)]
        bb = [sb("rl_b%d" % i, [128, 1024], BF16) for i in range(NBUF)]
        rf = [Reg() for i in range(NBUF)]
        rb = [Reg() for i in range(NBUF)]
        idx_s = sb("rl_idx", [128, 8], I32)
        r_idx = Reg()
        P.op("sp", "dma_start", dict(out=idx_s[:], in_=idx_ap), writes=[r_idx], dma=True)
        kf = kb = 0
        for in_ap, ic, out_ap, dt in jobs:
            W = out_ap.shape[-1]
            if dt == F32:
                buf, rr = bf[kf % NBUF], rf[kf % NBUF]
                kf += 1
            else:
                buf, rr = bb[kb % NBUF], rb[kb % NBUF]
                kb += 1
            G_, off_ = in_ap
            assert G_.shape[1] == W
            P.op("pool", "indirect_dma_start",
                 dict(out=buf[:, 0:W], out_offset=None, in_offset=bass.IndirectOffsetOnAxis(ap=idx_s[:, ic:ic + 1], axis=0), **gat(G_, off_, 0, W)),
                 reads=[r_idx], writes=[rr], dma=True)
            P.op("sp", "dma_start", dict(out=out_ap, in_=buf[:, 0:W]), reads=[rr], dma=True)
        P.emit()
    nc.all_engine_barrier()


def allgather_emit(nc, pairs):
    rg = [[0, 1, 2, 3], [4, 5, 6, 7]]
    sem = nc.alloc_semaphore(name=uniq() + "ag")
    with nc.Block() as block:
        @block.gpsimd
        def _(g):
            for src, dst in pairs:
                g.collective_compute(kind="AllGather", op=ALU.bypass, replica_groups=rg, ins=[src.opt()], outs=[dst.opt()]).then_inc(sem)
            g.wait_ge(sem, len(pairs))
    nc.all_engine_barrier()
    nc.clear_and_free_semaphores([sem])
    nc.all_engine_barrier()


def build_fused(stop=99):
    S, D, TPC, DFF = 4096, 4096, 1024, 11008
    nc = bass.Bass("TRN2", target_bir_lowering=False)
    ein = lambda name, shape, dt=F32: nc.dram_tensor(name, shape, dt, kind="ExternalInput").ap()
    itn = lambda name, shape, dt=F32: nc.dram_tensor(name, shape, dt, kind="Internal").ap()
    KC = D // 128
    SHAPES = {}
    SHAPES["xT"] = ([D, TPC], F32)
    SHAPES["idx"] = ([128, 8], I32)
    SHAPES["nw_fin"] = ([128, KC], F32)
    SHAPES["Win0"] = ([D, 17472], F32)
    SHAPES["Wo0"] = ([3072, D], F32)
    SHAPES["Wqkv"] = ([D, 6144], F32)
    SHAPES["Wo1"] = ([4096, D], F32)
    SHAPES["ropeCb"] = ([32, TPC], F32)
    SHAPES["ropeSb"] = ([32, TPC], F32)
    SHAPES["ropePb"] = ([32, 32], F32)
    SHAPES["ropeCc"] = ([128, TPC], F32)
    SHAPES["ropeSc"] = ([128, TPC], F32)
    SHAPES["ropePc"] = ([128, 128], F32)
    SHAPES["gq"] = ([128, 1], F32)
    SHAPES["gk"] = ([128, 1], F32)
    SHAPES["cw"] = ([128, 12, 5], F32)
    SHAPES["alog"] = ([128, 8], F32)
    SHAPES["dtb"] = ([128, 8], F32)
    SHAPES["onorm"] = ([128, 128], F32)
    SHAPES["ident_in"] = ([128, 128], F32)
    SHAPES["ucum_in"] = ([128, 2, 128], F32)
    SHAPES["nm_in"] = ([128, 4, 128], F32)
    SHAPES["bmask"] = ([128, 3, 512], BF16)
    for l in range(2):
        for nm_, sh_ in (("nw_mix", [128, KC]), ("nw_ffn", [128, KC]), ("Wg", [D, DFF]), ("Wu", [D, DFF]), ("Wd", [DFF, D])):
            SHAPES["%s%d" % (nm_, l)] = (sh_, F32)

    class _Lazy(dict):
        def __missing__(self, k):
            sh_, dt_ = SHAPES[k]
            v = ein(k, sh_, dt_)
            self[k] = v
            return v
    E = _Lazy()
    outT = nc.dram_tensor("outT", [D, TPC], F32, kind="ExternalOutput").ap()

    pairs1, pairs2, pairs3, pairs4 = [], [], [], []
    p1 = [[], []]
    p3 = [[], []]

    def blk(name, rows, W, dt, pairs):
        src = itn("s_" + name, [rows, W], dt)
        dst = itn("g_" + name, [4 * rows, W], dt)
        pairs.append((src, dst))
        return src, dst
    A_b = [[blk("aq%d_%d" % (ps, i), 512, 512, F32, p1[ps]) for i in range(12)] for ps in range(2)]
    Q_b = [[blk("bq%d_%d" % (ps, i), 1024, 512, BF16, p1[ps]) for i in range(6)] for ps in range(2)]
    Z_b = [[blk("az%d_%d" % (ps, i), 512, 512, F32, p1[ps]) for i in range(4)] for ps in range(2)]
    V_b = [[blk("bv%d_%d" % (ps, i), 512, 768, BF16, p1[ps]) for i in range(4)] for ps in range(2)]
    AB_b = [blk("ab%d" % ps, 2048, 16, F32, p1[ps]) for ps in range(2)]
    OA_b = [blk("oa%d" % i, 512, 512, BF16, pairs2) for i in range(8)]
    OB_b = [blk("ob%d" % i, 512, 512, BF16, pairs2) for i in range(4)]
    CQ_b = [[blk("cq%d_%d" % (ps, i), 512, 512, BF16, p3[ps]) for i in range(8)] for ps in range(2)]
    CK_b = [[blk("ck%d_%d" % (ps, i), 512, 512, BF16, p3[ps]) for i in range(2)] for ps in range(2)]
    CV_b = [[blk("cv%d_%d" % (ps, i), 512, 256, BF16, p3[ps]) for i in range(4)] for ps in range(2)]
    OC_b = [blk("oc%d" % i, 512, 512, BF16, pairs4) for i in range(16)]
    aqkvT_l = itn("l_aqkvT", [1536, S])
    az_l = itn("l_az", [S, 512])
    abr_l = itn("l_abr", [S, 16])
    bqT_l = itn("l_bqT", [768, S], BF16)
    bkT_l = itn("l_bkT", [768, S], BF16)
    bv_l = itn("l_bv", [S, 768], BF16)
    x2T = itn("i_x2T", [D, TPC])

    def dst_fm1(name, i, ps):
        if name == "aqkv":
            t, head = i // 16, i % 16
            return A_b[ps][t * 4 + head % 4][0][(head // 4) * 128:(head // 4 + 1) * 128, :]
        qk, hd = i // 24, i % 24
        g, hs = hd // 8, hd % 8
        return Q_b[ps][qk * 3 + g][0][hs * 128:(hs + 1) * 128, :]

    def dst_tm1(name, i, ps, tt):
        if name == "az":
            j, hl = i // 4, i % 4
            return Z_b[ps][tt][0][j * 128:(j + 1) * 128, hl * 128:(hl + 1) * 128]
        g, hs = i // 8, i % 8
        j, hl = hs // 2, hs % 2
        return V_b[ps][tt][0][j * 128:(j + 1) * 128, (hl * 3 + g) * 128:(hl * 3 + g + 1) * 128]

    def dst_ab1(ps, g, tt):
        return AB_b[ps][0][g * 512 + tt * 128:g * 512 + (tt + 1) * 128, :]
    ab = dict(nqkv=6144, nz=2048, nab=64, nbqk=6144, nbv=3072)
    tp_build(dict(nc=nc, T_tot=TPC, inproj="ab", ab=ab, dst_fm=dst_fm1, dst_tm=dst_tm1, dst_ab=dst_ab1, cc_pairs=[p1[0], []],
                  TD=dict(xT=E["xT"], nwb=E["nw_mix0"], Win=E["Win0"], ropeC=E["ropeCb"], ropeS=E["ropeSb"], ropeP=E["ropePb"])))
    if stop == 1:
        return nc
    allgather_emit(nc, p1[1])
    if stop == 2:
        return nc
    jobs = []
    for hl in range(4):
        for t in range(3):
            for r in range(4):
                for ps in range(2):
                    c0 = r * TPC + ps * 512
                    jobs.append(((A_b[ps][t * 4 + hl][1], r * 512), 0, aqkvT_l[(hl * 3 + t) * 128:(hl * 3 + t + 1) * 128, c0:c0 + 512], F32))
    for r in range(4):
        for ps in range(2):
            for tt in range(4):
                r0 = r * TPC + ps * 512 + tt * 128
                jobs.append(((Z_b[ps][tt][1], r * 512), 0, az_l[r0:r0 + 128, :], F32))
                jobs.append(((V_b[ps][tt][1], r * 512), 0, bv_l[r0:r0 + 128, :], BF16))
                jobs.append(((AB_b[ps][1], r * 2048 + tt * 128), 3, abr_l[r0:r0 + 128, :], F32))
    for hl in range(2):
        for g in range(3):
            for r in range(4):
                for ps in range(2):
                    c0 = r * TPC + ps * 512
                    jobs.append(((Q_b[ps][g][1], r * 1024 + hl * 128), 2, bqT_l[(hl * 3 + g) * 128:(hl * 3 + g + 1) * 128, c0:c0 + 512], BF16))
                    jobs.append(((Q_b[ps][3 + g][1], r * 1024 + hl * 128), 2, bkT_l[(hl * 3 + g) * 128:(hl * 3 + g + 1) * 128, c0:c0 + 512], BF16))
    relayout_emit(nc, jobs, E["idx"])
    if stop == 3:
        return nc
    gdn_build(dict(nc=nc, S=S, NH=4, TD=dict(aqkvT=aqkvT_l, az=az_l, abr=abr_l, cw=E["cw"], alog=E["alog"], dtb=E["dtb"], onorm=E["onorm"],
                                             ident_in=E["ident_in"], ucum_in=E["ucum_in"], nm_in=E["nm_in"], oa_q=[b_[0] for b_ in OA_b])))
    if stop == 4:
        return nc
    dil_b_build(dict(nc=nc, S=S, NHS=2, TD=dict(bqT=bqT_l, bkT=bkT_l, bv=bv_l, bmask=E["bmask"], ob_q=[b_[0] for b_ in OB_b])))
    if stop == 5:
        return nc
    allgather_emit(nc, pairs2)
    if stop == 6:
        return nc
    o_src = []
    for ps in range(2):
        lst = []
        for k in range(16):
            j, hl = k // 4, k % 4
            lst.append((OA_b[hl * 2 + ps][1], j * 512, 0))
        for kk in range(8):
            j, hl = kk // 2, kk % 2
            lst.append((OB_b[hl * 2 + ps][1], j * 512, 0))
        o_src.append(lst)

    def dst_fm3(name, i, ps):
        if i < 32:
            return CQ_b[ps][i % 8][0][(i // 8) * 128:(i // 8 + 1) * 128, :]
        kvh = i - 32
        return CK_b[ps][kvh % 2][0][(kvh // 2) * 128:(kvh // 2 + 1) * 128, :]

    def dst_tm3(name, i, ps, tt):
        return CV_b[ps][tt][0][(i // 2) * 128:(i // 2 + 1) * 128, (i % 2) * 128:(i % 2 + 1) * 128]
    cc = dict(nq=4096, nk=1024, nv=1024)
    tp_build(dict(nc=nc, T_tot=TPC, oproj_K=3072, dff=DFF, inproj="c", c=cc, store_x=True, o_src=o_src, dst_fm=dst_fm3, dst_tm=dst_tm3, cc_pairs=[p3[0], []],
                  TD=dict(xT=E["xT"], idx=E["idx"], Wo=E["Wo0"], nwa=E["nw_ffn0"], Wg=E["Wg0"], Wu=E["Wu0"], Wd=E["Wd0"], nwb=E["nw_mix1"],
                          Win=E["Wqkv"], ropeC=E["ropeCc"], ropeS=E["ropeSc"], ropeP=E["ropePc"], gq=E["gq"], gk=E["gk"], xoT=x2T)))
    if stop == 7:
        return nc
    allgather_emit(nc, p3[1])
    if stop == 8:
        return nc
    attn_c_build(dict(nc=nc, S=S, NKV=2, REP=4, TD=dict(GQ=[[b_[1] for b_ in CQ_b[ps]] for ps in range(2)], GK=[[b_[1] for b_ in CK_b[ps]] for ps in range(2)],
                                                          GV=[[b_[1] for b_ in CV_b[ps]] for ps in range(2)], OCB=[b_[0] for b_ in OC_b], idx=E["idx"])))
    if stop == 9:
        return nc
    allgather_emit(nc, pairs4)
    if stop == 10:
        return nc
    o_src = []
    for ps in range(2):
        o_src.append([(OC_b[(k % 8) * 2 + ps][1], (k // 8) * 512, 0) for k in range(32)])
    tp_build(dict(nc=nc, T_tot=TPC, oproj_K=4096, dff=DFF, final_norm=True, o_src=o_src,
                  TD=dict(xT=x2T, idx=E["idx"], Wo=E["Wo1"], nwa=E["nw_ffn1"], Wg=E["Wg1"], Wu=E["Wu1"], Wd=E["Wd1"], nwb=E["nw_fin"], outT=outT)))
    return nc

import ml_dtypes as _mld

_BF = _mld.bfloat16
_PROGS = {}


def _lay(w):
    return np.ascontiguousarray(np.asarray(w, np.float32).reshape(-1, 128).T)


def _rope_tables_b(pos):
    inv = 1.0 / (500000.0 ** (np.arange(0, 32, 2, dtype=np.float32) / 32))
    ang = pos.astype(np.float32)[:, None] * inv[None, :]
    c, s = np.cos(ang), np.sin(ang)
    C = np.ascontiguousarray(np.concatenate([c, c], 1).T.astype(np.float32))
    S = np.ascontiguousarray(np.concatenate([s, s], 1).T.astype(np.float32))
    Pm = np.zeros((32, 32), np.float32)
    for i in range(16):
        Pm[i, 16 + i] = -1
        Pm[16 + i, i] = 1
    return C, S, np.ascontiguousarray(Pm.T)


def _rope_tables_c(pos):
    inv = 1.0 / (10000.0 ** (np.arange(0, 64, 2, dtype=np.float32) / 64))
    ar = (pos // 64).astype(np.float32)[:, None] * inv[None, :]
    ac = (pos % 64).astype(np.float32)[:, None] * inv[None, :]
    C = np.ascontiguousarray(np.concatenate([np.cos(ar), np.cos(ar), np.cos(ac), np.cos(ac)], 1).T.astype(np.float32))
    S = np.ascontiguousarray(np.concatenate([np.sin(ar), np.sin(ar), np.sin(ac), np.sin(ac)], 1).T.astype(np.float32))
    Pm = np.zeros((128, 128), np.float32)
    for i in range(32):
        Pm[i, 32 + i] = -1
        Pm[32 + i, i] = 1
        Pm[64 + i, 96 + i] = -1
        Pm[96 + i, 64 + i] = 1
    return C, S, np.ascontiguousarray(Pm.T)


def kernel(x, norm_mix, norm_ffn, norm_final, ab_w_in, ab_conv_w, ab_a_log, ab_dt_bias,
           ab_out_norm, ab_w_out, c_w_qkv, c_q_norm, c_k_norm, c_w_out,
           ffn_w_gate, ffn_w_up, ffn_w_down):
    f32 = np.float32
    A = lambda a: np.ascontiguousarray(np.asarray(a, f32))
    x = A(x)
    B, S, D = x.shape
    NCORE = 8
    TPC = B * S // NCORE
    QPB = S // TPC
    xf = x.reshape(B * S, D)
    if "F" not in _PROGS:
        _PROGS["F"] = build_fused()
    nc = _PROGS["F"]
    cst = gdn_consts()
    conv_w = A(ab_conv_w[0])
    a_log = A(ab_a_log[0])
    dt_b = A(ab_dt_bias[0])
    shared = dict(
        nw_mix0=_lay(norm_mix[0]), nw_mix1=_lay(norm_mix[1]), nw_ffn0=_lay(norm_ffn[0]), nw_ffn1=_lay(norm_ffn[1]), nw_fin=_lay(norm_final),
        Wg0=A(ffn_w_gate[0]), Wu0=A(ffn_w_up[0]), Wd0=A(ffn_w_down[0]), Wg1=A(ffn_w_gate[1]), Wu1=A(ffn_w_up[1]), Wd1=A(ffn_w_down[1]),
        Win0=A(ab_w_in[0]), Wo0=A(ab_w_out[0]), Wqkv=A(c_w_qkv[0]), Wo1=A(c_w_out[0]),
        gq=A(c_q_norm[0]).reshape(128, 1), gk=A(c_k_norm[0]).reshape(128, 1),
        onorm=np.ascontiguousarray(np.tile(A(ab_out_norm[0])[None, :], (128, 1))),
        ident_in=cst["ident"], ucum_in=cst["ucum"], nm_in=cst["nm"], bmask=dil_mask().astype(_BF))
    ims = []
    p = np.arange(128, dtype=np.int32)
    for c in range(NCORE):
        rk = c % QPB
        pos = np.arange(rk * TPC, (rk + 1) * TPC)
        Cb, Sb, Pb = _rope_tables_b(pos)
        Cc, Sc, Pc = _rope_tables_c(pos)
        heads = [4 * rk + i for i in range(4)]
        cwl = np.zeros((128, 12, 5), f32)
        for i, h in enumerate(heads):
            for t in range(3):
                cwl[:, i * 3 + t, :] = conv_w[:, t * 2048 + h * 128: t * 2048 + (h + 1) * 128].T
        al = np.array([a_log[d, h] for d in range(2) for h in heads], f32)
        db = np.array([dt_b[d, h] for d in range(2) for h in heads], f32)
        idx = np.stack([rk * 128 + p, 2 * (rk * 128 + p), rk * 256 + p, rk * 512 + p, p, p, p, p], 1).astype(np.int32)
        im = dict(shared)
        im.update(xT=np.ascontiguousarray(xf[c * TPC:(c + 1) * TPC].T), idx=np.ascontiguousarray(idx),
                  ropeCb=Cb, ropeSb=Sb, ropePb=Pb, ropeCc=Cc, ropeSc=Sc, ropePc=Pc, cw=cwl,
                  alog=np.ascontiguousarray(np.tile(al[None, :], (128, 1))), dtb=np.ascontiguousarray(np.tile(db[None, :], (128, 1))))
        ims.append(im)
    res = run_bass_kernel_spmd(nc, ims, core_ids=list(range(NCORE))).results
    out = np.concatenate([np.asarray(res[c]["outT"]).T for c in range(NCORE)], 0)
    return np.ascontiguousarray(out.reshape(B, S, D).astype(f32, copy=False))
```

```python
from contextlib import ExitStack
import numpy as np
import concourse.bass as bass
import concourse.mybir as mybir
from concourse.bass_utils import run_bass_kernel_spmd

F32 = mybir.dt.float32
BF16 = mybir.dt.bfloat16
AF = mybir.ActivationFunctionType
ALU = mybir.AluOpType
AX = mybir.AxisListType

ENGS = ("pe", "act", "dve", "pool", "sp")
_UNIQ = [0]


def uniq():
    _UNIQ[0] += 1
    return "u%d_" % _UNIQ[0]
NS_DMA = 6
SAME_ENGINE_SYNC = True


class Reg:
    __slots__ = ("name", "w", "r", "rd", "psum")

    def __init__(self, name="", psum=False):
        self.name = name
        self.psum = psum
        self.w = None
        self.r = {}
        self.rd = []


class Ins:
    __slots__ = ("eng", "fn", "deps", "inc", "dma", "idx", "sem", "target", "cnt", "cc")


class Prog:
    def __init__(self, nc):
        self.nc = nc
        self.q = {e: [] for e in ENGS}
        self.dmas = {e: [] for e in ENGS}
        self.ccs = []
        self.nshared = 0

    def op(self, eng, meth, kw, reads=(), writes=(), dma=False, cc=False):
        ins = Ins()
        ins.cc = cc
        if cc == "shared":
            dma = True
            self.nshared += 1
        elif cc:
            dma = True
            self.ccs.append(ins)
        ins.eng = eng
        ins.fn = (meth, kw)
        ins.dma = dma
        ins.inc = dma
        ins.idx = len(self.q[eng])
        ins.cnt = 0
        deps = set()
        for r in reads:
            if r.w is not None:
                deps.add(r.w)
            if r.psum:
                for e2, i2 in r.r.items():
                    if e2 != eng:
                        deps.add(i2)
        for w in writes:
            if w.w is not None:
                deps.add(w.w)
            deps.update(w.r.values())
            deps.update(w.rd)
        for r in reads:
            if dma:
                r.rd.append(ins)
            else:
                r.r[eng] = ins
        for w in writes:
            w.w = ins
            w.r = {}
            w.rd = []
        if dma and not cc:
            lst = self.dmas[eng]
            if len(lst) >= NS_DMA:
                deps.add(lst[len(lst) - NS_DMA])
            lst.append(ins)
        deps.discard(ins)
        fin = []
        for d in deps:
            if d.eng == eng and not d.dma:
                if eng == "pe" or not SAME_ENGINE_SYNC:
                    continue
            d.inc = True
            fin.append(d)
        ins.deps = fin
        self.q[eng].append(ins)
        return ins

    def emit(self, final_waits=()):
        nc = self.nc
        allsems = []

        def newsem(name):
            h = nc.alloc_semaphore(name=name)
            allsems.append(h)
            return h
        with ExitStack() as es:
            pf = uniq()
            csem = {e: newsem(pf + "c_" + e) for e in ENGS}
            dsem = {e: [newsem(pf + "d_%s%d" % (e, i)) for i in range(NS_DMA)]
                    for e in ENGS if self.dmas[e]}
            es_cc = []
            shared_sem = [newsem(pf + "ccs")] if self.nshared else [None]
            for e in ENGS:
                c = 0
                k = 0
                for ins in self.q[e]:
                    if ins.cc == "shared":
                        ins.sem = shared_sem[0]
                        ins.target = 0
                    elif ins.cc:
                        ins.sem = newsem(pf + "cc%d" % len(es_cc))
                        es_cc.append(ins)
                        ins.target = 1
                    elif ins.dma:
                        ins.sem = dsem[e][k % NS_DMA]
                        ins.target = 16 * (k // NS_DMA + 1)
                        k += 1
                    elif ins.inc:
                        c += 1
                        ins.cnt = c
            block = es.enter_context(nc.Block())

            def run(e, eng):
                waited = {}
                nsh = [0]
                for ins in self.q[e]:
                    need = {}
                    for d in ins.deps:
                        if d.dma:
                            key = d.sem
                            val = d.target
                        else:
                            key = csem[d.eng]
                            val = d.cnt
                        if need.get(key, 0) < val:
                            need[key] = val
                    for key, val in need.items():
                        if waited.get(key, 0) < val:
                            eng.wait_ge(key, val)
                            waited[key] = val
                    inst = getattr(eng, ins.fn[0])(**ins.fn[1])
                    if ins.cc:
                        inst.then_inc(ins.sem)
                        if ins.cc == "shared":
                            nsh[0] += 1
                    elif ins.dma:
                        inst.then_inc(ins.sem, 16)
                    elif ins.inc:
                        inst.then_inc(csem[e], 1)
                if nsh[0]:
                    eng.wait_ge(shared_sem[0], nsh[0])
                for d in self.ccs:
                    if d.eng == e and waited.get(d.sem, 0) < d.target:
                        eng.wait_ge(d.sem, d.target)
                        waited[d.sem] = d.target
                if self.dmas[e]:
                    lst = self.dmas[e]
                    for d in lst[-NS_DMA:]:
                        if waited.get(d.sem, 0) < d.target:
                            eng.wait_ge(d.sem, d.target)
                            waited[d.sem] = d.target

            @block.tensor
            def _(eng):
                run("pe", eng)

            @block.scalar
            def _(eng):
                run("act", eng)

            @block.vector
            def _(eng):
                run("dve", eng)

            @block.gpsimd
            def _(eng):
                run("pool", eng)

            @block.sync
            def _(eng):
                run("sp", eng)
        nc.all_engine_barrier()
        nc.clear_and_free_semaphores(allsems)
        nc.all_engine_barrier()


def gat(G, row_off, col_off, W):
    full = G.shape[1]
    A = full // W
    v = G if A == 1 else G.rearrange("r (a w) -> (r a) w", w=W)
    return dict(in_=v, element_offset=row_off * full + col_off)


def tp_build(cfg):
    D = cfg.get("D", 4096)
    KC = D // 128
    T = cfg.get("T", 512)
    T_tot = cfg["T_tot"]
    NTT = T // 128
    oK = cfg.get("oproj_K", 0)
    dff = cfg.get("dff", 0)
    inproj = cfg.get("inproj")
    final_norm = cfg.get("final_norm", False)
    store_x = cfg.get("store_x", False)
    FB = 8

    TT_ = cfg.get("TD")
    FUSED = TT_ is not None
    nc = cfg["nc"] if FUSED else bass.Bass("TRN2", target_bir_lowering=False)
    if FUSED:
        din = lambda name, shape, dt=F32: TT_[name]
        dout = lambda name, shape, dt=F32: TT_[name]
    else:
        din = lambda name, shape, dt=F32: nc.dram_tensor(name, shape, dt, kind="ExternalInput").ap()
        dout = lambda name, shape, dt=F32: nc.dram_tensor(name, shape, dt, kind="ExternalOutput").ap()
    xT = din("xT", [D, T_tot])
    if oK:
        if not FUSED:
            oT = din("oT", [oK, T_tot], BF16)
        Wo = din("Wo", [oK, D])
    if dff:
        nwa = din("nwa", [128, KC])
        Wg = din("Wg", [D, dff])
        Wu = din("Wu", [D, dff])
        Wd = din("Wd", [dff, D])
    if inproj or final_norm:
        nwb = din("nwb", [128, KC])
    if store_x:
        xoT = dout("xoT", [D, T_tot])
    if final_norm:
        outT = dout("outT", [D, T_tot])
    if inproj == "ab":
        ab = cfg["ab"]
        ncols = ab["nqkv"] + ab["nz"] + ab["nab"] + ab["nbqk"] + ab["nbv"]
        Win = din("Win", [D, ncols])
        ropeC = din("ropeC", [32, T_tot])
        ropeS = din("ropeS", [32, T_tot])
        ropeP = din("ropeP", [32, 32])
        if FUSED:
            aqkvT = az = abo = bqkT = bv = None
        else:
            aqkvT = dout("aqkvT", [ab["nqkv"], T_tot])
            az = dout("az", [T_tot, ab["nz"]])
            abo = dout("ab", [T_tot, ab["nab"]])
            bqkT = dout("bqkT", [ab["nbqk"], T_tot], BF16)
            bv = dout("bv", [T_tot, ab["nbv"]], BF16)
    if inproj == "c":
        cc = cfg["c"]
        ncols = cc["nq"] + cc["nk"] + cc["nv"]
        Win = din("Win", [D, ncols])
        ropeC = din("ropeC", [128, T_tot])
        ropeS = din("ropeS", [128, T_tot])
        ropeP = din("ropeP", [128, 128])
        gq = din("gq", [128, 1])
        gk = din("gk", [128, 1])
        if FUSED:
            cqkT = cv = None
        else:
            cqkT = dout("cqkT", [cc["nq"] + cc["nk"], T_tot], BF16)
            cv = dout("cv", [T_tot, cc["nv"]], BF16)

    P = Prog(nc)
    with ExitStack() as es:
        es.enter_context(nc.allow_low_precision("bf16 matmul operands, fp32 accumulate"))
        pfx = uniq()
        sb = lambda name, shape, dt: es.enter_context(nc.sbuf_tensor(pfx + name, shape, dt))
        psb = lambda name: es.enter_context(nc.psum_tensor(pfx + name, [128, 512], F32))
        R = Reg
        xs = sb("xs", [128, KC, T], F32)
        hT = sb("hT", [128, KC, T], BF16)
        ones = sb("ones", [128, 128], BF16)
        epsb = sb("epsb", [128, 1], F32)
        sq = [sb("sq%d" % i, [128, T], BF16) for i in range(2)]
        rstd = sb("rstd", [128, T], F32)
        st = [sb("st%d" % i, [128, T], F32) for i in range(2)]
        stb = [sb("stb%d" % i, [128, T], BF16) for i in range(2)]
        aT = [sb("aT%d" % i, [128, FB, T], BF16) for i in range(2)]
        wA = [sb("wA%d" % i, [128, KC, 128], BF16) for i in range(4)]
        wd = [sb("wd%d" % i, [128, FB, 512], BF16) for i in range(2)]
        nwa_s = sb("nwa_s", [128, KC], F32)
        nwb_s = sb("nwb_s", [128, KC], F32)
        ps_ss = psb("ps_ss")
        psA = [psb("psA%d" % i) for i in range(4)]
        ps_y = [psb("ps_y%d" % i) for i in range(2)]
        r_xs = [R() for c in range(KC)]
        r_hT = [R() for c in range(KC)]
        r_ones, r_eps, r_rstd, r_ss, r_nwa, r_nwb = R(), R(), R(), R(psum=True), R(), R()
        r_sq = [R(), R()]
        r_st = [R(), R()]
        r_stb = [R(), R()]
        r_aT = [[R() for j in range(FB)] for i in range(2)]
        r_wA = [R() for i in range(4)]
        r_wd = [R(), R()]
        r_psA = [R(psum=True) for i in range(4)]
        r_py = [R(psum=True), R(psum=True)]
        cnt = dict(wA=0, wd=0, psA=0, py=0, st=0, stb=0)

        def nxt(k, n):
            v = cnt[k] % n
            cnt[k] += 1
            return v

        P.op("pool", "memset", dict(ap=ones[:], constant=1.0), writes=[r_ones])
        P.op("pool", "memset", dict(ap=epsb[:], constant=1e-6), writes=[r_eps])
        if dff:
            P.op("sp", "dma_start", dict(out=nwa_s[:], in_=nwa), writes=[r_nwa], dma=True)
        if inproj or final_norm:
            P.op("sp", "dma_start", dict(out=nwb_s[:], in_=nwb), writes=[r_nwb], dma=True)
        if inproj == "ab":
            rC = sb("rC", [32, T_tot], F32)
            rS = sb("rS", [32, T_tot], F32)
            rP = sb("rP", [32, 32], BF16)
            t1 = sb("t1", [32, T], F32)
            t2 = sb("t2", [32, T], F32)
            r_rope, r_t1, r_t2 = R(), R(), R()
            P.op("sp", "dma_start", dict(out=rC[:], in_=ropeC), writes=[r_rope], dma=True)
            P.op("sp", "dma_start", dict(out=rS[:], in_=ropeS), writes=[r_rope], dma=True)
            P.op("pool", "dma_start", dict(out=rP[:], in_=ropeP), writes=[r_rope], dma=True)
        if inproj == "c":
            rC = sb("rC", [128, T_tot], F32)
            rS = sb("rS", [128, T_tot], F32)
            rP = sb("rP", [128, 128], BF16)
            gq_s = sb("gq_s", [128, 1], F32)
            gk_s = sb("gk_s", [128, 1], F32)
            t1 = sb("t1", [128, T], F32)
            t2 = sb("t2", [128, T], F32)
            rn = sb("rn", [128, T], F32)
            r_rope, r_t1, r_t2, r_rn = R(), R(), R(), R()
            P.op("sp", "dma_start", dict(out=rC[:], in_=ropeC), writes=[r_rope], dma=True)
            P.op("sp", "dma_start", dict(out=rS[:], in_=ropeS), writes=[r_rope], dma=True)
            P.op("pool", "dma_start", dict(out=rP[:], in_=ropeP), writes=[r_rope], dma=True)
            P.op("sp", "dma_start", dict(out=gq_s[:], in_=gq), writes=[r_rope], dma=True)
            P.op("sp", "dma_start", dict(out=gk_s[:], in_=gk), writes=[r_rope], dma=True)

        xT_v = xT.rearrange("(c p) t -> p c t", p=128)
        if FUSED and oK:
            idx_s = sb("idx_s", [128, 8], mybir.dt.int32)
            r_idx = R()
            P.op("sp", "dma_start", dict(out=idx_s[:], in_=TT_["idx"]), writes=[r_idx], dma=True)

        def rmsnorm(nw_s, r_nw, to_x=False):
            for c in range(KC):
                s = c % 2
                P.op("act", "activation", dict(out=sq[s][:], in_=xs[:, c, :], func=AF.Square),
                     reads=[r_xs[c]], writes=[r_sq[s]])
                P.op("pe", "matmul", dict(out=ps_ss[:], lhsT=ones[:], rhs=sq[s][:], start=(c == 0), stop=(c == KC - 1)),
                     reads=[r_ones, r_sq[s]], writes=[r_ss])
            P.op("act", "activation", dict(out=rstd[:], in_=ps_ss[:], func=AF.Sqrt, scale=1.0 / D, bias=epsb[:, 0:1]),
                 reads=[r_ss, r_eps], writes=[r_rstd])
            P.op("dve", "reciprocal", dict(out=rstd[:], in_=rstd[:]), reads=[r_rstd], writes=[r_rstd])
            for c in range(KC):
                if to_x:
                    P.op("dve", "scalar_tensor_tensor",
                         dict(out=xs[:, c, :], in0=xs[:, c, :], scalar=nw_s[:, c:c + 1], in1=rstd[:], op0=ALU.mult, op1=ALU.mult),
                         reads=[r_xs[c], r_nw, r_rstd], writes=[r_xs[c]])
                else:
                    P.op("dve", "scalar_tensor_tensor",
                         dict(out=hT[:, c, :], in0=xs[:, c, :], scalar=nw_s[:, c:c + 1], in1=rstd[:], op0=ALU.mult, op1=ALU.mult),
                         reads=[r_xs[c], r_nw, r_rstd], writes=[r_hT[c]])

        def accum_block(W_v, k0, nk, sl, rhs_regs):
            for cb in range(D // 512):
                w = nxt("wd", 2)
                P.op("pool", "dma_start", dict(out=wd[w][:, 0:nk, :], in_=W_v[:, k0:k0 + nk, cb * 512:(cb + 1) * 512]),
                     writes=[r_wd[w]], dma=True)
                for ct in range(4):
                    pb = nxt("py", 2)
                    for j in range(nk):
                        P.op("pe", "matmul", dict(out=ps_y[pb][:], lhsT=wd[w][:, j, ct * 128:(ct + 1) * 128], rhs=aT[sl][:, j, :],
                                                  start=(j == 0), stop=(j == nk - 1)),
                             reads=[r_wd[w], rhs_regs[j]], writes=[r_py[pb]])
                    c = cb * 4 + ct
                    P.op("dve", "tensor_tensor", dict(out=xs[:, c, :], in0=xs[:, c, :], in1=ps_y[pb][:], op=ALU.add),
                         reads=[r_xs[c], r_py[pb]], writes=[r_xs[c]])

        def fm_tile(W_v, col0, epilogue):
            w = nxt("wA", 4)
            P.op("pool", "dma_start", dict(out=wA[w][:], in_=W_v[:, :, col0:col0 + 128]), writes=[r_wA[w]], dma=True)
            pb = nxt("psA", 4)
            for kc in range(KC):
                P.op("pe", "matmul", dict(out=psA[pb][:], lhsT=wA[w][:, kc, :], rhs=hT[:, kc, :], start=(kc == 0), stop=(kc == KC - 1)),
                     reads=[r_wA[w], r_hT[kc]], writes=[r_psA[pb]])
            epilogue(psA[pb], r_psA[pb])

        def tm_tile(W_v, col0, ncl, out_ap, t0, dt, ocol, ab_special=False, fname=None, ti=0):
            w = nxt("wA", 4)
            P.op("pool", "dma_start", dict(out=wA[w][:, :, 0:ncl], in_=W_v[:, :, col0:col0 + ncl]), writes=[r_wA[w]], dma=True)
            pb = nxt("psA", 4)
            for tt in range(NTT):
                for kc in range(KC):
                    P.op("pe", "matmul", dict(out=psA[pb][:, tt * 128:tt * 128 + ncl], lhsT=hT[:, kc, tt * 128:(tt + 1) * 128],
                                              rhs=wA[w][:, kc, 0:ncl], start=(kc == 0), stop=(kc == KC - 1)),
                         reads=[r_wA[w], r_hT[kc]], writes=[r_psA[pb]])
            src = psA[pb][:].rearrange("p (t c) -> p t c", c=128)[:, :, 0:ncl]
            if dt == F32:
                s = nxt("st", 2)
                buf, rb = st[s], r_st[s]
            else:
                s = nxt("stb", 2)
                buf, rb = stb[s], r_stb[s]
            dst = buf[:].rearrange("p (t c) -> p t c", c=128)[:, :, 0:ncl]
            P.op("act", "copy", dict(out=dst, in_=src), reads=[r_psA[pb]], writes=[rb])
            if ab_special:
                for g_ in range(4):
                    for tt_ in range(NTT):
                        srcv = dst.rearrange("p t (x g h) -> p t x g h", x=4, g=4)[:, tt_, :, g_, :]
                        dstv = cfg["dst_ab"](t0 // T, g_, tt_).rearrange("p (x h) -> p x h", x=4)
                        P.op("sp", "dma_start", dict(out=dstv, in_=srcv), reads=[rb] + XB, dma=True)
                return
            if FUSED:
                for tt_ in range(NTT):
                    P.op("sp", "dma_start", dict(out=cfg["dst_tm"](fname, ti, t0 // T, tt_), in_=dst[:, tt_, :]), reads=[rb] + XB, dma=True)
                return
            P.op("sp", "dma_start", dict(out=out_ap[t0:t0 + T, ocol:ocol + ncl].rearrange("(t p) c -> p t c", p=128), in_=dst),
                 reads=[rb], dma=True)

        for t0 in range(0, T_tot, T):
            r_xb = R()
            XB = [r_xb] if FUSED else []
            for c in range(KC):
                P.op("sp", "dma_start", dict(out=xs[:, c, :], in_=xT_v[:, c, t0:t0 + T]), writes=[r_xs[c]], dma=True)
            if oK:
                if not FUSED:
                    oT_v = oT.rearrange("(c p) t -> p c t", p=128)
                Wo_v = Wo.rearrange("(c p) n -> p c n", p=128)
                nob = oK // 128
                blocks = [(k0, min(k0 + FB, nob)) for k0 in range(0, nob, FB)]
                for bi, (k0, k1) in enumerate(blocks):
                    sl = bi % 2
                    for j in range(k1 - k0):
                        if FUSED:
                            G_, off_, ic_ = cfg["o_src"][t0 // T][k0 + j]
                            P.op("pool", "indirect_dma_start",
                                 dict(out=aT[sl][:, j, :], out_offset=None,
                                      in_offset=bass.IndirectOffsetOnAxis(ap=idx_s[:, ic_:ic_ + 1], axis=0), **gat(G_, off_, 0, T)),
                                 reads=[r_idx], writes=[r_aT[sl][j]], dma=True)
                        else:
                            P.op("sp", "dma_start", dict(out=aT[sl][:, j, :], in_=oT_v[:, k0 + j, t0:t0 + T]),
                                 writes=[r_aT[sl][j]], dma=True)
                    accum_block(Wo_v, k0, k1 - k0, sl, r_aT[sl])
            if dff:
                Wg_v = Wg.rearrange("(c p) n -> p c n", p=128)
                Wu_v = Wu.rearrange("(c p) n -> p c n", p=128)
                Wd_v = Wd.rearrange("(c p) n -> p c n", p=128)
                rmsnorm(nwa_s, r_nwa)
                NF = dff // 128
                blocks = [(f0, min(f0 + FB, NF)) for f0 in range(0, NF, FB)]

                def gate_up(bi):
                    f0, f1 = blocks[bi]
                    sl = bi % 2
                    for f in range(f0, f1):
                        res = []

                        def keep(ps_t, r_t):
                            res.append((ps_t, r_t))
                        fm_tile(Wg_v, f * 128, keep)
                        fm_tile(Wu_v, f * 128, keep)
                        (pg, rg), (pu, ru) = res
                        s = nxt("st", 2)
                        P.op("act", "activation", dict(out=st[s][:], in_=pg[:], func=AF.Silu), reads=[rg], writes=[r_st[s]])
                        P.op("dve", "tensor_tensor", dict(out=aT[sl][:, f - f0, :], in0=st[s][:], in1=pu[:], op=ALU.mult),
                             reads=[r_st[s], ru], writes=[r_aT[sl][f - f0]])

                gate_up(0)
                for bi in range(1, len(blocks)):
                    gate_up(bi)
                    f0, f1 = blocks[bi - 1]
                    accum_block(Wd_v, f0, f1 - f0, (bi - 1) % 2, r_aT[(bi - 1) % 2])
                f0, f1 = blocks[-1]
                accum_block(Wd_v, f0, f1 - f0, (len(blocks) - 1) % 2, r_aT[(len(blocks) - 1) % 2])
            if store_x:
                xo_v = xoT.rearrange("(c p) t -> p c t", p=128)
                for c in range(KC):
                    P.op("sp", "dma_start", dict(out=xo_v[:, c, t0:t0 + T], in_=xs[:, c, :]), reads=[r_xs[c]], dma=True)
            if inproj:
                Win_v = Win.rearrange("(c p) n -> p c n", p=128)
                rmsnorm(nwb_s, r_nwb)

                def ep_copy(out_ap, row0, dt, fname=None, ti=0):
                    def ep(ps_t, r_t):
                        if dt == F32:
                            s = nxt("st", 2)
                            buf, rb = st[s], r_st[s]
                        else:
                            s = nxt("stb", 2)
                            buf, rb = stb[s], r_stb[s]
                        P.op("act", "copy", dict(out=buf[:], in_=ps_t[:]), reads=[r_t], writes=[rb])
                        dst_ = cfg["dst_fm"](fname, ti, t0 // T) if FUSED else out_ap[row0:row0 + 128, t0:t0 + T]
                        P.op("sp", "dma_start", dict(out=dst_, in_=buf[:]), reads=[rb] + XB, dma=True)
                    return ep

            if inproj == "ab":
                def ep_rope_b(row0, ti=0):
                    def ep(ps_t, r_t):
                        s = nxt("stb", 2)
                        buf, rb = stb[s], r_stb[s]
                        P.op("act", "copy", dict(out=buf[:], in_=ps_t[:]), reads=[r_t], writes=[rb])
                        pb = nxt("py", 2)
                        P.op("pe", "matmul", dict(out=ps_y[pb][0:32, :], lhsT=rP[:, :], rhs=buf[0:32, :], start=True, stop=True),
                             reads=[r_rope, rb], writes=[r_py[pb]])
                        P.op("dve", "tensor_tensor", dict(out=t1[:], in0=buf[0:32, :], in1=rC[:, t0:t0 + T], op=ALU.mult),
                             reads=[rb, r_rope], writes=[r_t1])
                        P.op("dve", "tensor_tensor", dict(out=t2[:], in0=ps_y[pb][0:32, :], in1=rS[:, t0:t0 + T], op=ALU.mult),
                             reads=[r_py[pb], r_rope], writes=[r_t2])
                        P.op("dve", "tensor_tensor", dict(out=buf[0:32, :], in0=t1[:], in1=t2[:], op=ALU.add),
                             reads=[r_t1, r_t2], writes=[rb])
                        dst_ = cfg["dst_fm"]("bqk", ti, t0 // T) if FUSED else bqkT[row0:row0 + 128, t0:t0 + T]
                        P.op("sp", "dma_start", dict(out=dst_, in_=buf[:]), reads=[rb] + XB, dma=True)
                    return ep
                c0 = 0
                for i in range(ab["nqkv"] // 128):
                    fm_tile(Win_v, c0 + i * 128, ep_copy(aqkvT, i * 128, F32, "aqkv", i))
                c0 += ab["nqkv"]
                for i in range(ab["nz"] // 128):
                    if FUSED:
                        tm_tile(Win_v, c0 + i * 128, 128, None, t0, F32, 0, fname="az", ti=i)
                    else:
                        tm_tile(Win_v, c0 + i * 128, 128, az, t0, F32, i * 128)
                c0 += ab["nz"]
                tm_tile(Win_v, c0, ab["nab"], abo, t0, F32, 0, ab_special=FUSED)
                c0 += ab["nab"]
                for i in range(ab["nbqk"] // 128):
                    fm_tile(Win_v, c0 + i * 128, ep_rope_b(i * 128, i))
                c0 += ab["nbqk"]
                for i in range(ab["nbv"] // 128):
                    if FUSED:
                        tm_tile(Win_v, c0 + i * 128, 128, None, t0, BF16, 0, fname="bv", ti=i)
                    else:
                        tm_tile(Win_v, c0 + i * 128, 128, bv, t0, BF16, i * 128)
            if inproj == "c":
                def ep_c(row0, g_s, ti=0):
                    def ep(ps_t, r_t):
                        s = nxt("stb", 2)
                        buf, rb = stb[s], r_stb[s]
                        P.op("act", "activation", dict(out=sq[0][:], in_=ps_t[:], func=AF.Square), reads=[r_t], writes=[r_sq[0]])
                        pb = nxt("py", 2)
                        P.op("pe", "matmul", dict(out=ps_y[pb][:], lhsT=ones[:], rhs=sq[0][:], start=True, stop=True),
                             reads=[r_ones, r_sq[0]], writes=[r_py[pb]])
                        P.op("act", "activation", dict(out=rn[:], in_=ps_y[pb][:], func=AF.Sqrt, scale=1.0 / 128, bias=epsb[:, 0:1]),
                             reads=[r_py[pb], r_eps], writes=[r_rn])
                        P.op("dve", "reciprocal", dict(out=rn[:], in_=rn[:]), reads=[r_rn], writes=[r_rn])
                        P.op("dve", "scalar_tensor_tensor",
                             dict(out=buf[:], in0=ps_t[:], scalar=g_s[:, 0:1], in1=rn[:], op0=ALU.mult, op1=ALU.mult),
                             reads=[r_t, r_rope, r_rn], writes=[rb])
                        pb2 = nxt("py", 2)
                        P.op("pe", "matmul", dict(out=ps_y[pb2][:], lhsT=rP[:, :], rhs=buf[:], start=True, stop=True),
                             reads=[r_rope, rb], writes=[r_py[pb2]])
                        P.op("dve", "tensor_tensor", dict(out=t1[:], in0=buf[:], in1=rC[:, t0:t0 + T], op=ALU.mult),
                             reads=[rb, r_rope], writes=[r_t1])
                        P.op("dve", "tensor_tensor", dict(out=t2[:], in0=ps_y[pb2][:], in1=rS[:, t0:t0 + T], op=ALU.mult),
                             reads=[r_py[pb2], r_rope], writes=[r_t2])
                        P.op("dve", "tensor_tensor", dict(out=buf[:], in0=t1[:], in1=t2[:], op=ALU.add),
                             reads=[r_t1, r_t2], writes=[rb])
                        dst_ = cfg["dst_fm"]("cqk", ti, t0 // T) if FUSED else cqkT[row0:row0 + 128, t0:t0 + T]
                        P.op("sp", "dma_start", dict(out=dst_, in_=buf[:]), reads=[rb] + XB, dma=True)
                    return ep
                nqt = cc["nq"] // 128
                nkt = cc["nk"] // 128
                for i in range(nqt):
                    fm_tile(Win_v, i * 128, ep_c(i * 128, gq_s, i))
                for i in range(nkt):
                    fm_tile(Win_v, cc["nq"] + i * 128, ep_c(cc["nq"] + i * 128, gk_s, nqt + i))
                for i in range(cc["nv"] // 128):
                    if FUSED:
                        tm_tile(Win_v, cc["nq"] + cc["nk"] + i * 128, 128, None, t0, BF16, 0, fname="cv", ti=i)
                    else:
                        tm_tile(Win_v, cc["nq"] + cc["nk"] + i * 128, 128, cv, t0, BF16, i * 128)
            if FUSED and cfg.get("cc_pairs") and cfg["cc_pairs"][t0 // T]:
                rg_ = [[0, 1, 2, 3], [4, 5, 6, 7]]
                for i_, (src_, dst_g) in enumerate(cfg["cc_pairs"][t0 // T]):
                    P.op("pool", "collective_compute", dict(kind="AllGather", op=ALU.bypass, replica_groups=rg_, ins=[src_.opt()], outs=[dst_g.opt()]),
                         writes=([r_xb] if i_ == 0 else []), cc="shared")
            if final_norm:
                rmsnorm(nwb_s, r_nwb, to_x=True)
                o_v = outT.rearrange("(c p) t -> p c t", p=128)
                for c in range(KC):
                    P.op("sp", "dma_start", dict(out=o_v[:, c, t0:t0 + T], in_=xs[:, c, :]), reads=[r_xs[c]], dma=True)
        P.emit()
    if FUSED:
        nc.all_engine_barrier()
    return nc


def attn_c_build(cfg):
    S = cfg.get("S", 4096)
    NKV = cfg.get("NKV", 2)
    REP = cfg.get("REP", 4)
    NQH = NKV * REP
    NKC = S // 128
    QB = 512
    scale = 128 ** -0.5
    TT_ = cfg.get("TD")
    FUSED = TT_ is not None
    if FUSED:
        nc = cfg["nc"]
        GQ, GK, GV, OCB = TT_["GQ"], TT_["GK"], TT_["GV"], TT_["OCB"]
    else:
        nc = bass.Bass("TRN2", target_bir_lowering=False)
        cqT = nc.dram_tensor("cqT", [NQH * 128, S], BF16, kind="ExternalInput").ap()
        ckT = nc.dram_tensor("ckT", [NKV * 128, S], BF16, kind="ExternalInput").ap()
        cv = nc.dram_tensor("cv", [S, NKV * 128], BF16, kind="ExternalInput").ap()
        ocT = nc.dram_tensor("ocT", [NQH * 128, S], BF16, kind="ExternalOutput").ap()
    P = Prog(nc)
    with ExitStack() as es:
        es.enter_context(nc.allow_low_precision("bf16 matmul operands, fp32 accumulate"))
        pfx = uniq()
        sb = lambda name, shape, dt: es.enter_context(nc.sbuf_tensor(pfx + name, shape, dt))
        psb = lambda name: es.enter_context(nc.psum_tensor(pfx + name, [128, 512], F32))
        R = Reg
        kT = [sb("kT%d" % i, [128, S], BF16) for i in range(2)]
        vv = [sb("v%d" % i, [128, NKC, 128], BF16) for i in range(2)]
        qT = [sb("qT%d" % i, [128, S], BF16) for i in range(2)]
        NE = 3
        ee = [sb("e%d" % i, [128, QB], BF16) for i in range(NE)]
        rz = sb("rz", [128, QB], F32)
        ob = [sb("ob%d" % i, [128, QB], BF16) for i in range(2)]
        ones = sb("ones", [128, 128], BF16)
        ps_s = [psb("ps_s%d" % i) for i in range(NE)]
        ps_o = [psb("ps_o%d" % i) for i in range(2)]
        ps_z = [psb("ps_z%d" % i) for i in range(2)]
        r_kT, r_v, r_qT = [R(), R()], [R(), R()], [R(), R()]
        r_e = [R() for i in range(NE)]
        r_rz, r_ones = R(), R()
        r_ob = [R(), R()]
        r_ps = [R(psum=True) for i in range(NE)]
        r_po, r_pz = [R(psum=True), R(psum=True)], [R(psum=True), R(psum=True)]
        P.op("pool", "memset", dict(ap=ones[:], constant=1.0), writes=[r_ones])
        if FUSED:
            idx_s = sb("idx_s", [128, 8], mybir.dt.int32)
            r_idx = R()
            P.op("sp", "dma_start", dict(out=idx_s[:], in_=TT_["idx"]), writes=[r_idx], dma=True)
            QS = S // 4

            def gather(out_ap, G_, off_, span_, ic_, cols, wr):
                P.op("pool", "indirect_dma_start",
                     dict(out=out_ap, out_offset=None,
                          in_offset=bass.IndirectOffsetOnAxis(ap=idx_s[:, ic_:ic_ + 1], axis=0), **gat(G_, off_, cols.start, cols.stop - cols.start)),
                     reads=[r_idx], writes=[wr], dma=True)
        it = 0
        blk = 0
        pending = []
        rg_ = [[0, 1, 2, 3], [4, 5, 6, 7]]

        def flush():
            for h_, rr_ in pending:
                for th_ in range(2):
                    P.op("pool", "collective_compute", dict(kind="AllGather", op=ALU.bypass, replica_groups=rg_,
                                                            ins=[OCB[h_ * 2 + th_].opt()], outs=[TT_["OCG"][h_ * 2 + th_].opt()]),
                         writes=([rr_] if th_ == 0 else []), cc="shared")
            del pending[:]
        for kv in range(NKV):
            ks = kv % 2
            if FUSED:
                for r_ in range(4):
                    for ps_ in range(2):
                        gather(kT[ks][:, r_ * QS + ps_ * 512:r_ * QS + (ps_ + 1) * 512], GK[ps_][kv], r_ * 512, None, 0, slice(0, 512), r_kT[ks])
                for c_ in range(NKC):
                    r_, tl_ = c_ // 8, c_ % 8
                    gather(vv[ks][:, c_, :], GV[tl_ // 4][tl_ % 4], r_ * 512, None, 1, slice(kv * 128, (kv + 1) * 128), r_v[ks])
            else:
                P.op("sp", "dma_start", dict(out=kT[ks][:], in_=ckT[kv * 128:(kv + 1) * 128, :]), writes=[r_kT[ks]], dma=True)
                P.op("sp", "dma_start", dict(out=vv[ks][:], in_=cv[:, kv * 128:(kv + 1) * 128].rearrange("(c p) d -> p c d", p=128)),
                     writes=[r_v[ks]], dma=True)
            for r in range(REP):
                h = kv * REP + r
                qs = h % 2
                if FUSED:
                    for r_ in range(4):
                        for ps_ in range(2):
                            gather(qT[qs][:, r_ * QS + ps_ * 512:r_ * QS + (ps_ + 1) * 512], GQ[ps_][h], r_ * 512, None, 0, slice(0, 512), r_qT[qs])
                else:
                    P.op("sp", "dma_start", dict(out=qT[qs][:], in_=cqT[h * 128:(h + 1) * 128, :]), writes=[r_qT[qs]], dma=True)
                if FUSED:
                    flush()
                    r_oh = R()
                for qb in range(S // QB):
                    pb = blk % 2
                    blk += 1
                    qsl = qT[qs][:, qb * QB:(qb + 1) * QB]

                    def smm(kc, i):
                        P.op("pe", "matmul", dict(out=ps_s[i][:], lhsT=kT[ks][:, kc * 128:(kc + 1) * 128], rhs=qsl, start=True, stop=True),
                             reads=[r_kT[ks], r_qT[qs]], writes=[r_ps[i]])
                    smm(0, it % NE)
                    for kc in range(NKC):
                        i = it % NE
                        it += 1
                        if kc + 1 < NKC:
                            smm(kc + 1, it % NE)
                        P.op("act", "activation", dict(out=ee[i][:], in_=ps_s[i][:], func=AF.Exp, scale=scale),
                             reads=[r_ps[i]], writes=[r_e[i]])
                        P.op("pe", "matmul", dict(out=ps_o[pb][:], lhsT=vv[ks][:, kc, :], rhs=ee[i][:], start=(kc == 0), stop=(kc == NKC - 1)),
                             reads=[r_v[ks], r_e[i]], writes=[r_po[pb]])
                        P.op("pe", "matmul", dict(out=ps_z[pb][:], lhsT=ones[:], rhs=ee[i][:], start=(kc == 0), stop=(kc == NKC - 1)),
                             reads=[r_ones, r_e[i]], writes=[r_pz[pb]])
                    P.op("dve", "reciprocal", dict(out=rz[:], in_=ps_z[pb][:]), reads=[r_pz[pb]], writes=[r_rz])
                    P.op("dve", "tensor_tensor", dict(out=ob[pb][:], in0=ps_o[pb][:], in1=rz[:], op=ALU.mult),
                         reads=[r_po[pb], r_rz], writes=[r_ob[pb]])
                    if FUSED:
                        q_, th_ = qb // 2, qb % 2
                        P.op("sp", "dma_start", dict(out=OCB[h * 2 + th_][q_ * 128:(q_ + 1) * 128, :], in_=ob[pb][:]),
                             reads=[r_ob[pb], r_oh], dma=True)
                    else:
                        P.op("sp", "dma_start", dict(out=ocT[h * 128:(h + 1) * 128, qb * QB:(qb + 1) * QB], in_=ob[pb][:]),
                             reads=[r_ob[pb]], dma=True)
                if FUSED:
                    pending.append((h, r_oh))
        if FUSED:
            flush()
        P.emit()
    if FUSED:
        nc.all_engine_barrier()
    return nc


B_PATTERNS = ((128, 1), (512, 4), (2048, 16))


def ssl(a, n, d):
    return slice(a, a + d * (n - 1) + 1, d)


def dil_b_build(cfg):
    S = cfg.get("S", 4096)
    NHS = cfg.get("NHS", 2)
    pats = cfg.get("pats", B_PATTERNS)
    NG = len(pats)
    scale = 128 ** -0.5
    TT_ = cfg.get("TD")
    FUSED = TT_ is not None
    if FUSED:
        nc = cfg["nc"]
        bqT, bkT, bv, bmask, ob_q = TT_["bqT"], TT_["bkT"], TT_["bv"], TT_["bmask"], TT_["ob_q"]
    else:
        nc = bass.Bass("TRN2", target_bir_lowering=False)
        bqT = nc.dram_tensor("bqT", [NHS * NG * 128, S], BF16, kind="ExternalInput").ap()
        bkT = nc.dram_tensor("bkT", [NHS * NG * 128, S], BF16, kind="ExternalInput").ap()
        bv = nc.dram_tensor("bv", [S, NHS * NG * 128], BF16, kind="ExternalInput").ap()
        bmask = nc.dram_tensor("bmask", [128, 3, 512], BF16, kind="ExternalInput").ap()
        obT = nc.dram_tensor("obT", [NHS * 128, S], BF16, kind="ExternalOutput").ap()
    P = Prog(nc)
    with ExitStack() as es:
        es.enter_context(nc.allow_low_precision("bf16 matmul operands, fp32 accumulate"))
        pfx = uniq()
        sb = lambda name, shape, dt: es.enter_context(nc.sbuf_tensor(pfx + name, shape, dt))
        psb = lambda name: es.enter_context(nc.psum_tensor(pfx + name, [128, 512], F32))
        R = Reg
        qT = [sb("qT%d" % i, [128, S], BF16) for i in range(2)]
        kT = [sb("kT%d" % i, [128, S], BF16) for i in range(2)]
        vp = [sb("vp%d" % i, [128, S // 128, 128], BF16) for i in range(2)]
        Uacc = sb("Uacc", [128, S], F32)
        Zacc = sb("Zacc", [128, S], F32)
        ob = sb("ob", [128, S], BF16)
        ee = [sb("e%d" % i, [128, 512], BF16) for i in range(2)]
        em = [sb("em%d" % i, [128, 512], BF16) for i in range(2)]
        mk = sb("mk", [128, 3, 512], BF16)
        ones = sb("ones", [128, 128], BF16)
        ps_s = [psb("ps_s%d" % i) for i in range(2)]
        ps_o = [psb("ps_o%d" % i) for i in range(2)]
        ps_z = [psb("ps_z%d" % i) for i in range(2)]
        r_q, r_k, r_v = [R(), R()], [R(), R()], [R(), R()]
        r_U, r_Z, r_ob, r_mk, r_ones = R(), R(), R(), R(), R()
        r_e, r_em = [R(), R()], [R(), R()]
        r_ps, r_po, r_pz = [R(psum=True), R(psum=True)], [R(psum=True), R(psum=True)], [R(psum=True), R(psum=True)]
        P.op("pool", "memset", dict(ap=ones[:], constant=1.0), writes=[r_ones])
        P.op("sp", "dma_start", dict(out=mk[:], in_=bmask), writes=[r_mk], dma=True)
        if FUSED:
            for src_, dst_ in TT_.get("cc_pairs", []):
                P.op("pool", "collective_compute", dict(kind="AllGather", op=ALU.bypass, replica_groups=[[0, 1, 2, 3], [4, 5, 6, 7]],
                                                        ins=[src_.opt()], outs=[dst_.opt()]), cc="shared")
        gi = 0
        sc = 0
        bc = 0
        for hs in range(NHS):
            for g, (wd_, d) in enumerate(pats):
                s = gi % 2
                gi += 1
                row = (hs * NG + g) * 128
                L = S // d
                nblk = L // 128
                P.op("sp", "dma_start", dict(out=qT[s][:], in_=bqT[row:row + 128, :]), writes=[r_q[s]], dma=True)
                P.op("sp", "dma_start", dict(out=kT[s][:], in_=bkT[row:row + 128, :]), writes=[r_k[s]], dma=True)
                P.op("sp", "dma_start",
                     dict(out=vp[s][:].rearrange("p (r i) c -> p r i c", r=d),
                          in_=bv[:, row:row + 128].rearrange("(i p r) c -> p r i c", p=128, r=d)),
                     writes=[r_v[s]], dma=True)
                for r in range(d):
                    for i0 in range(0, nblk, 4):
                        nb = min(4, nblk - i0)
                        pb = bc % 2
                        bc += 1
                        for o in (0, -1, 1):
                            blo = 0
                            bhi = nb
                            if o == -1 and i0 == 0:
                                blo = 1
                            if o == 1 and i0 + nb == nblk:
                                bhi = nb - 1
                            if bhi <= blo:
                                continue
                            ss = sc % 2
                            sc += 1
                            for b in range(blo, bhi):
                                i = i0 + b
                                ka = r + d * 128 * (i + o)
                                qa = r + d * 128 * i
                                P.op("pe", "matmul", dict(out=ps_s[ss][:, b * 128:(b + 1) * 128],
                                                          lhsT=kT[s][:, ssl(ka, 128, d)], rhs=qT[s][:, ssl(qa, 128, d)],
                                                          start=True, stop=True),
                                     reads=[r_k[s], r_q[s]], writes=[r_ps[ss]])
                            cs = slice(blo * 128, bhi * 128)
                            P.op("act", "activation", dict(out=ee[ss][:, cs], in_=ps_s[ss][:, cs], func=AF.Exp, scale=scale),
                                 reads=[r_ps[ss]], writes=[r_e[ss]])
                            P.op("dve", "tensor_tensor", dict(out=em[ss][:, cs], in0=ee[ss][:, cs], in1=mk[:, o + 1, cs], op=ALU.mult),
                                 reads=[r_e[ss], r_mk], writes=[r_em[ss]])
                            for b in range(blo, bhi):
                                i = i0 + b
                                last = (o == 1) or (o == -1 and i == nblk - 1) or (o == 0 and nblk == 1)
                                bs = slice(b * 128, (b + 1) * 128)
                                P.op("pe", "matmul", dict(out=ps_o[pb][:, bs], lhsT=vp[s][:, r * nblk + i + o, :], rhs=em[ss][:, bs],
                                                          start=(o == 0 and b == 0), stop=last, skip_group_check=True),
                                     reads=[r_v[s], r_em[ss]], writes=[r_po[pb]])
                                P.op("pe", "matmul", dict(out=ps_z[pb][:, bs], lhsT=ones[:], rhs=em[ss][:, bs],
                                                          start=(o == 0 and b == 0), stop=last, skip_group_check=True),
                                     reads=[r_ones, r_em[ss]], writes=[r_pz[pb]])
                        a0 = r + d * 128 * i0
                        usl = Uacc[:, ssl(a0, 128 * nb, d)]
                        zsl = Zacc[:, ssl(a0, 128 * nb, d)]
                        if g == 0:
                            P.op("act", "copy", dict(out=usl, in_=ps_o[pb][:, 0:nb * 128]), reads=[r_po[pb]], writes=[r_U])
                            P.op("dve", "tensor_copy", dict(out=zsl, in_=ps_z[pb][:, 0:nb * 128]), reads=[r_pz[pb]], writes=[r_Z])
                        else:
                            P.op("dve", "tensor_tensor", dict(out=usl, in0=usl, in1=ps_o[pb][:, 0:nb * 128], op=ALU.add),
                                 reads=[r_po[pb], r_U], writes=[r_U])
                            P.op("dve", "tensor_tensor", dict(out=zsl, in0=zsl, in1=ps_z[pb][:, 0:nb * 128], op=ALU.add),
                                 reads=[r_pz[pb], r_Z], writes=[r_Z])
            P.op("dve", "reciprocal", dict(out=Zacc[:], in_=Zacc[:]), reads=[r_Z], writes=[r_Z])
            P.op("dve", "tensor_tensor", dict(out=ob[:], in0=Uacc[:], in1=Zacc[:], op=ALU.mult), reads=[r_U, r_Z], writes=[r_ob])
            if FUSED:
                for th_ in range(2):
                    P.op("sp", "dma_start", dict(out=ob_q[hs * 2 + th_].rearrange("(q p) t -> p q t", q=4),
                                                 in_=ob[:].rearrange("p (q h t) -> p q h t", q=4, h=2)[:, :, th_, :]), reads=[r_ob], dma=True)
            else:
                P.op("sp", "dma_start", dict(out=obT[hs * 128:(hs + 1) * 128, :], in_=ob[:]), reads=[r_ob], dma=True)
        P.emit()
    if FUSED:
        nc.all_engine_barrier()
    return nc


def dil_mask():
    import numpy as _np
    m = _np.zeros((128, 3, 512), _np.float32)
    p = _np.arange(128)[:, None]
    n = _np.arange(128)[None, :]
    for o in (-1, 0, 1):
        mm = (_np.abs(128 * o + p - n) <= 64).astype(_np.float32)
        m[:, o + 1, :] = _np.tile(mm, (1, 4))
    return m

import numpy as _np

BIG = 30000.0


def gdn_consts():
    k = _np.arange(128)[:, None]
    i = _np.arange(128)[None, :]
    c = {}
    c["ident"] = _np.eye(128, dtype=_np.float32)
    c["ucum"] = _np.stack([(k <= i), (k >= i)], 1).astype(_np.float32)
    nmd_f = BIG * (k <= i)
    nmd_b = BIG * (k >= i)
    nmt_f = -BIG * (i < k)
    nmt_b = -BIG * (i > k)
    c["nm"] = _np.stack([nmd_f, nmt_f, nmd_b, nmt_b], 1).astype(_np.float32)
    return c


def gdn_build(cfg):
    S = cfg.get("S", 4096)
    NH = cfg.get("NH", 4)
    NCH = S // 128
    NB = S // 512
    STOP = cfg.get("stop", 9)
    CHD = F32 if cfg.get("chain_fp32", True) else BF16
    SUB = cfg.get("sub", 9)
    TT_ = cfg.get("TD")
    FUSED = TT_ is not None
    nc = cfg["nc"] if FUSED else bass.Bass("TRN2", target_bir_lowering=False)
    if FUSED:
        din = lambda name, shape, dt=F32: TT_[name]
    else:
        din = lambda name, shape, dt=F32: nc.dram_tensor(name, shape, dt, kind="ExternalInput").ap()
    aqkvT = din("aqkvT", [NH * 3 * 128, S])
    az = din("az", [S, NH * 128])
    abr = din("abr", [S, 4 * NH])
    cw = din("cw", [128, NH * 3, 5])
    alog = din("alog", [128, 2 * NH])
    dtb = din("dtb", [128, 2 * NH])
    onorm = din("onorm", [128, 128])
    ident_d = din("ident_in", [128, 128])
    ucum_d = din("ucum_in", [128, 2, 128])
    nm_d = din("nm_in", [128, 4, 128])
    oaT = TT_["oa_q"] if FUSED else nc.dram_tensor("oaT", [NH * 128, S], BF16, kind="ExternalOutput").ap()
    P = Prog(nc)
    NC2 = 2 * NH
    with ExitStack() as es:
        es.enter_context(nc.allow_low_precision("bf16 matmul operands, fp32 accumulate"))
        pfx = uniq()
        sb = lambda name, shape, dt: es.enter_context(nc.sbuf_tensor(pfx + name, shape, dt))
        psb = lambda name, dt=F32, n=512: es.enter_context(nc.psum_tensor(pfx + name, [128, n], dt))
        R = Reg
        identf = sb("identf", [128, 128], F32)
        identb = sb("identb", [128, 128], BF16)
        ucum = sb("ucum", [128, 2, 128], F32)
        nm = sb("nm", [128, 4, 128], F32)
        onesf = sb("onesf", [128, 128], F32)
        onesb = sb("onesb", [128, 128], BF16)
        epsb = sb("epsb", [128, 1], F32)
        oneb = sb("oneb", [128, 1], F32)
        cws = sb("cws", [128, NH * 3, 5], F32)
        onorm_s = sb("onorm_s", [128, 128], F32)
        r_c = R()
        identc = identf if CHD == F32 else identb
        P.op("sp", "dma_start", dict(out=identf[:], in_=ident_d), writes=[r_c], dma=True)
        P.op("pool", "dma_start", dict(out=identb[:], in_=ident_d), writes=[r_c], dma=True)
        P.op("sp", "dma_start", dict(out=ucum[:], in_=ucum_d), writes=[r_c], dma=True)
        P.op("sp", "dma_start", dict(out=nm[:], in_=nm_d), writes=[r_c], dma=True)
        P.op("sp", "dma_start", dict(out=cws[:], in_=cw), writes=[r_c], dma=True)
        P.op("sp", "dma_start", dict(out=onorm_s[:], in_=onorm), writes=[r_c], dma=True)
        P.op("pool", "memset", dict(ap=onesf[:], constant=1.0), writes=[r_c])
        P.op("pool", "memset", dict(ap=onesb[:], constant=1.0), writes=[r_c])
        P.op("pool", "memset", dict(ap=epsb[:], constant=1e-6), writes=[r_c])
        P.op("pool", "memset", dict(ap=oneb[:], constant=1.0), writes=[r_c])

        NCOL = NCH * NC2
        raw = sb("raw", [128, NCH, 2 * NC2], F32)
        alog_s = sb("alog_s", [128, NC2], F32)
        dtb_s = sb("dtb_s", [128, NC2], F32)
        beta = sb("beta", [128, NCH, NC2], F32)
        nbeta = sb("nbeta", [128, NCH, NC2], F32)
        gg = sb("gg", [128, NCH, NC2], F32)
        gc = sb("gc", [128, NCH, NC2], F32)
        ngc = sb("ngc", [128, NCH, NC2], F32)
        gtot = sb("gtot", [128, NCH, NC2], F32)
        egc = sb("egc", [128, NCH, NC2], F32)
        begc = sb("begc", [128, NCH, NC2], F32)
        ekd = sb("ekd", [128, NCH, NC2], F32)
        egl = sb("egl", [128, NCH, NC2], F32)
        r_g = R()
        ps_m = psb("ps_m")
        r_pm = R(psum=True)
        P.op("sp", "dma_start", dict(out=raw[:], in_=abr.rearrange("(c p) n -> p c n", p=128)), writes=[r_g], dma=True)
        P.op("sp", "dma_start", dict(out=alog_s[:], in_=alog), writes=[r_g], dma=True)
        P.op("sp", "dma_start", dict(out=dtb_s[:], in_=dtb), writes=[r_g], dma=True)
        P.op("act", "activation", dict(out=beta[:], in_=raw[:, :, 0:NC2], func=AF.Sigmoid), reads=[r_g], writes=[r_g])
        P.op("dve", "tensor_scalar", dict(out=nbeta[:], in0=beta[:], scalar1=-1.0, scalar2=None, op0=ALU.mult), reads=[r_g], writes=[r_g])
        P.op("dve", "tensor_tensor", dict(out=gg[:], in0=raw[:, :, NC2:2 * NC2], in1=dtb_s[:, None, :].to_broadcast([128, NCH, NC2]), op=ALU.add),
             reads=[r_g], writes=[r_g])
        sp1 = sb("sp1", [128, NCH, NC2], F32)
        sp2 = sb("sp2", [128, NCH, NC2], F32)
        sp3 = sb("sp3", [128, NCH, NC2], F32)
        P.op("dve", "tensor_scalar", dict(out=sp1[:], in0=gg[:], scalar1=-1.0, scalar2=None, op0=ALU.mult), reads=[r_g], writes=[r_g])
        P.op("dve", "tensor_tensor", dict(out=sp1[:], in0=sp1[:], in1=gg[:], op=ALU.max), reads=[r_g], writes=[r_g])
        P.op("act", "activation", dict(out=sp1[:], in_=sp1[:], func=AF.Exp, scale=-1.0), reads=[r_g], writes=[r_g])
        P.op("dve", "tensor_scalar", dict(out=sp2[:], in0=sp1[:], scalar1=2.0, scalar2=None, op0=ALU.add), reads=[r_g], writes=[r_g])
        P.op("dve", "reciprocal", dict(out=sp2[:], in_=sp2[:]), reads=[r_g], writes=[r_g])
        P.op("dve", "tensor_tensor", dict(out=sp1[:], in0=sp1[:], in1=sp2[:], op=ALU.mult), reads=[r_g], writes=[r_g])
        P.op("dve", "tensor_tensor", dict(out=sp2[:], in0=sp1[:], in1=sp1[:], op=ALU.mult), reads=[r_g], writes=[r_g])
        P.op("dve", "tensor_scalar", dict(out=sp3[:], in0=sp2[:], scalar1=1.0 / 11, scalar2=1.0 / 9, op0=ALU.mult, op1=ALU.add), reads=[r_g], writes=[r_g])
        for cst_ in (1.0 / 7, 1.0 / 5, 1.0 / 3, 1.0):
            P.op("dve", "tensor_tensor", dict(out=sp3[:], in0=sp3[:], in1=sp2[:], op=ALU.mult), reads=[r_g], writes=[r_g])
            P.op("dve", "tensor_scalar", dict(out=sp3[:], in0=sp3[:], scalar1=cst_, scalar2=None, op0=ALU.add), reads=[r_g], writes=[r_g])
        P.op("dve", "tensor_tensor", dict(out=sp3[:], in0=sp3[:], in1=sp1[:], op=ALU.mult), reads=[r_g], writes=[r_g])
        P.op("dve", "tensor_scalar", dict(out=sp1[:], in0=gg[:], scalar1=0.0, scalar2=None, op0=ALU.max), reads=[r_g], writes=[r_g])
        P.op("dve", "scalar_tensor_tensor", dict(out=gg[:], in0=sp3[:], scalar=2.0, in1=sp1[:], op0=ALU.mult, op1=ALU.add), reads=[r_g], writes=[r_g])
        P.op("act", "activation", dict(out=alog_s[:], in_=alog_s[:], func=AF.Exp), reads=[r_g], writes=[r_g])
        P.op("dve", "scalar_tensor_tensor", dict(out=gg[:], in0=gg[:], scalar=-1.0, in1=alog_s[:, None, :].to_broadcast([128, NCH, NC2]),
                                                 op0=ALU.mult, op1=ALU.mult), reads=[r_g], writes=[r_g])
        ggv = gg[:].rearrange("p c (d h) -> p c d h", d=2)
        gcv = gc[:].rearrange("p c (d h) -> p c d h", d=2)
        gtv = gtot[:].rearrange("p c (d h) -> p c d h", d=2)
        psv = ps_m[:, 0:NCH * NC2].rearrange("p (c d h) -> p c d h", c=NCH, d=2)
        pst = ps_m[:, 256:256 + NCH * NC2].rearrange("p (c d h) -> p c d h", c=NCH, d=2)
        assert NCH * NC2 <= 256
        for d in range(2):
            P.op("pe", "matmul", dict(out=psv[:, :, d, :], lhsT=ucum[:, d, :], rhs=ggv[:, :, d, :], start=(d == 0), stop=True, skip_group_check=True),
                 reads=[r_c, r_g], writes=[r_pm])
        P.op("pe", "matmul", dict(out=ps_m[:, 256:256 + NCH * NC2], lhsT=onesf[:], rhs=gg[:].rearrange("p c n -> p (c n)"),
                                  start=False, stop=True, skip_group_check=True), reads=[r_c, r_g], writes=[r_pm])
        P.op("dve", "tensor_copy", dict(out=gc[:].rearrange("p c n -> p (c n)"), in_=ps_m[:, 0:NCH * NC2]), reads=[r_pm], writes=[r_g])
        P.op("dve", "tensor_copy", dict(out=gtot[:].rearrange("p c n -> p (c n)"), in_=ps_m[:, 256:256 + NCH * NC2]), reads=[r_pm], writes=[r_g])
        P.op("dve", "tensor_scalar", dict(out=ngc[:], in0=gc[:], scalar1=-1.0, scalar2=None, op0=ALU.mult), reads=[r_g], writes=[r_g])
        P.op("act", "activation", dict(out=egc[:], in_=gc[:], func=AF.Exp), reads=[r_g], writes=[r_g])
        P.op("dve", "tensor_tensor", dict(out=begc[:], in0=egc[:], in1=beta[:], op=ALU.mult), reads=[r_g], writes=[r_g])
        P.op("dve", "tensor_tensor", dict(out=ekd[:], in0=gtot[:], in1=gc[:], op=ALU.subtract), reads=[r_g], writes=[r_g])
        P.op("act", "activation", dict(out=ekd[:], in_=ekd[:], func=AF.Exp), reads=[r_g], writes=[r_g])
        P.op("act", "activation", dict(out=egl[:], in_=gtot[:], func=AF.Exp), reads=[r_g], writes=[r_g])

        NHX = NH if STOP >= 1 else 0
        xin = sb("xin", [128, S + 4], F32)
        acc = sb("acc", [128, S], F32)
        sqb = sb("sqb", [128, S], BF16)
        fT = [sb("fT%d" % i, [128, S], BF16) for i in range(3)]
        kbg = [sb("kbg%d" % i, [128, NCH, 128], BF16) for i in range(2)]
        kdd = [sb("kdd%d" % i, [128, NCH, 128], BF16) for i in range(2)]
        vbd = [sb("vbd%d" % i, [128, NCH, 128], BF16) for i in range(2)]
        oacc = sb("oacc", [128, NCH, 128], F32)
        zt = sb("zt", [128, NCH, 128], F32)
        rn = sb("rn", [128, 512], F32)
        ssn = sb("ssn", [128, NCH], F32)
        ogb = sb("ogb", [128, NCH, 128], BF16)
        oTs = sb("oTs", [128, S], BF16)
        r_xin, r_acc, r_sqb, r_rn = R(), R(), R(), R()
        r_fT = [R(), R(), R()]
        r_tok = R()
        r_oacc = [R() for c in range(NCH)]
        r_zt, r_ssn, r_ogb, r_oTs = R(), R(), R(), R()
        Gb = [[sb("Gb%d%d" % (d, i), [128, 128], F32) for i in range(2)] for d in range(2)]
        dec = [[sb("dec%d%d" % (d, i), [128, 2, 128], F32) for i in range(2)] for d in range(2)]
        Nb = [[sb("Nb%d%d" % (d, i), [128, 128], CHD) for i in range(2)] for d in range(2)]
        Mb = [[sb("Mb%d%d" % (d, i), [128, 128], CHD) for i in range(2)] for d in range(2)]
        Pb = [[sb("Pb%d%d" % (d, i), [128, 128], CHD) for i in range(2)] for d in range(2)]
        TT = [[sb("TT%d%d" % (d, i), [128, 128], BF16) for i in range(2)] for d in range(2)]
        wTn = [[sb("wTn%d%d" % (d, i), [128, 128], BF16) for i in range(2)] for d in range(2)]
        atT = [[sb("atT%d%d" % (d, i), [128, 128], BF16) for i in range(2)] for d in range(2)]
        vnew = [sb("vnew%d" % d, [128, 128], BF16) for d in range(2)]
        tmpo = [sb("tmpo%d" % d, [128, 128], F32) for d in range(2)]
        tmpo2 = [sb("tmpo2%d" % d, [128, 128], F32) for d in range(2)]
        Sf = [sb("Sf%d" % d, [128, 128], F32) for d in range(2)]
        Sb_ = [sb("Sb%d" % d, [128, 128], BF16) for d in range(2)]
        Sl_ = [sb("Sl%d" % d, [128, 128], BF16) for d in range(2)]
        vnl = [sb("vnl%d" % d, [128, 128], BF16) for d in range(2)]
        r_Gb = [[R(), R()], [R(), R()]]
        r_dec = [[R(), R()], [R(), R()]]
        r_N = [[R(), R()], [R(), R()]]
        r_M = [[R(), R()], [R(), R()]]
        r_P = [[R(), R()], [R(), R()]]
        r_TT = [[R(), R()], [R(), R()]]
        r_w = [[R(), R()], [R(), R()]]
        r_at = [[R(), R()], [R(), R()]]
        r_vn, r_to, r_to2, r_Sf, r_Sb = [R(), R()], [R(), R()], [R(), R()], [R(), R()], [R(), R()]
        ps_X = [psb("ps_X%d" % d) for d in range(2)]
        ps_kk = psb("ps_kk")
        ps_ch = [psb("ps_ch%d" % d) for d in range(2)]
        ps_sc = [psb("ps_sc%d" % d) for d in range(2)]
        r_pX, r_pch, r_psc = [R(psum=True), R(psum=True)], [R(psum=True), R(psum=True)], [R(psum=True), R(psum=True)]
        r_pkk = R(psum=True)
        ps_tb = ps_m[:].bitcast(BF16)

        for h in range(NHX):
            for t in range(3):
                row = (h * 3 + t) * 128
                P.op("pool", "memset", dict(ap=xin[:, 0:2], constant=0.0), writes=[r_xin])
                P.op("pool", "memset", dict(ap=xin[:, S + 2:S + 4], constant=0.0), writes=[r_xin])
                P.op("sp", "dma_start", dict(out=xin[:, 2:S + 2], in_=aqkvT[row:row + 128, :]), writes=[r_xin], dma=True)
                P.op("dve", "tensor_scalar", dict(out=acc[:], in0=xin[:, 0:S], scalar1=cws[:, h * 3 + t, 0:1], scalar2=None, op0=ALU.mult),
                     reads=[r_xin, r_c], writes=[r_acc])
                for w in range(1, 5):
                    P.op("dve", "scalar_tensor_tensor", dict(out=acc[:], in0=xin[:, w:w + S], scalar=cws[:, h * 3 + t, w:w + 1], in1=acc[:],
                                                             op0=ALU.mult, op1=ALU.add), reads=[r_xin, r_c, r_acc], writes=[r_acc])
                if t == 2:
                    P.op("act", "activation", dict(out=fT[2][:], in_=acc[:], func=AF.Silu), reads=[r_acc], writes=[r_fT[2]])
                else:
                    P.op("act", "activation", dict(out=acc[:], in_=acc[:], func=AF.Silu), reads=[r_acc], writes=[r_acc])
                    P.op("act", "activation", dict(out=sqb[:], in_=acc[:], func=AF.Square), reads=[r_acc], writes=[r_sqb])
                    for b in range(NB):
                        bs = slice(b * 512, (b + 1) * 512)
                        P.op("pe", "matmul", dict(out=ps_m[:], lhsT=onesb[:], rhs=sqb[:, bs], start=True, stop=True),
                             reads=[r_c, r_sqb], writes=[r_pm])
                        P.op("act", "activation", dict(out=rn[:], in_=ps_m[:], func=AF.Sqrt, bias=epsb[:, 0:1]), reads=[r_pm, r_c], writes=[r_rn])
                        P.op("dve", "reciprocal", dict(out=rn[:], in_=rn[:]), reads=[r_rn], writes=[r_rn])
                        P.op("dve", "scalar_tensor_tensor", dict(out=fT[t][:, bs], in0=acc[:, bs], scalar=(128 ** -0.5 if t == 0 else 1.0), in1=rn[:],
                                                                 op0=ALU.mult, op1=ALU.mult), reads=[r_acc, r_rn], writes=[r_fT[t]])
            if STOP < 2:
                continue
            for c4 in range(0, NCH, 4):
                for t in (1, 2):
                    for j in range(4):
                        c = c4 + j
                        P.op("pe", "transpose", dict(out=ps_tb[:, j * 128:(j + 1) * 128], in_=fT[t][:, c * 128:(c + 1) * 128], identity=identb[:]),
                             reads=[r_fT[t], r_c], writes=[r_pm])
                    src = ps_tb[:, 0:512].rearrange("p (c k) -> p c k", c=4)
                    for d in range(2):
                        col = d * NH + h
                        if t == 1:
                            P.op("dve", "tensor_tensor", dict(out=kbg[d][:, c4:c4 + 4, :], in0=src,
                                                              in1=begc[:, c4:c4 + 4, col:col + 1].to_broadcast([128, 4, 128]), op=ALU.mult),
                                 reads=[r_pm, r_g], writes=[r_tok])
                            P.op("dve", "tensor_tensor", dict(out=kdd[d][:, c4:c4 + 4, :], in0=src,
                                                              in1=ekd[:, c4:c4 + 4, col:col + 1].to_broadcast([128, 4, 128]), op=ALU.mult),
                                 reads=[r_pm, r_g], writes=[r_tok])
                        else:
                            P.op("dve", "tensor_tensor", dict(out=vbd[d][:, c4:c4 + 4, :], in0=src,
                                                              in1=beta[:, c4:c4 + 4, col:col + 1].to_broadcast([128, 4, 128]), op=ALU.mult),
                                 reads=[r_pm, r_g], writes=[r_tok])
            if STOP < 3:
                continue
            P.op("sp", "dma_start", dict(out=zt[:], in_=az[:, h * 128:(h + 1) * 128].rearrange("(c p) n -> p c n", p=128)), writes=[r_zt], dma=True)

            for d in range(2):
                P.op("pool", "memset", dict(ap=Sf[d][:], constant=0.0), writes=[r_Sf[d]])
                P.op("pool", "memset", dict(ap=Sb_[d][:], constant=0.0), writes=[r_Sb[d]])
                P.op("pool", "memset", dict(ap=Sl_[d][:], constant=0.0), writes=[r_Sb[d]])

            def precompute(d, c, par):
                col = d * NH + h
                cs = slice(c * 128, (c + 1) * 128)
                P.op("dve", "tensor_scalar", dict(out=Gb[d][par][:], in0=onesf[:], scalar1=gg[:, c, col:col + 1], scalar2=None, op0=ALU.mult),
                     reads=[r_c, r_g], writes=[r_Gb[d][par]])
                X = ps_X[d]
                P.op("pe", "matmul", dict(out=X[:, 0:128], lhsT=Gb[d][par][:], rhs=ucum[:, d, :], start=True, stop=False, skip_group_check=True),
                     reads=[r_Gb[d][par], r_c], writes=[r_pX[d]])
                P.op("pe", "matmul", dict(out=X[:, 0:128], lhsT=identf[:], rhs=nm[:, 2 * d, :], start=False, stop=True, skip_group_check=True),
                     reads=[r_c], writes=[r_pX[d]])
                P.op("pe", "matmul", dict(out=X[:, 128:256], lhsT=Gb[d][par][:], rhs=ucum[:, d, :], start=False, stop=False, skip_group_check=True),
                     reads=[r_Gb[d][par], r_c], writes=[r_pX[d]])
                P.op("pe", "matmul", dict(out=X[:, 128:256], lhsT=identf[:], rhs=nm[:, 2 * d + 1, :], start=False, stop=True, skip_group_check=True),
                     reads=[r_c], writes=[r_pX[d]])
                yield
                P.op("act", "activation", dict(out=dec[d][par][:, 0, :], in_=X[:, 0:128], func=AF.Exp, scale=-1.0, bias=gc[:, c, col:col + 1]),
                     reads=[r_pX[d], r_g], writes=[r_dec[d][par]])
                P.op("act", "activation", dict(out=dec[d][par][:, 1, :], in_=X[:, 128:256], func=AF.Exp, scale=1.0, bias=ngc[:, c, col:col + 1]),
                     reads=[r_pX[d], r_g], writes=[r_dec[d][par]])
                if SUB < 1:
                    return
                P.op("pe", "matmul", dict(out=X[:, 256:384], lhsT=fT[1][:, cs], rhs=fT[1][:, cs], start=False, stop=True, skip_group_check=True),
                     reads=[r_fT[1]], writes=[r_pX[d]])
                P.op("pe", "matmul", dict(out=X[:, 384:512], lhsT=fT[1][:, cs], rhs=fT[0][:, cs], start=False, stop=True, skip_group_check=True),
                     reads=[r_fT[1], r_fT[0]], writes=[r_pX[d]])
                yield
                P.op("dve", "scalar_tensor_tensor", dict(out=Nb[d][par][:], in0=X[:, 256:384], scalar=nbeta[:, c, col:col + 1], in1=dec[d][par][:, 0, :],
                                                         op0=ALU.mult, op1=ALU.mult), reads=[r_pX[d], r_g, r_dec[d][par]], writes=[r_N[d][par]])
                P.op("dve", "tensor_tensor", dict(out=atT[d][par][:], in0=X[:, 384:512], in1=dec[d][par][:, 1, :], op=ALU.mult),
                     reads=[r_pX[d], r_dec[d][par]], writes=[r_at[d][par]])
                yield
                if SUB < 2:
                    return
                ch = ps_ch[d]
                P.op("pe", "matmul", dict(out=ch[:, 128:256], lhsT=Nb[d][par][:], rhs=identc[:], start=True, stop=True, skip_group_check=True),
                     reads=[r_N[d][par], r_c], writes=[r_pch[d]])
                if SUB == 2 and cfg.get("sub2", 0) == 1:
                    P.op("act", "copy", dict(out=Mb[d][par][:], in_=ch[:, 128:256]), reads=[r_pch[d]], writes=[r_M[d][par]])
                    return
                P.op("pe", "matmul", dict(out=ch[:, 256:384], lhsT=identc[:], rhs=identc[:], start=False, stop=False, skip_group_check=True),
                     reads=[r_c], writes=[r_pch[d]])
                P.op("pe", "matmul", dict(out=ch[:, 256:384], lhsT=Nb[d][par][:], rhs=identc[:], start=False, stop=True, skip_group_check=True),
                     reads=[r_N[d][par], r_c], writes=[r_pch[d]])
                yield
                P.op("act", "copy", dict(out=Mb[d][par][:], in_=ch[:, 128:256]), reads=[r_pch[d]], writes=[r_M[d][par]])
                if cfg.get("sub2", 0) == 2:
                    P.op("act", "copy", dict(out=Pb[d][par][:], in_=ch[:, 256:384]), reads=[r_pch[d]], writes=[r_P[d][par]])
                else:
                    P.op("dve", "tensor_scalar", dict(scalar1=1.0, scalar2=None, op0=ALU.mult, out=Pb[d][par][:], in0=ch[:, 256:384]), reads=[r_pch[d]], writes=[r_P[d][par]])
                if SUB < 3:
                    return
                for k in range(6):
                    P.op("pe", "matmul", dict(out=ch[:, 0:128], lhsT=Mb[d][par][:], rhs=Nb[d][par][:], start=True, stop=True, skip_group_check=True),
                         reads=[r_M[d][par], r_N[d][par]], writes=[r_pch[d]])
                    if k < 5:
                        P.op("pe", "matmul", dict(out=ch[:, 128:256], lhsT=Nb[d][par][:], rhs=Mb[d][par][:], start=False, stop=True, skip_group_check=True),
                             reads=[r_M[d][par], r_N[d][par]], writes=[r_pch[d]])
                    yield
                    P.op("act", "copy", dict(out=Nb[d][par][:], in_=ch[:, 0:128]), reads=[r_pch[d]], writes=[r_N[d][par]])
                    if k < 5:
                        P.op("dve", "tensor_scalar", dict(scalar1=1.0, scalar2=None, op0=ALU.mult, out=Mb[d][par][:], in0=ch[:, 128:256]), reads=[r_pch[d]], writes=[r_M[d][par]])
                    yield
                    P.op("pe", "matmul", dict(out=ch[:, 256:384], lhsT=identc[:], rhs=Pb[d][par][:], start=False, stop=False, skip_group_check=True),
                         reads=[r_c, r_P[d][par]], writes=[r_pch[d]])
                    P.op("pe", "matmul", dict(out=ch[:, 256:384], lhsT=Nb[d][par][:], rhs=Pb[d][par][:], start=False, stop=True, skip_group_check=True),
                         reads=[r_N[d][par], r_P[d][par]], writes=[r_pch[d]])
                    yield
                    if k < 5:
                        P.op("dve", "tensor_scalar", dict(scalar1=1.0, scalar2=None, op0=ALU.mult, out=Pb[d][par][:], in0=ch[:, 256:384]), reads=[r_pch[d]], writes=[r_P[d][par]])
                    else:
                        P.op("dve", "tensor_scalar", dict(scalar1=1.0, scalar2=None, op0=ALU.mult, out=TT[d][par][:], in0=ch[:, 256:384]), reads=[r_pch[d]], writes=[r_TT[d][par]])
                if SUB < 4:
                    return
                yield
                P.op("pe", "matmul", dict(out=ch[:, 384:512], lhsT=kbg[d][:, c, :], rhs=TT[d][par][:], start=False, stop=True, skip_group_check=True),
                     reads=[r_tok, r_TT[d][par]], writes=[r_pch[d]])
                yield
                P.op("act", "activation", dict(out=wTn[d][par][:], in_=ch[:, 384:512], func=AF.Copy, scale=-1.0), reads=[r_pch[d]], writes=[r_w[d][par]])

            first_visit = [True] * NCH

            def scan(d, c, par):
                col = d * NH + h
                cs = slice(c * 128, (c + 1) * 128)
                sc = ps_sc[d]
                P.op("pe", "matmul", dict(out=sc[:, 0:128], lhsT=TT[d][par][:], rhs=vbd[d][:, c, :], start=True, stop=False, skip_group_check=True),
                     reads=[r_TT[d][par], r_tok], writes=[r_psc[d]])
                P.op("pe", "matmul", dict(out=sc[:, 0:128], lhsT=wTn[d][par][:], rhs=Sb_[d][:], start=False, stop=False, skip_group_check=True),
                     reads=[r_w[d][par], r_Sb[d]], writes=[r_psc[d]])
                P.op("pe", "matmul", dict(out=sc[:, 0:128], lhsT=wTn[d][par][:], rhs=Sl_[d][:], start=False, stop=True, skip_group_check=True),
                     reads=[r_w[d][par], r_Sb[d]], writes=[r_psc[d]])
                yield
                P.op("act", "copy", dict(out=vnew[d][:], in_=sc[:, 0:128]), reads=[r_psc[d]], writes=[r_vn[d]])
                P.op("dve", "tensor_tensor", dict(out=vnl[d][:], in0=sc[:, 0:128], in1=vnew[d][:], op=ALU.subtract), reads=[r_psc[d], r_vn[d]], writes=[r_vn[d]])
                yield
                P.op("pe", "matmul", dict(out=sc[:, 128:256], lhsT=fT[0][:, cs], rhs=Sb_[d][:], start=False, stop=False, skip_group_check=True),
                     reads=[r_fT[0], r_Sb[d]], writes=[r_psc[d]])
                P.op("pe", "matmul", dict(out=sc[:, 128:256], lhsT=fT[0][:, cs], rhs=Sl_[d][:], start=False, stop=True, skip_group_check=True),
                     reads=[r_fT[0], r_Sb[d]], writes=[r_psc[d]])
                for vv_ in (vnew, vnl):
                    P.op("pe", "matmul", dict(out=sc[:, 256:384], lhsT=atT[d][par][:], rhs=vv_[d][:], start=False, stop=(vv_ is vnl), skip_group_check=True),
                         reads=[r_at[d][par], r_vn[d]], writes=[r_psc[d]])
                for vv_ in (vnew, vnl):
                    P.op("pe", "matmul", dict(out=sc[:, 384:512], lhsT=kdd[d][:, c, :], rhs=vv_[d][:], start=False, stop=(vv_ is vnl), skip_group_check=True),
                         reads=[r_tok, r_vn[d]], writes=[r_psc[d]])
                yield
                P.op("act", "copy", dict(out=tmpo[d][:], in_=sc[:, 256:384]), reads=[r_psc[d]], writes=[r_to[d]])
                if first_visit[c]:
                    first_visit[c] = False
                    P.op("dve", "scalar_tensor_tensor", dict(out=oacc[:, c, :], in0=sc[:, 128:256], scalar=egc[:, c, col:col + 1], in1=tmpo[d][:],
                                                             op0=ALU.mult, op1=ALU.add), reads=[r_psc[d], r_g, r_to[d]], writes=[r_oacc[c]])
                else:
                    P.op("dve", "scalar_tensor_tensor", dict(out=tmpo2[d][:], in0=sc[:, 128:256], scalar=egc[:, c, col:col + 1], in1=tmpo[d][:],
                                                             op0=ALU.mult, op1=ALU.add), reads=[r_psc[d], r_g, r_to[d]], writes=[r_to2[d]])
                    P.op("dve", "tensor_tensor", dict(out=oacc[:, c, :], in0=oacc[:, c, :], in1=tmpo2[d][:], op=ALU.add),
                         reads=[r_to2[d], r_oacc[c]], writes=[r_oacc[c]])
                P.op("dve", "scalar_tensor_tensor", dict(out=Sf[d][:], in0=Sf[d][:], scalar=egl[:, c, col:col + 1], in1=sc[:, 384:512],
                                                         op0=ALU.mult, op1=ALU.add), reads=[r_psc[d], r_g, r_Sf[d]], writes=[r_Sf[d]])
                yield
                P.op("act", "copy", dict(out=Sb_[d][:], in_=Sf[d][:]), reads=[r_Sf[d]], writes=[r_Sb[d]])
                P.op("dve", "tensor_tensor", dict(out=Sl_[d][:], in0=Sf[d][:], in1=Sb_[d][:], op=ALU.subtract), reads=[r_Sf[d], r_Sb[d]], writes=[r_Sb[d]])

            order = [[c for c in range(NCH)], [NCH - 1 - c for c in range(NCH)]]
            def run_gens(gens):
                gens = list(gens)
                while gens:
                    alive = []
                    for g_ in gens:
                        try:
                            next(g_)
                            alive.append(g_)
                        except StopIteration:
                            pass
                    gens = alive
            run_gens([precompute(d, order[d][0], 0) for d in range(2)])
            for s in range(NCH):
                gens = []
                if s + 1 < NCH:
                    gens += [precompute(d, order[d][s + 1], (s + 1) % 2) for d in range(2)]
                gens += [scan(d, order[d][s], s % 2) for d in range(2)]
                run_gens(gens)

            allo = r_oacc
            accv = acc[:].rearrange("p (c k) -> p c k", c=NCH)
            P.op("dve", "tensor_tensor", dict(out=accv, in0=oacc[:], in1=oacc[:], op=ALU.mult), reads=allo + [r_acc], writes=[r_acc])
            P.op("dve", "tensor_reduce", dict(out=ssn[:], in_=accv, axis=AX.X, op=ALU.add), reads=[r_acc], writes=[r_ssn])
            P.op("act", "activation", dict(out=ssn[:], in_=ssn[:], func=AF.Sqrt, scale=1.0 / 128, bias=epsb[:, 0:1]), reads=[r_ssn, r_c], writes=[r_ssn])
            P.op("dve", "reciprocal", dict(out=ssn[:], in_=ssn[:]), reads=[r_ssn], writes=[r_ssn])
            P.op("dve", "tensor_tensor", dict(out=accv, in0=oacc[:], in1=ssn[:, :, None].to_broadcast([128, NCH, 128]), op=ALU.mult),
                 reads=allo + [r_ssn, r_acc], writes=[r_acc])
            P.op("dve", "tensor_tensor", dict(out=accv, in0=accv, in1=onorm_s[:, None, :].to_broadcast([128, NCH, 128]), op=ALU.mult),
                 reads=[r_c, r_acc], writes=[r_acc])
            P.op("act", "activation", dict(out=zt[:], in_=zt[:], func=AF.Silu), reads=[r_zt], writes=[r_zt])
            P.op("dve", "tensor_tensor", dict(out=ogb[:], in0=accv, in1=zt[:], op=ALU.mult), reads=[r_acc, r_zt], writes=[r_ogb])
            for c4 in range(0, NCH, 4):
                for j in range(4):
                    c = c4 + j
                    P.op("pe", "matmul", dict(out=ps_m[:, j * 128:(j + 1) * 128], lhsT=ogb[:, c, :], rhs=identb[:], start=(j == 0), stop=True, skip_group_check=True),
                         reads=[r_ogb, r_c], writes=[r_pm])
                P.op("act", "copy", dict(out=oTs[:, c4 * 128:(c4 + 4) * 128], in_=ps_m[:, 0:512]), reads=[r_pm], writes=[r_oTs])
            if FUSED:
                for th_ in range(2):
                    P.op("sp", "dma_start", dict(out=oaT[h * 2 + th_].rearrange("(q p) t -> p q t", q=4),
                                                 in_=oTs[:].rearrange("p (q h t) -> p q h t", q=4, h=2)[:, :, th_, :]), reads=[r_oTs], dma=True)
            else:
                P.op("sp", "dma_start", dict(out=oaT[h * 128:(h + 1) * 128, :], in_=oTs[:]), reads=[r_oTs], dma=True)
        P.emit()
    if FUSED:
        nc.all_engine_barrier()
    return nc

I32 = mybir.dt.int32


def relayout_emit(nc, jobs, idx_ap):
    P = Prog(nc)
    with ExitStack() as es:
        pfx = uniq()
        sb = lambda name, shape, dt: es.enter_context(nc.sbuf_tensor(pfx + name, shape, dt))
        NBUF = 6
        bf = [sb("rl_f%d" % i, [128, 1024], F32) for i in range(NBUF)]
        bb = [sb("rl_b%d" % i, [128, 1024], BF16) for i in range(NBUF)]
        rf = [Reg() for i in range(NBUF)]
        rb = [Reg() for i in range(NBUF)]
        idx_s = sb("rl_idx", [128, 8], I32)
        r_idx = Reg()
        P.op("sp", "dma_start", dict(out=idx_s[:], in_=idx_ap), writes=[r_idx], dma=True)
        kf = kb = 0
        for in_ap, ic, out_ap, dt in jobs:
            W = out_ap.shape[-1]
            if dt == F32:
                buf, rr = bf[kf % NBUF], rf[kf % NBUF]
                kf += 1
            else:
                buf, rr = bb[kb % NBUF], rb[kb % NBUF]
                kb += 1
            G_, off_ = in_ap
            assert G_.shape[1] == W
            P.op("pool", "indirect_dma_start",
                 dict(out=buf[:, 0:W], out_offset=None, in_offset=bass.IndirectOffsetOnAxis(ap=idx_s[:, ic:ic + 1], axis=0), **gat(G_, off_, 0, W)),
                 reads=[r_idx], writes=[rr], dma=True)
            P.op("sp", "dma_start", dict(out=out_ap, in_=buf[:, 0:W]), reads=[rr], dma=True)
        P.emit()
    nc.all_engine_barrier()


def allgather_emit(nc, pairs):
    rg = [[0, 1, 2, 3], [4, 5, 6, 7]]
    sem = nc.alloc_semaphore(name=uniq() + "ag")
    with nc.Block() as block:
        @block.gpsimd
        def _(g):
            for src, dst in pairs:
                g.collective_compute(kind="AllGather", op=ALU.bypass, replica_groups=rg, ins=[src.opt()], outs=[dst.opt()]).then_inc(sem)
            g.wait_ge(sem, len(pairs))
    nc.all_engine_barrier()
    nc.clear_and_free_semaphores([sem])
    nc.all_engine_barrier()


def build_fused(stop=99):
    S, D, TPC, DFF = 4096, 4096, 1024, 11008
    nc = bass.Bass("TRN2", target_bir_lowering=False)
    ein = lambda name, shape, dt=F32: nc.dram_tensor(name, shape, dt, kind="ExternalInput").ap()
    itn = lambda name, shape, dt=F32: nc.dram_tensor(name, shape, dt, kind="Internal").ap()
    KC = D // 128
    SHAPES = {}
    SHAPES["xT"] = ([D, TPC], F32)
    SHAPES["idx"] = ([128, 8], I32)
    SHAPES["nw_fin"] = ([128, KC], F32)
    SHAPES["Win0"] = ([D, 17472], F32)
    SHAPES["Wo0"] = ([3072, D], F32)
    SHAPES["Wqkv"] = ([D, 6144], F32)
    SHAPES["Wo1"] = ([4096, D], F32)
    SHAPES["ropeCb"] = ([32, TPC], F32)
    SHAPES["ropeSb"] = ([32, TPC], F32)
    SHAPES["ropePb"] = ([32, 32], F32)
    SHAPES["ropeCc"] = ([128, TPC], F32)
    SHAPES["ropeSc"] = ([128, TPC], F32)
    SHAPES["ropePc"] = ([128, 128], F32)
    SHAPES["gq"] = ([128, 1], F32)
    SHAPES["gk"] = ([128, 1], F32)
    SHAPES["cw"] = ([128, 12, 5], F32)
    SHAPES["alog"] = ([128, 8], F32)
    SHAPES["dtb"] = ([128, 8], F32)
    SHAPES["onorm"] = ([128, 128], F32)
    SHAPES["ident_in"] = ([128, 128], F32)
    SHAPES["ucum_in"] = ([128, 2, 128], F32)
    SHAPES["nm_in"] = ([128, 4, 128], F32)
    SHAPES["bmask"] = ([128, 3, 512], BF16)
    for l in range(2):
        for nm_, sh_ in (("nw_mix", [128, KC]), ("nw_ffn", [128, KC]), ("Wg", [D, DFF]), ("Wu", [D, DFF]), ("Wd", [DFF, D])):
            SHAPES["%s%d" % (nm_, l)] = (sh_, F32)

    class _Lazy(dict):
        def __missing__(self, k):
            sh_, dt_ = SHAPES[k]
            v = ein(k, sh_, dt_)
            self[k] = v
            return v
    E = _Lazy()
    outT = nc.dram_tensor("outT", [D, TPC], F32, kind="ExternalOutput").ap()

    pairs1, pairs2, pairs3, pairs4 = [], [], [], []
    p1 = [[], []]
    p3 = [[], []]

    def blk(name, rows, W, dt, pairs):
        src = itn("s_" + name, [rows, W], dt)
        dst = itn("g_" + name, [4 * rows, W], dt)
        pairs.append((src, dst))
        return src, dst
    A_b = [[blk("aq%d_%d" % (ps, i), 512, 512, F32, p1[ps]) for i in range(12)] for ps in range(2)]
    Q_b = [[blk("bq%d_%d" % (ps, i), 1024, 512, BF16, p1[ps]) for i in range(6)] for ps in range(2)]
    Z_b = [[blk("az%d_%d" % (ps, i), 512, 512, F32, p1[ps]) for i in range(4)] for ps in range(2)]
    V_b = [[blk("bv%d_%d" % (ps, i), 512, 768, BF16, p1[ps]) for i in range(4)] for ps in range(2)]
    AB_b = [blk("ab%d" % ps, 2048, 16, F32, p1[ps]) for ps in range(2)]
    OA_b = [blk("oa%d" % i, 512, 512, BF16, pairs2) for i in range(8)]
    OB_b = [blk("ob%d" % i, 512, 512, BF16, pairs2) for i in range(4)]
    CQ_b = [[blk("cq%d_%d" % (ps, i), 512, 512, BF16, p3[ps]) for i in range(8)] for ps in range(2)]
    CK_b = [[blk("ck%d_%d" % (ps, i), 512, 512, BF16, p3[ps]) for i in range(2)] for ps in range(2)]
    CV_b = [[blk("cv%d_%d" % (ps, i), 512, 256, BF16, p3[ps]) for i in range(4)] for ps in range(2)]
    OC_b = [blk("oc%d" % i, 512, 512, BF16, pairs4) for i in range(16)]
    aqkvT_l = itn("l_aqkvT", [1536, S])
    az_l = itn("l_az", [S, 512])
    abr_l = itn("l_abr", [S, 16])
    bqT_l = itn("l_bqT", [768, S], BF16)
    bkT_l = itn("l_bkT", [768, S], BF16)
    bv_l = itn("l_bv", [S, 768], BF16)
    x2T = itn("i_x2T", [D, TPC])

    def dst_fm1(name, i, ps):
        if name == "aqkv":
            t, head = i // 16, i % 16
            return A_b[ps][t * 4 + head % 4][0][(head // 4) * 128:(head // 4 + 1) * 128, :]
        qk, hd = i // 24, i % 24
        g, hs = hd // 8, hd % 8
        return Q_b[ps][qk * 3 + g][0][hs * 128:(hs + 1) * 128, :]

    def dst_tm1(name, i, ps, tt):
        if name == "az":
            j, hl = i // 4, i % 4
            return Z_b[ps][tt][0][j * 128:(j + 1) * 128, hl * 128:(hl + 1) * 128]
        g, hs = i // 8, i % 8
        j, hl = hs // 2, hs % 2
        return V_b[ps][tt][0][j * 128:(j + 1) * 128, (hl * 3 + g) * 128:(hl * 3 + g + 1) * 128]

    def dst_ab1(ps, g, tt):
        return AB_b[ps][0][g * 512 + tt * 128:g * 512 + (tt + 1) * 128, :]
    ab = dict(nqkv=6144, nz=2048, nab=64, nbqk=6144, nbv=3072)
    tp_build(dict(nc=nc, T_tot=TPC, inproj="ab", ab=ab, dst_fm=dst_fm1, dst_tm=dst_tm1, dst_ab=dst_ab1, cc_pairs=[p1[0], []],
                  TD=dict(xT=E["xT"], nwb=E["nw_mix0"], Win=E["Win0"], ropeC=E["ropeCb"], ropeS=E["ropeSb"], ropeP=E["ropePb"])))
    if stop == 1:
        return nc
    allgather_emit(nc, p1[1])
    if stop == 2:
        return nc
    jobs = []
    for hl in range(4):
        for t in range(3):
            for r in range(4):
                for ps in range(2):
                    c0 = r * TPC + ps * 512
                    jobs.append(((A_b[ps][t * 4 + hl][1], r * 512), 0, aqkvT_l[(hl * 3 + t) * 128:(hl * 3 + t + 1) * 128, c0:c0 + 512], F32))
    for r in range(4):
        for ps in range(2):
            for tt in range(4):
                r0 = r * TPC + ps * 512 + tt * 128
                jobs.append(((Z_b[ps][tt][1], r * 512), 0, az_l[r0:r0 + 128, :], F32))
                jobs.append(((V_b[ps][tt][1], r * 512), 0, bv_l[r0:r0 + 128, :], BF16))
                jobs.append(((AB_b[ps][1], r * 2048 + tt * 128), 3, abr_l[r0:r0 + 128, :], F32))
    for hl in range(2):
        for g in range(3):
            for r in range(4):
                for ps in range(2):
                    c0 = r * TPC + ps * 512
                    jobs.append(((Q_b[ps][g][1], r * 1024 + hl * 128), 2, bqT_l[(hl * 3 + g) * 128:(hl * 3 + g + 1) * 128, c0:c0 + 512], BF16))
                    jobs.append(((Q_b[ps][3 + g][1], r * 1024 + hl * 128), 2, bkT_l[(hl * 3 + g) * 128:(hl * 3 + g + 1) * 128, c0:c0 + 512], BF16))
    relayout_emit(nc, jobs, E["idx"])
    if stop == 3:
        return nc
    gdn_build(dict(nc=nc, S=S, NH=4, TD=dict(aqkvT=aqkvT_l, az=az_l, abr=abr_l, cw=E["cw"], alog=E["alog"], dtb=E["dtb"], onorm=E["onorm"],
                                             ident_in=E["ident_in"], ucum_in=E["ucum_in"], nm_in=E["nm_in"], oa_q=[b_[0] for b_ in OA_b])))
    if stop == 4:
        return nc
    dil_b_build(dict(nc=nc, S=S, NHS=2, TD=dict(bqT=bqT_l, bkT=bkT_l, bv=bv_l, bmask=E["bmask"], ob_q=[b_[0] for b_ in OB_b], cc_pairs=pairs2[:8])))
    if stop == 5:
        return nc
    allgather_emit(nc, pairs2[8:])
    if stop == 6:
        return nc
    o_src = []
    for ps in range(2):
        lst = []
        for k in range(16):
            j, hl = k // 4, k % 4
            lst.append((OA_b[hl * 2 + ps][1], j * 512, 0))
        for kk in range(8):
            j, hl = kk // 2, kk % 2
            lst.append((OB_b[hl * 2 + ps][1], j * 512, 0))
        o_src.append(lst)

    def dst_fm3(name, i, ps):
        if i < 32:
            return CQ_b[ps][i % 8][0][(i // 8) * 128:(i // 8 + 1) * 128, :]
        kvh = i - 32
        return CK_b[ps][kvh % 2][0][(kvh // 2) * 128:(kvh // 2 + 1) * 128, :]

    def dst_tm3(name, i, ps, tt):
        return CV_b[ps][tt][0][(i // 2) * 128:(i // 2 + 1) * 128, (i % 2) * 128:(i % 2 + 1) * 128]
    cc = dict(nq=4096, nk=1024, nv=1024)
    tp_build(dict(nc=nc, T_tot=TPC, oproj_K=3072, dff=DFF, inproj="c", c=cc, store_x=True, o_src=o_src, dst_fm=dst_fm3, dst_tm=dst_tm3, cc_pairs=[p3[0], []],
                  TD=dict(xT=E["xT"], idx=E["idx"], Wo=E["Wo0"], nwa=E["nw_ffn0"], Wg=E["Wg0"], Wu=E["Wu0"], Wd=E["Wd0"], nwb=E["nw_mix1"],
                          Win=E["Wqkv"], ropeC=E["ropeCc"], ropeS=E["ropeSc"], ropeP=E["ropePc"], gq=E["gq"], gk=E["gk"], xoT=x2T)))
    if stop == 7:
        return nc
    allgather_emit(nc, p3[1])
    if stop == 8:
        return nc
    attn_c_build(dict(nc=nc, S=S, NKV=2, REP=4, TD=dict(GQ=[[b_[1] for b_ in CQ_b[ps]] for ps in range(2)], GK=[[b_[1] for b_ in CK_b[ps]] for ps in range(2)],
                                                          GV=[[b_[1] for b_ in CV_b[ps]] for ps in range(2)], OCB=[b_[0] for b_ in OC_b], OCG=[b_[1] for b_ in OC_b], idx=E["idx"])))
    if stop == 9:
        return nc
    if stop == 10:
        return nc
    o_src = []
    for ps in range(2):
        o_src.append([(OC_b[(k % 8) * 2 + ps][1], (k // 8) * 512, 0) for k in range(32)])
    tp_build(dict(nc=nc, T_tot=TPC, oproj_K=4096, dff=DFF, final_norm=True, o_src=o_src,
                  TD=dict(xT=x2T, idx=E["idx"], Wo=E["Wo1"], nwa=E["nw_ffn1"], Wg=E["Wg1"], Wu=E["Wu1"], Wd=E["Wd1"], nwb=E["nw_fin"], outT=outT)))
    return nc

import ml_dtypes as _mld

_BF = _mld.bfloat16
_PROGS = {}


def _lay(w):
    return np.ascontiguousarray(np.asarray(w, np.float32).reshape(-1, 128).T)


def _rope_tables_b(pos):
    inv = 1.0 / (500000.0 ** (np.arange(0, 32, 2, dtype=np.float32) / 32))
    ang = pos.astype(np.float32)[:, None] * inv[None, :]
    c, s = np.cos(ang), np.sin(ang)
    C = np.ascontiguousarray(np.concatenate([c, c], 1).T.astype(np.float32))
    S = np.ascontiguousarray(np.concatenate([s, s], 1).T.astype(np.float32))
    Pm = np.zeros((32, 32), np.float32)
    for i in range(16):
        Pm[i, 16 + i] = -1
        Pm[16 + i, i] = 1
    return C, S, np.ascontiguousarray(Pm.T)


def _rope_tables_c(pos):
    inv = 1.0 / (10000.0 ** (np.arange(0, 64, 2, dtype=np.float32) / 64))
    ar = (pos // 64).astype(np.float32)[:, None] * inv[None, :]
    ac = (pos % 64).astype(np.float32)[:, None] * inv[None, :]
    C = np.ascontiguousarray(np.concatenate([np.cos(ar), np.cos(ar), np.cos(ac), np.cos(ac)], 1).T.astype(np.float32))
    S = np.ascontiguousarray(np.concatenate([np.sin(ar), np.sin(ar), np.sin(ac), np.sin(ac)], 1).T.astype(np.float32))
    Pm = np.zeros((128, 128), np.float32)
    for i in range(32):
        Pm[i, 32 + i] = -1
        Pm[32 + i, i] = 1
        Pm[64 + i, 96 + i] = -1
        Pm[96 + i, 64 + i] = 1
    return C, S, np.ascontiguousarray(Pm.T)


def kernel(x, norm_mix, norm_ffn, norm_final, ab_w_in, ab_conv_w, ab_a_log, ab_dt_bias,
           ab_out_norm, ab_w_out, c_w_qkv, c_q_norm, c_k_norm, c_w_out,
           ffn_w_gate, ffn_w_up, ffn_w_down):
    f32 = np.float32
    A = lambda a: np.ascontiguousarray(np.asarray(a, f32))
    x = A(x)
    B, S, D = x.shape
    NCORE = 8
    TPC = B * S // NCORE
    QPB = S // TPC
    xf = x.reshape(B * S, D)
    if "F" not in _PROGS:
        _PROGS["F"] = build_fused()
    nc = _PROGS["F"]
    cst = gdn_consts()
    conv_w = A(ab_conv_w[0])
    a_log = A(ab_a_log[0])
    dt_b = A(ab_dt_bias[0])
    shared = dict(
        nw_mix0=_lay(norm_mix[0]), nw_mix1=_lay(norm_mix[1]), nw_ffn0=_lay(norm_ffn[0]), nw_ffn1=_lay(norm_ffn[1]), nw_fin=_lay(norm_final),
        Wg0=A(ffn_w_gate[0]), Wu0=A(ffn_w_up[0]), Wd0=A(ffn_w_down[0]), Wg1=A(ffn_w_gate[1]), Wu1=A(ffn_w_up[1]), Wd1=A(ffn_w_down[1]),
        Win0=A(ab_w_in[0]), Wo0=A(ab_w_out[0]), Wqkv=A(c_w_qkv[0]), Wo1=A(c_w_out[0]),
        gq=A(c_q_norm[0]).reshape(128, 1), gk=A(c_k_norm[0]).reshape(128, 1),
        onorm=np.ascontiguousarray(np.tile(A(ab_out_norm[0])[None, :], (128, 1))),
        ident_in=cst["ident"], ucum_in=cst["ucum"], nm_in=cst["nm"], bmask=dil_mask().astype(_BF))
    ims = []
    p = np.arange(128, dtype=np.int32)
    for c in range(NCORE):
        rk = c % QPB
        pos = np.arange(rk * TPC, (rk + 1) * TPC)
        Cb, Sb, Pb = _rope_tables_b(pos)
        Cc, Sc, Pc = _rope_tables_c(pos)
        heads = [4 * rk + i for i in range(4)]
        cwl = np.zeros((128, 12, 5), f32)
        for i, h in enumerate(heads):
            for t in range(3):
                cwl[:, i * 3 + t, :] = conv_w[:, t * 2048 + h * 128: t * 2048 + (h + 1) * 128].T
        al = np.array([a_log[d, h] for d in range(2) for h in heads], f32)
        db = np.array([dt_b[d, h] for d in range(2) for h in heads], f32)
        idx = np.stack([rk * 128 + p, 2 * (rk * 128 + p), rk * 256 + p, rk * 512 + p, p, p, p, p], 1).astype(np.int32)
        im = dict(shared)
        im.update(xT=np.ascontiguousarray(xf[c * TPC:(c + 1) * TPC].T), idx=np.ascontiguousarray(idx),
                  ropeCb=Cb, ropeSb=Sb, ropePb=Pb, ropeCc=Cc, ropeSc=Sc, ropePc=Pc, cw=cwl,
                  alog=np.ascontiguousarray(np.tile(al[None, :], (128, 1))), dtb=np.ascontiguousarray(np.tile(db[None, :], (128, 1))))
        ims.append(im)
    res = run_bass_kernel_spmd(nc, ims, core_ids=list(range(NCORE))).results
    out = np.concatenate([np.asarray(res[c]["outT"]).T for c in range(NCORE)], 0)
    return np.ascontiguousarray(out.reshape(B, S, D).astype(f32, copy=False))
```

```python
from contextlib import ExitStack
import numpy as np
import concourse.bass as bass
import concourse.mybir as mybir
from concourse.bass_utils import run_bass_kernel_spmd

F32 = mybir.dt.float32
BF16 = mybir.dt.bfloat16
AF = mybir.ActivationFunctionType
ALU = mybir.AluOpType
AX = mybir.AxisListType

ENGS = ("pe", "act", "dve", "pool", "sp")
_UNIQ = [0]


def uniq():
    _UNIQ[0] += 1
    return "u%d_" % _UNIQ[0]
NS_DMA = 6
SAME_ENGINE_SYNC = True


class Reg:
    __slots__ = ("name", "w", "r", "rd", "psum")

    def __init__(self, name="", psum=False):
        self.name = name
        self.psum = psum
        self.w = None
        self.r = {}
        self.rd = []


class Ins:
    __slots__ = ("eng", "fn", "deps", "inc", "dma", "idx", "sem", "target", "cnt", "cc")


class Prog:
    def __init__(self, nc):
        self.nc = nc
        self.q = {e: [] for e in ENGS}
        self.dmas = {e: [] for e in ENGS}
        self.ccs = []
        self.nshared = 0

    def op(self, eng, meth, kw, reads=(), writes=(), dma=False, cc=False):
        ins = Ins()
        ins.cc = cc
        if cc == "shared":
            dma = True
            self.nshared += 1
        elif cc:
            dma = True
            self.ccs.append(ins)
        ins.eng = eng
        ins.fn = (meth, kw)
        ins.dma = dma
        ins.inc = dma
        ins.idx = len(self.q[eng])
        ins.cnt = 0
        deps = set()
        for r in reads:
            if r.w is not None:
                deps.add(r.w)
            if r.psum:
                for e2, i2 in r.r.items():
                    if e2 != eng:
                        deps.add(i2)
        for w in writes:
            if w.w is not None:
                deps.add(w.w)
            deps.update(w.r.values())
            deps.update(w.rd)
        for r in reads:
            if dma:
                r.rd.append(ins)
            else:
                r.r[eng] = ins
        for w in writes:
            w.w = ins
            w.r = {}
            w.rd = []
        if dma and not cc:
            lst = self.dmas[eng]
            if len(lst) >= NS_DMA:
                deps.add(lst[len(lst) - NS_DMA])
            lst.append(ins)
        deps.discard(ins)
        fin = []
        for d in deps:
            if d.eng == eng and not d.dma:
                if eng == "pe" or not SAME_ENGINE_SYNC:
                    continue
            d.inc = True
            fin.append(d)
        ins.deps = fin
        self.q[eng].append(ins)
        return ins

    def emit(self, final_waits=()):
        nc = self.nc
        allsems = []

        def newsem(name):
            h = nc.alloc_semaphore(name=name)
            allsems.append(h)
            return h
        with ExitStack() as es:
            pf = uniq()
            csem = {e: newsem(pf + "c_" + e) for e in ENGS}
            dsem = {e: [newsem(pf + "d_%s%d" % (e, i)) for i in range(NS_DMA)]
                    for e in ENGS if self.dmas[e]}
            es_cc = []
            shared_sem = [newsem(pf + "ccs")] if self.nshared else [None]
            for e in ENGS:
                c = 0
                k = 0
                for ins in self.q[e]:
                    if ins.cc == "shared":
                        ins.sem = shared_sem[0]
                        ins.target = 0
                    elif ins.cc:
                        ins.sem = newsem(pf + "cc%d" % len(es_cc))
                        es_cc.append(ins)
                        ins.target = 1
                    elif ins.dma:
                        ins.sem = dsem[e][k % NS_DMA]
                        ins.target = 16 * (k // NS_DMA + 1)
                        k += 1
                    elif ins.inc:
                        c += 1
                        ins.cnt = c
            block = es.enter_context(nc.Block())

            def run(e, eng):
                waited = {}
                nsh = [0]
                for ins in self.q[e]:
                    need = {}
                    for d in ins.deps:
                        if d.dma:
                            key = d.sem
                            val = d.target
                        else:
                            key = csem[d.eng]
                            val = d.cnt
                        if need.get(key, 0) < val:
                            need[key] = val
                    for key, val in need.items():
                        if waited.get(key, 0) < val:
                            eng.wait_ge(key, val)
                            waited[key] = val
                    inst = getattr(eng, ins.fn[0])(**ins.fn[1])
                    if ins.cc:
                        inst.then_inc(ins.sem)
                        if ins.cc == "shared":
                            nsh[0] += 1
                    elif ins.dma:
                        inst.then_inc(ins.sem, 16)
                    elif ins.inc:
                        inst.then_inc(csem[e], 1)
                if nsh[0]:
                    eng.wait_ge(shared_sem[0], nsh[0])
                for d in self.ccs:
                    if d.eng == e and waited.get(d.sem, 0) < d.target:
                        eng.wait_ge(d.sem, d.target)
                        waited[d.sem] = d.target
                if self.dmas[e]:
                    lst = self.dmas[e]
                    for d in lst[-NS_DMA:]:
                        if waited.get(d.sem, 0) < d.target:
                            eng.wait_ge(d.sem, d.target)
                            waited[d.sem] = d.target

            @block.tensor
            def _(eng):
                run("pe", eng)

            @block.scalar
            def _(eng):
                run("act", eng)

            @block.vector
            def _(eng):
                run("dve", eng)

            @block.gpsimd
            def _(eng):
                run("pool", eng)

            @block.sync
            def _(eng):
                run("sp", eng)
        nc.all_engine_barrier()
        nc.clear_and_free_semaphores(allsems)
        nc.all_engine_barrier()


def gat(G, row_off, col_off, W):
    full = G.shape[1]
    A = full // W
    v = G if A == 1 else G.rearrange("r (a w) -> (r a) w", w=W)
    return dict(in_=v, element_offset=row_off * full + col_off)


def tp_build(cfg):
    D = cfg.get("D", 4096)
    KC = D // 128
    T = cfg.get("T", 512)
    T_tot = cfg["T_tot"]
    NTT = T // 128
    oK = cfg.get("oproj_K", 0)
    dff = cfg.get("dff", 0)
    inproj = cfg.get("inproj")
    final_norm = cfg.get("final_norm", False)
    store_x = cfg.get("store_x", False)
    FB = 8

    TT_ = cfg.get("TD")
    FUSED = TT_ is not None
    nc = cfg["nc"] if FUSED else bass.Bass("TRN2", target_bir_lowering=False)
    if FUSED:
        din = lambda name, shape, dt=F32: TT_[name]
        dout = lambda name, shape, dt=F32: TT_[name]
    else:
        din = lambda name, shape, dt=F32: nc.dram_tensor(name, shape, dt, kind="ExternalInput").ap()
        dout = lambda name, shape, dt=F32: nc.dram_tensor(name, shape, dt, kind="ExternalOutput").ap()
    xT = din("xT", [D, T_tot])
    if oK:
        if not FUSED:
            oT = din("oT", [oK, T_tot], BF16)
        Wo = din("Wo", [oK, D])
    if dff:
        nwa = din("nwa", [128, KC])
        Wg = din("Wg", [D, dff])
        Wu = din("Wu", [D, dff])
        Wd = din("Wd", [dff, D])
    if inproj or final_norm:
        nwb = din("nwb", [128, KC])
    if store_x:
        xoT = dout("xoT", [D, T_tot])
    if final_norm:
        outT = dout("outT", [D, T_tot])
    if inproj == "ab":
        ab = cfg["ab"]
        ncols = ab["nqkv"] + ab["nz"] + ab["nab"] + ab["nbqk"] + ab["nbv"]
        Win = din("Win", [D, ncols])
        ropeC = din("ropeC", [32, T_tot])
        ropeS = din("ropeS", [32, T_tot])
        ropeP = din("ropeP", [32, 32])
        if FUSED:
            aqkvT = az = abo = bqkT = bv = None
        else:
            aqkvT = dout("aqkvT", [ab["nqkv"], T_tot])
            az = dout("az", [T_tot, ab["nz"]])
            abo = dout("ab", [T_tot, ab["nab"]])
            bqkT = dout("bqkT", [ab["nbqk"], T_tot], BF16)
            bv = dout("bv", [T_tot, ab["nbv"]], BF16)
    if inproj == "c":
        cc = cfg["c"]
        ncols = cc["nq"] + cc["nk"] + cc["nv"]
        Win = din("Win", [D, ncols])
        ropeC = din("ropeC", [128, T_tot])
        ropeS = din("ropeS", [128, T_tot])
        ropeP = din("ropeP", [128, 128])
        gq = din("gq", [128, 1])
        gk = din("gk", [128, 1])
        if FUSED:
            cqkT = cv = None
        else:
            cqkT = dout("cqkT", [cc["nq"] + cc["nk"], T_tot], BF16)
            cv = dout("cv", [T_tot, cc["nv"]], BF16)

    P = Prog(nc)
    with ExitStack() as es:
        es.enter_context(nc.allow_low_precision("bf16 matmul operands, fp32 accumulate"))
        pfx = uniq()
        sb = lambda name, shape, dt: es.enter_context(nc.sbuf_tensor(pfx + name, shape, dt))
        psb = lambda name: es.enter_context(nc.psum_tensor(pfx + name, [128, 512], F32))
        R = Reg
        xs = sb("xs", [128, KC, T], F32)
        hT = sb("hT", [128, KC, T], BF16)
        ones = sb("ones", [128, 128], BF16)
        epsb = sb("epsb", [128, 1], F32)
        sq = [sb("sq%d" % i, [128, T], BF16) for i in range(2)]
        rstd = sb("rstd", [128, T], F32)
        st = [sb("st%d" % i, [128, T], F32) for i in range(2)]
        stb = [sb("stb%d" % i, [128, T], BF16) for i in range(2)]
        aT = [sb("aT%d" % i, [128, FB, T], BF16) for i in range(2)]
        wA = [sb("wA%d" % i, [128, KC, 128], BF16) for i in range(4)]
        wd = [sb("wd%d" % i, [128, FB, 512], BF16) for i in range(2)]
        nwa_s = sb("nwa_s", [128, KC], F32)
        nwb_s = sb("nwb_s", [128, KC], F32)
        ps_ss = psb("ps_ss")
        psA = [psb("psA%d" % i) for i in range(4)]
        ps_y = [psb("ps_y%d" % i) for i in range(2)]
        r_xs = [R() for c in range(KC)]
        r_hT = [R() for c in range(KC)]
        r_ones, r_eps, r_rstd, r_ss, r_nwa, r_nwb = R(), R(), R(), R(psum=True), R(), R()
        r_sq = [R(), R()]
        r_st = [R(), R()]
        r_stb = [R(), R()]
        r_aT = [[R() for j in range(FB)] for i in range(2)]
        r_wA = [R() for i in range(4)]
        r_wd = [R(), R()]
        r_psA = [R(psum=True) for i in range(4)]
        r_py = [R(psum=True), R(psum=True)]
        cnt = dict(wA=0, wd=0, psA=0, py=0, st=0, stb=0)

        def nxt(k, n):
            v = cnt[k] % n
            cnt[k] += 1
            return v

        P.op("pool", "memset", dict(ap=ones[:], constant=1.0), writes=[r_ones])
        P.op("pool", "memset", dict(ap=epsb[:], constant=1e-6), writes=[r_eps])
        if dff:
            P.op("sp", "dma_start", dict(out=nwa_s[:], in_=nwa), writes=[r_nwa], dma=True)
        if inproj or final_norm:
            P.op("sp", "dma_start", dict(out=nwb_s[:], in_=nwb), writes=[r_nwb], dma=True)
        if inproj == "ab":
            rC = sb("rC", [32, T_tot], F32)
            rS = sb("rS", [32, T_tot], F32)
            rP = sb("rP", [32, 32], BF16)
            t1 = sb("t1", [32, T], F32)
            t2 = sb("t2", [32, T], F32)
            r_rope, r_t1, r_t2 = R(), R(), R()
            P.op("sp", "dma_start", dict(out=rC[:], in_=ropeC), writes=[r_rope], dma=True)
            P.op("sp", "dma_start", dict(out=rS[:], in_=ropeS), writes=[r_rope], dma=True)
            P.op("pool", "dma_start", dict(out=rP[:], in_=ropeP), writes=[r_rope], dma=True)
        if inproj == "c":
            rC = sb("rC", [128, T_tot], F32)
            rS = sb("rS", [128, T_tot], F32)
            rP = sb("rP", [128, 128], BF16)
            gq_s = sb("gq_s", [128, 1], F32)
            gk_s = sb("gk_s", [128, 1], F32)
            t1 = sb("t1", [128, T], F32)
            t2 = sb("t2", [128, T], F32)
            rn = sb("rn", [128, T], F32)
            r_rope, r_t1, r_t2, r_rn = R(), R(), R(), R()
            P.op("sp", "dma_start", dict(out=rC[:], in_=ropeC), writes=[r_rope], dma=True)
            P.op("sp", "dma_start", dict(out=rS[:], in_=ropeS), writes=[r_rope], dma=True)
            P.op("pool", "dma_start", dict(out=rP[:], in_=ropeP), writes=[r_rope], dma=True)
            P.op("sp", "dma_start", dict(out=gq_s[:], in_=gq), writes=[r_rope], dma=True)
            P.op("sp", "dma_start", dict(out=gk_s[:], in_=gk), writes=[r_rope], dma=True)

        xT_v = xT.rearrange("(c p) t -> p c t", p=128)
        if FUSED and oK:
            idx_s = sb("idx_s", [128, 8], mybir.dt.int32)
            r_idx = R()
            P.op("sp", "dma_start", dict(out=idx_s[:], in_=TT_["idx"]), writes=[r_idx], dma=True)

        def rmsnorm(nw_s, r_nw, to_x=False):
            for c in range(KC):
                s = c % 2
                P.op("act", "activation", dict(out=sq[s][:], in_=xs[:, c, :], func=AF.Square),
                     reads=[r_xs[c]], writes=[r_sq[s]])
                P.op("pe", "matmul", dict(out=ps_ss[:], lhsT=ones[:], rhs=sq[s][:], start=(c == 0), stop=(c == KC - 1)),
                     reads=[r_ones, r_sq[s]], writes=[r_ss])
            P.op("act", "activation", dict(out=rstd[:], in_=ps_ss[:], func=AF.Sqrt, scale=1.0 / D, bias=epsb[:, 0:1]),
                 reads=[r_ss, r_eps], writes=[r_rstd])
            P.op("dve", "reciprocal", dict(out=rstd[:], in_=rstd[:]), reads=[r_rstd], writes=[r_rstd])
            for c in range(KC):
                if to_x:
                    P.op("dve", "scalar_tensor_tensor",
                         dict(out=xs[:, c, :], in0=xs[:, c, :], scalar=nw_s[:, c:c + 1], in1=rstd[:], op0=ALU.mult, op1=ALU.mult),
                         reads=[r_xs[c], r_nw, r_rstd], writes=[r_xs[c]])
                else:
                    P.op("dve", "scalar_tensor_tensor",
                         dict(out=hT[:, c, :], in0=xs[:, c, :], scalar=nw_s[:, c:c + 1], in1=rstd[:], op0=ALU.mult, op1=ALU.mult),
                         reads=[r_xs[c], r_nw, r_rstd], writes=[r_hT[c]])

        def accum_block(W_v, k0, nk, sl, rhs_regs):
            for cb in range(D // 512):
                w = nxt("wd", 2)
                P.op("pool", "dma_start", dict(out=wd[w][:, 0:nk, :], in_=W_v[:, k0:k0 + nk, cb * 512:(cb + 1) * 512]),
                     writes=[r_wd[w]], dma=True)
                for ct in range(4):
                    pb = nxt("py", 2)
                    for j in range(nk):
                        P.op("pe", "matmul", dict(out=ps_y[pb][:], lhsT=wd[w][:, j, ct * 128:(ct + 1) * 128], rhs=aT[sl][:, j, :],
                                                  start=(j == 0), stop=(j == nk - 1)),
                             reads=[r_wd[w], rhs_regs[j]], writes=[r_py[pb]])
                    c = cb * 4 + ct
                    P.op("dve", "tensor_tensor", dict(out=xs[:, c, :], in0=xs[:, c, :], in1=ps_y[pb][:], op=ALU.add),
                         reads=[r_xs[c], r_py[pb]], writes=[r_xs[c]])

        def fm_tile(W_v, col0, epilogue):
            w = nxt("wA", 4)
            P.op("pool", "dma_start", dict(out=wA[w][:], in_=W_v[:, :, col0:col0 + 128]), writes=[r_wA[w]], dma=True)
            pb = nxt("psA", 4)
            for kc in range(KC):
                P.op("pe", "matmul", dict(out=psA[pb][:], lhsT=wA[w][:, kc, :], rhs=hT[:, kc, :], start=(kc == 0), stop=(kc == KC - 1)),
                     reads=[r_wA[w], r_hT[kc]], writes=[r_psA[pb]])
            epilogue(psA[pb], r_psA[pb])

        def tm_tile(W_v, col0, ncl, out_ap, t0, dt, ocol, ab_special=False, fname=None, ti=0):
            w = nxt("wA", 4)
            P.op("pool", "dma_start", dict(out=wA[w][:, :, 0:ncl], in_=W_v[:, :, col0:col0 + ncl]), writes=[r_wA[w]], dma=True)
            pb = nxt("psA", 4)
            for tt in range(NTT):
                for kc in range(KC):
                    P.op("pe", "matmul", dict(out=psA[pb][:, tt * 128:tt * 128 + ncl], lhsT=hT[:, kc, tt * 128:(tt + 1) * 128],
                                              rhs=wA[w][:, kc, 0:ncl], start=(kc == 0), stop=(kc == KC - 1)),
                         reads=[r_wA[w], r_hT[kc]], writes=[r_psA[pb]])
            src = psA[pb][:].rearrange("p (t c) -> p t c", c=128)[:, :, 0:ncl]
            if dt == F32:
                s = nxt("st", 2)
                buf, rb = st[s], r_st[s]
            else:
                s = nxt("stb", 2)
                buf, rb = stb[s], r_stb[s]
            dst = buf[:].rearrange("p (t c) -> p t c", c=128)[:, :, 0:ncl]
            P.op("act", "copy", dict(out=dst, in_=src), reads=[r_psA[pb]], writes=[rb])
            if ab_special:
                for g_ in range(4):
                    for tt_ in range(NTT):
                        srcv = dst.rearrange("p t (x g h) -> p t x g h", x=4, g=4)[:, tt_, :, g_, :]
                        dstv = cfg["dst_ab"](t0 // T, g_, tt_).rearrange("p (x h) -> p x h", x=4)
                        P.op("sp", "dma_start", dict(out=dstv, in_=srcv), reads=[rb] + XB, dma=True)
                return
            if FUSED:
                for tt_ in range(NTT):
                    P.op("sp", "dma_start", dict(out=cfg["dst_tm"](fname, ti, t0 // T, tt_), in_=dst[:, tt_, :]), reads=[rb] + XB, dma=True)
                return
            P.op("sp", "dma_start", dict(out=out_ap[t0:t0 + T, ocol:ocol + ncl].rearrange("(t p) c -> p t c", p=128), in_=dst),
                 reads=[rb], dma=True)

        for t0 in range(0, T_tot, T):
            r_xb = R()
            XB = [r_xb] if FUSED else []

            def issue_cc(key, r_xb=r_xb, ps_=t0 // T):
                if not (FUSED and cfg.get("cc_sections")):
                    return
                for i_, (src_, dst_g) in enumerate(cfg["cc_sections"][ps_].get(key, [])):
                    P.op("pool", "collective_compute", dict(kind="AllGather", op=ALU.bypass, replica_groups=[[0, 1, 2, 3], [4, 5, 6, 7]],
                                                            ins=[src_.opt()], outs=[dst_g.opt()]),
                         writes=([r_xb] if i_ == 0 else []), cc="shared")
            for c in range(KC):
                P.op("sp", "dma_start", dict(out=xs[:, c, :], in_=xT_v[:, c, t0:t0 + T]), writes=[r_xs[c]], dma=True)
            if oK:
                if not FUSED:
                    oT_v = oT.rearrange("(c p) t -> p c t", p=128)
                Wo_v = Wo.rearrange("(c p) n -> p c n", p=128)
                nob = oK // 128
                blocks = [(k0, min(k0 + FB, nob)) for k0 in range(0, nob, FB)]
                for bi, (k0, k1) in enumerate(blocks):
                    sl = bi % 2
                    for j in range(k1 - k0):
                        if FUSED:
                            G_, off_, ic_ = cfg["o_src"][t0 // T][k0 + j]
                            P.op("pool", "indirect_dma_start",
                                 dict(out=aT[sl][:, j, :], out_offset=None,
                                      in_offset=bass.IndirectOffsetOnAxis(ap=idx_s[:, ic_:ic_ + 1], axis=0), **gat(G_, off_, 0, T)),
                                 reads=[r_idx], writes=[r_aT[sl][j]], dma=True)
                        else:
                            P.op("sp", "dma_start", dict(out=aT[sl][:, j, :], in_=oT_v[:, k0 + j, t0:t0 + T]),
                                 writes=[r_aT[sl][j]], dma=True)
                    accum_block(Wo_v, k0, k1 - k0, sl, r_aT[sl])
            if dff:
                Wg_v = Wg.rearrange("(c p) n -> p c n", p=128)
                Wu_v = Wu.rearrange("(c p) n -> p c n", p=128)
                Wd_v = Wd.rearrange("(c p) n -> p c n", p=128)
                rmsnorm(nwa_s, r_nwa)
                NF = dff // 128
                blocks = [(f0, min(f0 + FB, NF)) for f0 in range(0, NF, FB)]

                def gate_up(bi):
                    f0, f1 = blocks[bi]
                    sl = bi % 2
                    for f in range(f0, f1):
                        res = []

                        def keep(ps_t, r_t):
                            res.append((ps_t, r_t))
                        fm_tile(Wg_v, f * 128, keep)
                        fm_tile(Wu_v, f * 128, keep)
                        (pg, rg), (pu, ru) = res
                        s = nxt("st", 2)
                        P.op("act", "activation", dict(out=st[s][:], in_=pg[:], func=AF.Silu), reads=[rg], writes=[r_st[s]])
                        P.op("dve", "tensor_tensor", dict(out=aT[sl][:, f - f0, :], in0=st[s][:], in1=pu[:], op=ALU.mult),
                             reads=[r_st[s], ru], writes=[r_aT[sl][f - f0]])

                gate_up(0)
                for bi in range(1, len(blocks)):
                    gate_up(bi)
                    f0, f1 = blocks[bi - 1]
                    accum_block(Wd_v, f0, f1 - f0, (bi - 1) % 2, r_aT[(bi - 1) % 2])
                f0, f1 = blocks[-1]
                accum_block(Wd_v, f0, f1 - f0, (len(blocks) - 1) % 2, r_aT[(len(blocks) - 1) % 2])
            if store_x:
                xo_v = xoT.rearrange("(c p) t -> p c t", p=128)
                for c in range(KC):
                    P.op("sp", "dma_start", dict(out=xo_v[:, c, t0:t0 + T], in_=xs[:, c, :]), reads=[r_xs[c]], dma=True)
            if inproj:
                Win_v = Win.rearrange("(c p) n -> p c n", p=128)
                rmsnorm(nwb_s, r_nwb)

                def ep_copy(out_ap, row0, dt, fname=None, ti=0):
                    def ep(ps_t, r_t):
                        if dt == F32:
                            s = nxt("st", 2)
                            buf, rb = st[s], r_st[s]
                        else:
                            s = nxt("stb", 2)
                            buf, rb = stb[s], r_stb[s]
                        P.op("act", "copy", dict(out=buf[:], in_=ps_t[:]), reads=[r_t], writes=[rb])
                        dst_ = cfg["dst_fm"](fname, ti, t0 // T) if FUSED else out_ap[row0:row0 + 128, t0:t0 + T]
                        P.op("sp", "dma_start", dict(out=dst_, in_=buf[:]), reads=[rb] + XB, dma=True)
                    return ep

            if inproj == "ab":
                def ep_rope_b(row0, ti=0):
                    def ep(ps_t, r_t):
                        s = nxt("stb", 2)
                        buf, rb = stb[s], r_stb[s]
                        P.op("act", "copy", dict(out=buf[:], in_=ps_t[:]), reads=[r_t], writes=[rb])
                        pb = nxt("py", 2)
                        P.op("pe", "matmul", dict(out=ps_y[pb][0:32, :], lhsT=rP[:, :], rhs=buf[0:32, :], start=True, stop=True),
                             reads=[r_rope, rb], writes=[r_py[pb]])
                        P.op("dve", "tensor_tensor", dict(out=t1[:], in0=buf[0:32, :], in1=rC[:, t0:t0 + T], op=ALU.mult),
                             reads=[rb, r_rope], writes=[r_t1])
                        P.op("dve", "tensor_tensor", dict(out=t2[:], in0=ps_y[pb][0:32, :], in1=rS[:, t0:t0 + T], op=ALU.mult),
                             reads=[r_py[pb], r_rope], writes=[r_t2])
                        P.op("dve", "tensor_tensor", dict(out=buf[0:32, :], in0=t1[:], in1=t2[:], op=ALU.add),
                             reads=[r_t1, r_t2], writes=[rb])
                        dst_ = cfg["dst_fm"]("bqk", ti, t0 // T) if FUSED else bqkT[row0:row0 + 128, t0:t0 + T]
                        P.op("sp", "dma_start", dict(out=dst_, in_=buf[:]), reads=[rb] + XB, dma=True)
                    return ep
                c0 = 0
                for i in range(ab["nqkv"] // 128):
                    fm_tile(Win_v, c0 + i * 128, ep_copy(aqkvT, i * 128, F32, "aqkv", i))
                issue_cc("aqkv")
                c0 += ab["nqkv"]
                for i in range(ab["nz"] // 128):
                    if FUSED:
                        tm_tile(Win_v, c0 + i * 128, 128, None, t0, F32, 0, fname="az", ti=i)
                    else:
                        tm_tile(Win_v, c0 + i * 128, 128, az, t0, F32, i * 128)
                issue_cc("az")
                c0 += ab["nz"]
                tm_tile(Win_v, c0, ab["nab"], abo, t0, F32, 0, ab_special=FUSED)
                issue_cc("ab")
                c0 += ab["nab"]
                for i in range(ab["nbqk"] // 128):
                    fm_tile(Win_v, c0 + i * 128, ep_rope_b(i * 128, i))
                issue_cc("bqk")
                c0 += ab["nbqk"]
                for i in range(ab["nbv"] // 128):
                    if FUSED:
                        tm_tile(Win_v, c0 + i * 128, 128, None, t0, BF16, 0, fname="bv", ti=i)
                    else:
                        tm_tile(Win_v, c0 + i * 128, 128, bv, t0, BF16, i * 128)
            if inproj == "ab":
                issue_cc("bv")
            if inproj == "c":
                def ep_c(row0, g_s, ti=0):
                    def ep(ps_t, r_t):
                        s = nxt("stb", 2)
                        buf, rb = stb[s], r_stb[s]
                        P.op("act", "activation", dict(out=sq[0][:], in_=ps_t[:], func=AF.Square), reads=[r_t], writes=[r_sq[0]])
                        pb = nxt("py", 2)
                        P.op("pe", "matmul", dict(out=ps_y[pb][:], lhsT=ones[:], rhs=sq[0][:], start=True, stop=True),
                             reads=[r_ones, r_sq[0]], writes=[r_py[pb]])
                        P.op("act", "activation", dict(out=rn[:], in_=ps_y[pb][:], func=AF.Sqrt, scale=1.0 / 128, bias=epsb[:, 0:1]),
                             reads=[r_py[pb], r_eps], writes=[r_rn])
                        P.op("dve", "reciprocal", dict(out=rn[:], in_=rn[:]), reads=[r_rn], writes=[r_rn])
                        P.op("dve", "scalar_tensor_tensor",
                             dict(out=buf[:], in0=ps_t[:], scalar=g_s[:, 0:1], in1=rn[:], op0=ALU.mult, op1=ALU.mult),
                             reads=[r_t, r_rope, r_rn], writes=[rb])
                        pb2 = nxt("py", 2)
                        P.op("pe", "matmul", dict(out=ps_y[pb2][:], lhsT=rP[:, :], rhs=buf[:], start=True, stop=True),
                             reads=[r_rope, rb], writes=[r_py[pb2]])
                        P.op("dve", "tensor_tensor", dict(out=t1[:], in0=buf[:], in1=rC[:, t0:t0 + T], op=ALU.mult),
                             reads=[rb, r_rope], writes=[r_t1])
                        P.op("dve", "tensor_tensor", dict(out=t2[:], in0=ps_y[pb2][:], in1=rS[:, t0:t0 + T], op=ALU.mult),
                             reads=[r_py[pb2], r_rope], writes=[r_t2])
                        P.op("dve", "tensor_tensor", dict(out=buf[:], in0=t1[:], in1=t2[:], op=ALU.add),
                             reads=[r_t1, r_t2], writes=[rb])
                        dst_ = cfg["dst_fm"]("cqk", ti, t0 // T) if FUSED else cqkT[row0:row0 + 128, t0:t0 + T]
                        P.op("sp", "dma_start", dict(out=dst_, in_=buf[:]), reads=[rb] + XB, dma=True)
                    return ep
                nqt = cc["nq"] // 128
                nkt = cc["nk"] // 128
                for i in range(nqt):
                    fm_tile(Win_v, i * 128, ep_c(i * 128, gq_s, i))
                for i in range(nkt):
                    fm_tile(Win_v, cc["nq"] + i * 128, ep_c(cc["nq"] + i * 128, gk_s, nqt + i))
                issue_cc("cqk")
                for i in range(cc["nv"] // 128):
                    if FUSED:
                        tm_tile(Win_v, cc["nq"] + cc["nk"] + i * 128, 128, None, t0, BF16, 0, fname="cv", ti=i)
                    else:
                        tm_tile(Win_v, cc["nq"] + cc["nk"] + i * 128, 128, cv, t0, BF16, i * 128)
                issue_cc("cv")
            if FUSED and cfg.get("cc_pairs") and cfg["cc_pairs"][t0 // T]:
                rg_ = [[0, 1, 2, 3], [4, 5, 6, 7]]
                for i_, (src_, dst_g) in enumerate(cfg["cc_pairs"][t0 // T]):
                    P.op("pool", "collective_compute", dict(kind="AllGather", op=ALU.bypass, replica_groups=rg_, ins=[src_.opt()], outs=[dst_g.opt()]),
                         writes=([r_xb] if i_ == 0 else []), cc="shared")
            if final_norm:
                rmsnorm(nwb_s, r_nwb, to_x=True)
                o_v = outT.rearrange("(c p) t -> p c t", p=128)
                for c in range(KC):
                    P.op("sp", "dma_start", dict(out=o_v[:, c, t0:t0 + T], in_=xs[:, c, :]), reads=[r_xs[c]], dma=True)
        P.emit()
    if FUSED:
        nc.all_engine_barrier()
    return nc


def attn_c_build(cfg):
    S = cfg.get("S", 4096)
    NKV = cfg.get("NKV", 2)
    REP = cfg.get("REP", 4)
    NQH = NKV * REP
    NKC = S // 128
    QB = 512
    scale = 128 ** -0.5
    TT_ = cfg.get("TD")
    FUSED = TT_ is not None
    if FUSED:
        nc = cfg["nc"]
        GQ, GK, GV, OCB = TT_["GQ"], TT_["GK"], TT_["GV"], TT_["OCB"]
    else:
        nc = bass.Bass("TRN2", target_bir_lowering=False)
        cqT = nc.dram_tensor("cqT", [NQH * 128, S], BF16, kind="ExternalInput").ap()
        ckT = nc.dram_tensor("ckT", [NKV * 128, S], BF16, kind="ExternalInput").ap()
        cv = nc.dram_tensor("cv", [S, NKV * 128], BF16, kind="ExternalInput").ap()
        ocT = nc.dram_tensor("ocT", [NQH * 128, S], BF16, kind="ExternalOutput").ap()
    P = Prog(nc)
    with ExitStack() as es:
        es.enter_context(nc.allow_low_precision("bf16 matmul operands, fp32 accumulate"))
        pfx = uniq()
        sb = lambda name, shape, dt: es.enter_context(nc.sbuf_tensor(pfx + name, shape, dt))
        psb = lambda name: es.enter_context(nc.psum_tensor(pfx + name, [128, 512], F32))
        R = Reg
        kT = [sb("kT%d" % i, [128, S], BF16) for i in range(2)]
        vv = [sb("v%d" % i, [128, NKC, 128], BF16) for i in range(2)]
        qT = [sb("qT%d" % i, [128, S], BF16) for i in range(2)]
        NE = 3
        ee = [sb("e%d" % i, [128, QB], BF16) for i in range(NE)]
        rz = sb("rz", [128, QB], F32)
        ob = [sb("ob%d" % i, [128, QB], BF16) for i in range(2)]
        ones = sb("ones", [128, 128], BF16)
        ps_s = [psb("ps_s%d" % i) for i in range(NE)]
        ps_o = [psb("ps_o%d" % i) for i in range(2)]
        ps_z = [psb("ps_z%d" % i) for i in range(2)]
        r_kT, r_v, r_qT = [R(), R()], [R(), R()], [R(), R()]
        r_e = [R() for i in range(NE)]
        r_rz, r_ones = R(), R()
        r_ob = [R(), R()]
        r_ps = [R(psum=True) for i in range(NE)]
        r_po, r_pz = [R(psum=True), R(psum=True)], [R(psum=True), R(psum=True)]
        P.op("pool", "memset", dict(ap=ones[:], constant=1.0), writes=[r_ones])
        if FUSED:
            idx_s = sb("idx_s", [128, 8], mybir.dt.int32)
            r_idx = R()
            P.op("sp", "dma_start", dict(out=idx_s[:], in_=TT_["idx"]), writes=[r_idx], dma=True)
            QS = S // 4

            def gather(out_ap, G_, off_, span_, ic_, cols, wr):
                P.op("pool", "indirect_dma_start",
                     dict(out=out_ap, out_offset=None,
                          in_offset=bass.IndirectOffsetOnAxis(ap=idx_s[:, ic_:ic_ + 1], axis=0), **gat(G_, off_, cols.start, cols.stop - cols.start)),
                     reads=[r_idx], writes=[wr], dma=True)
        it = 0
        blk = 0
        pending = []
        rg_ = [[0, 1, 2, 3], [4, 5, 6, 7]]

        def flush():
            for h_, rr_ in pending:
                for th_ in range(2):
                    P.op("pool", "collective_compute", dict(kind="AllGather", op=ALU.bypass, replica_groups=rg_,
                                                            ins=[OCB[h_ * 2 + th_].opt()], outs=[TT_["OCG"][h_ * 2 + th_].opt()]),
                         writes=([rr_] if th_ == 0 else []), cc="shared")
            del pending[:]
        for kv in range(NKV):
            ks = kv % 2
            if FUSED:
                for r_ in range(4):
                    for ps_ in range(2):
                        gather(kT[ks][:, r_ * QS + ps_ * 512:r_ * QS + (ps_ + 1) * 512], GK[ps_][kv], r_ * 512, None, 0, slice(0, 512), r_kT[ks])
                for c_ in range(NKC):
                    r_, tl_ = c_ // 8, c_ % 8
                    gather(vv[ks][:, c_, :], GV[tl_ // 4][tl_ % 4], r_ * 512, None, 1, slice(kv * 128, (kv + 1) * 128), r_v[ks])
            else:
                P.op("sp", "dma_start", dict(out=kT[ks][:], in_=ckT[kv * 128:(kv + 1) * 128, :]), writes=[r_kT[ks]], dma=True)
                P.op("sp", "dma_start", dict(out=vv[ks][:], in_=cv[:, kv * 128:(kv + 1) * 128].rearrange("(c p) d -> p c d", p=128)),
                     writes=[r_v[ks]], dma=True)
            for r in range(REP):
                h = kv * REP + r
                qs = h % 2
                if FUSED:
                    for r_ in range(4):
                        for ps_ in range(2):
                            gather(qT[qs][:, r_ * QS + ps_ * 512:r_ * QS + (ps_ + 1) * 512], GQ[ps_][h], r_ * 512, None, 0, slice(0, 512), r_qT[qs])
                else:
                    P.op("sp", "dma_start", dict(out=qT[qs][:], in_=cqT[h * 128:(h + 1) * 128, :]), writes=[r_qT[qs]], dma=True)
                if FUSED:
                    flush()
                    r_oh = R()
                for qb in range(S // QB):
                    pb = blk % 2
                    blk += 1
                    qsl = qT[qs][:, qb * QB:(qb + 1) * QB]

                    def smm(kc, i):
                        P.op("pe", "matmul", dict(out=ps_s[i][:], lhsT=kT[ks][:, kc * 128:(kc + 1) * 128], rhs=qsl, start=True, stop=True),
                             reads=[r_kT[ks], r_qT[qs]], writes=[r_ps[i]])
                    smm(0, it % NE)
                    for kc in range(NKC):
                        i = it % NE
                        it += 1
                        if kc + 1 < NKC:
                            smm(kc + 1, it % NE)
                        P.op("act", "activation", dict(out=ee[i][:], in_=ps_s[i][:], func=AF.Exp, scale=scale),
                             reads=[r_ps[i]], writes=[r_e[i]])
                        P.op("pe", "matmul", dict(out=ps_o[pb][:], lhsT=vv[ks][:, kc, :], rhs=ee[i][:], start=(kc == 0), stop=(kc == NKC - 1)),
                             reads=[r_v[ks], r_e[i]], writes=[r_po[pb]])
                        P.op("pe", "matmul", dict(out=ps_z[pb][:], lhsT=ones[:], rhs=ee[i][:], start=(kc == 0), stop=(kc == NKC - 1)),
                             reads=[r_ones, r_e[i]], writes=[r_pz[pb]])
                    P.op("dve", "reciprocal", dict(out=rz[:], in_=ps_z[pb][:]), reads=[r_pz[pb]], writes=[r_rz])
                    P.op("dve", "tensor_tensor", dict(out=ob[pb][:], in0=ps_o[pb][:], in1=rz[:], op=ALU.mult),
                         reads=[r_po[pb], r_rz], writes=[r_ob[pb]])
                    if FUSED:
                        q_, th_ = qb // 2, qb % 2
                        P.op("sp", "dma_start", dict(out=OCB[h * 2 + th_][q_ * 128:(q_ + 1) * 128, :], in_=ob[pb][:]),
                             reads=[r_ob[pb], r_oh], dma=True)
                    else:
                        P.op("sp", "dma_start", dict(out=ocT[h * 128:(h + 1) * 128, qb * QB:(qb + 1) * QB], in_=ob[pb][:]),
                             reads=[r_ob[pb]], dma=True)
                if FUSED:
                    pending.append((h, r_oh))
        if FUSED:
            flush()
        P.emit()
    if FUSED:
        nc.all_engine_barrier()
    return nc


B_PATTERNS = ((128, 1), (512, 4), (2048, 16))


def ssl(a, n, d):
    return slice(a, a + d * (n - 1) + 1, d)


def dil_b_build(cfg):
    S = cfg.get("S", 4096)
    NHS = cfg.get("NHS", 2)
    pats = cfg.get("pats", B_PATTERNS)
    NG = len(pats)
    scale = 128 ** -0.5
    TT_ = cfg.get("TD")
    FUSED = TT_ is not None
    if FUSED:
        nc = cfg["nc"]
        bqT, bkT, bv, bmask, ob_q = TT_["bqT"], TT_["bkT"], TT_["bv"], TT_["bmask"], TT_["ob_q"]
    else:
        nc = bass.Bass("TRN2", target_bir_lowering=False)
        bqT = nc.dram_tensor("bqT", [NHS * NG * 128, S], BF16, kind="ExternalInput").ap()
        bkT = nc.dram_tensor("bkT", [NHS * NG * 128, S], BF16, kind="ExternalInput").ap()
        bv = nc.dram_tensor("bv", [S, NHS * NG * 128], BF16, kind="ExternalInput").ap()
        bmask = nc.dram_tensor("bmask", [128, 3, 512], BF16, kind="ExternalInput").ap()
        obT = nc.dram_tensor("obT", [NHS * 128, S], BF16, kind="ExternalOutput").ap()
    P = Prog(nc)
    with ExitStack() as es:
        es.enter_context(nc.allow_low_precision("bf16 matmul operands, fp32 accumulate"))
        pfx = uniq()
        sb = lambda name, shape, dt: es.enter_context(nc.sbuf_tensor(pfx + name, shape, dt))
        psb = lambda name: es.enter_context(nc.psum_tensor(pfx + name, [128, 512], F32))
        R = Reg
        qT = [sb("qT%d" % i, [128, S], BF16) for i in range(2)]
        kT = [sb("kT%d" % i, [128, S], BF16) for i in range(2)]
        vp = [sb("vp%d" % i, [128, S // 128, 128], BF16) for i in range(2)]
        Uacc = sb("Uacc", [128, S], F32)
        Zacc = sb("Zacc", [128, S], F32)
        ob = sb("ob", [128, S], BF16)
        ee = [sb("e%d" % i, [128, 512], BF16) for i in range(2)]
        em = [sb("em%d" % i, [128, 512], BF16) for i in range(2)]
        mk = sb("mk", [128, 3, 512], BF16)
        ones = sb("ones", [128, 128], BF16)
        ps_s = [psb("ps_s%d" % i) for i in range(2)]
        ps_o = [psb("ps_o%d" % i) for i in range(2)]
        ps_z = [psb("ps_z%d" % i) for i in range(2)]
        r_q, r_k, r_v = [R(), R()], [R(), R()], [R(), R()]
        r_U, r_Z, r_ob, r_mk, r_ones = R(), R(), R(), R(), R()
        r_e, r_em = [R(), R()], [R(), R()]
        r_ps, r_po, r_pz = [R(psum=True), R(psum=True)], [R(psum=True), R(psum=True)], [R(psum=True), R(psum=True)]
        P.op("pool", "memset", dict(ap=ones[:], constant=1.0), writes=[r_ones])
        P.op("sp", "dma_start", dict(out=mk[:], in_=bmask), writes=[r_mk], dma=True)
        if FUSED:
            for src_, dst_ in TT_.get("cc_pairs", []):
                P.op("pool", "collective_compute", dict(kind="AllGather", op=ALU.bypass, replica_groups=[[0, 1, 2, 3], [4, 5, 6, 7]],
                                                        ins=[src_.opt()], outs=[dst_.opt()]), cc="shared")
        gi = 0
        sc = 0
        bc = 0
        for hs in range(NHS):
            for g, (wd_, d) in enumerate(pats):
                s = gi % 2
                gi += 1
                row = (hs * NG + g) * 128
                L = S // d
                nblk = L // 128
                P.op("sp", "dma_start", dict(out=qT[s][:], in_=bqT[row:row + 128, :]), writes=[r_q[s]], dma=True)
                P.op("sp", "dma_start", dict(out=kT[s][:], in_=bkT[row:row + 128, :]), writes=[r_k[s]], dma=True)
                P.op("sp", "dma_start",
                     dict(out=vp[s][:].rearrange("p (r i) c -> p r i c", r=d),
                          in_=bv[:, row:row + 128].rearrange("(i p r) c -> p r i c", p=128, r=d)),
                     writes=[r_v[s]], dma=True)
                for r in range(d):
                    for i0 in range(0, nblk, 4):
                        nb = min(4, nblk - i0)
                        pb = bc % 2
                        bc += 1
                        for o in (0, -1, 1):
                            blo = 0
                            bhi = nb
                            if o == -1 and i0 == 0:
                                blo = 1
                            if o == 1 and i0 + nb == nblk:
                                bhi = nb - 1
                            if bhi <= blo:
                                continue
                            ss = sc % 2
                            sc += 1
                            for b in range(blo, bhi):
                                i = i0 + b
                                ka = r + d * 128 * (i + o)
                                qa = r + d * 128 * i
                                P.op("pe", "matmul", dict(out=ps_s[ss][:, b * 128:(b + 1) * 128],
                                                          lhsT=kT[s][:, ssl(ka, 128, d)], rhs=qT[s][:, ssl(qa, 128, d)],
                                                          start=True, stop=True),
                                     reads=[r_k[s], r_q[s]], writes=[r_ps[ss]])
                            cs = slice(blo * 128, bhi * 128)
                            P.op("act", "activation", dict(out=ee[ss][:, cs], in_=ps_s[ss][:, cs], func=AF.Exp, scale=scale),
                                 reads=[r_ps[ss]], writes=[r_e[ss]])
                            P.op("dve", "tensor_tensor", dict(out=em[ss][:, cs], in0=ee[ss][:, cs], in1=mk[:, o + 1, cs], op=ALU.mult),
                                 reads=[r_e[ss], r_mk], writes=[r_em[ss]])
                            for b in range(blo, bhi):
                                i = i0 + b
                                last = (o == 1) or (o == -1 and i == nblk - 1) or (o == 0 and nblk == 1)
                                bs = slice(b * 128, (b + 1) * 128)
                                P.op("pe", "matmul", dict(out=ps_o[pb][:, bs], lhsT=vp[s][:, r * nblk + i + o, :], rhs=em[ss][:, bs],
                                                          start=(o == 0 and b == 0), stop=last, skip_group_check=True),
                                     reads=[r_v[s], r_em[ss]], writes=[r_po[pb]])
                                P.op("pe", "matmul", dict(out=ps_z[pb][:, bs], lhsT=ones[:], rhs=em[ss][:, bs],
                                                          start=(o == 0 and b == 0), stop=last, skip_group_check=True),
                                     reads=[r_ones, r_em[ss]], writes=[r_pz[pb]])
                        a0 = r + d * 128 * i0
                        usl = Uacc[:, ssl(a0, 128 * nb, d)]
                        zsl = Zacc[:, ssl(a0, 128 * nb, d)]
                        if g == 0:
                            P.op("act", "copy", dict(out=usl, in_=ps_o[pb][:, 0:nb * 128]), reads=[r_po[pb]], writes=[r_U])
                            P.op("dve", "tensor_copy", dict(out=zsl, in_=ps_z[pb][:, 0:nb * 128]), reads=[r_pz[pb]], writes=[r_Z])
                        else:
                            P.op("dve", "tensor_tensor", dict(out=usl, in0=usl, in1=ps_o[pb][:, 0:nb * 128], op=ALU.add),
                                 reads=[r_po[pb], r_U], writes=[r_U])
                            P.op("dve", "tensor_tensor", dict(out=zsl, in0=zsl, in1=ps_z[pb][:, 0:nb * 128], op=ALU.add),
                                 reads=[r_pz[pb], r_Z], writes=[r_Z])
            P.op("dve", "reciprocal", dict(out=Zacc[:], in_=Zacc[:]), reads=[r_Z], writes=[r_Z])
            P.op("dve", "tensor_tensor", dict(out=ob[:], in0=Uacc[:], in1=Zacc[:], op=ALU.mult), reads=[r_U, r_Z], writes=[r_ob])
            if FUSED:
                for th_ in range(2):
                    P.op("sp", "dma_start", dict(out=ob_q[hs * 2 + th_].rearrange("(q p) t -> p q t", q=4),
                                                 in_=ob[:].rearrange("p (q h t) -> p q h t", q=4, h=2)[:, :, th_, :]), reads=[r_ob], dma=True)
            else:
                P.op("sp", "dma_start", dict(out=obT[hs * 128:(hs + 1) * 128, :], in_=ob[:]), reads=[r_ob], dma=True)
        P.emit()
    if FUSED:
        nc.all_engine_barrier()
    return nc


def dil_mask():
    import numpy as _np
    m = _np.zeros((128, 3, 512), _np.float32)
    p = _np.arange(128)[:, None]
    n = _np.arange(128)[None, :]
    for o in (-1, 0, 1):
        mm = (_np.abs(128 * o + p - n) <= 64).astype(_np.float32)
        m[:, o + 1, :] = _np.tile(mm, (1, 4))
    return m

import numpy as _np

BIG = 30000.0


def gdn_consts():
    k = _np.arange(128)[:, None]
    i = _np.arange(128)[None, :]
    c = {}
    c["ident"] = _np.eye(128, dtype=_np.float32)
    c["ucum"] = _np.stack([(k <= i), (k >= i)], 1).astype(_np.float32)
    nmd_f = BIG * (k <= i)
    nmd_b = BIG * (k >= i)
    nmt_f = -BIG * (i < k)
    nmt_b = -BIG * (i > k)
    c["nm"] = _np.stack([nmd_f, nmt_f, nmd_b, nmt_b], 1).astype(_np.float32)
    return c


def gdn_build(cfg):
    S = cfg.get("S", 4096)
    NH = cfg.get("NH", 4)
    NCH = S // 128
    NB = S // 512
    STOP = cfg.get("stop", 9)
    CHD = F32 if cfg.get("chain_fp32", True) else BF16
    SUB = cfg.get("sub", 9)
    TT_ = cfg.get("TD")
    FUSED = TT_ is not None
    nc = cfg["nc"] if FUSED else bass.Bass("TRN2", target_bir_lowering=False)
    if FUSED:
        din = lambda name, shape, dt=F32: TT_[name]
    else:
        din = lambda name, shape, dt=F32: nc.dram_tensor(name, shape, dt, kind="ExternalInput").ap()
    aqkvT = din("aqkvT", [NH * 3 * 128, S])
    az = din("az", [S, NH * 128])
    abr = din("abr", [S, 4 * NH])
    cw = din("cw", [128, NH * 3, 5])
    alog = din("alog", [128, 2 * NH])
    dtb = din("dtb", [128, 2 * NH])
    onorm = din("onorm", [128, 128])
    ident_d = din("ident_in", [128, 128])
    ucum_d = din("ucum_in", [128, 2, 128])
    nm_d = din("nm_in", [128, 4, 128])
    oaT = TT_["oa_q"] if FUSED else nc.dram_tensor("oaT", [NH * 128, S], BF16, kind="ExternalOutput").ap()
    P = Prog(nc)
    NC2 = 2 * NH
    with ExitStack() as es:
        es.enter_context(nc.allow_low_precision("bf16 matmul operands, fp32 accumulate"))
        pfx = uniq()
        sb = lambda name, shape, dt: es.enter_context(nc.sbuf_tensor(pfx + name, shape, dt))
        psb = lambda name, dt=F32, n=512: es.enter_context(nc.psum_tensor(pfx + name, [128, n], dt))
        R = Reg
        identf = sb("identf", [128, 128], F32)
        identb = sb("identb", [128, 128], BF16)
        ucum = sb("ucum", [128, 2, 128], F32)
        nm = sb("nm", [128, 4, 128], F32)
        onesf = sb("onesf", [128, 128], F32)
        onesb = sb("onesb", [128, 128], BF16)
        epsb = sb("epsb", [128, 1], F32)
        oneb = sb("oneb", [128, 1], F32)
        cws = sb("cws", [128, NH * 3, 5], F32)
        onorm_s = sb("onorm_s", [128, 128], F32)
        r_c = R()
        identc = identf if CHD == F32 else identb
        P.op("sp", "dma_start", dict(out=identf[:], in_=ident_d), writes=[r_c], dma=True)
        P.op("pool", "dma_start", dict(out=identb[:], in_=ident_d), writes=[r_c], dma=True)
        P.op("sp", "dma_start", dict(out=ucum[:], in_=ucum_d), writes=[r_c], dma=True)
        P.op("sp", "dma_start", dict(out=nm[:], in_=nm_d), writes=[r_c], dma=True)
        P.op("sp", "dma_start", dict(out=cws[:], in_=cw), writes=[r_c], dma=True)
        P.op("sp", "dma_start", dict(out=onorm_s[:], in_=onorm), writes=[r_c], dma=True)
        P.op("pool", "memset", dict(ap=onesf[:], constant=1.0), writes=[r_c])
        P.op("pool", "memset", dict(ap=onesb[:], constant=1.0), writes=[r_c])
        P.op("pool", "memset", dict(ap=epsb[:], constant=1e-6), writes=[r_c])
        P.op("pool", "memset", dict(ap=oneb[:], constant=1.0), writes=[r_c])

        NCOL = NCH * NC2
        raw = sb("raw", [128, NCH, 2 * NC2], F32)
        alog_s = sb("alog_s", [128, NC2], F32)
        dtb_s = sb("dtb_s", [128, NC2], F32)
        beta = sb("beta", [128, NCH, NC2], F32)
        nbeta = sb("nbeta", [128, NCH, NC2], F32)
        gg = sb("gg", [128, NCH, NC2], F32)
        gc = sb("gc", [128, NCH, NC2], F32)
        ngc = sb("ngc", [128, NCH, NC2], F32)
        gtot = sb("gtot", [128, NCH, NC2], F32)
        egc = sb("egc", [128, NCH, NC2], F32)
        begc = sb("begc", [128, NCH, NC2], F32)
        ekd = sb("ekd", [128, NCH, NC2], F32)
        egl = sb("egl", [128, NCH, NC2], F32)
        r_g = R()
        ps_m = psb("ps_m")
        r_pm = R(psum=True)
        P.op("sp", "dma_start", dict(out=raw[:], in_=abr.rearrange("(c p) n -> p c n", p=128)), writes=[r_g], dma=True)
        P.op("sp", "dma_start", dict(out=alog_s[:], in_=alog), writes=[r_g], dma=True)
        P.op("sp", "dma_start", dict(out=dtb_s[:], in_=dtb), writes=[r_g], dma=True)
        P.op("act", "activation", dict(out=beta[:], in_=raw[:, :, 0:NC2], func=AF.Sigmoid), reads=[r_g], writes=[r_g])
        P.op("dve", "tensor_scalar", dict(out=nbeta[:], in0=beta[:], scalar1=-1.0, scalar2=None, op0=ALU.mult), reads=[r_g], writes=[r_g])
        P.op("dve", "tensor_tensor", dict(out=gg[:], in0=raw[:, :, NC2:2 * NC2], in1=dtb_s[:, None, :].to_broadcast([128, NCH, NC2]), op=ALU.add),
             reads=[r_g], writes=[r_g])
        sp1 = sb("sp1", [128, NCH, NC2], F32)
        sp2 = sb("sp2", [128, NCH, NC2], F32)
        sp3 = sb("sp3", [128, NCH, NC2], F32)
        P.op("dve", "tensor_scalar", dict(out=sp1[:], in0=gg[:], scalar1=-1.0, scalar2=None, op0=ALU.mult), reads=[r_g], writes=[r_g])
        P.op("dve", "tensor_tensor", dict(out=sp1[:], in0=sp1[:], in1=gg[:], op=ALU.max), reads=[r_g], writes=[r_g])
        P.op("act", "activation", dict(out=sp1[:], in_=sp1[:], func=AF.Exp, scale=-1.0), reads=[r_g], writes=[r_g])
        P.op("dve", "tensor_scalar", dict(out=sp2[:], in0=sp1[:], scalar1=2.0, scalar2=None, op0=ALU.add), reads=[r_g], writes=[r_g])
        P.op("dve", "reciprocal", dict(out=sp2[:], in_=sp2[:]), reads=[r_g], writes=[r_g])
        P.op("dve", "tensor_tensor", dict(out=sp1[:], in0=sp1[:], in1=sp2[:], op=ALU.mult), reads=[r_g], writes=[r_g])
        P.op("dve", "tensor_tensor", dict(out=sp2[:], in0=sp1[:], in1=sp1[:], op=ALU.mult), reads=[r_g], writes=[r_g])
        P.op("dve", "tensor_scalar", dict(out=sp3[:], in0=sp2[:], scalar1=1.0 / 11, scalar2=1.0 / 9, op0=ALU.mult, op1=ALU.add), reads=[r_g], writes=[r_g])
        for cst_ in (1.0 / 7, 1.0 / 5, 1.0 / 3, 1.0):
            P.op("dve", "tensor_tensor", dict(out=sp3[:], in0=sp3[:], in1=sp2[:], op=ALU.mult), reads=[r_g], writes=[r_g])
            P.op("dve", "tensor_scalar", dict(out=sp3[:], in0=sp3[:], scalar1=cst_, scalar2=None, op0=ALU.add), reads=[r_g], writes=[r_g])
        P.op("dve", "tensor_tensor", dict(out=sp3[:], in0=sp3[:], in1=sp1[:], op=ALU.mult), reads=[r_g], writes=[r_g])
        P.op("dve", "tensor_scalar", dict(out=sp1[:], in0=gg[:], scalar1=0.0, scalar2=None, op0=ALU.max), reads=[r_g], writes=[r_g])
        P.op("dve", "scalar_tensor_tensor", dict(out=gg[:], in0=sp3[:], scalar=2.0, in1=sp1[:], op0=ALU.mult, op1=ALU.add), reads=[r_g], writes=[r_g])
        P.op("act", "activation", dict(out=alog_s[:], in_=alog_s[:], func=AF.Exp), reads=[r_g], writes=[r_g])
        P.op("dve", "scalar_tensor_tensor", dict(out=gg[:], in0=gg[:], scalar=-1.0, in1=alog_s[:, None, :].to_broadcast([128, NCH, NC2]),
                                                 op0=ALU.mult, op1=ALU.mult), reads=[r_g], writes=[r_g])
        ggv = gg[:].rearrange("p c (d h) -> p c d h", d=2)
        gcv = gc[:].rearrange("p c (d h) -> p c d h", d=2)
        gtv = gtot[:].rearrange("p c (d h) -> p c d h", d=2)
        psv = ps_m[:, 0:NCH * NC2].rearrange("p (c d h) -> p c d h", c=NCH, d=2)
        pst = ps_m[:, 256:256 + NCH * NC2].rearrange("p (c d h) -> p c d h", c=NCH, d=2)
        assert NCH * NC2 <= 256
        for d in range(2):
            P.op("pe", "matmul", dict(out=psv[:, :, d, :], lhsT=ucum[:, d, :], rhs=ggv[:, :, d, :], start=(d == 0), stop=True, skip_group_check=True),
                 reads=[r_c, r_g], writes=[r_pm])
        P.op("pe", "matmul", dict(out=ps_m[:, 256:256 + NCH * NC2], lhsT=onesf[:], rhs=gg[:].rearrange("p c n -> p (c n)"),
                                  start=False, stop=True, skip_group_check=True), reads=[r_c, r_g], writes=[r_pm])
        P.op("dve", "tensor_copy", dict(out=gc[:].rearrange("p c n -> p (c n)"), in_=ps_m[:, 0:NCH * NC2]), reads=[r_pm], writes=[r_g])
        P.op("dve", "tensor_copy", dict(out=gtot[:].rearrange("p c n -> p (c n)"), in_=ps_m[:, 256:256 + NCH * NC2]), reads=[r_pm], writes=[r_g])
        P.op("dve", "tensor_scalar", dict(out=ngc[:], in0=gc[:], scalar1=-1.0, scalar2=None, op0=ALU.mult), reads=[r_g], writes=[r_g])
        P.op("act", "activation", dict(out=egc[:], in_=gc[:], func=AF.Exp), reads=[r_g], writes=[r_g])
        P.op("dve", "tensor_tensor", dict(out=begc[:], in0=egc[:], in1=beta[:], op=ALU.mult), reads=[r_g], writes=[r_g])
        P.op("dve", "tensor_tensor", dict(out=ekd[:], in0=gtot[:], in1=gc[:], op=ALU.subtract), reads=[r_g], writes=[r_g])
        P.op("act", "activation", dict(out=ekd[:], in_=ekd[:], func=AF.Exp), reads=[r_g], writes=[r_g])
        P.op("act", "activation", dict(out=egl[:], in_=gtot[:], func=AF.Exp), reads=[r_g], writes=[r_g])

        NHX = NH if STOP >= 1 else 0
        xin = sb("xin", [128, S + 4], F32)
        acc = sb("acc", [128, S], F32)
        sqb = sb("sqb", [128, S], BF16)
        fT = [sb("fT%d" % i, [128, S], BF16) for i in range(3)]
        kbg = [sb("kbg%d" % i, [128, NCH, 128], BF16) for i in range(2)]
        kdd = [sb("kdd%d" % i, [128, NCH, 128], BF16) for i in range(2)]
        vbd = [sb("vbd%d" % i, [128, NCH, 128], BF16) for i in range(2)]
        oacc = sb("oacc", [128, NCH, 128], F32)
        zt = sb("zt", [128, NCH, 128], F32)
        rn = sb("rn", [128, 512], F32)
        ssn = sb("ssn", [128, NCH], F32)
        ogb = sb("ogb", [128, NCH, 128], BF16)
        oTs = sb("oTs", [128, S], BF16)
        r_xin, r_acc, r_sqb, r_rn = R(), R(), R(), R()
        r_fT = [R(), R(), R()]
        r_tok = R()
        r_oacc = [R() for c in range(NCH)]
        r_zt, r_ssn, r_ogb, r_oTs = R(), R(), R(), R()
        Gb = [[sb("Gb%d%d" % (d, i), [128, 128], F32) for i in range(2)] for d in range(2)]
        dec = [[sb("dec%d%d" % (d, i), [128, 2, 128], F32) for i in range(2)] for d in range(2)]
        Nb = [[sb("Nb%d%d" % (d, i), [128, 128], CHD) for i in range(2)] for d in range(2)]
        Mb = [[sb("Mb%d%d" % (d, i), [128, 128], CHD) for i in range(2)] for d in range(2)]
        Pb = [[sb("Pb%d%d" % (d, i), [128, 128], CHD) for i in range(2)] for d in range(2)]
        TT = [[sb("TT%d%d" % (d, i), [128, 128], BF16) for i in range(2)] for d in range(2)]
        wTn = [[sb("wTn%d%d" % (d, i), [128, 128], BF16) for i in range(2)] for d in range(2)]
        atT = [[sb("atT%d%d" % (d, i), [128, 128], BF16) for i in range(2)] for d in range(2)]
        vnew = [sb("vnew%d" % d, [128, 128], BF16) for d in range(2)]
        tmpo = [sb("tmpo%d" % d, [128, 128], F32) for d in range(2)]
        tmpo2 = [sb("tmpo2%d" % d, [128, 128], F32) for d in range(2)]
        Sf = [sb("Sf%d" % d, [128, 128], F32) for d in range(2)]
        Sb_ = [sb("Sb%d" % d, [128, 128], BF16) for d in range(2)]
        Sl_ = [sb("Sl%d" % d, [128, 128], BF16) for d in range(2)]
        vnl = [sb("vnl%d" % d, [128, 128], BF16) for d in range(2)]
        r_Gb = [[R(), R()], [R(), R()]]
        r_dec = [[R(), R()], [R(), R()]]
        r_N = [[R(), R()], [R(), R()]]
        r_M = [[R(), R()], [R(), R()]]
        r_P = [[R(), R()], [R(), R()]]
        r_TT = [[R(), R()], [R(), R()]]
        r_w = [[R(), R()], [R(), R()]]
        r_at = [[R(), R()], [R(), R()]]
        r_vn, r_to, r_to2, r_Sf, r_Sb = [R(), R()], [R(), R()], [R(), R()], [R(), R()], [R(), R()]
        ps_X = [psb("ps_X%d" % d) for d in range(2)]
        ps_kk = psb("ps_kk")
        ps_ch = [psb("ps_ch%d" % d) for d in range(2)]
        ps_sc = [psb("ps_sc%d" % d) for d in range(2)]
        r_pX, r_pch, r_psc = [R(psum=True), R(psum=True)], [R(psum=True), R(psum=True)], [R(psum=True), R(psum=True)]
        r_pkk = R(psum=True)
        ps_tb = ps_m[:].bitcast(BF16)

        for h in range(NHX):
            for t in range(3):
                row = (h * 3 + t) * 128
                P.op("pool", "memset", dict(ap=xin[:, 0:2], constant=0.0), writes=[r_xin])
                P.op("pool", "memset", dict(ap=xin[:, S + 2:S + 4], constant=0.0), writes=[r_xin])
                P.op("sp", "dma_start", dict(out=xin[:, 2:S + 2], in_=aqkvT[row:row + 128, :]), writes=[r_xin], dma=True)
                P.op("dve", "tensor_scalar", dict(out=acc[:], in0=xin[:, 0:S], scalar1=cws[:, h * 3 + t, 0:1], scalar2=None, op0=ALU.mult),
                     reads=[r_xin, r_c], writes=[r_acc])
                for w in range(1, 5):
                    P.op("dve", "scalar_tensor_tensor", dict(out=acc[:], in0=xin[:, w:w + S], scalar=cws[:, h * 3 + t, w:w + 1], in1=acc[:],
                                                             op0=ALU.mult, op1=ALU.add), reads=[r_xin, r_c, r_acc], writes=[r_acc])
                if t == 2:
                    P.op("act", "activation", dict(out=fT[2][:], in_=acc[:], func=AF.Silu), reads=[r_acc], writes=[r_fT[2]])
                else:
                    P.op("act", "activation", dict(out=acc[:], in_=acc[:], func=AF.Silu), reads=[r_acc], writes=[r_acc])
                    P.op("act", "activation", dict(out=sqb[:], in_=acc[:], func=AF.Square), reads=[r_acc], writes=[r_sqb])
                    for b in range(NB):
                        bs = slice(b * 512, (b + 1) * 512)
                        P.op("pe", "matmul", dict(out=ps_m[:], lhsT=onesb[:], rhs=sqb[:, bs], start=True, stop=True),
                             reads=[r_c, r_sqb], writes=[r_pm])
                        P.op("act", "activation", dict(out=rn[:], in_=ps_m[:], func=AF.Sqrt, bias=epsb[:, 0:1]), reads=[r_pm, r_c], writes=[r_rn])
                        P.op("dve", "reciprocal", dict(out=rn[:], in_=rn[:]), reads=[r_rn], writes=[r_rn])
                        P.op("dve", "scalar_tensor_tensor", dict(out=fT[t][:, bs], in0=acc[:, bs], scalar=(128 ** -0.5 if t == 0 else 1.0), in1=rn[:],
                                                                 op0=ALU.mult, op1=ALU.mult), reads=[r_acc, r_rn], writes=[r_fT[t]])
            if STOP < 2:
                continue
            for c4 in range(0, NCH, 4):
                for t in (1, 2):
                    for j in range(4):
                        c = c4 + j
                        P.op("pe", "transpose", dict(out=ps_tb[:, j * 128:(j + 1) * 128], in_=fT[t][:, c * 128:(c + 1) * 128], identity=identb[:]),
                             reads=[r_fT[t], r_c], writes=[r_pm])
                    src = ps_tb[:, 0:512].rearrange("p (c k) -> p c k", c=4)
                    for d in range(2):
                        col = d * NH + h
                        if t == 1:
                            P.op("dve", "tensor_tensor", dict(out=kbg[d][:, c4:c4 + 4, :], in0=src,
                                                              in1=begc[:, c4:c4 + 4, col:col + 1].to_broadcast([128, 4, 128]), op=ALU.mult),
                                 reads=[r_pm, r_g], writes=[r_tok])
                            P.op("dve", "tensor_tensor", dict(out=kdd[d][:, c4:c4 + 4, :], in0=src,
                                                              in1=ekd[:, c4:c4 + 4, col:col + 1].to_broadcast([128, 4, 128]), op=ALU.mult),
                                 reads=[r_pm, r_g], writes=[r_tok])
                        else:
                            P.op("dve", "tensor_tensor", dict(out=vbd[d][:, c4:c4 + 4, :], in0=src,
                                                              in1=beta[:, c4:c4 + 4, col:col + 1].to_broadcast([128, 4, 128]), op=ALU.mult),
                                 reads=[r_pm, r_g], writes=[r_tok])
            if STOP < 3:
                continue
            P.op("sp", "dma_start", dict(out=zt[:], in_=az[:, h * 128:(h + 1) * 128].rearrange("(c p) n -> p c n", p=128)), writes=[r_zt], dma=True)

            for d in range(2):
                P.op("pool", "memset", dict(ap=Sf[d][:], constant=0.0), writes=[r_Sf[d]])
                P.op("pool", "memset", dict(ap=Sb_[d][:], constant=0.0), writes=[r_Sb[d]])
                P.op("pool", "memset", dict(ap=Sl_[d][:], constant=0.0), writes=[r_Sb[d]])

            def precompute(d, c, par):
                col = d * NH + h
                cs = slice(c * 128, (c + 1) * 128)
                P.op("dve", "tensor_scalar", dict(out=Gb[d][par][:], in0=onesf[:], scalar1=gg[:, c, col:col + 1], scalar2=None, op0=ALU.mult),
                     reads=[r_c, r_g], writes=[r_Gb[d][par]])
                X = ps_X[d]
                P.op("pe", "matmul", dict(out=X[:, 0:128], lhsT=Gb[d][par][:], rhs=ucum[:, d, :], start=True, stop=False, skip_group_check=True),
                     reads=[r_Gb[d][par], r_c], writes=[r_pX[d]])
                P.op("pe", "matmul", dict(out=X[:, 0:128], lhsT=identf[:], rhs=nm[:, 2 * d, :], start=False, stop=True, skip_group_check=True),
                     reads=[r_c], writes=[r_pX[d]])
                P.op("pe", "matmul", dict(out=X[:, 128:256], lhsT=Gb[d][par][:], rhs=ucum[:, d, :], start=False, stop=False, skip_group_check=True),
                     reads=[r_Gb[d][par], r_c], writes=[r_pX[d]])
                P.op("pe", "matmul", dict(out=X[:, 128:256], lhsT=identf[:], rhs=nm[:, 2 * d + 1, :], start=False, stop=True, skip_group_check=True),
                     reads=[r_c], writes=[r_pX[d]])
                yield
                P.op("act", "activation", dict(out=dec[d][par][:, 0, :], in_=X[:, 0:128], func=AF.Exp, scale=-1.0, bias=gc[:, c, col:col + 1]),
                     reads=[r_pX[d], r_g], writes=[r_dec[d][par]])
                P.op("act", "activation", dict(out=dec[d][par][:, 1, :], in_=X[:, 128:256], func=AF.Exp, scale=1.0, bias=ngc[:, c, col:col + 1]),
                     reads=[r_pX[d], r_g], writes=[r_dec[d][par]])
                if SUB < 1:
                    return
                P.op("pe", "matmul", dict(out=X[:, 256:384], lhsT=fT[1][:, cs], rhs=fT[1][:, cs], start=False, stop=True, skip_group_check=True),
                     reads=[r_fT[1]], writes=[r_pX[d]])
                P.op("pe", "matmul", dict(out=X[:, 384:512], lhsT=fT[1][:, cs], rhs=fT[0][:, cs], start=False, stop=True, skip_group_check=True),
                     reads=[r_fT[1], r_fT[0]], writes=[r_pX[d]])
                yield
                P.op("dve", "scalar_tensor_tensor", dict(out=Nb[d][par][:], in0=X[:, 256:384], scalar=nbeta[:, c, col:col + 1], in1=dec[d][par][:, 0, :],
                                                         op0=ALU.mult, op1=ALU.mult), reads=[r_pX[d], r_g, r_dec[d][par]], writes=[r_N[d][par]])
                P.op("dve", "tensor_tensor", dict(out=atT[d][par][:], in0=X[:, 384:512], in1=dec[d][par][:, 1, :], op=ALU.mult),
                     reads=[r_pX[d], r_dec[d][par]], writes=[r_at[d][par]])
                yield
                if SUB < 2:
                    return
                ch = ps_ch[d]
                P.op("pe", "matmul", dict(out=ch[:, 128:256], lhsT=Nb[d][par][:], rhs=identc[:], start=True, stop=True, skip_group_check=True),
                     reads=[r_N[d][par], r_c], writes=[r_pch[d]])
                if SUB == 2 and cfg.get("sub2", 0) == 1:
                    P.op("act", "copy", dict(out=Mb[d][par][:], in_=ch[:, 128:256]), reads=[r_pch[d]], writes=[r_M[d][par]])
                    return
                P.op("pe", "matmul", dict(out=ch[:, 256:384], lhsT=identc[:], rhs=identc[:], start=False, stop=False, skip_group_check=True),
                     reads=[r_c], writes=[r_pch[d]])
                P.op("pe", "matmul", dict(out=ch[:, 256:384], lhsT=Nb[d][par][:], rhs=identc[:], start=False, stop=True, skip_group_check=True),
                     reads=[r_N[d][par], r_c], writes=[r_pch[d]])
                yield
                P.op("act", "copy", dict(out=Mb[d][par][:], in_=ch[:, 128:256]), reads=[r_pch[d]], writes=[r_M[d][par]])
                if cfg.get("sub2", 0) == 2:
                    P.op("act", "copy", dict(out=Pb[d][par][:], in_=ch[:, 256:384]), reads=[r_pch[d]], writes=[r_P[d][par]])
                else:
                    P.op("dve", "tensor_scalar", dict(scalar1=1.0, scalar2=None, op0=ALU.mult, out=Pb[d][par][:], in0=ch[:, 256:384]), reads=[r_pch[d]], writes=[r_P[d][par]])
                if SUB < 3:
                    return
                for k in range(6):
                    P.op("pe", "matmul", dict(out=ch[:, 0:128], lhsT=Mb[d][par][:], rhs=Nb[d][par][:], start=True, stop=True, skip_group_check=True),
                         reads=[r_M[d][par], r_N[d][par]], writes=[r_pch[d]])
                    if k < 5:
                        P.op("pe", "matmul", dict(out=ch[:, 128:256], lhsT=Nb[d][par][:], rhs=Mb[d][par][:], start=False, stop=True, skip_group_check=True),
                             reads=[r_M[d][par], r_N[d][par]], writes=[r_pch[d]])
                    yield
                    P.op("act", "copy", dict(out=Nb[d][par][:], in_=ch[:, 0:128]), reads=[r_pch[d]], writes=[r_N[d][par]])
                    if k < 5:
                        P.op("dve", "tensor_scalar", dict(scalar1=1.0, scalar2=None, op0=ALU.mult, out=Mb[d][par][:], in0=ch[:, 128:256]), reads=[r_pch[d]], writes=[r_M[d][par]])
                    yield
                    P.op("pe", "matmul", dict(out=ch[:, 256:384], lhsT=identc[:], rhs=Pb[d][par][:], start=False, stop=False, skip_group_check=True),
                         reads=[r_c, r_P[d][par]], writes=[r_pch[d]])
                    P.op("pe", "matmul", dict(out=ch[:, 256:384], lhsT=Nb[d][par][:], rhs=Pb[d][par][:], start=False, stop=True, skip_group_check=True),
                         reads=[r_N[d][par], r_P[d][par]], writes=[r_pch[d]])
                    yield
                    if k < 5:
                        P.op("dve", "tensor_scalar", dict(scalar1=1.0, scalar2=None, op0=ALU.mult, out=Pb[d][par][:], in0=ch[:, 256:384]), reads=[r_pch[d]], writes=[r_P[d][par]])
                    else:
                        P.op("dve", "tensor_scalar", dict(scalar1=1.0, scalar2=None, op0=ALU.mult, out=TT[d][par][:], in0=ch[:, 256:384]), reads=[r_pch[d]], writes=[r_TT[d][par]])
                if SUB < 4:
                    return
                yield
                P.op("pe", "matmul", dict(out=ch[:, 384:512], lhsT=kbg[d][:, c, :], rhs=TT[d][par][:], start=False, stop=True, skip_group_check=True),
                     reads=[r_tok, r_TT[d][par]], writes=[r_pch[d]])
                yield
                P.op("act", "activation", dict(out=wTn[d][par][:], in_=ch[:, 384:512], func=AF.Copy, scale=-1.0), reads=[r_pch[d]], writes=[r_w[d][par]])

            first_visit = [True] * NCH

            def scan(d, c, par):
                col = d * NH + h
                cs = slice(c * 128, (c + 1) * 128)
                sc = ps_sc[d]
                P.op("pe", "matmul", dict(out=sc[:, 0:128], lhsT=TT[d][par][:], rhs=vbd[d][:, c, :], start=True, stop=False, skip_group_check=True),
                     reads=[r_TT[d][par], r_tok], writes=[r_psc[d]])
                P.op("pe", "matmul", dict(out=sc[:, 0:128], lhsT=wTn[d][par][:], rhs=Sb_[d][:], start=False, stop=False, skip_group_check=True),
                     reads=[r_w[d][par], r_Sb[d]], writes=[r_psc[d]])
                P.op("pe", "matmul", dict(out=sc[:, 0:128], lhsT=wTn[d][par][:], rhs=Sl_[d][:], start=False, stop=True, skip_group_check=True),
                     reads=[r_w[d][par], r_Sb[d]], writes=[r_psc[d]])
                yield
                P.op("act", "copy", dict(out=vnew[d][:], in_=sc[:, 0:128]), reads=[r_psc[d]], writes=[r_vn[d]])
                P.op("dve", "tensor_tensor", dict(out=vnl[d][:], in0=sc[:, 0:128], in1=vnew[d][:], op=ALU.subtract), reads=[r_psc[d], r_vn[d]], writes=[r_vn[d]])
                yield
                P.op("pe", "matmul", dict(out=sc[:, 128:256], lhsT=fT[0][:, cs], rhs=Sb_[d][:], start=False, stop=False, skip_group_check=True),
                     reads=[r_fT[0], r_Sb[d]], writes=[r_psc[d]])
                P.op("pe", "matmul", dict(out=sc[:, 128:256], lhsT=fT[0][:, cs], rhs=Sl_[d][:], start=False, stop=True, skip_group_check=True),
                     reads=[r_fT[0], r_Sb[d]], writes=[r_psc[d]])
                for vv_ in (vnew, vnl):
                    P.op("pe", "matmul", dict(out=sc[:, 256:384], lhsT=atT[d][par][:], rhs=vv_[d][:], start=False, stop=(vv_ is vnl), skip_group_check=True),
                         reads=[r_at[d][par], r_vn[d]], writes=[r_psc[d]])
                for vv_ in (vnew, vnl):
                    P.op("pe", "matmul", dict(out=sc[:, 384:512], lhsT=kdd[d][:, c, :], rhs=vv_[d][:], start=False, stop=(vv_ is vnl), skip_group_check=True),
                         reads=[r_tok, r_vn[d]], writes=[r_psc[d]])
                yield
                P.op("act", "copy", dict(out=tmpo[d][:], in_=sc[:, 256:384]), reads=[r_psc[d]], writes=[r_to[d]])
                if first_visit[c]:
                    first_visit[c] = False
                    P.op("dve", "scalar_tensor_tensor", dict(out=oacc[:, c, :], in0=sc[:, 128:256], scalar=egc[:, c, col:col + 1], in1=tmpo[d][:],
                                                             op0=ALU.mult, op1=ALU.add), reads=[r_psc[d], r_g, r_to[d]], writes=[r_oacc[c]])
                else:
                    P.op("dve", "scalar_tensor_tensor", dict(out=tmpo2[d][:], in0=sc[:, 128:256], scalar=egc[:, c, col:col + 1], in1=tmpo[d][:],
                                                             op0=ALU.mult, op1=ALU.add), reads=[r_psc[d], r_g, r_to[d]], writes=[r_to2[d]])
                    P.op("dve", "tensor_tensor", dict(out=oacc[:, c, :], in0=oacc[:, c, :], in1=tmpo2[d][:], op=ALU.add),
                         reads=[r_to2[d], r_oacc[c]], writes=[r_oacc[c]])
                P.op("dve", "scalar_tensor_tensor", dict(out=Sf[d][:], in0=Sf[d][:], scalar=egl[:, c, col:col + 1], in1=sc[:, 384:512],
                                                         op0=ALU.mult, op1=ALU.add), reads=[r_psc[d], r_g, r_Sf[d]], writes=[r_Sf[d]])
                yield
                P.op("act", "copy", dict(out=Sb_[d][:], in_=Sf[d][:]), reads=[r_Sf[d]], writes=[r_Sb[d]])
                P.op("dve", "tensor_tensor", dict(out=Sl_[d][:], in0=Sf[d][:], in1=Sb_[d][:], op=ALU.subtract), reads=[r_Sf[d], r_Sb[d]], writes=[r_Sb[d]])

            order = [[c for c in range(NCH)], [NCH - 1 - c for c in range(NCH)]]
            def run_gens(gens):
                gens = list(gens)
                while gens:
                    alive = []
                    for g_ in gens:
                        try:
                            next(g_)
                            alive.append(g_)
                        except StopIteration:
                            pass
                    gens = alive
            run_gens([precompute(d, order[d][0], 0) for d in range(2)])
            for s in range(NCH):
                gens = []
                if s + 1 < NCH:
                    gens += [precompute(d, order[d][s + 1], (s + 1) % 2) for d in range(2)]
                gens += [scan(d, order[d][s], s % 2) for d in range(2)]
                run_gens(gens)

            allo = r_oacc
            accv = acc[:].rearrange("p (c k) -> p c k", c=NCH)
            P.op("dve", "tensor_tensor", dict(out=accv, in0=oacc[:], in1=oacc[:], op=ALU.mult), reads=allo + [r_acc], writes=[r_acc])
            P.op("dve", "tensor_reduce", dict(out=ssn[:], in_=accv, axis=AX.X, op=ALU.add), reads=[r_acc], writes=[r_ssn])
            P.op("act", "activation", dict(out=ssn[:], in_=ssn[:], func=AF.Sqrt, scale=1.0 / 128, bias=epsb[:, 0:1]), reads=[r_ssn, r_c], writes=[r_ssn])
            P.op("dve", "reciprocal", dict(out=ssn[:], in_=ssn[:]), reads=[r_ssn], writes=[r_ssn])
            P.op("dve", "tensor_tensor", dict(out=accv, in0=oacc[:], in1=ssn[:, :, None].to_broadcast([128, NCH, 128]), op=ALU.mult),
                 reads=allo + [r_ssn, r_acc], writes=[r_acc])
            P.op("dve", "tensor_tensor", dict(out=accv, in0=accv, in1=onorm_s[:, None, :].to_broadcast([128, NCH, 128]), op=ALU.mult),
                 reads=[r_c, r_acc], writes=[r_acc])
            P.op("act", "activation", dict(out=zt[:], in_=zt[:], func=AF.Silu), reads=[r_zt], writes=[r_zt])
            P.op("dve", "tensor_tensor", dict(out=ogb[:], in0=accv, in1=zt[:], op=ALU.mult), reads=[r_acc, r_zt], writes=[r_ogb])
            for c4 in range(0, NCH, 4):
                for j in range(4):
                    c = c4 + j
                    P.op("pe", "matmul", dict(out=ps_m[:, j * 128:(j + 1) * 128], lhsT=ogb[:, c, :], rhs=identb[:], start=(j == 0), stop=True, skip_group_check=True),
                         reads=[r_ogb, r_c], writes=[r_pm])
                P.op("act", "copy", dict(out=oTs[:, c4 * 128:(c4 + 4) * 128], in_=ps_m[:, 0:512]), reads=[r_pm], writes=[r_oTs])
            if FUSED:
                for th_ in range(2):
                    P.op("sp", "dma_start", dict(out=oaT[h * 2 + th_].rearrange("(q p) t -> p q t", q=4),
                                                 in_=oTs[:].rearrange("p (q h t) -> p q h t", q=4, h=2)[:, :, th_, :]), reads=[r_oTs], dma=True)
            else:
                P.op("sp", "dma_start", dict(out=oaT[h * 128:(h + 1) * 128, :], in_=oTs[:]), reads=[r_oTs], dma=True)
        P.emit()
    if FUSED:
        nc.all_engine_barrier()
    return nc

I32 = mybir.dt.int32


def relayout_emit(nc, jobs, idx_ap):
    P = Prog(nc)
    with ExitStack() as es:
        pfx = uniq()
        sb = lambda name, shape, dt: es.enter_context(nc.sbuf_tensor(pfx + name, shape, dt))
        NBUF = 6
        bf = [sb("rl_f%d" % i, [128, 1024], F32) for i in range(NBUF)]
        bb = [sb("rl_b%d" % i, [128, 1024], BF16) for i in range(NBUF)]
        rf = [Reg() for i in range(NBUF)]
        rb = [Reg() for i in range(NBUF)]
        idx_s = sb("rl_idx", [128, 8], I32)
        r_idx = Reg()
        P.op("sp", "dma_start", dict(out=idx_s[:], in_=idx_ap), writes=[r_idx], dma=True)
        kf = kb = 0
        for in_ap, ic, out_ap, dt in jobs:
            W = out_ap.shape[-1]
            if dt == F32:
                buf, rr = bf[kf % NBUF], rf[kf % NBUF]
                kf += 1
            else:
                buf, rr = bb[kb % NBUF], rb[kb % NBUF]
                kb += 1
            G_, off_ = in_ap
            assert G_.shape[1] == W
            P.op("pool", "indirect_dma_start",
                 dict(out=buf[:, 0:W], out_offset=None, in_offset=bass.IndirectOffsetOnAxis(ap=idx_s[:, ic:ic + 1], axis=0), **gat(G_, off_, 0, W)),
                 reads=[r_idx], writes=[rr], dma=True)
            P.op("sp", "dma_start", dict(out=out_ap, in_=buf[:, 0:W]), reads=[rr], dma=True)
        P.emit()
    nc.all_engine_barrier()


def allgather_emit(nc, pairs):
    rg = [[0, 1, 2, 3], [4, 5, 6, 7]]
    sem = nc.alloc_semaphore(name=uniq() + "ag")
    with nc.Block() as block:
        @block.gpsimd
        def _(g):
            for src, dst in pairs:
                g.collective_compute(kind="AllGather", op=ALU.bypass, replica_groups=rg, ins=[src.opt()], outs=[dst.opt()]).then_inc(sem)
            g.wait_ge(sem, len(pairs))
    nc.all_engine_barrier()
    nc.clear_and_free_semaphores([sem])
    nc.all_engine_barrier()


def build_fused(stop=99):
    S, D, TPC, DFF = 4096, 4096, 1024, 11008
    nc = bass.Bass("TRN2", target_bir_lowering=False)
    ein = lambda name, shape, dt=F32: nc.dram_tensor(name, shape, dt, kind="ExternalInput").ap()
    itn = lambda name, shape, dt=F32: nc.dram_tensor(name, shape, dt, kind="Internal").ap()
    KC = D // 128
    SHAPES = {}
    SHAPES["xT"] = ([D, TPC], F32)
    SHAPES["idx"] = ([128, 8], I32)
    SHAPES["nw_fin"] = ([128, KC], F32)
    SHAPES["Win0"] = ([D, 17472], F32)
    SHAPES["Wo0"] = ([3072, D], F32)
    SHAPES["Wqkv"] = ([D, 6144], F32)
    SHAPES["Wo1"] = ([4096, D], F32)
    SHAPES["ropeCb"] = ([32, TPC], F32)
    SHAPES["ropeSb"] = ([32, TPC], F32)
    SHAPES["ropePb"] = ([32, 32], F32)
    SHAPES["ropeCc"] = ([128, TPC], F32)
    SHAPES["ropeSc"] = ([128, TPC], F32)
    SHAPES["ropePc"] = ([128, 128], F32)
    SHAPES["gq"] = ([128, 1], F32)
    SHAPES["gk"] = ([128, 1], F32)
    SHAPES["cw"] = ([128, 12, 5], F32)
    SHAPES["alog"] = ([128, 8], F32)
    SHAPES["dtb"] = ([128, 8], F32)
    SHAPES["onorm"] = ([128, 128], F32)
    SHAPES["ident_in"] = ([128, 128], F32)
    SHAPES["ucum_in"] = ([128, 2, 128], F32)
    SHAPES["nm_in"] = ([128, 4, 128], F32)
    SHAPES["bmask"] = ([128, 3, 512], BF16)
    for l in range(2):
        for nm_, sh_ in (("nw_mix", [128, KC]), ("nw_ffn", [128, KC]), ("Wg", [D, DFF]), ("Wu", [D, DFF]), ("Wd", [DFF, D])):
            SHAPES["%s%d" % (nm_, l)] = (sh_, F32)

    class _Lazy(dict):
        def __missing__(self, k):
            sh_, dt_ = SHAPES[k]
            v = ein(k, sh_, dt_)
            self[k] = v
            return v
    E = _Lazy()
    outT = nc.dram_tensor("outT", [D, TPC], F32, kind="ExternalOutput").ap()

    pairs1, pairs2, pairs3, pairs4 = [], [], [], []
    p1 = [[], []]
    p3 = [[], []]

    def blk(name, rows, W, dt, pairs):
        src = itn("s_" + name, [rows, W], dt)
        dst = itn("g_" + name, [4 * rows, W], dt)
        pairs.append((src, dst))
        return src, dst
    A_b = [[blk("aq%d_%d" % (ps, i), 512, 512, F32, p1[ps]) for i in range(12)] for ps in range(2)]
    Q_b = [[blk("bq%d_%d" % (ps, i), 1024, 512, BF16, p1[ps]) for i in range(6)] for ps in range(2)]
    Z_b = [[blk("az%d_%d" % (ps, i), 512, 512, F32, p1[ps]) for i in range(4)] for ps in range(2)]
    V_b = [[blk("bv%d_%d" % (ps, i), 512, 768, BF16, p1[ps]) for i in range(4)] for ps in range(2)]
    AB_b = [blk("ab%d" % ps, 2048, 16, F32, p1[ps]) for ps in range(2)]
    OA_b = [blk("oa%d" % i, 512, 512, BF16, pairs2) for i in range(8)]
    OB_b = [blk("ob%d" % i, 512, 512, BF16, pairs2) for i in range(4)]
    CQ_b = [[blk("cq%d_%d" % (ps, i), 512, 512, BF16, p3[ps]) for i in range(8)] for ps in range(2)]
    CK_b = [[blk("ck%d_%d" % (ps, i), 512, 512, BF16, p3[ps]) for i in range(2)] for ps in range(2)]
    CV_b = [[blk("cv%d_%d" % (ps, i), 512, 256, BF16, p3[ps]) for i in range(4)] for ps in range(2)]
    OC_b = [blk("oc%d" % i, 512, 512, BF16, pairs4) for i in range(16)]
    aqkvT_l = itn("l_aqkvT", [1536, S])
    az_l = itn("l_az", [S, 512])
    abr_l = itn("l_abr", [S, 16])
    bqT_l = itn("l_bqT", [768, S], BF16)
    bkT_l = itn("l_bkT", [768, S], BF16)
    bv_l = itn("l_bv", [S, 768], BF16)
    x2T = itn("i_x2T", [D, TPC])

    sec1 = [dict(aqkv=[tuple(b_) for b_ in A_b[ps]], az=[tuple(b_) for b_ in Z_b[ps]], ab=[tuple(AB_b[ps])],
                 bqk=[tuple(b_) for b_ in Q_b[ps]], bv=[tuple(b_) for b_ in V_b[ps]]) for ps in range(2)]
    sec3 = [dict(cqk=[tuple(b_) for b_ in CQ_b[ps]] + [tuple(b_) for b_ in CK_b[ps]], cv=[tuple(b_) for b_ in CV_b[ps]]) for ps in range(2)]
    def dst_fm1(name, i, ps):
        if name == "aqkv":
            t, head = i // 16, i % 16
            return A_b[ps][t * 4 + head % 4][0][(head // 4) * 128:(head // 4 + 1) * 128, :]
        qk, hd = i // 24, i % 24
        g, hs = hd // 8, hd % 8
        return Q_b[ps][qk * 3 + g][0][hs * 128:(hs + 1) * 128, :]

    def dst_tm1(name, i, ps, tt):
        if name == "az":
            j, hl = i // 4, i % 4
            return Z_b[ps][tt][0][j * 128:(j + 1) * 128, hl * 128:(hl + 1) * 128]
        g, hs = i // 8, i % 8
        j, hl = hs // 2, hs % 2
        return V_b[ps][tt][0][j * 128:(j + 1) * 128, (hl * 3 + g) * 128:(hl * 3 + g + 1) * 128]

    def dst_ab1(ps, g, tt):
        return AB_b[ps][0][g * 512 + tt * 128:g * 512 + (tt + 1) * 128, :]
    ab = dict(nqkv=6144, nz=2048, nab=64, nbqk=6144, nbv=3072)
    tp_build(dict(nc=nc, T_tot=TPC, inproj="ab", ab=ab, dst_fm=dst_fm1, dst_tm=dst_tm1, dst_ab=dst_ab1, cc_sections=sec1,
                  TD=dict(xT=E["xT"], nwb=E["nw_mix0"], Win=E["Win0"], ropeC=E["ropeCb"], ropeS=E["ropeSb"], ropeP=E["ropePb"])))
    if stop == 1:
        return nc
    if stop == 2:
        return nc
    jobs = []
    for hl in range(4):
        for t in range(3):
            for r in range(4):
                for ps in range(2):
                    c0 = r * TPC + ps * 512
                    jobs.append(((A_b[ps][t * 4 + hl][1], r * 512), 0, aqkvT_l[(hl * 3 + t) * 128:(hl * 3 + t + 1) * 128, c0:c0 + 512], F32))
    for r in range(4):
        for ps in range(2):
            for tt in range(4):
                r0 = r * TPC + ps * 512 + tt * 128
                jobs.append(((Z_b[ps][tt][1], r * 512), 0, az_l[r0:r0 + 128, :], F32))
                jobs.append(((V_b[ps][tt][1], r * 512), 0, bv_l[r0:r0 + 128, :], BF16))
                jobs.append(((AB_b[ps][1], r * 2048 + tt * 128), 3, abr_l[r0:r0 + 128, :], F32))
    for hl in range(2):
        for g in range(3):
            for r in range(4):
                for ps in range(2):
                    c0 = r * TPC + ps * 512
                    jobs.append(((Q_b[ps][g][1], r * 1024 + hl * 128), 2, bqT_l[(hl * 3 + g) * 128:(hl * 3 + g + 1) * 128, c0:c0 + 512], BF16))
                    jobs.append(((Q_b[ps][3 + g][1], r * 1024 + hl * 128), 2, bkT_l[(hl * 3 + g) * 128:(hl * 3 + g + 1) * 128, c0:c0 + 512], BF16))
    relayout_emit(nc, jobs, E["idx"])
    if stop == 3:
        return nc
    gdn_build(dict(nc=nc, S=S, NH=4, TD=dict(aqkvT=aqkvT_l, az=az_l, abr=abr_l, cw=E["cw"], alog=E["alog"], dtb=E["dtb"], onorm=E["onorm"],
                                             ident_in=E["ident_in"], ucum_in=E["ucum_in"], nm_in=E["nm_in"], oa_q=[b_[0] for b_ in OA_b])))
    if stop == 4:
        return nc
    dil_b_build(dict(nc=nc, S=S, NHS=2, TD=dict(bqT=bqT_l, bkT=bkT_l, bv=bv_l, bmask=E["bmask"], ob_q=[b_[0] for b_ in OB_b], cc_pairs=pairs2[:8])))
    if stop == 5:
        return nc
    allgather_emit(nc, pairs2[8:])
    if stop == 6:
        return nc
    o_src = []
    for ps in range(2):
        lst = []
        for k in range(16):
            j, hl = k // 4, k % 4
            lst.append((OA_b[hl * 2 + ps][1], j * 512, 0))
        for kk in range(8):
            j, hl = kk // 2, kk % 2
            lst.append((OB_b[hl * 2 + ps][1], j * 512, 0))
        o_src.append(lst)

    def dst_fm3(name, i, ps):
        if i < 32:
            return CQ_b[ps][i % 8][0][(i // 8) * 128:(i // 8 + 1) * 128, :]
        kvh = i - 32
        return CK_b[ps][kvh % 2][0][(kvh // 2) * 128:(kvh // 2 + 1) * 128, :]

    def dst_tm3(name, i, ps, tt):
        return CV_b[ps][tt][0][(i // 2) * 128:(i // 2 + 1) * 128, (i % 2) * 128:(i % 2 + 1) * 128]
    cc = dict(nq=4096, nk=1024, nv=1024)
    tp_build(dict(nc=nc, T_tot=TPC, oproj_K=3072, dff=DFF, inproj="c", c=cc, store_x=True, o_src=o_src, dst_fm=dst_fm3, dst_tm=dst_tm3, cc_sections=sec3,
                  TD=dict(xT=E["xT"], idx=E["idx"], Wo=E["Wo0"], nwa=E["nw_ffn0"], Wg=E["Wg0"], Wu=E["Wu0"], Wd=E["Wd0"], nwb=E["nw_mix1"],
                          Win=E["Wqkv"], ropeC=E["ropeCc"], ropeS=E["ropeSc"], ropeP=E["ropePc"], gq=E["gq"], gk=E["gk"], xoT=x2T)))
    if stop == 7:
        return nc
    if stop == 8:
        return nc
    attn_c_build(dict(nc=nc, S=S, NKV=2, REP=4, TD=dict(GQ=[[b_[1] for b_ in CQ_b[ps]] for ps in range(2)], GK=[[b_[1] for b_ in CK_b[ps]] for ps in range(2)],
                                                          GV=[[b_[1] for b_ in CV_b[ps]] for ps in range(2)], OCB=[b_[0] for b_ in OC_b], OCG=[b_[1] for b_ in OC_b], idx=E["idx"])))
    if stop == 9:
        return nc
    if stop == 10:
        return nc
    o_src = []
    for ps in range(2):
        o_src.append([(OC_b[(k % 8) * 2 + ps][1], (k // 8) * 512, 0) for k in range(32)])
    tp_build(dict(nc=nc, T_tot=TPC, oproj_K=4096, dff=DFF, final_norm=True, o_src=o_src,
                  TD=dict(xT=x2T, idx=E["idx"], Wo=E["Wo1"], nwa=E["nw_ffn1"], Wg=E["Wg1"], Wu=E["Wu1"], Wd=E["Wd1"], nwb=E["nw_fin"], outT=outT)))
    return nc

import ml_dtypes as _mld

_BF = _mld.bfloat16
_PROGS = {}


def _lay(w):
    return np.ascontiguousarray(np.asarray(w, np.float32).reshape(-1, 128).T)


def _rope_tables_b(pos):
    inv = 1.0 / (500000.0 ** (np.arange(0, 32, 2, dtype=np.float32) / 32))
    ang = pos.astype(np.float32)[:, None] * inv[None, :]
    c, s = np.cos(ang), np.sin(ang)
    C = np.ascontiguousarray(np.concatenate([c, c], 1).T.astype(np.float32))
    S = np.ascontiguousarray(np.concatenate([s, s], 1).T.astype(np.float32))
    Pm = np.zeros((32, 32), np.float32)
    for i in range(16):
        Pm[i, 16 + i] = -1
        Pm[16 + i, i] = 1
    return C, S, np.ascontiguousarray(Pm.T)


def _rope_tables_c(pos):
    inv = 1.0 / (10000.0 ** (np.arange(0, 64, 2, dtype=np.float32) / 64))
    ar = (pos // 64).astype(np.float32)[:, None] * inv[None, :]
    ac = (pos % 64).astype(np.float32)[:, None] * inv[None, :]
    C = np.ascontiguousarray(np.concatenate([np.cos(ar), np.cos(ar), np.cos(ac), np.cos(ac)], 1).T.astype(np.float32))
    S = np.ascontiguousarray(np.concatenate([np.sin(ar), np.sin(ar), np.sin(ac), np.sin(ac)], 1).T.astype(np.float32))
    Pm = np.zeros((128, 128), np.float32)
    for i in range(32):
        Pm[i, 32 + i] = -1
        Pm[32 + i, i] = 1
        Pm[64 + i, 96 + i] = -1
        Pm[96 + i, 64 + i] = 1
    return C, S, np.ascontiguousarray(Pm.T)


def kernel(x, norm_mix, norm_ffn, norm_final, ab_w_in, ab_conv_w, ab_a_log, ab_dt_bias,
           ab_out_norm, ab_w_out, c_w_qkv, c_q_norm, c_k_norm, c_w_out,
           ffn_w_gate, ffn_w_up, ffn_w_down):
    f32 = np.float32
    A = lambda a: np.ascontiguousarray(np.asarray(a, f32))
    x = A(x)
    B, S, D = x.shape
    NCORE = 8
    TPC = B * S // NCORE
    QPB = S // TPC
    xf = x.reshape(B * S, D)
    if "F" not in _PROGS:
        _PROGS["F"] = build_fused()
    nc = _PROGS["F"]
    cst = gdn_consts()
    conv_w = A(ab_conv_w[0])
    a_log = A(ab_a_log[0])
    dt_b = A(ab_dt_bias[0])
    shared = dict(
        nw_mix0=_lay(norm_mix[0]), nw_mix1=_lay(norm_mix[1]), nw_ffn0=_lay(norm_ffn[0]), nw_ffn1=_lay(norm_ffn[1]), nw_fin=_lay(norm_final),
        Wg0=A(ffn_w_gate[0]), Wu0=A(ffn_w_up[0]), Wd0=A(ffn_w_down[0]), Wg1=A(ffn_w_gate[1]), Wu1=A(ffn_w_up[1]), Wd1=A(ffn_w_down[1]),
        Win0=A(ab_w_in[0]), Wo0=A(ab_w_out[0]), Wqkv=A(c_w_qkv[0]), Wo1=A(c_w_out[0]),
        gq=A(c_q_norm[0]).reshape(128, 1), gk=A(c_k_norm[0]).reshape(128, 1),
        onorm=np.ascontiguousarray(np.tile(A(ab_out_norm[0])[None, :], (128, 1))),
        ident_in=cst["ident"], ucum_in=cst["ucum"], nm_in=cst["nm"], bmask=dil_mask().astype(_BF))
    ims = []
    p = np.arange(128, dtype=np.int32)
    for c in range(NCORE):
        rk = c % QPB
        pos = np.arange(rk * TPC, (rk + 1) * TPC)
        Cb, Sb, Pb = _rope_tables_b(pos)
        Cc, Sc, Pc = _rope_tables_c(pos)
        heads = [4 * rk + i for i in range(4)]
        cwl = np.zeros((128, 12, 5), f32)
        for i, h in enumerate(heads):
            for t in range(3):
                cwl[:, i * 3 + t, :] = conv_w[:, t * 2048 + h * 128: t * 2048 + (h + 1) * 128].T
        al = np.array([a_log[d, h] for d in range(2) for h in heads], f32)
        db = np.array([dt_b[d, h] for d in range(2) for h in heads], f32)
        idx = np.stack([rk * 128 + p, 2 * (rk * 128 + p), rk * 256 + p, rk * 512 + p, p, p, p, p], 1).astype(np.int32)
        im = dict(shared)
        im.update(xT=np.ascontiguousarray(xf[c * TPC:(c + 1) * TPC].T), idx=np.ascontiguousarray(idx),
                  ropeCb=Cb, ropeSb=Sb, ropePb=Pb, ropeCc=Cc, ropeSc=Sc, ropePc=Pc, cw=cwl,
                  alog=np.ascontiguousarray(np.tile(al[None, :], (128, 1))), dtb=np.ascontiguousarray(np.tile(db[None, :], (128, 1))))
        ims.append(im)
    res = run_bass_kernel_spmd(nc, ims, core_ids=list(range(NCORE))).results
    out = np.concatenate([np.asarray(res[c]["outT"]).T for c in range(NCORE)], 0)
    return np.ascontiguousarray(out.reshape(B, S, D).astype(f32, copy=False))
```
